# Optimizing a Trainium2 kernel written in Bass

```python
import math, functools
import jax, jax.numpy as jnp
from jax import lax
import numpy as np

D_MODEL = 1024
BATCH = 4
SEQ = 4096
DEPTH = 2
DEC_BATCH = 32
DEC_SEQ = 4
PAST_LEN = 8192
PAGE_SIZE = 128

HEAD_DIM = 64
H_A = 6
H_B = 6
H_C = 4
W_A = H_A * HEAD_DIM
W_B = H_B * HEAD_DIM
W_C = H_C * HEAD_DIM
MIX_WIDTH = W_A + W_B + W_C
LORA_W = 64
LORA_A = 64
LORA_G = 128
A_COLS = 3 * W_A + LORA_W + LORA_A + LORA_G
B_COLS = 3 * W_B
C_COLS = 4 * W_C
IN_COLS = A_COLS + B_COLS + C_COLS
DILATED = ((128, 1), (512, 4), (2048, 16))
WIN_MAX = max(w for w, _ in DILATED)
ROT_DIM = HEAD_DIM // 4
ROPE_THETA = 500000.0
RET_THETA = 10000.0
RET_CHUNK = 128
D_FF = -(-8 * D_MODEL // (3 * 256)) * 256
DEEPNORM_ALPHA = (2 * DEPTH) ** 0.25
DEEPNORM_BETA = (8 * DEPTH) ** -0.25
RWKV_DECAY_SCALE = math.exp(-0.5)
RWKV_GN_EPS = 64e-5
LN_EPS = 1e-5
NEG = -1e30

kernel_name = 'hybrid_rwkv7_dilated_retention_step'

F32 = jnp.float32


def _layer_norm(x, g, b):
    mu = jnp.mean(x, axis=-1, keepdims=True)
    var = jnp.mean(jnp.square(x - mu), axis=-1, keepdims=True)
    return (x - mu) * lax.rsqrt(var + LN_EPS) * g + b


def _head_norm(x, g, b, eps):
    h, e = x.shape[-2], x.shape[-1]
    mu = jnp.mean(x, axis=-1, keepdims=True)
    var = jnp.mean(jnp.square(x - mu), axis=-1, keepdims=True)
    return (x - mu) * lax.rsqrt(var + eps) * g.reshape(h, e) + b.reshape(h, e)


def _rope_inv_freq():
    return ROPE_THETA ** (-jnp.arange(0, ROT_DIM, 2, dtype=F32) / ROT_DIM)


def _ret_inv_freq():
    return 1.0 / (RET_THETA ** jnp.linspace(0.0, 1.0, HEAD_DIM // 2, dtype=F32))


def _apply_rotary(x, pos, inv_freq):
    half = inv_freq.shape[0]
    ang = pos[:, None] * inv_freq[None, :]
    cos = jnp.cos(ang)[None, :, None, :]
    sin = jnp.sin(ang)[None, :, None, :]
    x1 = x[..., :half]
    x2 = x[..., half:2 * half]
    return jnp.concatenate([x1 * cos - x2 * sin, x2 * cos + x1 * sin, x[..., 2 * half:]], axis=-1)


def _wkv7_scan(r, w, k, v, kk, a, s0):
    def step(s, inp):
        r_t, w_t, k_t, v_t, kk_t, a_t = inp
        sa = jnp.einsum('bhvk,bhk->bhv', s, kk_t)
        s = (s * w_t[:, :, None, :] - sa[..., None] * (kk_t * a_t)[:, :, None, :]
             + v_t[..., None] * k_t[:, :, None, :])
        return s, jnp.einsum('bhvk,bhk->bhv', s, r_t)
    seq = tuple(jnp.swapaxes(z, 0, 1) for z in (r, w, k, v, kk, a))
    s_fin, o = lax.scan(step, s0, seq)
    return jnp.swapaxes(o, 0, 1), s_fin


def _rwkv7_time_mix(h_a, shift_prev, s0, p):
    bsz, t, _ = h_a.shape
    prev = jnp.concatenate([shift_prev[:, None].astype(F32), h_a[:, :-1]], axis=1)
    xs = h_a + (prev - h_a) * p['rwkv_mu']
    r = xs[..., :W_A]
    k = xs[..., W_A:2 * W_A]
    v = xs[..., 2 * W_A:3 * W_A]
    o0 = 3 * W_A
    xw = xs[..., o0:o0 + LORA_W]
    xa = xs[..., o0 + LORA_W:o0 + LORA_W + LORA_A]
    xg = xs[..., o0 + LORA_W + LORA_A:]
    log_w = -RWKV_DECAY_SCALE * jax.nn.sigmoid(p['rwkv_w0'] + jnp.tanh(xw) @ p['rwkv_w_lora'])
    a = jax.nn.sigmoid(p['rwkv_a0'] + xa @ p['rwkv_a_lora'])
    g = jax.nn.sigmoid(xg) @ p['rwkv_g_lora']
    heads = lambda z: z.reshape(bsz, t, H_A, HEAD_DIM)
    kk = heads(k * p['rwkv_k_k'])
    kk = kk * lax.rsqrt(jnp.maximum(jnp.sum(kk * kk, axis=-1, keepdims=True), 1e-12))
    k = k * (1.0 + (a - 1.0) * p['rwkv_k_a'])
    r_h, k_h, v_h, a_h = heads(r), heads(k), heads(v), heads(a)
    o, s_fin = _wkv7_scan(r_h, jnp.exp(heads(log_w)), k_h, v_h, kk, a_h, s0.astype(F32))
    o = _head_norm(o, p['rwkv_gn_g'], p['rwkv_gn_b'], RWKV_GN_EPS)
    o = o + jnp.sum(r_h * k_h * p['rwkv_r_k'], axis=-1, keepdims=True) * v_h
    return o.reshape(bsz, t, W_A) * g, s_fin


def _dilated_branch_prompt(q, k, v, win, dil):
    bsz, t, h, e = q.shape
    blk = win // dil
    seg = dil * blk
    t_pad = -(-t // seg) * seg
    length = t_pad // dil
    nb = length // blk

    def to_blocks(z):
        z = jnp.pad(z, ((0, 0), (0, t_pad - t), (0, 0), (0, 0)))
        z = z.reshape(bsz, length, dil, h, e).transpose(0, 2, 1, 3, 4)
        return z.reshape(bsz, dil, nb, blk, h, e)

    def with_prev(z):
        prev = jnp.pad(z, ((0, 0), (0, 0), (1, 0), (0, 0), (0, 0), (0, 0)))[:, :, :-1]
        return jnp.concatenate([prev, z], axis=3)

    qb = to_blocks(q)
    kc = with_prev(to_blocks(k))
    vc = with_prev(to_blocks(v))
    s = jnp.einsum('brnqhe,brnkhe->brnhqk', qb, kc) * (e ** -0.5)
    qi = jnp.arange(blk)[:, None]
    ki = jnp.arange(2 * blk)[None, :]
    j = blk + qi - ki
    ok = (j >= 0) & (j <= blk)
    first = (jnp.arange(nb)[:, None, None] == 0) & (ki < blk)[None]
    ok = ok[None] & ~first
    s = jnp.where(ok[:, None], s, NEG)
    m = jnp.max(s, axis=-1, keepdims=True)
    pr = jnp.exp(s - m)
    l = jnp.sum(pr, axis=-1)
    o = jnp.einsum('brnhqk,brnkhe->brnqhe', pr, vc) / jnp.swapaxes(l, -1, -2)[..., None]
    lse = jnp.swapaxes(m[..., 0] + jnp.log(l), -1, -2)

    def from_blocks(z):
        rest = z.shape[4:]
        z = z.reshape((bsz, dil, length) + rest)
        z = jnp.moveaxis(z, 1, 2).reshape((bsz, t_pad) + rest)
        return z[:, :t]

    return from_blocks(o), from_blocks(lse)


def _dilated_branch_sample(q, k_all, v_all, win, dil, w_buf):
    t, e = q.shape[1], q.shape[-1]
    blk = win // dil
    self_idx = w_buf + jnp.arange(t)
    idx = self_idx[:, None] - dil * jnp.arange(blk + 1)[None, :]
    ok = idx >= 0
    idxc = jnp.maximum(idx, 0)
    kg = k_all[:, idxc]
    vg = v_all[:, idxc]
    s = jnp.einsum('bthe,btjhe->bthj', q, kg) * (e ** -0.5)
    s = jnp.where(ok[:, None, :], s, NEG)
    m = jnp.max(s, axis=-1, keepdims=True)
    pr = jnp.exp(s - m)
    l = jnp.sum(pr, axis=-1)
    o = jnp.einsum('bthj,btjhe->bthe', pr, vg) / l[..., None]
    return o, m[..., 0] + jnp.log(l)


def _merge_by_denominator(outs, lses):
    wts = jax.nn.softmax(jnp.stack(lses, axis=0), axis=0)
    return jnp.einsum('gbth,gbthe->bthe', wts, jnp.stack(outs, axis=0))


def _dilated_prompt(q, k, v):
    outs, lses = [], []
    for win, dil in DILATED:
        o, l = _dilated_branch_prompt(q, k, v, win, dil)
        outs.append(o)
        lses.append(l)
    w_keep = min(WIN_MAX, q.shape[1])
    return _merge_by_denominator(outs, lses), k[:, -w_keep:], v[:, -w_keep:]


def _dilated_sample(q, k, v, buf_k, buf_v):
    w_buf = buf_k.shape[1]
    k_all = jnp.concatenate([buf_k.astype(F32), k], axis=1)
    v_all = jnp.concatenate([buf_v.astype(F32), v], axis=1)
    outs, lses = [], []
    for win, dil in DILATED:
        o, l = _dilated_branch_sample(q, k_all, v_all, win, dil, w_buf)
        outs.append(o)
        lses.append(l)
    return _merge_by_denominator(outs, lses), k_all[:, -w_buf:], v_all[:, -w_buf:]


def _retention(q, k, v, r0):
    bsz, t, h, e = q.shape
    ch = RET_CHUNK if t % RET_CHUNK == 0 else t
    nc = t // ch
    log_gamma = jnp.log(1.0 - 2.0 ** (-5.0 - jnp.arange(h, dtype=F32)))
    qc = q.reshape(bsz, nc, ch, h, e)
    kc = k.reshape(bsz, nc, ch, h, e)
    vc = v.reshape(bsz, nc, ch, h, e)
    i = jnp.arange(ch, dtype=F32)
    rel = i[:, None] - i[None, :]
    dmask = jnp.where(rel >= 0, jnp.exp(log_gamma[:, None, None] * jnp.maximum(rel, 0.0)), 0.0)
    att = jnp.einsum('bcihe,bcjhe->bchij', qc, kc) * dmask
    o_intra = jnp.einsum('bchij,bcjhf->bcihf', att, vc)
    kdec = jnp.exp(log_gamma[None, :] * (ch - 1.0 - i)[:, None])
    kv = jnp.einsum('bcjhe,bcjhf->bchef', kc * kdec[:, :, None], vc)
    chunk_decay = jnp.exp(log_gamma * ch)[None, :, None, None]

    def step(r, kv_c):
        return r * chunk_decay + kv_c, r

    r_fin, r_prev = lax.scan(step, r0.astype(F32), jnp.swapaxes(kv, 0, 1))
    r_prev = jnp.swapaxes(r_prev, 0, 1)
    qdec = jnp.exp(log_gamma[None, :] * (i + 1.0)[:, None])
    o_cross = jnp.einsum('bcihe,bchef->bcihf', qc * qdec[:, :, None], r_prev)
    return (o_intra + o_cross).reshape(bsz, t, h, e), r_fin


def _trunk_layer(x, shift_prev, wkv0, ret0, pos0, attn_fn, p):
    bsz, t, _ = x.shape
    xf = x.astype(F32)
    h = jnp.matmul(x, p['w_in']).astype(F32)
    h_a = h[..., :A_COLS]
    h_b = h[..., A_COLS:A_COLS + B_COLS]
    h_c = h[..., A_COLS + B_COLS:]
    pos = pos0 + jnp.arange(t, dtype=F32)

    o_a, wkv_new = _rwkv7_time_mix(h_a, shift_prev, wkv0, p)
    shift_new = h_a[:, -1]

    q_b, k_b, v_b = [z.reshape(bsz, t, H_B, HEAD_DIM) for z in jnp.split(h_b, 3, axis=-1)]
    inv_b = _rope_inv_freq()
    q_b = _apply_rotary(q_b, pos, inv_b)
    k_b = _apply_rotary(k_b, pos, inv_b)
    o_b, k_keep, v_keep = attn_fn(q_b, k_b, v_b)

    q_c, k_c, v_c, g_c = jnp.split(h_c, 4, axis=-1)
    inv_c = _ret_inv_freq()
    q_c = _apply_rotary(q_c.reshape(bsz, t, H_C, HEAD_DIM), pos, inv_c)
    k_c = _apply_rotary(k_c.reshape(bsz, t, H_C, HEAD_DIM), pos, inv_c) * (HEAD_DIM ** -0.5)
    o_c, ret_new = _retention(q_c, k_c, v_c.reshape(bsz, t, H_C, HEAD_DIM), ret0)
    o_c = _head_norm(o_c, p['ret_gn_g'], p['ret_gn_b'], LN_EPS).reshape(bsz, t, W_C) * jax.nn.silu(g_c)

    mix = jnp.concatenate([o_a, o_b.reshape(bsz, t, W_B), o_c], axis=-1) @ p['w_out']
    x1 = _layer_norm(DEEPNORM_ALPHA * xf + mix, p['ln1_g'], p['ln1_b'])
    ffn = (jax.nn.silu(x1 @ p['w_ffn_gate']) * (x1 @ p['w_ffn_up'])) @ p['w_ffn_down']
    x2 = _layer_norm(DEEPNORM_ALPHA * x1 + ffn, p['ln2_g'], p['ln2_b'])
    return x2.astype(x.dtype), (shift_new, wkv_new, k_keep, v_keep, ret_new)


def setup_inputs(seed: int = 0) -> dict:
    key = jax.random.key(seed)
    ks = jax.random.split(key, 32)
    w_buf = min(WIN_MAX, PAST_LEN)

    def nrm(k, shape, s):
        return s * jax.random.normal(k, shape, F32)

    return {
        'x_prompt': nrm(ks[0], (BATCH, SEQ, D_MODEL), 1.0),
        'x_sample': nrm(ks[1], (DEC_BATCH, DEC_SEQ, D_MODEL), 1.0),
        'state_rwkv_shift': nrm(ks[2], (DEPTH, DEC_BATCH, A_COLS), 1.0),
        'state_rwkv_wkv': nrm(ks[3], (DEPTH, DEC_BATCH, H_A, HEAD_DIM, HEAD_DIM), 0.5),
        'cache_win_k': nrm(ks[4], (DEPTH, DEC_BATCH, w_buf, H_B, HEAD_DIM), 1.0),
        'cache_win_v': nrm(ks[5], (DEPTH, DEC_BATCH, w_buf, H_B, HEAD_DIM), 1.0),
        'state_ret': nrm(ks[6], (DEPTH, DEC_BATCH, H_C, HEAD_DIM, HEAD_DIM), 2.0),
        'w_in': nrm(ks[7], (DEPTH, D_MODEL, IN_COLS), D_MODEL ** -0.5),
        'rwkv_mu': jax.random.uniform(ks[8], (DEPTH, A_COLS), F32),
        'rwkv_w0': nrm(ks[9], (DEPTH, W_A), 1.0),
        'rwkv_w_lora': nrm(ks[10], (DEPTH, LORA_W, W_A), 0.1),
        'rwkv_a0': nrm(ks[11], (DEPTH, W_A), 0.5),
        'rwkv_a_lora': nrm(ks[12], (DEPTH, LORA_A, W_A), 0.1),
        'rwkv_g_lora': nrm(ks[13], (DEPTH, LORA_G, W_A), LORA_G ** -0.5),
        'rwkv_k_k': 0.85 + nrm(ks[14], (DEPTH, W_A), 0.05),
        'rwkv_k_a': 1.0 + nrm(ks[15], (DEPTH, W_A), 0.05),
        'rwkv_r_k': nrm(ks[16], (DEPTH, H_A, HEAD_DIM), 0.1),
        'rwkv_gn_g': 1.0 + nrm(ks[17], (DEPTH, W_A), 0.02),
        'rwkv_gn_b': nrm(ks[18], (DEPTH, W_A), 0.02),
        'ret_gn_g': 1.0 + nrm(ks[19], (DEPTH, W_C), 0.02),
        'ret_gn_b': nrm(ks[20], (DEPTH, W_C), 0.02),
        'w_out': nrm(ks[21], (DEPTH, MIX_WIDTH, D_MODEL), MIX_WIDTH ** -0.5 * DEEPNORM_BETA),
        'ln1_g': 1.0 + nrm(ks[22], (DEPTH, D_MODEL), 0.02),
        'ln1_b': nrm(ks[23], (DEPTH, D_MODEL), 0.02),
        'w_ffn_gate': nrm(ks[24], (DEPTH, D_MODEL, D_FF), D_MODEL ** -0.5),
        'w_ffn_up': nrm(ks[25], (DEPTH, D_MODEL, D_FF), D_MODEL ** -0.5),
        'w_ffn_down': nrm(ks[26], (DEPTH, D_FF, D_MODEL), D_FF ** -0.5 * DEEPNORM_BETA),
        'ln2_g': 1.0 + nrm(ks[27], (DEPTH, D_MODEL), 0.02),
        'ln2_b': nrm(ks[28], (DEPTH, D_MODEL), 0.02),
    }


def reference(x_prompt, x_sample, state_rwkv_shift, state_rwkv_wkv, cache_win_k, cache_win_v, state_ret,
              w_in, rwkv_mu, rwkv_w0, rwkv_w_lora, rwkv_a0, rwkv_a_lora, rwkv_g_lora, rwkv_k_k, rwkv_k_a,
              rwkv_r_k, rwkv_gn_g, rwkv_gn_b, ret_gn_g, ret_gn_b, w_out, ln1_g, ln1_b,
              w_ffn_gate, w_ffn_up, w_ffn_down, ln2_g, ln2_b):
    xp, xs = x_prompt, x_sample
    bp = x_prompt.shape[0]
    p_states, s_states = [], []
    for l in range(DEPTH):
        p = {
            'w_in': w_in[l], 'rwkv_mu': rwkv_mu[l], 'rwkv_w0': rwkv_w0[l], 'rwkv_w_lora': rwkv_w_lora[l],
            'rwkv_a0': rwkv_a0[l], 'rwkv_a_lora': rwkv_a_lora[l], 'rwkv_g_lora': rwkv_g_lora[l],
            'rwkv_k_k': rwkv_k_k[l], 'rwkv_k_a': rwkv_k_a[l], 'rwkv_r_k': rwkv_r_k[l],
            'rwkv_gn_g': rwkv_gn_g[l], 'rwkv_gn_b': rwkv_gn_b[l], 'ret_gn_g': ret_gn_g[l], 'ret_gn_b': ret_gn_b[l],
            'w_out': w_out[l], 'ln1_g': ln1_g[l], 'ln1_b': ln1_b[l], 'w_ffn_gate': w_ffn_gate[l],
            'w_ffn_up': w_ffn_up[l], 'w_ffn_down': w_ffn_down[l], 'ln2_g': ln2_g[l], 'ln2_b': ln2_b[l],
        }
        xp, sp = _trunk_layer(
            xp,
            jnp.zeros((bp, A_COLS), F32),
            jnp.zeros((bp, H_A, HEAD_DIM, HEAD_DIM), F32),
            jnp.zeros((bp, H_C, HEAD_DIM, HEAD_DIM), F32),
            0.0, _dilated_prompt, p)
        xs, ss = _trunk_layer(
            xs, state_rwkv_shift[l], state_rwkv_wkv[l], state_ret[l], float(PAST_LEN),
            functools.partial(_dilated_sample, buf_k=cache_win_k[l], buf_v=cache_win_v[l]), p)
        p_states.append(sp)
        s_states.append(ss)
    p_shift = jnp.stack([s[0] for s in p_states])
    p_wkv = jnp.stack([s[1] for s in p_states])
    p_win_k = jnp.stack([s[2] for s in p_states])
    p_win_v = jnp.stack([s[3] for s in p_states])
    p_ret = jnp.stack([s[4] for s in p_states])
    s_shift = jnp.stack([s[0] for s in s_states])
    s_wkv = jnp.stack([s[1] for s in s_states])
    s_win_k = jnp.stack([s[2] for s in s_states])
    s_win_v = jnp.stack([s[3] for s in s_states])
    s_ret = jnp.stack([s[4] for s in s_states])
    return (xp, xs, p_shift, p_wkv, p_win_k, p_win_v, p_ret, s_shift, s_wkv, s_win_k, s_win_v, s_ret)
```

```python
import math
import os
_STOP = float(os.environ.get('KSTOP', '99'))
_NOP2 = int(os.environ.get('KNOP2', '0'))
_NOSAMP = int(os.environ.get('KNOSAMP', '0'))
from contextlib import ExitStack
import numpy as np
import ml_dtypes
import concourse.bass as bass
import concourse.mybir as mybir
from concourse.bass_utils import run_bass_kernel_spmd

F32 = mybir.dt.float32
BF16 = mybir.dt.bfloat16
AF = mybir.ActivationFunctionType
ALU = mybir.AluOpType
AX = mybir.AxisListType

D = 1024
DFF = 2816
NFC = DFF // 128
HD = 64
H_A, H_B, H_C = 6, 6, 4
W_A, W_B, W_C = 384, 384, 256
A_COLS = 1408
IN_COLS = 3584
WIN = 2048
NBLK = 17
RING = 18
PAST = 8192
DEPTH = 2
ALPHA = (2 * DEPTH) ** 0.25
DECAY = math.exp(-0.5)
GN_EPS = 64e-5
LN_EPS = 1e-5
GAMMA = [1.0 - 2.0 ** (-5.0 - h) for h in range(H_C)]
PADN = 124
NSB = 4
FM_COLS = [(0, 1408), (1408, 1408 + 768), (2560, 2560 + 512), (3328, 3584)]
FM_CHUNK_COL = []
for a_, b_ in FM_COLS:
    for c_ in range(a_, b_, 128):
        FM_CHUNK_COL.append(c_)
NFM = len(FM_CHUNK_COL)
CH_BQ, CH_BK, CH_CQ, CH_CK, CH_CG = 11, 14, 17, 19, 21
BV_COL, CV_COL = 2176, 3072


class Sched:
    ENGS = ("pe", "dve", "act", "pool", "sp")

    def __init__(self, nc, es):
        self.nc = nc
        self.es_sem = es
        self.es = es
        self.eng = {"pe": nc.tensor, "dve": nc.vector, "act": nc.scalar, "pool": nc.gpsimd, "sp": nc.sync}
        self.sem, self.cnt = {}, {}
        for e in self.ENGS:
            self.sem[e] = es.enter_context(nc.semaphore("q_" + e))
            self.cnt[e] = 0
        self.waited = {e: {} for e in self.ENGS}
        self.last_w, self.readers, self.dsem = {}, {}, {}
        self.kids = {}
        self.bank_last = {}
        self.n_ins = 0
        self.uid = 0
        self.keyname = {}
        self.keep = []

    def _rel(self, k):
        if k.startswith("ps") and len(k) >= 3 and k[2].isdigit():
            bank = k[:3]
            if k == bank:
                return [bank] + list(self.kids.get(bank, ()))
            self.kids.setdefault(bank, set()).add(k)
            return [k, bank]
        return [k]

    def sb(self, name, shape, dt=F32):
        self.uid += 1
        t = self.es.enter_context(self.nc.sbuf_tensor("%s_u%d" % (name, self.uid), list(shape), dt))
        self.keyname[id(t)] = name
        self.keep.append(t)
        return t

    def kn(self, t):
        return self.keyname[id(t)]

    def ps(self, name, shape, dt=F32):
        return self.es.enter_context(self.nc.psum_tensor(name, list(shape), dt))

    def _dma_sem(self, key):
        if key not in self.dsem:
            p = "d_" + key
            self.sem[p] = self.es_sem.enter_context(self.nc.semaphore(p))
            self.cnt[p] = 0
            self.dsem[key] = p
        return self.dsem[key]

    def _wait(self, E, p, c):
        if self.waited[E].get(p, 0) >= c:
            return
        self.eng[E].wait_ge(self.sem[p], c)
        self.waited[E][p] = c

    @staticmethod
    def _bank(k):
        if k.startswith("ps") and len(k) >= 3 and k[2].isdigit():
            return k[:3]
        return None

    def _need(self, E, reads, writes):
        need = {}

        def add(p, c):
            if c > need.get(p, 0):
                need[p] = c
        for r in reads:
            b = self._bank(r)
            if b is not None:
                for p, c in self.bank_last.get(b, {}).items():
                    if p != E:
                        add(p, c)
                continue
            lw = self.last_w.get(r)
            if lw is not None:
                add(*lw)
        for w in writes:
            b = self._bank(w)
            if b is not None:
                for p, c in self.bank_last.get(b, {}).items():
                    if p != E:
                        add(p, c)
                continue
            lw = self.last_w.get(w)
            if lw is not None and lw[0] != E:
                add(*lw)
            for p, c in self.readers.get(w, {}).items():
                if p != E:
                    add(p, c)
        for p, c in need.items():
            if p.startswith("d_"):
                c = self.cnt[p]
            self._wait(E, p, c)

    def _commit(self, P, c, reads, writes):
        for r in reads:
            b = self._bank(r)
            if b is not None:
                self.bank_last.setdefault(b, {})[P] = c
                continue
            self.readers.setdefault(r, {})[P] = c
        for w in writes:
            b = self._bank(w)
            if b is not None:
                self.bank_last.setdefault(b, {})[P] = c
                continue
            self.last_w[w] = (P, c)
            self.readers[w] = {}

    def op(self, E, fn, reads=(), writes=()):
        self._need(E, reads, writes)
        ins = fn(self.eng[E])
        self.cnt[E] += 1
        ins.then_inc(self.sem[E], 1)
        self._commit(E, self.cnt[E], reads, writes)
        self.n_ins += 1
        return ins

    def mm(self, out, lhsT, rhs, start, stop, reads, writes):
        return self.op("pe", lambda e: e.matmul(out, lhsT=lhsT, rhs=rhs, start=start, stop=stop), reads, writes)

    def tr(self, out, in_, ident, reads, writes):
        return self.op("pe", lambda e: e.transpose(out=out, in_=in_, identity=ident), reads, writes)

    def dma(self, Q, out, in_, key, reads=(), writes=()):
        p = self._dma_sem(key)
        self._need(Q, reads, writes)
        ins = self.eng[Q].dma_start(out=out, in_=in_)
        self.cnt[p] += 16
        ins.then_inc(self.sem[p], 16)
        self._commit(p, self.cnt[p], reads, writes)
        self.n_ins += 1
        return ins

    def barrier(self):
        for E in self.ENGS:
            for p in list(self.sem.keys()):
                if p != E and self.cnt[p] > 0:
                    self._wait(E, p, self.cnt[p])
        self.last_w, self.readers = {}, {}
        self.bank_last = {}

    def tt(self, E, out, in0, in1, op, r, w):
        return self.op(E, lambda e: e.tensor_tensor(out=out, in0=in0, in1=in1, op=op), r, w)

    def ts(self, E, out, in0, s1, s2, op0, op1, r, w):
        if s2 is None:
            return self.op(E, lambda e: e.tensor_scalar(out=out, in0=in0, scalar1=s1, scalar2=None, op0=op0), r, w)
        return self.op(E, lambda e: e.tensor_scalar(out=out, in0=in0, scalar1=s1, scalar2=s2, op0=op0, op1=op1), r, w)

    def stt(self, out, in0, sc, in1, op0, op1, r, w):
        return self.op("dve", lambda e: e.scalar_tensor_tensor(out=out, in0=in0, scalar=sc, in1=in1, op0=op0, op1=op1), r, w)

    def act(self, out, in_, func, r, w, scale=None, bias=None):
        kw = {}
        if scale is not None:
            kw["scale"] = scale
        if bias is not None:
            kw["bias"] = bias
        return self.op("act", lambda e: e.activation(out=out, in_=in_, func=func, **kw), r, w)

    def cp(self, E, out, in_, r, w):
        if E == "act":
            return self.act(out, in_, AF.Copy, r, w)
        return self.op(E, lambda e: e.tensor_copy(out=out, in_=in_), r, w)

    def memset(self, E, ap, val, w):
        return self.op(E, lambda e: e.memset(ap, val), (), w)


def _mult(d):
    d = np.asarray(d)
    ok = d >= 0
    m = ((d <= 128) & ok).astype(np.float32)
    m += ((d % 4 == 0) & (d <= 512) & ok)
    m += ((d % 16 == 0) & (d <= 2048) & ok)
    return m.astype(np.float32)


def make_consts(T):
    NT = T // 128
    c = {}
    i = np.arange(128)
    same = (i[:, None] // 64) == (i[None, :] // 64)
    su = ((i[:, None] < i[None, :]) & same).astype(np.float32)
    ui = ((i[:, None] <= i[None, :]) & same).astype(np.float32)
    ident = np.eye(128, dtype=np.float32)
    bones = same.astype(np.float32)
    def pm(half, hd=64):
        m = np.zeros((128, 128), np.float32)
        for blk in range(2):
            o = blk * hd
            for e in range(half):
                m[o + e + half, o + e] = -1.0
                m[o + e, o + e + half] = 1.0
        return m
    packs = [ident, su, ui, su.T.copy(), -ui, bones, pm(8), pm(32)]
    names = ["ident", "su", "ui", "sl", "nui", "bones", "pmb", "pmc"]
    for h in range(H_C):
        g = GAMMA[h]
        rel = i[None, :] - i[:, None]
        dm = np.where(rel >= 0, np.exp(np.log(g) * np.maximum(rel, 0)), 0.0).astype(np.float32)
        packs.append(dm)
        names.append("dmask%d" % h)
    for p in range(2):
        qd = np.zeros((128, 128), np.float32)
        kd = np.zeros((128, 128), np.float32)
        for hh in range(2):
            g = GAMMA[2 * p + hh]
            qd[hh * 64:(hh + 1) * 64, :] = np.exp(np.log(g) * (i + 1.0))[None, :]
            kd[hh * 64:(hh + 1) * 64, :] = (np.exp(np.log(g) * (127.0 - i)) * (HD ** -0.5))[None, :]
        packs += [qd, kd]
        names += ["qdec%d" % p, "kdec%d" % p]
    c["cf"] = np.stack(packs, axis=1).astype(np.float32)
    c["cf_names"] = names
    mk = np.zeros((128, NBLK, 128), np.float32)
    for dl in range(NBLK):
        d = dl * 128 + i[None, :] - i[:, None]
        mk[:, dl, :] = _mult(d)
    c["amask"] = mk.astype(ml_dtypes.bfloat16)
    ntile = NT + 1
    rt = np.zeros((ntile, 128, 4, 128), np.float32)
    inv_b = (500000.0 ** (-np.arange(0, 16, 2, dtype=np.float32) / np.float32(16))).astype(np.float32)
    inv_c = (1.0 / (np.float32(10000.0) ** np.linspace(0.0, 1.0, 32, dtype=np.float32))).astype(np.float32)
    for ti in range(ntile):
        if ti < NT:
            pos = (ti * 128 + i).astype(np.float32)
        else:
            pos = (np.float32(PAST) + np.maximum(i - PADN, 0)).astype(np.float32)
        angb = (pos[:, None] * inv_b[None, :]).astype(np.float32)
        angc = (pos[:, None] * inv_c[None, :]).astype(np.float32)
        cb = np.ones((64, 128), np.float32)
        sbb = np.zeros((64, 128), np.float32)
        cb[0:8] = np.cos(angb).T
        cb[8:16] = np.cos(angb).T
        sbb[0:8] = np.sin(angb).T
        sbb[8:16] = np.sin(angb).T
        cc = np.concatenate([np.cos(angc).T, np.cos(angc).T], axis=0)
        sc = np.concatenate([np.sin(angc).T, np.sin(angc).T], axis=0)
        for hh in range(2):
            rt[ti, hh * 64:(hh + 1) * 64, 0] = cb
            rt[ti, hh * 64:(hh + 1) * 64, 1] = sbb
            rt[ti, hh * 64:(hh + 1) * 64, 2] = cc
            rt[ti, hh * 64:(hh + 1) * 64, 3] = sc
    c["rot"] = rt
    return c


def build(T):
    NT = T // 128
    NTS = NT + NSB
    WK = min(WIN, T)
    NKEEP = WK // 128
    nc = bass.Bass("TRN2", target_bir_lowering=False)

    def din(name, shape, dt=F32):
        return nc.dram_tensor(name, list(shape), dt, kind="ExternalInput").ap()

    def dout(name, shape, dt=F32):
        return nc.dram_tensor(name, list(shape), dt, kind="ExternalOutput").ap()

    def dscr(name, shape, dt=F32):
        return nc.dram_tensor(name, list(shape), dt, kind="Internal").ap()

    xp = din("xp", [T, D])
    xs = din("xs", [NSB * 4, D])
    st_shift = din("st_shift", [DEPTH, NSB, A_COLS])
    st_wkv = din("st_wkv", [DEPTH, NSB, H_A, 64, 64])
    ck = din("ck", [DEPTH, NSB, WIN, W_B])
    cv = din("cv", [DEPTH, NSB, WIN, W_B])
    st_ret = din("st_ret", [DEPTH, NSB, H_C, 64, 64])
    w_in = din("w_in", [DEPTH, D, IN_COLS])
    p_mu = din("rwkv_mu", [DEPTH, A_COLS])
    p_w0 = din("rwkv_w0", [DEPTH, W_A])
    p_wl = din("rwkv_w_lora", [DEPTH, 64, W_A])
    p_a0 = din("rwkv_a0", [DEPTH, W_A])
    p_al = din("rwkv_a_lora", [DEPTH, 64, W_A])
    p_gl = din("rwkv_g_lora", [DEPTH, 128, W_A])
    p_kk = din("rwkv_k_k", [DEPTH, W_A])
    p_ka = din("rwkv_k_a", [DEPTH, W_A])
    p_rk = din("rwkv_r_k", [DEPTH, W_A])
    p_gg = din("rwkv_gn_g", [DEPTH, W_A])
    p_gb = din("rwkv_gn_b", [DEPTH, W_A])
    p_rg = din("ret_gn_g", [DEPTH, W_C])
    p_rb = din("ret_gn_b", [DEPTH, W_C])
    w_out = din("w_out", [DEPTH, D, D])
    ln1_g = din("ln1_g", [DEPTH, D])
    ln1_b = din("ln1_b", [DEPTH, D])
    w_gate = din("w_ffn_gate", [DEPTH, D, DFF])
    w_up = din("w_ffn_up", [DEPTH, D, DFF])
    w_down = din("w_ffn_down", [DEPTH, DFF, D])
    ln2_g = din("ln2_g", [DEPTH, D])
    ln2_b = din("ln2_b", [DEPTH, D])
    c_f = din("cf", [128, 16, 128])
    c_am = din("amask", [128, NBLK, 128], BF16)
    c_rot = din("rot", [NT + 1, 128, 4, 128])

    y_p = dout("y_p", [T, D])
    y_s = dout("y_s", [NSB * 4, D])
    o_pshift = dout("p_shift", [DEPTH, A_COLS])
    o_pwkv = dout("p_wkv", [DEPTH, H_A, 64, 64])
    o_pwk = dout("p_wk", [DEPTH, WK, W_B])
    o_pwv = dout("p_wv", [DEPTH, WK, W_B])
    o_pret = dout("p_ret", [DEPTH, H_C, 64, 64])
    o_sshift = dout("s_shift", [DEPTH, NSB, A_COLS])
    o_swkv = dout("s_wkv", [DEPTH, NSB, H_A, 64, 64])
    o_swk = dout("s_wk", [DEPTH, NSB, WIN, W_B])
    o_swv = dout("s_wv", [DEPTH, NSB, WIN, W_B])
    o_sret = dout("s_ret", [DEPTH, NSB, H_C, 64, 64])

    mixs = dscr("mixs", [8, 128, NTS * 128], BF16)
    wob = dscr("wob", [DEPTH, D, D], BF16)
    wgb = dscr("wgb", [DEPTH, D, DFF], BF16)
    wub = dscr("wub", [DEPTH, D, DFF], BF16)
    wdb = dscr("wdb", [DEPTH, DFF, D], BF16)
    winb = dscr("winb", [D, IN_COLS], BF16)
    bg_jobs = []
    x2s = dscr("x2s", [NTS * 128, D])

    CI = {"ident": 0, "su": 1, "ui": 2, "sl": 3, "nui": 4, "bones": 5, "pmb": 6, "pmc": 7,
          "dmask": 8, "qdec0": 12, "kdec0": 13, "qdec1": 14, "kdec1": 15}

    with ExitStack() as es_top:
        es_top.enter_context(nc.allow_non_contiguous_dma(reason="small strided parameter / state transfers"))
        S = Sched(nc, es_top)
        ps = [S.ps("ps%d" % b, [128, 512]) for b in range(8)]
        psb = [p[:, :].bitcast(BF16) for p in ps]

        def x_src(l, ti):
            if l == 0:
                return xp[ti * 128:(ti + 1) * 128, :]
            return x2s[ti * 128:(ti + 1) * 128, :]

        for l in range(DEPTH):
            with ExitStack() as es1:
                S.es = es1
                win = S.sb("win", [128, 8, IN_COLS], BF16)
                cf = S.sb("cf", [128, 16, 128])
                identb = S.sb("identb", [128, 128], BF16)
                amask = S.sb("amask", [128, NBLK, 128], BF16)
                mhalf = S.sb("mhalf", [128, 384])
                muc = S.sb("muc", [128, 11])
                pr = S.sb("pr", [128, 8, 3])
                prc = S.sb("prc", [128, 2, 2])
                wl = S.sb("wl", [128, 384])
                al = S.sb("al", [128, 384])
                gl = S.sb("gl", [128, 384])
                xt = [S.sb("xt%d" % k, [128, D]) for k in range(2)]
                xT = S.sb("xT", [128, 8, 128], BF16)
                H = S.sb("H", [128, NFM, 128])
                hm = S.sb("hm", [128, 11, 128])
                hml = S.sb("hml", [128, 11])
                rot = [S.sb("rot%d" % k, [128, 4, 128]) for k in range(2)]
                LI = S.sb("LI", [128, 2, 128])
                t3 = [S.sb("t3_%d" % k, [128, 3, 128]) for k in range(10)]
                TTt = S.sb("TTt", [128, 3, 4, 128], BF16)
                ktok = S.sb("ktok", [128, 384], BF16)
                nbtok = S.sb("nbtok", [128, 384], BF16)
                vtok = S.sb("vtok", [128, 384], BF16)
                U = S.sb("U", [128, 3, 64])
                UbP = [S.sb("UbP%d" % h, [128, 64], BF16) for h in range(6)]
                RbP = [S.sb("RbP%d" % h, [128, 64], BF16) for h in range(4)]
                chP = [[S.sb("chP%d_%d" % (h, k), [128, 128], BF16) for k in range(2)] for h in range(3)]
                chQ = [[S.sb("chQ%d_%d" % (h, k), [128, 128], BF16) for k in range(2)] for h in range(3)]
                chT = [[S.sb("chT%d_%d" % (h, k), [128, 128], BF16) for k in range(2)] for h in range(3)]
                cb16 = S.sb("cb16", [128, 3, 128], BF16)
                sqb = S.sb("sqb", [128, 3, 128], BF16)
                osb = S.sb("osb", [128, 3, 128], BF16)
                hb16 = S.sb("hb16", [128, 6, 128], BF16)
                invT = [S.sb("invT%d" % h, [128, 128], BF16) for h in range(3)]
                m4t = [S.sb("m4t%d" % h, [128, 128], BF16) for h in range(3)]
                lm3 = [S.sb("lm3_%d" % h, [128, 256], BF16) for h in range(3)]
                rhsb = [S.sb("rhsb%d" % h, [128, 64], BF16) for h in range(3)]
                umb = [S.sb("umb%d" % h, [128, 64], BF16) for h in range(3)]
                utmp = [S.sb("utmp%d" % h, [128, 64]) for h in range(3)]
                KTr = S.sb("KTr", [128, 3, RING, 128], BF16)
                Vr = S.sb("Vr", [128, RING, 6, 66], BF16)
                qTb = S.sb("qTb", [128, 3, 128], BF16)
                krot = S.sb("krot", [128, 3, 128])
                stg = [S.sb("stg%d" % k, [128, 384]) for k in range(2)]
                vstg = [S.sb("vstg%d" % k, [128, 384]) for k in range(2)]
                pT = [S.sb("pT%d" % k, [128, 512], BF16) for k in range(3)]
                ob = S.sb("ob", [128, 384])
                rl = S.sb("rl", [128, 6])
                c2 = [S.sb("c2_%d" % k, [128, 2, 128]) for k in range(4)]
                qcb = S.sb("qcb", [128, 2, 128], BF16)
                kcb = S.sb("kcb", [128, 2, 128], BF16)
                qdb = S.sb("qdb", [128, 2, 128], BF16)
                kdb = S.sb("kdb", [128, 2, 128], BF16)
                kdtok = S.sb("kdtok", [128, 256], BF16)
                vcb = S.sb("vcb", [128, 256], BF16)
                attb = [S.sb("attb%d" % k, [128, 128], BF16) for k in range(2)]
                R = S.sb("R", [128, 2, 64])
                mixT = [S.sb("mixT%d" % k, [128, 8, 128], BF16) for k in range(2)]
                cstage = S.sb("cstage", [128, 384])
                sst = S.sb("sst", [128, 32])
                swk = S.sb("swk", [128, 64])

                ident = cf[:, CI["ident"], :]

                for kc in range(8):
                    if l == 0:
                        S.dma("pool", win[:, kc, :], w_in[l, kc * 128:(kc + 1) * 128, :], "w1", writes=["win"])
                    else:
                        S.dma("sp", win[:, kc, :], winb[kc * 128:(kc + 1) * 128, :], "w1", writes=["win"])
                for kc in range(8):
                    bg_jobs.append((wob[l, kc * 128:(kc + 1) * 128, :], w_out[l, kc * 128:(kc + 1) * 128, :]))
                for kc in range(8):
                    bg_jobs.append((wgb[l, kc * 128:(kc + 1) * 128, :], w_gate[l, kc * 128:(kc + 1) * 128, :]))
                    bg_jobs.append((wub[l, kc * 128:(kc + 1) * 128, :], w_up[l, kc * 128:(kc + 1) * 128, :]))
                for c in range(NFC):
                    bg_jobs.append((wdb[l, c * 128:(c + 1) * 128, :], w_down[l, c * 128:(c + 1) * 128, :]))
                S.dma("sp", cf[:, :, :], c_f[:, :, :], "c1", writes=["cf"])
                S.dma("sp", amask[:, :, :], c_am[:, :, :], "c1", writes=["amask"])
                S.dma("sp", muc[:, :], p_mu[l].rearrange("(c p) -> p c", p=128), "c1", writes=["muc"])
                for k, prm in enumerate([p_w0, p_a0, p_kk, p_ka, p_rk, p_gg, p_gb]):
                    S.dma("sp", pr[:, k, :], prm[l].rearrange("(c p) -> p c", p=128), "c1", writes=["pr"])
                S.dma("sp", prc[:, 0, :], p_rg[l].rearrange("(c p) -> p c", p=128), "c1", writes=["prc"])
                S.dma("sp", prc[:, 1, :], p_rb[l].rearrange("(c p) -> p c", p=128), "c1", writes=["prc"])
                S.dma("sp", wl[0:64, :], p_wl[l], "c1", writes=["wl"])
                S.dma("sp", al[64:128, :], p_al[l], "c1", writes=["al"])
                S.dma("sp", gl[:, :], p_gl[l], "c1", writes=["gl"])
                S.cp("dve", identb[:, :], ident, ["cf"], ["identb"])
                S.cp("dve", cb16[:, :, :], cf[:, CI["bones"]:CI["bones"] + 3, :], ["cf"], ["cb16"])
                S.memset("dve", mhalf[:, :], -0.5, ["mhalf"])
                S.ts("dve", pr[:, 0:2, :], pr[:, 0:2, :], -1.0, None, ALU.mult, None, ["pr"], ["pr"])

                for sb_ in range(NSB):
                    S.dma("act", o_swk[l, sb_, 0:WIN - 4, :], ck[l, sb_, 4:WIN, :], "cpk")
                    S.dma("act", o_swv[l, sb_, 0:WIN - 4, :], cv[l, sb_, 4:WIN, :], "cpv")
                tile_ctr = [0]

                def process_tile(seq, ti, nseq, gi):
                    samp = seq != "p"
                    if _STOP <= 0:
                        return
                    for _ in range(2):
                        if bg_jobs:
                            d_, s_src = bg_jobs.pop(0)
                            S.dma("pool", d_, s_src, "bgc")
                    k2 = tile_ctr[0] % 2
                    tile_ctr[0] += 1
                    first = ti == 0
                    xk = "xt%d" % k2
                    rk_ = "rot%d" % k2
                    if samp and l == 0:
                        S.memset("pool", xt[k2][:, :], 0.0, [xk])
                        S.dma("sp", xt[k2][PADN:128, :], xs[seq * 4:(seq + 1) * 4, :], "ldx%d" % k2, writes=[xk])
                    else:
                        S.dma("sp", xt[k2][:, :], x_src(l, gi), "ldx%d" % k2, writes=[xk])
                    rti = NT if samp else ti
                    S.dma("sp", rot[k2][:, :, :], c_rot[rti], "ldx%d" % k2, writes=[rk_])
                    for half in range(2):
                        for j in range(4):
                            kc = half * 4 + j
                            S.tr(ps[half][:, j * 128:(j + 1) * 128], xt[k2][:, kc * 128:(kc + 1) * 128], ident,
                                 [xk, "cf"], ["ps%d" % half])
                        S.cp("act" if half == 0 else "dve", xT[:, half * 4:(half + 1) * 4, :],
                             ps[half][:, :].rearrange("p (a b) -> p a b", b=128), ["ps%d" % half], ["xT%d" % half])
                    for c in range(NFM):
                        b, j = c // 4, c % 4
                        col = FM_CHUNK_COL[c]
                        for kc in range(8):
                            S.mm(ps[b][:, j * 128:(j + 1) * 128], win[:, kc, col:col + 128], xT[:, kc, :],
                                 kc == 0, kc == 7, ["win", "xT0", "xT1"], ["ps%d" % b])
                    for kc in range(8):
                        S.mm(ps[6][:, 0:384], xT[:, kc, :], win[:, kc, BV_COL:BV_COL + 384], kc == 0, kc == 7,
                             ["win", "xT0", "xT1"], ["ps6"])
                    for kc in range(8):
                        S.mm(ps[7][:, 0:256], xT[:, kc, :], win[:, kc, CV_COL:CV_COL + 256], kc == 0, kc == 7,
                             ["win", "xT0", "xT1"], ["ps7"])
                    for b in range(6):
                        n = min(4, NFM - 4 * b)
                        S.cp("act" if b % 2 == 0 else "dve", H[:, 4 * b:4 * b + n, :],
                             ps[b][:, 0:n * 128].rearrange("p (a b) -> p a b", b=128), ["ps%d" % b], ["H%d" % b])
                    Hk = ["H%d" % b for b in range(6)]
                    if _STOP <= 1:
                        return
                    slot = ti % RING if not samp else 16
                    vk = "V%d" % slot
                    S.cp("dve", Vr[:, slot, :, 0:64], ps[6][:, 0:384].rearrange("p (h e) -> p h e", e=64), ["ps6"], [vk])
                    if _STOP <= 1.1:
                        return
                    S.memset("pool", Vr[:, slot, :, 64:65], 1.0, [vk + "o"])
                    if _STOP <= 1.2:
                        return
                    keep = samp or (ti >= NT - NKEEP)
                    if keep:
                        S.cp("dve", stg[0][:, :], ps[6][:, 0:384], ["ps6"], ["stg0"])
                        if samp:
                            S.dma("pool", o_swv[l, seq, WIN - 4:WIN, :], stg[0][PADN:128, :], "stv", reads=["stg0"])
                        else:
                            r0 = (ti - (NT - NKEEP)) * 128
                            if os.environ.get("KV1") == "nodma":
                                pass
                            else:
                                S.dma(os.environ.get("KV1", "pool"), o_pwv[l, r0:r0 + 128, :], stg[0][:, :], "stv", reads=["stg0"])
                    if _STOP <= 1.3:
                        return
                    S.cp("act", vcb[:, :], ps[7][:, 0:256], ["ps7"], ["vcb"])
                    if _STOP <= 1.5:
                        return
                    if samp:
                        S.cp("dve", sst[:, 0:11], H[:, 0:11, 127], Hk[0:3], ["sst"])
                        S.dma("pool", o_sshift[l, seq].rearrange("(c p) -> p c", p=128), sst[:, 0:11], "sts", reads=["sst"])
                        S.dma("sp", sst[:, 16:27], st_shift[l, seq].rearrange("(c p) -> p c", p=128), "ldx%d" % k2, writes=["sst2"])
                        S.cp("dve", H[:, 0:11, PADN - 1], sst[:, 16:27], ["sst2"] + Hk[0:3], Hk[0:3])
                    elif ti == NT - 1:
                        S.cp("dve", sst[:, 0:11], H[:, 0:11, 127], Hk[0:3], ["sst"])
                        S.dma("pool", o_pshift[l].rearrange("(c p) -> p c", p=128), sst[:, 0:11], "sts", reads=["sst"])
                    if _STOP <= 1.7:
                        return
                    if first:
                        S.memset("dve", hml[:, :], 0.0, ["hml"])
                    HA = H[:, 0:11, :]
                    S.tt("dve", hm[:, :, :], HA, muc[:, 0:11].unsqueeze(2).to_broadcast([128, 11, 128]), ALU.mult,
                         Hk[0:3] + ["muc"], ["hm"])
                    S.tt("dve", HA, HA, hm[:, :, :], ALU.subtract, Hk[0:3] + ["hm"], Hk[0:3])
                    S.tt("dve", H[:, 0:11, 1:128], H[:, 0:11, 1:128], hm[:, :, 0:127], ALU.add, Hk[0:3] + ["hm"], Hk[0:3])
                    S.tt("dve", H[:, 0:11, 0], H[:, 0:11, 0], hml[:, :], ALU.add, Hk[0:3] + ["hml"], Hk[0:3])
                    S.cp("dve", hml[:, :], hm[:, :, 127], ["hm"], ["hml"])
                    rT, kT, vT = H[:, 0:3, :], H[:, 3:6, :], H[:, 6:9, :]
                    if _STOP <= 2:
                        return
                    HAk = Hk[0:3]
                    S.act(LI[0:64, 0, :], H[0:64, 9, :], AF.Exp, HAk, ["LIa"], scale=-2.0)
                    S.ts("dve", LI[0:64, 0, :], LI[0:64, 0, :], 1.0, None, ALU.add, None, ["LIa"], ["LIa"])
                    S.op("dve", lambda e: e.reciprocal(out=LI[0:64, 0, :], in_=LI[0:64, 0, :]), ["LIa"], ["LIa"])
                    S.ts("dve", LI[0:64, 0, :], LI[0:64, 0, :], 2.0, -1.0, ALU.mult, ALU.add, ["LIa"], ["LIa"])
                    S.cp("act", LI[64:128, 0, :], H[64:128, 9, :], HAk, ["LIb"])
                    S.act(LI[:, 1, :], H[:, 10, :], AF.Exp, HAk, ["LIc"], scale=-1.0)
                    S.ts("dve", LI[:, 1, :], LI[:, 1, :], 1.0, None, ALU.add, None, ["LIc"], ["LIc"])
                    S.op("dve", lambda e: e.reciprocal(out=LI[:, 1, :], in_=LI[:, 1, :]), ["LIc"], ["LIc"])
                    for p in range(3):
                        S.mm(ps[0][:, p * 128:(p + 1) * 128], wl[0:64, p * 128:(p + 1) * 128], LI[0:64, 0, :], True, True,
                             ["wl", "LIa"], ["ps0"])
                        S.mm(ps[1][:, p * 128:(p + 1) * 128], al[64:128, p * 128:(p + 1) * 128], LI[64:128, 0, :], True, True,
                             ["al", "LIb"], ["ps1"])
                        S.mm(ps[2][:, p * 128:(p + 1) * 128], gl[:, p * 128:(p + 1) * 128], LI[:, 1, :], True, True,
                             ["gl", "LIc"], ["ps2"])
                    lw, cum, Wc, iW, Wp, aa, gT, kkn, kp, bb = t3
                    if _STOP <= 3:
                        return
                    n = S.kn
                    v3 = lambda b: ps[b][:, 0:384].rearrange("p (a b) -> p a b", b=128)
                    for p in range(3):
                        S.act(lw[:, p, :], ps[0][:, p * 128:(p + 1) * 128], AF.Exp, ["ps0", "pr"], [n(lw)], scale=-1.0, bias=pr[:, 0, p:p + 1])
                        S.act(aa[:, p, :], ps[1][:, p * 128:(p + 1) * 128], AF.Exp, ["ps1", "pr"], [n(aa)], scale=-1.0, bias=pr[:, 1, p:p + 1])
                    S.cp("act", gT[:, :, :], v3(2), ["ps2"], [n(gT)])
                    S.ts("dve", lw[:, :, :], lw[:, :, :], 1.0, None, ALU.add, None, [n(lw)], [n(lw)])
                    S.op("dve", lambda e: e.reciprocal(out=lw[:, :, :], in_=lw[:, :, :]), [n(lw)], [n(lw)])
                    S.ts("dve", lw[:, :, :], lw[:, :, :], -DECAY, None, ALU.mult, None, [n(lw)], [n(lw)])
                    S.ts("dve", aa[:, :, :], aa[:, :, :], 1.0, None, ALU.add, None, [n(aa)], [n(aa)])
                    S.op("dve", lambda e: e.reciprocal(out=aa[:, :, :], in_=aa[:, :, :]), [n(aa)], [n(aa)])
                    if samp:
                        S.memset("dve", lw[:, :, 0:PADN], 0.0, [n(lw)])
                    ones3 = mhalf
                    for p in range(3):
                        for cc in range(2):
                            sl = slice(cc * 64, (cc + 1) * 64)
                            S.op("dve", lambda e, p=p, sl=sl: e.tensor_tensor_scan(
                                out=cum[:, p, sl], data0=onesT[:, sl], data1=lw[:, p, sl], initial=0.0,
                                op0=ALU.mult, op1=ALU.add), [n(lw), "onesT"], [n(cum)])
                    S.act(Wc[:, :, :], cum[:, :, :], AF.Exp, [n(cum)], [n(Wc)])
                    S.act(iW[:, :, :], cum[:, :, :], AF.Exp, [n(cum)], [n(iW)], scale=-1.0)
                    S.tt("dve", Wp[:, :, :], cum[:, :, :], lw[:, :, :], ALU.subtract, [n(cum), n(lw)], [n(Wp)])
                    S.act(Wp[:, :, :], Wp[:, :, :], AF.Exp, [n(Wp)], [n(Wp)])
                    bc = lambda k: pr[:, k, :].unsqueeze(2).to_broadcast([128, 3, 128])
                    S.tt("dve", kkn[:, :, :], kT, bc(2), ALU.mult, HAk + ["pr"], [n(kkn)])
                    S.act(sqb[:, :, :], kkn[:, :, :], AF.Square, [n(kkn)], ["sqb"])
                    for p in range(3):
                        S.mm(ps[3][:, p * 128:(p + 1) * 128], cb16[:, 0, :], sqb[:, p, :], True, True, ["cb16", "sqb"], ["ps3"])
                    S.ts("dve", bb[:, :, :], v3(3), 1e-12, None, ALU.max, None, ["ps3"], [n(bb)])
                    S.act(bb[:, :, :], bb[:, :, :], AF.Ln, [n(bb)], [n(bb)])
                    S.act(bb[:, :, :], bb[:, :, :], AF.Exp, [n(bb)], [n(bb)], scale=-0.5)
                    S.tt("dve", kkn[:, :, :], kkn[:, :, :], bb[:, :, :], ALU.mult, [n(kkn), n(bb)], [n(kkn)])
                    S.stt(kp[:, :, :], aa[:, :, :], -1.0, bc(3), ALU.add, ALU.mult, [n(aa), "pr"], [n(kp)])
                    S.stt(kp[:, :, :], kp[:, :, :], 1.0, kT, ALU.add, ALU.mult, [n(kp)] + HAk, [n(kp)])
                    S.tt("dve", bb[:, :, :], kkn[:, :, :], aa[:, :, :], ALU.mult, [n(kkn), n(aa)], [n(bb)])
                    if samp:
                        S.memset("dve", kp[:, :, 0:PADN], 0.0, [n(kp)])
                        S.memset("pool", bb[:, :, 0:PADN], 0.0, [n(bb)])
                    S.tt("dve", TTt[:, :, 0, :], kkn[:, :, :], Wp[:, :, :], ALU.mult, [n(kkn), n(Wp)], ["TT0"])
                    S.tt("dve", TTt[:, :, 1, :], rT, Wc[:, :, :], ALU.mult, HAk + [n(Wc)], ["TT1"])
                    S.tt("dve", TTt[:, :, 2, :], kp[:, :, :], iW[:, :, :], ALU.mult, [n(kp), n(iW)], ["TT2"])
                    S.tt("dve", TTt[:, :, 3, :], bb[:, :, :], iW[:, :, :], ALU.mult, [n(bb), n(iW)], ["TT3"])
                    bon = cum
                    S.tt("dve", aa[:, :, :], rT, kp[:, :, :], ALU.mult, HAk + [n(kp), n(bb)], [n(aa)])
                    S.tt("dve", osb[:, :, :], aa[:, :, :], bc(4), ALU.mult, [n(aa), "pr"], ["osb"])
                    for p in range(3):
                        S.mm(ps[3][:, p * 128:(p + 1) * 128], cb16[:, 0, :], osb[:, p, :], True, True, ["cb16", "osb"], ["ps3"])
                    S.tt("dve", bon[:, :, :], v3(3), vT, ALU.mult, ["ps3", n(Wp), n(Wc), n(iW)] + HAk, [n(cum)])
                    for p in range(3):
                        S.tr(psb[4][:, p * 128:(p + 1) * 128], TTt[:, p, 2, :], identb[:, :], ["TT2", "identb"], ["ps4"])
                        S.tr(psb[4][:, 384 + p * 128:384 + (p + 1) * 128], TTt[:, p, 3, :], identb[:, :], ["TT3", "identb"], ["ps4"])
                        S.tr(ps[5][:, p * 128:(p + 1) * 128], H[:, 6 + p, :], ident, HAk + ["cf"], ["ps5"])
                    S.cp("dve", ktok[:, :], psb[4][:, 0:384], ["ps4"], ["ktok"])
                    S.ts("dve", nbtok[:, :], psb[4][:, 384:768], -1.0, None, ALU.mult, None, ["ps4"], ["nbtok"])
                    S.cp("act", vtok[:, :], ps[5][:, 0:384], ["ps5"], ["vtok"])
                    if _STOP <= 4:
                        return
                    if first:
                        if samp:
                            for p in range(3):
                                S.dma("sp", cstage[0:64, p * 128:(p + 1) * 128].rearrange("v (h k) -> v h k", k=64),
                                      st_wkv[l, seq, 2 * p:2 * p + 2].rearrange("h v k -> v h k"), "ldx%d" % k2, writes=["cstage"])
                            for p in range(3):
                                S.tr(ps[6][:, p * 64:(p + 1) * 64], cstage[0:64, p * 128:(p + 1) * 128], cf[0:64, CI["ident"], 0:64],
                                     ["cstage", "cf"], ["ps6"])
                            S.cp("dve", U[:, :, :], ps[6][:, 0:192].rearrange("p (a b) -> p a b", b=64), ["ps6"], ["U"])
                            for h in range(6):
                                p_, hb_ = h // 2, 64 * (h % 2)
                                S.memset("pool", UbP[h][:, :], 0.0, ["UbP%d" % h])
                                S.cp("dve", UbP[h][hb_:hb_ + 64, :], ps[6][hb_:hb_ + 64, p_ * 64:(p_ + 1) * 64], ["ps6"], ["UbP%d" % h])
                            for p in range(2):
                                for hh in range(2):
                                    S.dma("sp", R[hh * 64:(hh + 1) * 64, p, :], st_ret[l, seq, 2 * p + hh], "ldx%d" % k2, writes=["R"])
                            for p in range(2):
                                for hh in range(2):
                                    g = GAMMA[2 * p + hh] ** (-float(PADN))
                                    S.ts("dve", R[hh * 64:(hh + 1) * 64, p, :], R[hh * 64:(hh + 1) * 64, p, :], g, None, ALU.mult, None, ["R"], ["R"])
                            for h in range(4):
                                p_, hb_ = h // 2, 64 * (h % 2)
                                S.memset("pool", RbP[h][:, :], 0.0, ["RbP%d" % h])
                                S.cp("act", RbP[h][hb_:hb_ + 64, :], R[hb_:hb_ + 64, p_, :], ["R"], ["RbP%d" % h])
                        else:
                            S.memset("dve", U[:, :, :], 0.0, ["U"])
                            for h in range(6):
                                S.memset("pool", UbP[h][:, :], 0.0, ["UbP%d" % h])
                            S.memset("dve", R[:, :, :], 0.0, ["R"])
                            for h in range(4):
                                S.memset("pool", RbP[h][:, :], 0.0, ["RbP%d" % h])
                    mk2 = "mixT%d" % k2
                    cosB, sinB = rot[k2][:, 0, :], rot[k2][:, 1, :]
                    cosC, sinC = rot[k2][:, 2, :], rot[k2][:, 3, :]
                    S.cp("act", hb16[:, :, :], H[:, CH_BQ:CH_BQ + 6, :], ["H2", "H3", "H4"], ["hb16"])
                    for c in range(6):
                        S.mm(ps[0 + c // 4][:, (c % 4) * 128:(c % 4 + 1) * 128], cb16[:, 1, :], hb16[:, c, :], True, True,
                             ["cb16", "hb16"], ["ps%d" % (c // 4)])
                    qk = H[:, CH_BQ:CH_BQ + 6, :]
                    qkk = ["H2", "H3", "H4"]
                    b6 = lambda a: a.unsqueeze(1).to_broadcast([128, 6, 128])
                    S.tt("dve", qk, qk, b6(cosB), ALU.mult, qkk + [rk_], qkk)
                    rtmp = t3[3]
                    S.tt("dve", rtmp[:, :, :], v3(0) if False else ps[0][:, 0:384].rearrange("p (a b) -> p a b", b=128),
                         sinB.unsqueeze(1).to_broadcast([128, 3, 128]), ALU.mult, ["ps0", rk_], [n(rtmp)])
                    S.tt("dve", H[:, CH_BQ:CH_BQ + 3, :], H[:, CH_BQ:CH_BQ + 3, :], rtmp[:, :, :], ALU.add, qkk + [n(rtmp)], qkk)
                    S.tt("dve", rtmp[:, 0, :], ps[0][:, 384:512], sinB, ALU.mult, ["ps0", rk_], [n(rtmp)])
                    S.tt("dve", rtmp[:, 1:3, :], ps[1][:, 0:256].rearrange("p (a b) -> p a b", b=128),
                         sinB.unsqueeze(1).to_broadcast([128, 2, 128]), ALU.mult, ["ps1", rk_], [n(rtmp)])
                    S.tt("dve", krot[:, :, :], H[:, CH_BK:CH_BK + 3, :], rtmp[:, :, :], ALU.add, qkk + [n(rtmp)], ["krot"])
                    S.cp("act", qTb[:, :, :], H[:, CH_BQ:CH_BQ + 3, :], qkk, ["qTb"])
                    S.cp("act", KTr[:, :, slot, :], krot[:, :, :], ["krot"], ["K%d" % slot])
                    if keep:
                        for p in range(3):
                            S.tr(ps[2][:, p * 128:(p + 1) * 128], krot[:, p, :], ident, ["krot", "cf"], ["ps2"])
                        S.cp("dve", stg[1][:, :], ps[2][:, 0:384], ["ps2"], ["stg1"])
                        if samp:
                            S.dma("pool", o_swk[l, seq, WIN - 4:WIN, :], stg[1][PADN:128, :], "stk", reads=["stg1"])
                        else:
                            r0 = (ti - (NT - NKEEP)) * 128
                            S.dma("pool", o_pwk[l, r0:r0 + 128, :], stg[1][:, :], "stk", reads=["stg1"])
                    if samp:
                        for dl in range(NBLK):
                            sl_ = 16 - dl
                            r_lo = WIN - PADN - 128 * dl
                            lo = max(0, -r_lo)
                            hi = 128 if dl > 0 else PADN
                            sk = stg[dl % 2]
                            skn = "stg%d" % (dl % 2)
                            if lo > 0:
                                S.memset("pool", sk[0:lo, :], 0.0, [skn])
                            S.dma("sp", sk[lo:hi, :], ck[l, seq, r_lo + lo:r_lo + hi, :], "ldc%d" % (dl % 2), writes=[skn])
                            bnk = 2 + dl % 2
                            for p in range(3):
                                S.tr(ps[bnk][:, p * 128:(p + 1) * 128], sk[:, p * 128:(p + 1) * 128], ident, [skn, "cf"], ["ps%d" % bnk])
                            src = ps[bnk][:, 0:384].rearrange("p (a b) -> p a b", b=128)
                            if dl == 0:
                                S.cp("act", KTr[:, :, sl_, 0:PADN], src[:, :, 0:PADN], ["ps%d" % bnk], ["K%d" % sl_])
                            else:
                                S.cp("act", KTr[:, :, sl_, :], src, ["ps%d" % bnk], ["K%d" % sl_])
                            vkk = "V%d" % sl_
                            vs_ = vstg[dl % 2]
                            vsn = "vstg%d" % (dl % 2)
                            if lo > 0:
                                S.memset("pool", vs_[0:lo, :], 0.0, [vsn])
                            S.dma("sp", vs_[lo:hi, :], cv[l, seq, r_lo + lo:r_lo + hi, :], "ldcv%d" % (dl % 2), writes=[vsn])
                            S.cp("act", Vr[0:hi, sl_, :, 0:64], vs_[0:hi, :].rearrange("r (h e) -> r h e", e=64), [vsn], [vkk])
                            if dl > 0:
                                S.memset("pool", Vr[:, sl_, :, 64:65], 1.0, [vkk + "o"])
                    def rwkv_core():
                        su_ui = cf[:, CI["su"]:CI["su"] + 2, :]
                        for grp in range(3):
                            heads = [grp * 2 + k for k in range(2)]
                            for k, h in enumerate(heads):
                                p, hb = h // 2, 64 * (h % 2)
                                bA, bB = ps[2 * k], ps[2 * k + 1]
                                kA, kB = "ps%d" % (2 * k), "ps%d" % (2 * k + 1)
                                KKR = TTt[hb:hb + 64, p, 0:2, :]
                                S.mm(bA[:, 0:256], TTt[hb:hb + 64, p, 3, :], KKR, True, True, ["TT0", "TT1", "TT3"], [kA + "a"])
                                S.mm(bA[:, 256:512], TTt[hb:hb + 64, p, 2, :], KKR, True, True, ["TT0", "TT1", "TT2"], [kA + "b"])
                                S.mm(bB[:, 0:128], TTt[hb:hb + 64, p, 0, :], TTt[hb:hb + 64, p, 3, :], True, True, ["TT0", "TT3"], [kB + "a"])
                            yield
                            for k, h in enumerate(heads):
                                bA, bB = ps[2 * k], ps[2 * k + 1]
                                kA, kB = "ps%d" % (2 * k), "ps%d" % (2 * k + 1)
                                P0, Q0, T0 = chP[k][0], chQ[k][0], chT[k][0]
                                S.tt("dve", P0[:, :], bA[:, 0:128], cf[:, CI["su"], :], ALU.mult, [kA + "a", "cf"], [n(P0)])
                                S.tt("dve", m4t[k][:, :], bA[:, 128:256], cf[:, CI["nui"], :], ALU.mult, [kA + "a", "cf"], [n(m4t[k])])
                                S.tt("dve", lm3[k][:, :].rearrange("p (a b) -> p a b", b=128), bA[:, 256:512].rearrange("p (a b) -> p a b", b=128),
                                     su_ui, ALU.mult, [kA + "b", "cf"], [n(lm3[k])])
                                S.tt("dve", Q0[:, :], bB[:, 0:128], cf[:, CI["sl"], :], ALU.mult, [kB + "a", "cf"], [n(Q0)])
                                S.tt("dve", T0[:, :], identb[:, :], P0[:, :], ALU.subtract, ["identb", n(P0)], [n(T0)])
                            yield
                            cur = [0, 0, 0]
                            for lev in range(1, 6):
                                st = []
                                for k, h in enumerate(heads):
                                    c0 = cur[k]
                                    st.append((k, ps[2 * k], ps[2 * k + 1], "ps%d" % (2 * k), "ps%d" % (2 * k + 1),
                                               chP[k][c0], chQ[k][c0], chT[k][c0], chP[k][1 - c0], chQ[k][1 - c0], chT[k][1 - c0]))
                                    cur[k] = 1 - c0
                                for (k, bA, bB, kA, kB, Pc, Qc, Tc, Pn, Qn, Tn) in st:
                                    S.mm(bB[:, 256:384], Pc[:, :], Qc[:, :], True, True, [n(Pc), n(Qc)], [kB + "c"])
                                    if lev < 5:
                                        S.mm(bB[:, 128:256], Qc[:, :], Pc[:, :], True, True, [n(Pc), n(Qc)], [kB + "b"])
                                yield
                                for (k, bA, bB, kA, kB, Pc, Qc, Tc, Pn, Qn, Tn) in st:
                                    S.cp("dve", Qn[:, :], bB[:, 256:384], [kB + "c"], [n(Qn)])
                                    if lev < 5:
                                        S.cp("dve", Pn[:, :], bB[:, 128:256], [kB + "b"], [n(Pn)])
                                for (k, bA, bB, kA, kB, Pc, Qc, Tc, Pn, Qn, Tn) in st:
                                    S.mm(bA[:, 0:128], Qn[:, :], Tc[:, :], True, True, [n(Qn), n(Tc)], [kA + "d"])
                                yield
                                for (k, bA, bB, kA, kB, Pc, Qc, Tc, Pn, Qn, Tn) in st:
                                    if lev < 5:
                                        S.tt("dve", Tn[:, :], Tc[:, :], bA[:, 0:128], ALU.add, [n(Tc), kA + "d"], [n(Tn)])
                                    else:
                                        S.tt("dve", invT[k][:, :], Tc[:, :], bA[:, 0:128], ALU.add, [n(Tc), kA + "d"], [n(invT[k])])
                            for cidx in range(2):
                                pb = 64 * cidx
                                tk = slice(pb, pb + 64)
                                hd = []
                                for k, h in enumerate(heads):
                                    p, hb = h // 2, 64 * (h % 2)
                                    hd.append((k, h, p, slice(hb, hb + 64), ps[2 * k], ps[2 * k + 1], "ps%d" % (2 * k), "ps%d" % (2 * k + 1),
                                               "UbP%d" % h, vtok[:, h * 64:(h + 1) * 64], vtok[tk, h * 64:(h + 1) * 64]))
                                for (k, h, p, hs, bA, bB, kA, kB, ukey, vall, vh) in hd:
                                    S.mm(bA[tk, 0:64], TTt[:, p, 0, tk], UbP[h][:, :], True, False, ["TT0", ukey], [kA + "r"])
                                    S.mm(bA[tk, 0:64], lm3[k][:, pb:pb + 64], vall, False, True, [n(lm3[k]), "vtok"], [kA + "r"])
                                yield
                                for (k, h, p, hs, bA, bB, kA, kB, ukey, vall, vh) in hd:
                                    S.cp("dve", rhsb[k][tk, :], bA[tk, 0:64], [kA + "r"], [n(rhsb[k])])
                                for (k, h, p, hs, bA, bB, kA, kB, ukey, vall, vh) in hd:
                                    S.mm(bB[tk, 0:64], invT[k][tk, pb:pb + 64], rhsb[k][tk, :], True, True, [n(invT[k]), n(rhsb[k])], [kB + "u"])
                                yield
                                for (k, h, p, hs, bA, bB, kA, kB, ukey, vall, vh) in hd:
                                    S.cp("dve", umb[k][tk, :], bB[tk, 0:64], [kB + "u"], [n(umb[k])])
                                for (k, h, p, hs, bA, bB, kA, kB, ukey, vall, vh) in hd:
                                    oreg = ps[7][hs, p * 128 + pb:p * 128 + pb + 64]
                                    S.mm(oreg, UbP[h][:, :], TTt[:, p, 1, tk], True, False, [ukey, "TT1"], ["ps7o"])
                                    S.mm(oreg, vall, lm3[k][:, 128 + pb:128 + pb + 64], False, False, ["vtok", n(lm3[k])], ["ps7o"])
                                    S.mm(oreg, umb[k][:, :], m4t[k][:, pb:pb + 64], False, True, [n(umb[k]), n(m4t[k])], ["ps7o"])
                                    S.mm(bA[hs, 64:128], ktok[tk, h * 64:(h + 1) * 64], vh, True, False, ["ktok", "vtok"], [kA + "s"])
                                    S.mm(bA[hs, 64:128], nbtok[tk, h * 64:(h + 1) * 64], umb[k][tk, :], False, True, ["nbtok", n(umb[k])], [kA + "s"])
                                yield
                                for (k, h, p, hs, bA, bB, kA, kB, ukey, vall, vh) in hd:
                                    S.tt("dve", utmp[k][hs, :], U[hs, p, :], bA[hs, 64:128], ALU.add, ["U", kA + "s"], [n(utmp[k])])
                                    wcol = Wc[hs, p, pb + 63:pb + 64]
                                    S.ts("dve", U[hs, p, :], utmp[k][hs, :], wcol, None, ALU.mult, None, [n(utmp[k]), n(Wc)], ["U"])
                                    S.act(UbP[h][hs, :], utmp[k][hs, :], AF.Copy, [n(utmp[k]), n(Wc)], [ukey], scale=wcol)
                                yield
                    def attn_core():
                        unit = [0]
                        nb = NBLK if samp else min(NBLK, ti + 1)
                        units = []
                        for h in range(6):
                            for g0 in range(0, nb, 4):
                                units.append((h, g0, min(4, nb - g0)))
                        pend = None

                        def emit_pv(u):
                            (h, g0, gn_, pt) = u
                            for j in range(gn_):
                                dl = g0 + j
                                sl_ = (16 - dl) if samp else ((ti - dl) % RING)
                                S.mm(ps[6][:, h * 65:(h + 1) * 65], pt[:, j * 128:(j + 1) * 128], Vr[:, sl_, h, 0:65], dl == 0, dl == nb - 1,
                                     [n(pt), "V%d" % sl_, "V%do" % sl_], ["ps6"])
                        for ui, (h, g0, gn_) in enumerate(units):
                            p, hb = h // 2, 64 * (h % 2)
                            hs = slice(hb, hb + 64)
                            bnk = 4 + (ui % 2)
                            bk = "ps%d" % bnk
                            pt = pT[ui % 3]
                            for j in range(gn_):
                                dl = g0 + j
                                sl_ = (16 - dl) if samp else ((ti - dl) % RING)
                                S.mm(ps[bnk][:, j * 128:(j + 1) * 128], KTr[hs, p, sl_, :], qTb[hs, p, :], True, True,
                                     ["K%d" % sl_, "qTb"], [bk])
                            if pend is not None:
                                emit_pv(pend)
                            S.act(pt[:, 0:gn_ * 128], ps[bnk][:, 0:gn_ * 128], AF.Exp, [bk], [n(pt)], scale=HD ** -0.5)
                            pt3 = pt[:, 0:gn_ * 128].rearrange("p (a b) -> p a b", b=128)
                            S.tt("dve", pt3, pt3, amask[:, g0:g0 + gn_, :], ALU.mult, [n(pt), "amask"], [n(pt)])
                            pend = (h, g0, gn_, pt)
                            yield
                        emit_pv(pend)
                        O3 = ps[6][:, 0:390].rearrange("p (h e) -> p h e", e=65)
                        S.op("dve", lambda e: e.reciprocal(out=rl[:, :], in_=O3[:, :, 64]), ["ps6"], ["rl"])
                        S.tt("dve", ob[:, :].rearrange("p (h e) -> p h e", e=64), O3[:, :, 0:64], rl[:, :].unsqueeze(2).to_broadcast([128, 6, 64]),
                             ALU.mult, ["ps6", "rl"], ["ob"])
                        for p in range(3):
                            S.tr(ps[4][:, p * 128:(p + 1) * 128], ob[:, p * 128:(p + 1) * 128], ident, ["ob", "cf"], ["ps4"])
                        S.cp("act", mixT[k2][:, 3:6, :], ps[4][:, 0:384].rearrange("p (a b) -> p a b", b=128), ["ps4"], [mk2 + "b"])
                    if _STOP <= 5:
                        return
                    ga_, gb_ = rwkv_core(), attn_core()
                    a_live, b_live = True, True
                    while a_live or b_live:
                        for _ in range(2):
                            if a_live:
                                try:
                                    next(ga_)
                                except StopIteration:
                                    a_live = False
                        if b_live:
                            try:
                                next(gb_)
                            except StopIteration:
                                b_live = False
                    oS, sq = lw, kkn
                    S.cp("act", oS[:, :, :], v3(7), ["ps7o"], [n(oS)])
                    S.act(sqb[:, :, :], oS[:, :, :], AF.Square, [n(oS)], ["sqb"])
                    S.cp("dve", osb[:, :, :], oS[:, :, :], [n(oS)], ["osb"])
                    for p in range(3):
                        S.mm(ps[0][:, p * 128:(p + 1) * 128], cb16[:, 0, :], osb[:, p, :], True, True, ["cb16", "osb"], ["ps0"])
                        S.mm(ps[1][:, p * 128:(p + 1) * 128], cb16[:, 0, :], sqb[:, p, :], True, True, ["cb16", "sqb"], ["ps1"])
                    mean, var = kp, bb
                    S.ts("dve", mean[:, :, :], v3(0), 1.0 / 64, None, ALU.mult, None, ["ps0"], [n(mean)])
                    S.act(var[:, :, :], mean[:, :, :], AF.Square, [n(mean)], [n(var)])
                    S.stt(var[:, :, :], v3(1), 1.0 / 64, var[:, :, :], ALU.mult, ALU.subtract, ["ps1", n(var)], [n(var)])
                    S.ts("dve", var[:, :, :], var[:, :, :], GN_EPS, None, ALU.add, None, [n(var)], [n(var)])
                    S.act(var[:, :, :], var[:, :, :], AF.Ln, [n(var)], [n(var)])
                    S.act(var[:, :, :], var[:, :, :], AF.Exp, [n(var)], [n(var)], scale=-0.5)
                    S.tt("dve", oS[:, :, :], oS[:, :, :], mean[:, :, :], ALU.subtract, [n(oS), n(mean)], [n(oS)])
                    S.tt("dve", oS[:, :, :], oS[:, :, :], var[:, :, :], ALU.mult, [n(oS), n(var)], [n(oS)])
                    S.tt("dve", oS[:, :, :], oS[:, :, :], bc(5), ALU.mult, [n(oS), "pr"], [n(oS)])
                    S.tt("dve", oS[:, :, :], oS[:, :, :], bc(6), ALU.add, [n(oS), "pr"], [n(oS)])
                    S.tt("dve", oS[:, :, :], oS[:, :, :], bon[:, :, :], ALU.add, [n(oS), n(cum)], [n(oS)])
                    S.tt("dve", mixT[k2][:, 0:3, :], oS[:, :, :], gT[:, :, :], ALU.mult, [n(oS), n(gT)], [mk2 + "a"])
                    if _STOP <= 11:
                        return
                    qc, kc_, gc = H[:, CH_CQ:CH_CQ + 2, :], H[:, CH_CK:CH_CK + 2, :], H[:, CH_CG:CH_CG + 2, :]
                    ck_ = ["H4", "H5"]
                    S.cp("act", hb16[:, 0:4, :], H[:, CH_CQ:CH_CQ + 4, :], ck_, ["hb16"])
                    for c in range(4):
                        S.mm(ps[0][:, c * 128:(c + 1) * 128], cb16[:, 2, :], hb16[:, c, :], True, True, ["cb16", "hb16"], ["ps0"])
                    b4 = lambda a: a.unsqueeze(1).to_broadcast([128, 4, 128])
                    qkc = H[:, CH_CQ:CH_CQ + 4, :]
                    rt4 = t3[3]
                    r4 = S_r4
                    S.tt("dve", qkc, qkc, b4(cosC), ALU.mult, ck_ + [rk_], ck_)
                    S.tt("dve", r4[:, :, :], ps[0][:, :].rearrange("p (a b) -> p a b", b=128), b4(sinC), ALU.mult, ["ps0", rk_], ["r4"])
                    S.tt("dve", qkc, qkc, r4[:, :, :], ALU.add, ck_ + ["r4"], ck_)
                    if samp:
                        S.memset("dve", H[:, CH_CK:CH_CK + 2, 0:PADN], 0.0, ck_)
                    qd_t = cf[:, 12:15:2, :]
                    kd_t = cf[:, 13:16:2, :]
                    S.cp("act", qcb[:, :, :], qc, ck_, ["qcb"])
                    S.act(kcb[:, :, :], kc_, AF.Copy, ck_, ["kcb"], scale=HD ** -0.5)
                    S.tt("dve", qdb[:, :, :], qc, qd_t, ALU.mult, ck_ + ["cf"], ["qdb"])
                    S.tt("dve", kdb[:, :, :], kc_, kd_t, ALU.mult, ck_ + ["cf"], ["kdb"])
                    for p in range(2):
                        S.tr(psb[1][:, p * 128:(p + 1) * 128], kdb[:, p, :], identb[:, :], ["kdb", "identb"], ["ps1"])
                    S.cp("dve", kdtok[:, :], psb[1][:, 0:256], ["ps1"], ["kdtok"])
                    for h in range(4):
                        p, hb = h // 2, 64 * (h % 2)
                        hs = slice(hb, hb + 64)
                        bnk = 2 + h % 2
                        bk = "ps%d" % bnk
                        S.mm(ps[bnk][:, 0:128], kcb[hs, p, :], qcb[hs, p, :], True, True, ["kcb", "qcb"], [bk + "a"])
                        ab = attb[h % 2]
                        S.tt("dve", ab[:, :], ps[bnk][:, 0:128], cf[:, CI["dmask"] + h, :], ALU.mult, [bk + "a", "cf"], [n(ab)])
                        oreg = ps[4][hs, p * 128:(p + 1) * 128]
                        S.mm(oreg, vcb[:, h * 64:(h + 1) * 64], ab[:, :], True, False, ["vcb", n(ab)], ["ps4"])
                        S.mm(oreg, RbP[h][:, :], qdb[:, p, :], False, True, ["RbP%d" % h, "qdb"], ["ps4"])
                        S.mm(ps[bnk][hs, 128:192], kdtok[:, h * 64:(h + 1) * 64], vcb[:, h * 64:(h + 1) * 64], True, True,
                             ["kdtok", "vcb"], [bk + "s"])
                        S.stt(R[hs, p, :], R[hs, p, :], GAMMA[h] ** 128.0, ps[bnk][hs, 128:192], ALU.mult, ALU.add, ["R", bk + "s"], ["R"])
                        S.cp("act", RbP[h][hs, :], R[hs, p, :], ["R"], ["RbP%d" % h])
                    oc, sq2, mn2, vr2 = c2
                    v2 = lambda b: ps[b][:, 0:256].rearrange("p (a b) -> p a b", b=128)
                    S.cp("act", oc[:, :, :], v2(4), ["ps4"], [n(oc)])
                    S.act(sqb[:, 0:2, :], oc[:, :, :], AF.Square, [n(oc)], ["sqb"])
                    S.cp("dve", osb[:, 0:2, :], oc[:, :, :], [n(oc)], ["osb"])
                    for p in range(2):
                        S.mm(ps[0][:, p * 128:(p + 1) * 128], cb16[:, 0, :], osb[:, p, :], True, True, ["cb16", "osb"], ["ps0"])
                        S.mm(ps[1][:, p * 128:(p + 1) * 128], cb16[:, 0, :], sqb[:, p, :], True, True, ["cb16", "sqb"], ["ps1"])
                    S.ts("dve", mn2[:, :, :], v2(0), 1.0 / 64, None, ALU.mult, None, ["ps0"], [n(mn2)])
                    S.act(vr2[:, :, :], mn2[:, :, :], AF.Square, [n(mn2)], [n(vr2)])
                    S.stt(vr2[:, :, :], v2(1), 1.0 / 64, vr2[:, :, :], ALU.mult, ALU.subtract, ["ps1", n(vr2)], [n(vr2)])
                    S.ts("dve", vr2[:, :, :], vr2[:, :, :], LN_EPS, None, ALU.add, None, [n(vr2)], [n(vr2)])
                    S.act(vr2[:, :, :], vr2[:, :, :], AF.Ln, [n(vr2)], [n(vr2)])
                    S.act(vr2[:, :, :], vr2[:, :, :], AF.Exp, [n(vr2)], [n(vr2)], scale=-0.5)
                    S.tt("dve", oc[:, :, :], oc[:, :, :], mn2[:, :, :], ALU.subtract, [n(oc), n(mn2)], [n(oc)])
                    S.tt("dve", oc[:, :, :], oc[:, :, :], vr2[:, :, :], ALU.mult, [n(oc), n(vr2)], [n(oc)])
                    bcc = lambda k: prc[:, k, :].unsqueeze(2).to_broadcast([128, 2, 128])
                    S.tt("dve", oc[:, :, :], oc[:, :, :], bcc(0), ALU.mult, [n(oc), "prc"], [n(oc)])
                    S.tt("dve", oc[:, :, :], oc[:, :, :], bcc(1), ALU.add, [n(oc), "prc"], [n(oc)])
                    S.act(sq2[:, :, :], gc, AF.Exp, ["H5"], [n(sq2)], scale=-1.0)
                    S.ts("dve", sq2[:, :, :], sq2[:, :, :], 1.0, None, ALU.add, None, [n(sq2)], [n(sq2)])
                    S.op("dve", lambda e: e.reciprocal(out=sq2[:, :, :], in_=sq2[:, :, :]), [n(sq2)], [n(sq2)])
                    S.tt("dve", sq2[:, :, :], sq2[:, :, :], gc, ALU.mult, [n(sq2), "H5"], [n(sq2)])
                    S.tt("dve", mixT[k2][:, 6:8, :], oc[:, :, :], sq2[:, :, :], ALU.mult, [n(oc), n(sq2)], [mk2 + "c"])
                    if _STOP <= 12:
                        return
                    S.dma("pool", mixs[:, :, gi * 128:(gi + 1) * 128].rearrange("k p t -> p k t"), mixT[k2][:, :, :], "stm%d" % k2,
                          reads=[mk2 + "a", mk2 + "b", mk2 + "c"])
                    last = samp or ti == NT - 1
                    if last:
                        for p in range(3):
                            S.tr(ps[6][0:64, p * 128:(p + 1) * 128], U[:, p, :], ident, ["U", "cf"], ["ps6"])
                        S.cp("dve", cstage[0:64, :], ps[6][0:64, 0:384], ["ps6"], ["cstage"])
                        dst = o_swkv[l, seq] if samp else o_pwkv[l]
                        S.dma("pool", dst.rearrange("h v k -> v h k"), cstage[0:64, :].rearrange("v (h k) -> v h k", k=64), "sts", reads=["cstage"])
                        for p in range(2):
                            for hh in range(2):
                                dst = o_sret[l, seq, 2 * p + hh] if samp else o_pret[l, 2 * p + hh]
                                S.dma("pool", dst, R[hh * 64:(hh + 1) * 64, p, :], "sts", reads=["R"])

                for k_ in range(3):
                    S.memset("pool", rhsb[k_][:, :], 0.0, [S.kn(rhsb[k_])])
                    S.memset("pool", umb[k_][:, :], 0.0, [S.kn(umb[k_])])
                onesT = S.sb("onesT", [128, 128])
                S.memset("dve", onesT[:, :], 1.0, ["onesT"])
                S_r4 = S.sb("r4", [128, 4, 128])

                for ti in range(NT):
                    process_tile("p", ti, NT, ti)
                for sb_ in range(NSB):
                    if not _NOSAMP:
                        process_tile(sb_, 0, 1, NT + sb_)
                while bg_jobs:
                    d_, s_src = bg_jobs.pop(0)
                    S.dma("pool", d_, s_src, "bgc")
                S.barrier()
            with ExitStack() as es2:
                S.es = es2
                wo = S.sb("wo", [128, 8, D], BF16)
                wg = S.sb("wg", [128, 8, DFF], BF16)
                wu = S.sb("wu", [128, 8, DFF], BF16)
                wd = S.sb("wd", [128, NFC, D], BF16)
                lnp = S.sb("lnp", [128, 4, D])
                identf = S.sb("identf", [128, 128])
                mh1 = S.sb("mh1", [128, 1])
                xt2 = [S.sb("x2t%d" % k, [128, D]) for k in range(2)]
                mT2 = [S.sb("mT2_%d" % k, [128, 8, 128], BF16) for k in range(2)]
                pre = S.sb("pre", [128, D])
                x1 = S.sb("x1", [128, D])
                x1T = S.sb("x1T", [128, 8, 128], BF16)
                aT = S.sb("aT", [128, NFC, 128], BF16)
                sg = [S.sb("sg%d" % k, [128, 512]) for k in range(2)]
                outt = [S.sb("outt%d" % k, [128, D]) for k in range(1)]
                atok = S.sb("atok", [128, DFF], BF16)
                identb2 = S.sb("identb2", [128, 128], BF16)
                st6 = S.sb("st6", [128, 12])
                mv = S.sb("mv", [128, 4])

                S.dma("sp", wo[:, :, :], wob[l].rearrange("(k p) d -> p k d", p=128), "w2", writes=["wo"])
                for kc in range(0, 8, 2):
                    S.dma("sp", wg[:, kc:kc + 2, :], wgb[l, kc * 128:(kc + 2) * 128, :].rearrange("(k p) d -> p k d", p=128), "w2", writes=["wg"])
                    S.dma("act", wu[:, kc:kc + 2, :], wub[l, kc * 128:(kc + 2) * 128, :].rearrange("(k p) d -> p k d", p=128), "w2", writes=["wu"])
                S.dma("sp", wd[:, 0:11, :], wdb[l, 0:11 * 128, :].rearrange("(k p) d -> p k d", p=128), "w2", writes=["wd"])
                S.dma("act", wd[:, 11:22, :], wdb[l, 11 * 128:22 * 128, :].rearrange("(k p) d -> p k d", p=128), "w2", writes=["wd"])
                if l + 1 < DEPTH:
                    for kc in range(8):
                        bg_jobs.append((winb[kc * 128:(kc + 1) * 128, :], w_in[l + 1, kc * 128:(kc + 1) * 128, :]))
                for k, prm in enumerate([ln1_g, ln1_b, ln2_g, ln2_b]):
                    S.dma("sp", lnp[:, k, :], prm[l:l + 1, :].partition_broadcast(128), "c2", writes=["lnp"])
                S.dma("sp", identf[:, :], c_f[:, 0, :], "c2", writes=["identf"])
                S.memset("dve", mh1[:, :], -0.5, ["mh1"])
                S.cp("dve", identb2[:, :], identf[:, :], ["identf"], ["identb2"])

                def layer_norm(src, dst, gk, bk_, eps):
                    for c in range(2):
                        S.op("dve", lambda e, c=c: e.bn_stats(out=st6[:, c * 6:(c + 1) * 6], in_=src[:, c * 512:(c + 1) * 512]),
                             [S.kn(src)], ["st6"])
                    S.op("dve", lambda e: e.bn_aggr(out=mv[:, 0:2], in_=st6[:, 0:12]), ["st6"], ["mv"])
                    S.ts("dve", mv[:, 2:3], mv[:, 1:2], eps, None, ALU.add, None, ["mv"], ["mv"])
                    S.act(mv[:, 3:4], mv[:, 2:3], AF.Ln, ["mv"], ["mv"])
                    S.act(mv[:, 3:4], mv[:, 3:4], AF.Exp, ["mv"], ["mv"], scale=-0.5)
                    S.ts("dve", dst[:, :], src[:, :], mv[:, 0:1], mv[:, 3:4], ALU.subtract, ALU.mult, [S.kn(src), "mv"], [S.kn(dst)])
                    S.tt("dve", dst[:, :], dst[:, :], lnp[:, gk, :], ALU.mult, [S.kn(dst), "lnp"], [S.kn(dst)])
                    S.tt("dve", dst[:, :], dst[:, :], lnp[:, bk_, :], ALU.add, [S.kn(dst), "lnp"], [S.kn(dst)])

                for gi in range(NTS if not _NOP2 else 0):
                    k2 = gi % 2
                    samp = gi >= NT
                    xk = "x2t%d" % k2
                    mk = "mT2_%d" % k2
                    if bg_jobs:
                        d_, s_src = bg_jobs.pop(0)
                        S.dma("pool", d_, s_src, "bgc")
                    if samp and l == 0:
                        S.memset("pool", xt2[k2][:, :], 0.0, [xk])
                        S.dma("sp", xt2[k2][PADN:128, :], xs[(gi - NT) * 4:(gi - NT + 1) * 4, :], "l2x%d" % k2, writes=[xk])
                    else:
                        S.dma("sp", xt2[k2][:, :], x_src(l, gi), "l2x%d" % k2, writes=[xk])
                    S.dma("sp", mT2[k2][:, :, :], mixs[:, :, gi * 128:(gi + 1) * 128].rearrange("k p t -> p k t"), "l2x%d" % k2, writes=[mk])
                    for half in range(2):
                        for kc in range(8):
                            S.mm(ps[half][:, :], mT2[k2][:, kc, :], wo[:, kc, half * 512:(half + 1) * 512], kc == 0, kc == 7,
                                 [mk, "wo"], ["ps%d" % half])
                        S.stt(pre[:, half * 512:(half + 1) * 512], xt2[k2][:, half * 512:(half + 1) * 512], ALPHA, ps[half][:, :],
                              ALU.mult, ALU.add, [xk, "ps%d" % half], ["pre"])
                    layer_norm(pre, x1, 0, 1, LN_EPS)
                    for half in range(2):
                        for j in range(4):
                            kc = half * 4 + j
                            S.tr(ps[2 + half][:, j * 128:(j + 1) * 128], x1[:, kc * 128:(kc + 1) * 128], identf[:, :], ["x1", "identf"],
                                 ["ps%d" % (2 + half)])
                        S.cp("act", x1T[:, half * 4:(half + 1) * 4, :], ps[2 + half][:, :].rearrange("p (a b) -> p a b", b=128),
                             ["ps%d" % (2 + half)], ["x1T"])
                    NG = (DFF + 511) // 512
                    for g_ in range(NG):
                        c0 = g_ * 512
                        cw = min(512, DFF - c0)
                        bg, bu = 4 + (g_ % 2) * 2, 5 + (g_ % 2) * 2
                        for kc in range(8):
                            S.mm(ps[bg][:, 0:cw], x1T[:, kc, :], wg[:, kc, c0:c0 + cw], kc == 0, kc == 7, ["wg", "x1T"], ["ps%d" % bg])
                        for kc in range(8):
                            S.mm(ps[bu][:, 0:cw], x1T[:, kc, :], wu[:, kc, c0:c0 + cw], kc == 0, kc == 7, ["wu", "x1T"], ["ps%d" % bu])
                        s_ = sg[g_ % 2]
                        S.act(s_[:, 0:cw], ps[bg][:, 0:cw], AF.Silu, ["ps%d" % bg], [S.kn(s_)])
                        S.tt("dve", atok[:, c0:c0 + cw], s_[:, 0:cw], ps[bu][:, 0:cw], ALU.mult, [S.kn(s_), "ps%d" % bu], ["atok%d" % g_])
                        nch = cw // 128
                        bt = 2 + g_ % 2
                        for j in range(nch):
                            S.tr(psb[bt][:, j * 128:(j + 1) * 128], atok[:, c0 + j * 128:c0 + (j + 1) * 128], identb2[:, :],
                                 ["atok%d" % g_, "identb2"], ["ps%d" % bt])
                        S.cp("act", aT[:, g_ * 4:g_ * 4 + nch, :], psb[bt][:, 0:nch * 128].rearrange("p (a b) -> p a b", b=128),
                             ["ps%d" % bt], ["aT"])
                    for half in range(2):
                        for c in range(NFC):
                            S.mm(ps[half][:, :], aT[:, c, :], wd[:, c, half * 512:(half + 1) * 512], c == 0, c == NFC - 1,
                                 ["aT", "wd"], ["ps%d" % half])
                        S.stt(pre[:, half * 512:(half + 1) * 512], x1[:, half * 512:(half + 1) * 512], ALPHA, ps[half][:, :],
                              ALU.mult, ALU.add, ["x1", "ps%d" % half], ["pre"])
                    ot = outt[0]
                    layer_norm(pre, ot, 2, 3, LN_EPS)
                    if l < DEPTH - 1:
                        S.dma("pool", x2s[gi * 128:(gi + 1) * 128, :], ot[:, :], "st2_%d" % k2, reads=[S.kn(ot)])
                    elif samp:
                        sb_ = gi - NT
                        S.dma("pool", y_s[sb_ * 4:(sb_ + 1) * 4, :], ot[PADN:128, :], "st2_%d" % k2, reads=[S.kn(ot)])
                    else:
                        S.dma("pool", y_p[gi * 128:(gi + 1) * 128, :], ot[:, :], "st2_%d" % k2, reads=[S.kn(ot)])
                while bg_jobs:
                    d_, s_src = bg_jobs.pop(0)
                    S.dma("pool", d_, s_src, "bgc")
                S.barrier()
        S.es = es_top
    return nc


_W_NAMES = ["w_in", "rwkv_mu", "rwkv_w0", "rwkv_w_lora", "rwkv_a0", "rwkv_a_lora", "rwkv_g_lora", "rwkv_k_k", "rwkv_k_a",
            "rwkv_r_k", "rwkv_gn_g", "rwkv_gn_b", "ret_gn_g", "ret_gn_b", "w_out", "ln1_g", "ln1_b", "w_ffn_gate", "w_ffn_up",
            "w_ffn_down", "ln2_g", "ln2_b"]


def run(inputs, T, n_cores=8, trace=False):
    f = lambda a: np.ascontiguousarray(np.asarray(a, dtype=np.float32))
    x_prompt = f(inputs["x_prompt"])
    x_sample = f(inputs["x_sample"])
    nb = x_prompt.shape[0]
    consts = make_consts(T)
    shared = {k: f(inputs[k]) for k in _W_NAMES}
    shared["rwkv_r_k"] = shared["rwkv_r_k"].reshape(DEPTH, W_A)
    shared["cf"] = consts["cf"]
    shared["amask"] = consts["amask"]
    shared["rot"] = consts["rot"]
    st_shift, st_wkv = f(inputs["state_rwkv_shift"]), f(inputs["state_rwkv_wkv"])
    ckk, cvv, st_ret = f(inputs["cache_win_k"]), f(inputs["cache_win_v"]), f(inputs["state_ret"])
    in_maps = []
    for c in range(n_cores):
        b = (c * nb) // n_cores
        s0 = c * NSB
        m = dict(shared)
        m["xp"] = x_prompt[b]
        m["xs"] = np.ascontiguousarray(x_sample[s0:s0 + NSB].reshape(NSB * 4, D))
        m["st_shift"] = np.ascontiguousarray(st_shift[:, s0:s0 + NSB])
        m["st_wkv"] = np.ascontiguousarray(st_wkv[:, s0:s0 + NSB])
        m["ck"] = np.ascontiguousarray(ckk[:, s0:s0 + NSB].reshape(DEPTH, NSB, WIN, W_B))
        m["cv"] = np.ascontiguousarray(cvv[:, s0:s0 + NSB].reshape(DEPTH, NSB, WIN, W_B))
        m["st_ret"] = np.ascontiguousarray(st_ret[:, s0:s0 + NSB])
        in_maps.append(m)
    nc = build(T)
    res = run_bass_kernel_spmd(nc, in_maps, core_ids=list(range(n_cores)), trace=trace)
    R = res.results
    per = n_cores // nb
    WK = min(WIN, T)
    own = [R[b * per] for b in range(nb)]
    y_prompt = np.stack([o["y_p"] for o in own])
    y_sample = np.concatenate([r["y_s"].reshape(NSB, 4, D) for r in R], axis=0)
    p_shift = np.stack([o["p_shift"] for o in own], axis=1)
    p_wkv = np.stack([o["p_wkv"] for o in own], axis=1)
    p_wk = np.stack([o["p_wk"].reshape(DEPTH, WK, H_B, HD) for o in own], axis=1)
    p_wv = np.stack([o["p_wv"].reshape(DEPTH, WK, H_B, HD) for o in own], axis=1)
    p_ret = np.stack([o["p_ret"] for o in own], axis=1)
    s_shift = np.concatenate([r["s_shift"] for r in R], axis=1)
    s_wkv = np.concatenate([r["s_wkv"] for r in R], axis=1)
    s_wk = np.concatenate([r["s_wk"].reshape(DEPTH, NSB, WIN, H_B, HD) for r in R], axis=1)
    s_wv = np.concatenate([r["s_wv"].reshape(DEPTH, NSB, WIN, H_B, HD) for r in R], axis=1)
    s_ret = np.concatenate([r["s_ret"] for r in R], axis=1)
    outs = (y_prompt, y_sample, p_shift, p_wkv, p_wk, p_wv, p_ret, s_shift, s_wkv, s_wk, s_wv, s_ret)
    return tuple(np.ascontiguousarray(o, dtype=np.float32) for o in outs), res


def kernel(**inputs):
    T = int(np.asarray(inputs["x_prompt"]).shape[1])
    outs, _ = run(inputs, T)
    return outs
```

```python
import math
import os
_STOP = float(os.environ.get('KSTOP', '99'))
_NOP2 = int(os.environ.get('KNOP2', '0'))
_NOSAMP = int(os.environ.get('KNOSAMP', '0'))
from contextlib import ExitStack
import numpy as np
import ml_dtypes
import concourse.bass as bass
import concourse.mybir as mybir
from concourse.bass_utils import run_bass_kernel_spmd

F32 = mybir.dt.float32
BF16 = mybir.dt.bfloat16
AF = mybir.ActivationFunctionType
ALU = mybir.AluOpType
AX = mybir.AxisListType

D = 1024
DFF = 2816
NFC = DFF // 128
HD = 64
H_A, H_B, H_C = 6, 6, 4
W_A, W_B, W_C = 384, 384, 256
A_COLS = 1408
IN_COLS = 3584
WIN = 2048
NBLK = 17
RING = 18
PAST = 8192
DEPTH = 2
ALPHA = (2 * DEPTH) ** 0.25
DECAY = math.exp(-0.5)
GN_EPS = 64e-5
LN_EPS = 1e-5
GAMMA = [1.0 - 2.0 ** (-5.0 - h) for h in range(H_C)]
PADN = 124
NSB = 4
FM_COLS = [(0, 1408), (1408, 1408 + 768), (2560, 2560 + 512), (3328, 3584)]
FM_CHUNK_COL = []
for a_, b_ in FM_COLS:
    for c_ in range(a_, b_, 128):
        FM_CHUNK_COL.append(c_)
NFM = len(FM_CHUNK_COL)
CH_BQ, CH_BK, CH_CQ, CH_CK, CH_CG = 11, 14, 17, 19, 21
BV_COL, CV_COL = 2176, 3072


class Sched:
    ENGS = ("pe", "dve", "act", "pool", "sp")

    def __init__(self, nc, es):
        self.nc = nc
        self.es_sem = es
        self.es = es
        self.eng = {"pe": nc.tensor, "dve": nc.vector, "act": nc.scalar, "pool": nc.gpsimd, "sp": nc.sync}
        self.sem, self.cnt = {}, {}
        for e in self.ENGS:
            self.sem[e] = es.enter_context(nc.semaphore("q_" + e))
            self.cnt[e] = 0
        self.waited = {e: {} for e in self.ENGS}
        self.last_w, self.readers, self.dsem = {}, {}, {}
        self.kids = {}
        self.bank_last = {}
        self.n_ins = 0
        self.uid = 0
        self.keyname = {}
        self.keep = []

    def _rel(self, k):
        if k.startswith("ps") and len(k) >= 3 and k[2].isdigit():
            bank = k[:3]
            if k == bank:
                return [bank] + list(self.kids.get(bank, ()))
            self.kids.setdefault(bank, set()).add(k)
            return [k, bank]
        return [k]

    def sb(self, name, shape, dt=F32):
        self.uid += 1
        t = self.es.enter_context(self.nc.sbuf_tensor("%s_u%d" % (name, self.uid), list(shape), dt))
        self.keyname[id(t)] = name
        self.keep.append(t)
        return t

    def kn(self, t):
        return self.keyname[id(t)]

    def ps(self, name, shape, dt=F32):
        return self.es.enter_context(self.nc.psum_tensor(name, list(shape), dt))

    def _dma_sem(self, key):
        if key not in self.dsem:
            p = "d_" + key
            self.sem[p] = self.es_sem.enter_context(self.nc.semaphore(p))
            self.cnt[p] = 0
            self.dsem[key] = p
        return self.dsem[key]

    def _wait(self, E, p, c):
        if self.waited[E].get(p, 0) >= c:
            return
        self.eng[E].wait_ge(self.sem[p], c)
        self.waited[E][p] = c

    @staticmethod
    def _bank(k):
        if k.startswith("ps") and len(k) >= 3 and k[2].isdigit():
            return k[:3]
        return None

    def _need(self, E, reads, writes):
        need = {}

        def add(p, c):
            if c > need.get(p, 0):
                need[p] = c
        for r in reads:
            b = self._bank(r)
            if b is not None:
                for p, c in self.bank_last.get(b, {}).items():
                    if p != E:
                        add(p, c)
                continue
            lw = self.last_w.get(r)
            if lw is not None:
                add(*lw)
        for w in writes:
            b = self._bank(w)
            if b is not None:
                for p, c in self.bank_last.get(b, {}).items():
                    if p != E:
                        add(p, c)
                continue
            lw = self.last_w.get(w)
            if lw is not None and lw[0] != E:
                add(*lw)
            for p, c in self.readers.get(w, {}).items():
                if p != E:
                    add(p, c)
        for p, c in need.items():
            if p.startswith("d_"):
                c = self.cnt[p]
            self._wait(E, p, c)

    def _commit(self, P, c, reads, writes):
        for r in reads:
            b = self._bank(r)
            if b is not None:
                self.bank_last.setdefault(b, {})[P] = c
                continue
            self.readers.setdefault(r, {})[P] = c
        for w in writes:
            b = self._bank(w)
            if b is not None:
                self.bank_last.setdefault(b, {})[P] = c
                continue
            self.last_w[w] = (P, c)
            self.readers[w] = {}

    def op(self, E, fn, reads=(), writes=()):
        self._need(E, reads, writes)
        ins = fn(self.eng[E])
        self.cnt[E] += 1
        ins.then_inc(self.sem[E], 1)
        self._commit(E, self.cnt[E], reads, writes)
        self.n_ins += 1
        return ins

    def mm(self, out, lhsT, rhs, start, stop, reads, writes):
        return self.op("pe", lambda e: e.matmul(out, lhsT=lhsT, rhs=rhs, start=start, stop=stop), reads, writes)

    def tr(self, out, in_, ident, reads, writes):
        return self.op("pe", lambda e: e.transpose(out=out, in_=in_, identity=ident), reads, writes)

    def dma(self, Q, out, in_, key, reads=(), writes=()):
        p = self._dma_sem(key)
        self._need(Q, reads, writes)
        ins = self.eng[Q].dma_start(out=out, in_=in_)
        self.cnt[p] += 16
        ins.then_inc(self.sem[p], 16)
        self._commit(p, self.cnt[p], reads, writes)
        self.n_ins += 1
        return ins

    def barrier(self):
        for E in self.ENGS:
            for p in list(self.sem.keys()):
                if p != E and self.cnt[p] > 0:
                    self._wait(E, p, self.cnt[p])
        self.last_w, self.readers = {}, {}
        self.bank_last = {}

    def tt(self, E, out, in0, in1, op, r, w):
        return self.op(E, lambda e: e.tensor_tensor(out=out, in0=in0, in1=in1, op=op), r, w)

    def ts(self, E, out, in0, s1, s2, op0, op1, r, w):
        if s2 is None:
            return self.op(E, lambda e: e.tensor_scalar(out=out, in0=in0, scalar1=s1, scalar2=None, op0=op0), r, w)
        return self.op(E, lambda e: e.tensor_scalar(out=out, in0=in0, scalar1=s1, scalar2=s2, op0=op0, op1=op1), r, w)

    def stt(self, out, in0, sc, in1, op0, op1, r, w):
        return self.op("dve", lambda e: e.scalar_tensor_tensor(out=out, in0=in0, scalar=sc, in1=in1, op0=op0, op1=op1), r, w)

    def act(self, out, in_, func, r, w, scale=None, bias=None):
        kw = {}
        if scale is not None:
            kw["scale"] = scale
        if bias is not None:
            kw["bias"] = bias
        return self.op("act", lambda e: e.activation(out=out, in_=in_, func=func, **kw), r, w)

    def cp(self, E, out, in_, r, w):
        if E == "act":
            return self.act(out, in_, AF.Copy, r, w)
        return self.op(E, lambda e: e.tensor_copy(out=out, in_=in_), r, w)

    def memset(self, E, ap, val, w):
        return self.op(E, lambda e: e.memset(ap, val), (), w)


def _mult(d):
    d = np.asarray(d)
    ok = d >= 0
    m = ((d <= 128) & ok).astype(np.float32)
    m += ((d % 4 == 0) & (d <= 512) & ok)
    m += ((d % 16 == 0) & (d <= 2048) & ok)
    return m.astype(np.float32)


def make_consts(T):
    NT = T // 128
    c = {}
    i = np.arange(128)
    same = (i[:, None] // 64) == (i[None, :] // 64)
    su = ((i[:, None] < i[None, :]) & same).astype(np.float32)
    ui = ((i[:, None] <= i[None, :]) & same).astype(np.float32)
    ident = np.eye(128, dtype=np.float32)
    bones = same.astype(np.float32)
    def pm(half, hd=64):
        m = np.zeros((128, 128), np.float32)
        for blk in range(2):
            o = blk * hd
            for e in range(half):
                m[o + e + half, o + e] = -1.0
                m[o + e, o + e + half] = 1.0
        return m
    packs = [ident, su, ui, su.T.copy(), -ui, bones, pm(8), pm(32)]
    names = ["ident", "su", "ui", "sl", "nui", "bones", "pmb", "pmc"]
    for h in range(H_C):
        g = GAMMA[h]
        rel = i[None, :] - i[:, None]
        dm = np.where(rel >= 0, np.exp(np.log(g) * np.maximum(rel, 0)), 0.0).astype(np.float32)
        packs.append(dm)
        names.append("dmask%d" % h)
    for p in range(2):
        qd = np.zeros((128, 128), np.float32)
        kd = np.zeros((128, 128), np.float32)
        for hh in range(2):
            g = GAMMA[2 * p + hh]
            qd[hh * 64:(hh + 1) * 64, :] = np.exp(np.log(g) * (i + 1.0))[None, :]
            kd[hh * 64:(hh + 1) * 64, :] = (np.exp(np.log(g) * (127.0 - i)) * (HD ** -0.5))[None, :]
        packs += [qd, kd]
        names += ["qdec%d" % p, "kdec%d" % p]
    c["cf"] = np.stack(packs, axis=1).astype(np.float32)
    c["cf_names"] = names
    mk = np.zeros((128, NBLK, 128), np.float32)
    for dl in range(NBLK):
        d = dl * 128 + i[None, :] - i[:, None]
        mk[:, dl, :] = _mult(d)
    c["amask"] = mk.astype(ml_dtypes.bfloat16)
    ntile = NT + 1
    rt = np.zeros((ntile, 128, 4, 128), np.float32)
    inv_b = (500000.0 ** (-np.arange(0, 16, 2, dtype=np.float32) / np.float32(16))).astype(np.float32)
    inv_c = (1.0 / (np.float32(10000.0) ** np.linspace(0.0, 1.0, 32, dtype=np.float32))).astype(np.float32)
    for ti in range(ntile):
        if ti < NT:
            pos = (ti * 128 + i).astype(np.float32)
        else:
            pos = (np.float32(PAST) + np.maximum(i - PADN, 0)).astype(np.float32)
        angb = (pos[:, None] * inv_b[None, :]).astype(np.float32)
        angc = (pos[:, None] * inv_c[None, :]).astype(np.float32)
        cb = np.ones((64, 128), np.float32)
        sbb = np.zeros((64, 128), np.float32)
        cb[0:8] = np.cos(angb).T
        cb[8:16] = np.cos(angb).T
        sbb[0:8] = np.sin(angb).T
        sbb[8:16] = np.sin(angb).T
        cc = np.concatenate([np.cos(angc).T, np.cos(angc).T], axis=0)
        sc = np.concatenate([np.sin(angc).T, np.sin(angc).T], axis=0)
        for hh in range(2):
            rt[ti, hh * 64:(hh + 1) * 64, 0] = cb
            rt[ti, hh * 64:(hh + 1) * 64, 1] = sbb
            rt[ti, hh * 64:(hh + 1) * 64, 2] = cc
            rt[ti, hh * 64:(hh + 1) * 64, 3] = sc
    c["rot"] = rt
    return c


def build(T):
    NT = T // 128
    NTS = NT + NSB
    WK = min(WIN, T)
    NKEEP = WK // 128
    nc = bass.Bass("TRN2", target_bir_lowering=False)

    def din(name, shape, dt=F32):
        return nc.dram_tensor(name, list(shape), dt, kind="ExternalInput").ap()

    def dout(name, shape, dt=F32):
        return nc.dram_tensor(name, list(shape), dt, kind="ExternalOutput").ap()

    def dscr(name, shape, dt=F32):
        return nc.dram_tensor(name, list(shape), dt, kind="Internal").ap()

    xp = din("xp", [T, D])
    xs = din("xs", [NSB * 4, D])
    st_shift = din("st_shift", [DEPTH, NSB, A_COLS])
    st_wkv = din("st_wkv", [DEPTH, NSB, H_A, 64, 64])
    ck = din("ck", [DEPTH, NSB, WIN, W_B])
    cv = din("cv", [DEPTH, NSB, WIN, W_B])
    st_ret = din("st_ret", [DEPTH, NSB, H_C, 64, 64])
    w_in = din("w_in", [DEPTH, D, IN_COLS])
    p_mu = din("rwkv_mu", [DEPTH, A_COLS])
    p_w0 = din("rwkv_w0", [DEPTH, W_A])
    p_wl = din("rwkv_w_lora", [DEPTH, 64, W_A])
    p_a0 = din("rwkv_a0", [DEPTH, W_A])
    p_al = din("rwkv_a_lora", [DEPTH, 64, W_A])
    p_gl = din("rwkv_g_lora", [DEPTH, 128, W_A])
    p_kk = din("rwkv_k_k", [DEPTH, W_A])
    p_ka = din("rwkv_k_a", [DEPTH, W_A])
    p_rk = din("rwkv_r_k", [DEPTH, W_A])
    p_gg = din("rwkv_gn_g", [DEPTH, W_A])
    p_gb = din("rwkv_gn_b", [DEPTH, W_A])
    p_rg = din("ret_gn_g", [DEPTH, W_C])
    p_rb = din("ret_gn_b", [DEPTH, W_C])
    w_out = din("w_out", [DEPTH, D, D])
    ln1_g = din("ln1_g", [DEPTH, D])
    ln1_b = din("ln1_b", [DEPTH, D])
    w_gate = din("w_ffn_gate", [DEPTH, D, DFF])
    w_up = din("w_ffn_up", [DEPTH, D, DFF])
    w_down = din("w_ffn_down", [DEPTH, DFF, D])
    ln2_g = din("ln2_g", [DEPTH, D])
    ln2_b = din("ln2_b", [DEPTH, D])
    c_f = din("cf", [128, 16, 128])
    c_am = din("amask", [128, NBLK, 128], BF16)
    c_rot = din("rot", [NT + 1, 128, 4, 128])

    y_p = dout("y_p", [T, D])
    y_s = dout("y_s", [NSB * 4, D])
    o_pshift = dout("p_shift", [DEPTH, A_COLS])
    o_pwkv = dout("p_wkv", [DEPTH, H_A, 64, 64])
    o_pwk = dout("p_wk", [DEPTH, WK, W_B])
    o_pwv = dout("p_wv", [DEPTH, WK, W_B])
    o_pret = dout("p_ret", [DEPTH, H_C, 64, 64])
    o_sshift = dout("s_shift", [DEPTH, NSB, A_COLS])
    o_swkv = dout("s_wkv", [DEPTH, NSB, H_A, 64, 64])
    o_swk = dout("s_wk", [DEPTH, NSB, WIN, W_B])
    o_swv = dout("s_wv", [DEPTH, NSB, WIN, W_B])
    o_sret = dout("s_ret", [DEPTH, NSB, H_C, 64, 64])

    mixs = dscr("mixs", [8, 128, NTS * 128], BF16)
    wob = dscr("wob", [DEPTH, D, D], BF16)
    wgb = dscr("wgb", [DEPTH, D, DFF], BF16)
    wub = dscr("wub", [DEPTH, D, DFF], BF16)
    wdb = dscr("wdb", [DEPTH, DFF, D], BF16)
    winb = dscr("winb", [D, IN_COLS], BF16)
    bg_jobs = []
    x2s = dscr("x2s", [NTS * 128, D])

    CI = {"ident": 0, "su": 1, "ui": 2, "sl": 3, "nui": 4, "bones": 5, "pmb": 6, "pmc": 7,
          "dmask": 8, "qdec0": 12, "kdec0": 13, "qdec1": 14, "kdec1": 15}

    with ExitStack() as es_top:
        es_top.enter_context(nc.allow_non_contiguous_dma(reason="small strided parameter / state transfers"))
        S = Sched(nc, es_top)
        ps = [S.ps("ps%d" % b, [128, 512]) for b in range(8)]
        psb = [p[:, :].bitcast(BF16) for p in ps]

        def x_src(l, ti):
            if l == 0:
                return xp[ti * 128:(ti + 1) * 128, :]
            return x2s[ti * 128:(ti + 1) * 128, :]

        for l in range(DEPTH):
            with ExitStack() as es1:
                S.es = es1
                win = S.sb("win", [128, 8, IN_COLS], BF16)
                cf = S.sb("cf", [128, 16, 128])
                identb = S.sb("identb", [128, 128], BF16)
                amask = S.sb("amask", [128, NBLK, 128], BF16)
                mhalf = S.sb("mhalf", [128, 384])
                muc = S.sb("muc", [128, 11])
                pr = S.sb("pr", [128, 8, 3])
                prc = S.sb("prc", [128, 2, 2])
                wl = S.sb("wl", [128, 384])
                al = S.sb("al", [128, 384])
                gl = S.sb("gl", [128, 384])
                xt = [S.sb("xt%d" % k, [128, D]) for k in range(2)]
                xT = S.sb("xT", [128, 8, 128], BF16)
                H = S.sb("H", [128, NFM, 128])
                hm = S.sb("hm", [128, 11, 128])
                hml = S.sb("hml", [128, 11])
                rot = [S.sb("rot%d" % k, [128, 4, 128]) for k in range(2)]
                LI = S.sb("LI", [128, 2, 128])
                t3 = [S.sb("t3_%d" % k, [128, 3, 128]) for k in range(10)]
                TTt = S.sb("TTt", [128, 3, 4, 128], BF16)
                ktok = S.sb("ktok", [128, 384], BF16)
                nbtok = S.sb("nbtok", [128, 384], BF16)
                vtok = S.sb("vtok", [128, 384], BF16)
                U = S.sb("U", [128, 3, 64])
                UbP = [S.sb("UbP%d" % h, [128, 64], BF16) for h in range(6)]
                RbP = [S.sb("RbP%d" % h, [128, 64], BF16) for h in range(4)]
                chPQ = [[S.sb("chPQ%d_%d" % (h, k), [128, 2, 128], BF16) for k in range(2)] for h in range(3)]
                chT = [[S.sb("chT%d_%d" % (h, k), [128, 128], BF16) for k in range(2)] for h in range(3)]
                cb16 = S.sb("cb16", [128, 3, 128], BF16)
                sqb = S.sb("sqb", [128, 3, 128], BF16)
                osb = S.sb("osb", [128, 3, 128], BF16)
                hb16 = S.sb("hb16", [128, 6, 128], BF16)
                invT = [S.sb("invT%d" % h, [128, 128], BF16) for h in range(3)]
                m4t = [S.sb("m4t%d" % h, [128, 128], BF16) for h in range(3)]
                lm3 = [S.sb("lm3_%d" % h, [128, 256], BF16) for h in range(3)]
                rhsb = [S.sb("rhsb%d" % h, [128, 64], BF16) for h in range(3)]
                umb = [S.sb("umb%d" % h, [128, 64], BF16) for h in range(3)]
                utmp = [S.sb("utmp%d" % h, [128, 64]) for h in range(3)]
                KTr = S.sb("KTr", [128, 3, RING, 128], BF16)
                Vr = S.sb("Vr", [128, RING, 6, 66], BF16)
                qTb = S.sb("qTb", [128, 3, 128], BF16)
                krot = S.sb("krot", [128, 3, 128])
                stg = [S.sb("stg%d" % k, [128, 384]) for k in range(2)]
                vstg = [S.sb("vstg%d" % k, [128, 384]) for k in range(2)]
                pT = [S.sb("pT%d" % k, [128, 512], BF16) for k in range(3)]
                ob = S.sb("ob", [128, 384])
                rl = S.sb("rl", [128, 6])
                c2 = [S.sb("c2_%d" % k, [128, 2, 128]) for k in range(4)]
                qcb = S.sb("qcb", [128, 2, 128], BF16)
                kcb = S.sb("kcb", [128, 2, 128], BF16)
                qdb = S.sb("qdb", [128, 2, 128], BF16)
                kdb = S.sb("kdb", [128, 2, 128], BF16)
                kdtok = S.sb("kdtok", [128, 256], BF16)
                vcb = S.sb("vcb", [128, 256], BF16)
                attb = [S.sb("attb%d" % k, [128, 128], BF16) for k in range(2)]
                R = S.sb("R", [128, 2, 64])
                mixT = [S.sb("mixT%d" % k, [128, 8, 128], BF16) for k in range(2)]
                cstage = S.sb("cstage", [128, 384])
                sst = S.sb("sst", [128, 32])
                swk = S.sb("swk", [128, 64])

                ident = cf[:, CI["ident"], :]

                for kc in range(8):
                    if l == 0:
                        S.dma("pool", win[:, kc, :], w_in[l, kc * 128:(kc + 1) * 128, :], "w1", writes=["win"])
                    else:
                        S.dma("sp", win[:, kc, :], winb[kc * 128:(kc + 1) * 128, :], "w1", writes=["win"])
                for kc in range(8):
                    bg_jobs.append((wob[l, kc * 128:(kc + 1) * 128, :], w_out[l, kc * 128:(kc + 1) * 128, :]))
                for kc in range(8):
                    bg_jobs.append((wgb[l, kc * 128:(kc + 1) * 128, :], w_gate[l, kc * 128:(kc + 1) * 128, :]))
                    bg_jobs.append((wub[l, kc * 128:(kc + 1) * 128, :], w_up[l, kc * 128:(kc + 1) * 128, :]))
                for c in range(NFC):
                    bg_jobs.append((wdb[l, c * 128:(c + 1) * 128, :], w_down[l, c * 128:(c + 1) * 128, :]))
                S.dma("sp", cf[:, :, :], c_f[:, :, :], "c1", writes=["cf"])
                S.dma("sp", amask[:, :, :], c_am[:, :, :], "c1", writes=["amask"])
                S.dma("sp", muc[:, :], p_mu[l].rearrange("(c p) -> p c", p=128), "c1", writes=["muc"])
                for k, prm in enumerate([p_w0, p_a0, p_kk, p_ka, p_rk, p_gg, p_gb]):
                    S.dma("sp", pr[:, k, :], prm[l].rearrange("(c p) -> p c", p=128), "c1", writes=["pr"])
                S.dma("sp", prc[:, 0, :], p_rg[l].rearrange("(c p) -> p c", p=128), "c1", writes=["prc"])
                S.dma("sp", prc[:, 1, :], p_rb[l].rearrange("(c p) -> p c", p=128), "c1", writes=["prc"])
                S.dma("sp", wl[0:64, :], p_wl[l], "c1", writes=["wl"])
                S.dma("sp", al[64:128, :], p_al[l], "c1", writes=["al"])
                S.dma("sp", gl[:, :], p_gl[l], "c1", writes=["gl"])
                S.cp("dve", identb[:, :], ident, ["cf"], ["identb"])
                S.cp("dve", cb16[:, :, :], cf[:, CI["bones"]:CI["bones"] + 3, :], ["cf"], ["cb16"])
                S.memset("dve", mhalf[:, :], -0.5, ["mhalf"])
                S.ts("dve", pr[:, 0:2, :], pr[:, 0:2, :], -1.0, None, ALU.mult, None, ["pr"], ["pr"])

                for sb_ in range(NSB):
                    S.dma("act", o_swk[l, sb_, 0:WIN - 4, :], ck[l, sb_, 4:WIN, :], "cpk")
                    S.dma("act", o_swv[l, sb_, 0:WIN - 4, :], cv[l, sb_, 4:WIN, :], "cpv")
                tile_ctr = [0]

                def process_tile(seq, ti, nseq, gi):
                    samp = seq != "p"
                    if _STOP <= 0:
                        return
                    for _ in range(2):
                        if bg_jobs:
                            d_, s_src = bg_jobs.pop(0)
                            S.dma("pool", d_, s_src, "bgc")
                    k2 = tile_ctr[0] % 2
                    tile_ctr[0] += 1
                    first = ti == 0
                    xk = "xt%d" % k2
                    rk_ = "rot%d" % k2
                    if samp and l == 0:
                        S.memset("pool", xt[k2][:, :], 0.0, [xk])
                        S.dma("sp", xt[k2][PADN:128, :], xs[seq * 4:(seq + 1) * 4, :], "ldx%d" % k2, writes=[xk])
                    else:
                        S.dma("sp", xt[k2][:, :], x_src(l, gi), "ldx%d" % k2, writes=[xk])
                    rti = NT if samp else ti
                    S.dma("sp", rot[k2][:, :, :], c_rot[rti], "ldx%d" % k2, writes=[rk_])
                    for half in range(2):
                        for j in range(4):
                            kc = half * 4 + j
                            S.tr(ps[half][:, j * 128:(j + 1) * 128], xt[k2][:, kc * 128:(kc + 1) * 128], ident,
                                 [xk, "cf"], ["ps%d" % half])
                        S.cp("act" if half == 0 else "dve", xT[:, half * 4:(half + 1) * 4, :],
                             ps[half][:, :].rearrange("p (a b) -> p a b", b=128), ["ps%d" % half], ["xT%d" % half])
                    for c in range(NFM):
                        b, j = c // 4, c % 4
                        col = FM_CHUNK_COL[c]
                        for kc in range(8):
                            S.mm(ps[b][:, j * 128:(j + 1) * 128], win[:, kc, col:col + 128], xT[:, kc, :],
                                 kc == 0, kc == 7, ["win", "xT0", "xT1"], ["ps%d" % b])
                    for kc in range(8):
                        S.mm(ps[6][:, 0:384], xT[:, kc, :], win[:, kc, BV_COL:BV_COL + 384], kc == 0, kc == 7,
                             ["win", "xT0", "xT1"], ["ps6"])
                    for kc in range(8):
                        S.mm(ps[7][:, 0:256], xT[:, kc, :], win[:, kc, CV_COL:CV_COL + 256], kc == 0, kc == 7,
                             ["win", "xT0", "xT1"], ["ps7"])
                    for b in range(6):
                        n = min(4, NFM - 4 * b)
                        S.cp("act" if b % 2 == 0 else "dve", H[:, 4 * b:4 * b + n, :],
                             ps[b][:, 0:n * 128].rearrange("p (a b) -> p a b", b=128), ["ps%d" % b], ["H%d" % b])
                    Hk = ["H%d" % b for b in range(6)]
                    if _STOP <= 1:
                        return
                    slot = ti % RING if not samp else 16
                    vk = "V%d" % slot
                    S.cp("dve", Vr[:, slot, :, 0:64], ps[6][:, 0:384].rearrange("p (h e) -> p h e", e=64), ["ps6"], [vk])
                    if _STOP <= 1.1:
                        return
                    S.memset("pool", Vr[:, slot, :, 64:65], 1.0, [vk + "o"])
                    if _STOP <= 1.2:
                        return
                    keep = samp or (ti >= NT - NKEEP)
                    if keep:
                        S.cp("dve", stg[0][:, :], ps[6][:, 0:384], ["ps6"], ["stg0"])
                        if samp:
                            S.dma("pool", o_swv[l, seq, WIN - 4:WIN, :], stg[0][PADN:128, :], "stv", reads=["stg0"])
                        else:
                            r0 = (ti - (NT - NKEEP)) * 128
                            if os.environ.get("KV1") == "nodma":
                                pass
                            else:
                                S.dma(os.environ.get("KV1", "pool"), o_pwv[l, r0:r0 + 128, :], stg[0][:, :], "stv", reads=["stg0"])
                    if _STOP <= 1.3:
                        return
                    S.cp("act", vcb[:, :], ps[7][:, 0:256], ["ps7"], ["vcb"])
                    if _STOP <= 1.5:
                        return
                    if samp:
                        S.cp("dve", sst[:, 0:11], H[:, 0:11, 127], Hk[0:3], ["sst"])
                        S.dma("pool", o_sshift[l, seq].rearrange("(c p) -> p c", p=128), sst[:, 0:11], "sts", reads=["sst"])
                        S.dma("sp", sst[:, 16:27], st_shift[l, seq].rearrange("(c p) -> p c", p=128), "ldx%d" % k2, writes=["sst2"])
                        S.cp("dve", H[:, 0:11, PADN - 1], sst[:, 16:27], ["sst2"] + Hk[0:3], Hk[0:3])
                    elif ti == NT - 1:
                        S.cp("dve", sst[:, 0:11], H[:, 0:11, 127], Hk[0:3], ["sst"])
                        S.dma("pool", o_pshift[l].rearrange("(c p) -> p c", p=128), sst[:, 0:11], "sts", reads=["sst"])
                    if _STOP <= 1.7:
                        return
                    if first:
                        S.memset("dve", hml[:, :], 0.0, ["hml"])
                    HA = H[:, 0:11, :]
                    S.tt("dve", hm[:, :, :], HA, muc[:, 0:11].unsqueeze(2).to_broadcast([128, 11, 128]), ALU.mult,
                         Hk[0:3] + ["muc"], ["hm"])
                    S.tt("dve", HA, HA, hm[:, :, :], ALU.subtract, Hk[0:3] + ["hm"], Hk[0:3])
                    S.tt("dve", H[:, 0:11, 1:128], H[:, 0:11, 1:128], hm[:, :, 0:127], ALU.add, Hk[0:3] + ["hm"], Hk[0:3])
                    S.tt("dve", H[:, 0:11, 0], H[:, 0:11, 0], hml[:, :], ALU.add, Hk[0:3] + ["hml"], Hk[0:3])
                    S.cp("dve", hml[:, :], hm[:, :, 127], ["hm"], ["hml"])
                    rT, kT, vT = H[:, 0:3, :], H[:, 3:6, :], H[:, 6:9, :]
                    if _STOP <= 2:
                        return
                    HAk = Hk[0:3]
                    S.act(LI[0:64, 0, :], H[0:64, 9, :], AF.Exp, HAk, ["LIa"], scale=-2.0)
                    S.ts("dve", LI[0:64, 0, :], LI[0:64, 0, :], 1.0, None, ALU.add, None, ["LIa"], ["LIa"])
                    S.op("dve", lambda e: e.reciprocal(out=LI[0:64, 0, :], in_=LI[0:64, 0, :]), ["LIa"], ["LIa"])
                    S.ts("dve", LI[0:64, 0, :], LI[0:64, 0, :], 2.0, -1.0, ALU.mult, ALU.add, ["LIa"], ["LIa"])
                    S.cp("act", LI[64:128, 0, :], H[64:128, 9, :], HAk, ["LIb"])
                    S.act(LI[:, 1, :], H[:, 10, :], AF.Exp, HAk, ["LIc"], scale=-1.0)
                    S.ts("dve", LI[:, 1, :], LI[:, 1, :], 1.0, None, ALU.add, None, ["LIc"], ["LIc"])
                    S.op("dve", lambda e: e.reciprocal(out=LI[:, 1, :], in_=LI[:, 1, :]), ["LIc"], ["LIc"])
                    for p in range(3):
                        S.mm(ps[0][:, p * 128:(p + 1) * 128], wl[0:64, p * 128:(p + 1) * 128], LI[0:64, 0, :], True, True,
                             ["wl", "LIa"], ["ps0"])
                        S.mm(ps[1][:, p * 128:(p + 1) * 128], al[64:128, p * 128:(p + 1) * 128], LI[64:128, 0, :], True, True,
                             ["al", "LIb"], ["ps1"])
                        S.mm(ps[2][:, p * 128:(p + 1) * 128], gl[:, p * 128:(p + 1) * 128], LI[:, 1, :], True, True,
                             ["gl", "LIc"], ["ps2"])
                    lw, cum, Wc, iW, Wp, aa, gT, kkn, kp, bb = t3
                    if _STOP <= 3:
                        return
                    n = S.kn
                    v3 = lambda b: ps[b][:, 0:384].rearrange("p (a b) -> p a b", b=128)
                    for p in range(3):
                        S.act(lw[:, p, :], ps[0][:, p * 128:(p + 1) * 128], AF.Exp, ["ps0", "pr"], [n(lw)], scale=-1.0, bias=pr[:, 0, p:p + 1])
                        S.act(aa[:, p, :], ps[1][:, p * 128:(p + 1) * 128], AF.Exp, ["ps1", "pr"], [n(aa)], scale=-1.0, bias=pr[:, 1, p:p + 1])
                    S.cp("act", gT[:, :, :], v3(2), ["ps2"], [n(gT)])
                    S.ts("dve", lw[:, :, :], lw[:, :, :], 1.0, None, ALU.add, None, [n(lw)], [n(lw)])
                    S.op("dve", lambda e: e.reciprocal(out=lw[:, :, :], in_=lw[:, :, :]), [n(lw)], [n(lw)])
                    S.ts("dve", lw[:, :, :], lw[:, :, :], -DECAY, None, ALU.mult, None, [n(lw)], [n(lw)])
                    S.ts("dve", aa[:, :, :], aa[:, :, :], 1.0, None, ALU.add, None, [n(aa)], [n(aa)])
                    S.op("dve", lambda e: e.reciprocal(out=aa[:, :, :], in_=aa[:, :, :]), [n(aa)], [n(aa)])
                    if samp:
                        S.memset("dve", lw[:, :, 0:PADN], 0.0, [n(lw)])
                    ones3 = mhalf
                    for p in range(3):
                        for cc in range(2):
                            sl = slice(cc * 64, (cc + 1) * 64)
                            S.op("dve", lambda e, p=p, sl=sl: e.tensor_tensor_scan(
                                out=cum[:, p, sl], data0=onesT[:, sl], data1=lw[:, p, sl], initial=0.0,
                                op0=ALU.mult, op1=ALU.add), [n(lw), "onesT"], [n(cum)])
                    S.act(Wc[:, :, :], cum[:, :, :], AF.Exp, [n(cum)], [n(Wc)])
                    S.act(iW[:, :, :], cum[:, :, :], AF.Exp, [n(cum)], [n(iW)], scale=-1.0)
                    S.tt("dve", Wp[:, :, :], cum[:, :, :], lw[:, :, :], ALU.subtract, [n(cum), n(lw)], [n(Wp)])
                    S.act(Wp[:, :, :], Wp[:, :, :], AF.Exp, [n(Wp)], [n(Wp)])
                    bc = lambda k: pr[:, k, :].unsqueeze(2).to_broadcast([128, 3, 128])
                    S.tt("dve", kkn[:, :, :], kT, bc(2), ALU.mult, HAk + ["pr"], [n(kkn)])
                    S.act(sqb[:, :, :], kkn[:, :, :], AF.Square, [n(kkn)], ["sqb"])
                    for p in range(3):
                        S.mm(ps[3][:, p * 128:(p + 1) * 128], cb16[:, 0, :], sqb[:, p, :], True, True, ["cb16", "sqb"], ["ps3"])
                    S.ts("dve", bb[:, :, :], v3(3), 1e-12, None, ALU.max, None, ["ps3"], [n(bb)])
                    S.act(bb[:, :, :], bb[:, :, :], AF.Ln, [n(bb)], [n(bb)])
                    S.act(bb[:, :, :], bb[:, :, :], AF.Exp, [n(bb)], [n(bb)], scale=-0.5)
                    S.tt("dve", kkn[:, :, :], kkn[:, :, :], bb[:, :, :], ALU.mult, [n(kkn), n(bb)], [n(kkn)])
                    S.stt(kp[:, :, :], aa[:, :, :], -1.0, bc(3), ALU.add, ALU.mult, [n(aa), "pr"], [n(kp)])
                    S.stt(kp[:, :, :], kp[:, :, :], 1.0, kT, ALU.add, ALU.mult, [n(kp)] + HAk, [n(kp)])
                    S.tt("dve", bb[:, :, :], kkn[:, :, :], aa[:, :, :], ALU.mult, [n(kkn), n(aa)], [n(bb)])
                    if samp:
                        S.memset("dve", kp[:, :, 0:PADN], 0.0, [n(kp)])
                        S.memset("pool", bb[:, :, 0:PADN], 0.0, [n(bb)])
                    S.tt("dve", TTt[:, :, 0, :], kkn[:, :, :], Wp[:, :, :], ALU.mult, [n(kkn), n(Wp)], ["TT0"])
                    S.tt("dve", TTt[:, :, 1, :], rT, Wc[:, :, :], ALU.mult, HAk + [n(Wc)], ["TT1"])
                    S.tt("dve", TTt[:, :, 2, :], kp[:, :, :], iW[:, :, :], ALU.mult, [n(kp), n(iW)], ["TT2"])
                    S.tt("dve", TTt[:, :, 3, :], bb[:, :, :], iW[:, :, :], ALU.mult, [n(bb), n(iW)], ["TT3"])
                    bon = cum
                    S.tt("dve", aa[:, :, :], rT, kp[:, :, :], ALU.mult, HAk + [n(kp), n(bb)], [n(aa)])
                    S.tt("dve", osb[:, :, :], aa[:, :, :], bc(4), ALU.mult, [n(aa), "pr"], ["osb"])
                    for p in range(3):
                        S.mm(ps[3][:, p * 128:(p + 1) * 128], cb16[:, 0, :], osb[:, p, :], True, True, ["cb16", "osb"], ["ps3"])
                    S.tt("dve", bon[:, :, :], v3(3), vT, ALU.mult, ["ps3", n(Wp), n(Wc), n(iW)] + HAk, [n(cum)])
                    for p in range(3):
                        S.tr(psb[4][:, p * 128:(p + 1) * 128], TTt[:, p, 2, :], identb[:, :], ["TT2", "identb"], ["ps4"])
                        S.tr(psb[4][:, 384 + p * 128:384 + (p + 1) * 128], TTt[:, p, 3, :], identb[:, :], ["TT3", "identb"], ["ps4"])
                        S.tr(ps[5][:, p * 128:(p + 1) * 128], H[:, 6 + p, :], ident, HAk + ["cf"], ["ps5"])
                    S.cp("dve", ktok[:, :], psb[4][:, 0:384], ["ps4"], ["ktok"])
                    S.ts("dve", nbtok[:, :], psb[4][:, 384:768], -1.0, None, ALU.mult, None, ["ps4"], ["nbtok"])
                    S.cp("act", vtok[:, :], ps[5][:, 0:384], ["ps5"], ["vtok"])
                    if _STOP <= 4:
                        return
                    if first:
                        if samp:
                            for p in range(3):
                                S.dma("sp", cstage[0:64, p * 128:(p + 1) * 128].rearrange("v (h k) -> v h k", k=64),
                                      st_wkv[l, seq, 2 * p:2 * p + 2].rearrange("h v k -> v h k"), "ldx%d" % k2, writes=["cstage"])
                            for p in range(3):
                                S.tr(ps[6][:, p * 64:(p + 1) * 64], cstage[0:64, p * 128:(p + 1) * 128], cf[0:64, CI["ident"], 0:64],
                                     ["cstage", "cf"], ["ps6"])
                            S.cp("dve", U[:, :, :], ps[6][:, 0:192].rearrange("p (a b) -> p a b", b=64), ["ps6"], ["U"])
                            for h in range(6):
                                p_, hb_ = h // 2, 64 * (h % 2)
                                S.memset("pool", UbP[h][:, :], 0.0, ["UbP%d" % h])
                                S.cp("dve", UbP[h][hb_:hb_ + 64, :], ps[6][hb_:hb_ + 64, p_ * 64:(p_ + 1) * 64], ["ps6"], ["UbP%d" % h])
                            for p in range(2):
                                for hh in range(2):
                                    S.dma("sp", R[hh * 64:(hh + 1) * 64, p, :], st_ret[l, seq, 2 * p + hh], "ldx%d" % k2, writes=["R"])
                            for p in range(2):
                                for hh in range(2):
                                    g = GAMMA[2 * p + hh] ** (-float(PADN))
                                    S.ts("dve", R[hh * 64:(hh + 1) * 64, p, :], R[hh * 64:(hh + 1) * 64, p, :], g, None, ALU.mult, None, ["R"], ["R"])
                            for h in range(4):
                                p_, hb_ = h // 2, 64 * (h % 2)
                                S.memset("pool", RbP[h][:, :], 0.0, ["RbP%d" % h])
                                S.cp("act", RbP[h][hb_:hb_ + 64, :], R[hb_:hb_ + 64, p_, :], ["R"], ["RbP%d" % h])
                        else:
                            S.memset("dve", U[:, :, :], 0.0, ["U"])
                            for h in range(6):
                                S.memset("pool", UbP[h][:, :], 0.0, ["UbP%d" % h])
                            S.memset("dve", R[:, :, :], 0.0, ["R"])
                            for h in range(4):
                                S.memset("pool", RbP[h][:, :], 0.0, ["RbP%d" % h])
                    mk2 = "mixT%d" % k2
                    cosB, sinB = rot[k2][:, 0, :], rot[k2][:, 1, :]
                    cosC, sinC = rot[k2][:, 2, :], rot[k2][:, 3, :]
                    S.cp("act", hb16[:, :, :], H[:, CH_BQ:CH_BQ + 6, :], ["H2", "H3", "H4"], ["hb16"])
                    for c in range(6):
                        S.mm(ps[0 + c // 4][:, (c % 4) * 128:(c % 4 + 1) * 128], cb16[:, 1, :], hb16[:, c, :], True, True,
                             ["cb16", "hb16"], ["ps%d" % (c // 4)])
                    qk = H[:, CH_BQ:CH_BQ + 6, :]
                    qkk = ["H2", "H3", "H4"]
                    b6 = lambda a: a.unsqueeze(1).to_broadcast([128, 6, 128])
                    S.tt("dve", qk, qk, b6(cosB), ALU.mult, qkk + [rk_], qkk)
                    rtmp = t3[3]
                    S.tt("dve", rtmp[:, :, :], v3(0) if False else ps[0][:, 0:384].rearrange("p (a b) -> p a b", b=128),
                         sinB.unsqueeze(1).to_broadcast([128, 3, 128]), ALU.mult, ["ps0", rk_], [n(rtmp)])
                    S.tt("dve", H[:, CH_BQ:CH_BQ + 3, :], H[:, CH_BQ:CH_BQ + 3, :], rtmp[:, :, :], ALU.add, qkk + [n(rtmp)], qkk)
                    S.tt("dve", rtmp[:, 0, :], ps[0][:, 384:512], sinB, ALU.mult, ["ps0", rk_], [n(rtmp)])
                    S.tt("dve", rtmp[:, 1:3, :], ps[1][:, 0:256].rearrange("p (a b) -> p a b", b=128),
                         sinB.unsqueeze(1).to_broadcast([128, 2, 128]), ALU.mult, ["ps1", rk_], [n(rtmp)])
                    S.tt("dve", krot[:, :, :], H[:, CH_BK:CH_BK + 3, :], rtmp[:, :, :], ALU.add, qkk + [n(rtmp)], ["krot"])
                    S.cp("act", qTb[:, :, :], H[:, CH_BQ:CH_BQ + 3, :], qkk, ["qTb"])
                    S.cp("act", KTr[:, :, slot, :], krot[:, :, :], ["krot"], ["K%d" % slot])
                    if keep:
                        for p in range(3):
                            S.tr(ps[2][:, p * 128:(p + 1) * 128], krot[:, p, :], ident, ["krot", "cf"], ["ps2"])
                        S.cp("dve", stg[1][:, :], ps[2][:, 0:384], ["ps2"], ["stg1"])
                        if samp:
                            S.dma("pool", o_swk[l, seq, WIN - 4:WIN, :], stg[1][PADN:128, :], "stk", reads=["stg1"])
                        else:
                            r0 = (ti - (NT - NKEEP)) * 128
                            S.dma("pool", o_pwk[l, r0:r0 + 128, :], stg[1][:, :], "stk", reads=["stg1"])
                    if samp:
                        for dl in range(NBLK):
                            sl_ = 16 - dl
                            r_lo = WIN - PADN - 128 * dl
                            lo = max(0, -r_lo)
                            hi = 128 if dl > 0 else PADN
                            sk = stg[dl % 2]
                            skn = "stg%d" % (dl % 2)
                            if lo > 0:
                                S.memset("pool", sk[0:lo, :], 0.0, [skn])
                            S.dma("sp", sk[lo:hi, :], ck[l, seq, r_lo + lo:r_lo + hi, :], "ldc%d" % (dl % 2), writes=[skn])
                            bnk = 2 + dl % 2
                            for p in range(3):
                                S.tr(ps[bnk][:, p * 128:(p + 1) * 128], sk[:, p * 128:(p + 1) * 128], ident, [skn, "cf"], ["ps%d" % bnk])
                            src = ps[bnk][:, 0:384].rearrange("p (a b) -> p a b", b=128)
                            if dl == 0:
                                S.cp("act", KTr[:, :, sl_, 0:PADN], src[:, :, 0:PADN], ["ps%d" % bnk], ["K%d" % sl_])
                            else:
                                S.cp("act", KTr[:, :, sl_, :], src, ["ps%d" % bnk], ["K%d" % sl_])
                            vkk = "V%d" % sl_
                            vs_ = vstg[dl % 2]
                            vsn = "vstg%d" % (dl % 2)
                            if lo > 0:
                                S.memset("pool", vs_[0:lo, :], 0.0, [vsn])
                            S.dma("sp", vs_[lo:hi, :], cv[l, seq, r_lo + lo:r_lo + hi, :], "ldcv%d" % (dl % 2), writes=[vsn])
                            S.cp("act", Vr[0:hi, sl_, :, 0:64], vs_[0:hi, :].rearrange("r (h e) -> r h e", e=64), [vsn], [vkk])
                            if dl > 0:
                                S.memset("pool", Vr[:, sl_, :, 64:65], 1.0, [vkk + "o"])
                    def rwkv_core():
                        su_ui = cf[:, CI["su"]:CI["su"] + 2, :]
                        for grp in range(3):
                            heads = [grp * 2 + k for k in range(2)]
                            for k, h in enumerate(heads):
                                p, hb = h // 2, 64 * (h % 2)
                                bA, bB = ps[2 * k], ps[2 * k + 1]
                                kA, kB = "ps%d" % (2 * k), "ps%d" % (2 * k + 1)
                                KKR = TTt[hb:hb + 64, p, 0:2, :]
                                S.mm(bA[:, 0:256], TTt[hb:hb + 64, p, 3, :], KKR, True, True, ["TT0", "TT1", "TT3"], [kA + "a"])
                                S.mm(bA[:, 256:512], TTt[hb:hb + 64, p, 2, :], KKR, True, True, ["TT0", "TT1", "TT2"], [kA + "b"])
                                S.mm(bB[:, 0:128], TTt[hb:hb + 64, p, 0, :], TTt[hb:hb + 64, p, 3, :], True, True, ["TT0", "TT3"], [kB + "a"])
                            yield
                            for k, h in enumerate(heads):
                                bA, bB = ps[2 * k], ps[2 * k + 1]
                                kA, kB = "ps%d" % (2 * k), "ps%d" % (2 * k + 1)
                                PQ0, T0 = chPQ[k][0], chT[k][0]
                                P0, Q0 = PQ0[:, 0, :], PQ0[:, 1, :]
                                S.tt("dve", P0, bA[:, 0:128], cf[:, CI["su"], :], ALU.mult, [kA + "a", "cf"], [n(PQ0)])
                                S.tt("dve", m4t[k][:, :], bA[:, 128:256], cf[:, CI["nui"], :], ALU.mult, [kA + "a", "cf"], [n(m4t[k])])
                                S.tt("dve", lm3[k][:, :].rearrange("p (a b) -> p a b", b=128), bA[:, 256:512].rearrange("p (a b) -> p a b", b=128),
                                     su_ui, ALU.mult, [kA + "b", "cf"], [n(lm3[k])])
                                S.tt("dve", Q0, bB[:, 0:128], cf[:, CI["sl"], :], ALU.mult, [kB + "a", "cf"], [n(PQ0)])
                                S.tt("dve", T0[:, :], identb[:, :], P0, ALU.subtract, ["identb", n(PQ0)], [n(T0)])
                            yield
                            cur = [0, 0, 0]
                            for lev in range(1, 6):
                                st = []
                                for k, h in enumerate(heads):
                                    c0 = cur[k]
                                    st.append((k, ps[2 * k], ps[2 * k + 1], "ps%d" % (2 * k), "ps%d" % (2 * k + 1),
                                               chPQ[k][c0], None, chT[k][c0], chPQ[k][1 - c0], None, chT[k][1 - c0]))
                                    cur[k] = 1 - c0
                                for (k, bA, bB, kA, kB, Pc, Qc, Tc, Pn, Qn, Tn) in st:
                                    S.mm(bB[:, 256:384], Pc[:, 0, :], Pc[:, 1, :], True, True, [n(Pc)], [kB + "c"])
                                    if lev < 5:
                                        S.mm(bB[:, 128:256], Pc[:, 1, :], Pc[:, 0, :], True, True, [n(Pc)], [kB + "b"])
                                yield
                                for (k, bA, bB, kA, kB, Pc, Qc, Tc, Pn, Qn, Tn) in st:
                                    if lev < 5:
                                        S.cp("dve", Pn[:, :, :], bB[:, 128:384].rearrange("p (a b) -> p a b", b=128), [kB + "c"], [n(Pn)])
                                    else:
                                        S.cp("dve", Pn[:, 1, :], bB[:, 256:384], [kB + "c"], [n(Pn)])
                                for (k, bA, bB, kA, kB, Pc, Qc, Tc, Pn, Qn, Tn) in st:
                                    S.mm(bA[:, 0:128], Pn[:, 1, :], Tc[:, :], True, True, [n(Pn), n(Tc)], [kA + "d"])
                                yield
                                for (k, bA, bB, kA, kB, Pc, Qc, Tc, Pn, Qn, Tn) in st:
                                    if lev < 5:
                                        S.tt("dve", Tn[:, :], Tc[:, :], bA[:, 0:128], ALU.add, [n(Tc), kA + "d"], [n(Tn)])
                                    else:
                                        S.tt("dve", invT[k][:, :], Tc[:, :], bA[:, 0:128], ALU.add, [n(Tc), kA + "d"], [n(invT[k])])
                            for cidx in range(2):
                                pb = 64 * cidx
                                tk = slice(pb, pb + 64)
                                hd = []
                                for k, h in enumerate(heads):
                                    p, hb = h // 2, 64 * (h % 2)
                                    hd.append((k, h, p, slice(hb, hb + 64), ps[2 * k], ps[2 * k + 1], "ps%d" % (2 * k), "ps%d" % (2 * k + 1),
                                               "UbP%d" % h, vtok[:, h * 64:(h + 1) * 64], vtok[tk, h * 64:(h + 1) * 64]))
                                for (k, h, p, hs, bA, bB, kA, kB, ukey, vall, vh) in hd:
                                    S.mm(bA[tk, 0:64], TTt[:, p, 0, tk], UbP[h][:, :], True, False, ["TT0", ukey], [kA + "r"])
                                    S.mm(bA[tk, 0:64], lm3[k][:, pb:pb + 64], vall, False, True, [n(lm3[k]), "vtok"], [kA + "r"])
                                yield
                                for (k, h, p, hs, bA, bB, kA, kB, ukey, vall, vh) in hd:
                                    S.cp("dve", rhsb[k][tk, :], bA[tk, 0:64], [kA + "r"], [n(rhsb[k])])
                                for (k, h, p, hs, bA, bB, kA, kB, ukey, vall, vh) in hd:
                                    S.mm(bB[tk, 0:64], invT[k][tk, pb:pb + 64], rhsb[k][tk, :], True, True, [n(invT[k]), n(rhsb[k])], [kB + "u"])
                                yield
                                for (k, h, p, hs, bA, bB, kA, kB, ukey, vall, vh) in hd:
                                    S.cp("dve", umb[k][tk, :], bB[tk, 0:64], [kB + "u"], [n(umb[k])])
                                for (k, h, p, hs, bA, bB, kA, kB, ukey, vall, vh) in hd:
                                    oreg = ps[7][hs, p * 128 + pb:p * 128 + pb + 64]
                                    S.mm(oreg, UbP[h][:, :], TTt[:, p, 1, tk], True, False, [ukey, "TT1"], ["ps7o"])
                                    S.mm(oreg, vall, lm3[k][:, 128 + pb:128 + pb + 64], False, False, ["vtok", n(lm3[k])], ["ps7o"])
                                    S.mm(oreg, umb[k][:, :], m4t[k][:, pb:pb + 64], False, True, [n(umb[k]), n(m4t[k])], ["ps7o"])
                                    S.mm(bA[hs, 64:128], ktok[tk, h * 64:(h + 1) * 64], vh, True, False, ["ktok", "vtok"], [kA + "s"])
                                    S.mm(bA[hs, 64:128], nbtok[tk, h * 64:(h + 1) * 64], umb[k][tk, :], False, True, ["nbtok", n(umb[k])], [kA + "s"])
                                yield
                                for (k, h, p, hs, bA, bB, kA, kB, ukey, vall, vh) in hd:
                                    S.tt("dve", utmp[k][hs, :], U[hs, p, :], bA[hs, 64:128], ALU.add, ["U", kA + "s"], [n(utmp[k])])
                                    wcol = Wc[hs, p, pb + 63:pb + 64]
                                    S.ts("dve", U[hs, p, :], utmp[k][hs, :], wcol, None, ALU.mult, None, [n(utmp[k]), n(Wc)], ["U"])
                                    S.act(UbP[h][hs, :], utmp[k][hs, :], AF.Copy, [n(utmp[k]), n(Wc)], [ukey], scale=wcol)
                                yield
                    def attn_core():
                        unit = [0]
                        nb = NBLK if samp else min(NBLK, ti + 1)
                        units = []
                        for h in range(6):
                            for g0 in range(0, nb, 4):
                                units.append((h, g0, min(4, nb - g0)))
                        pend = None

                        def emit_pv(u):
                            (h, g0, gn_, pt) = u
                            for j in range(gn_):
                                dl = g0 + j
                                sl_ = (16 - dl) if samp else ((ti - dl) % RING)
                                S.mm(ps[6][:, h * 65:(h + 1) * 65], pt[:, j * 128:(j + 1) * 128], Vr[:, sl_, h, 0:65], dl == 0, dl == nb - 1,
                                     [n(pt), "V%d" % sl_, "V%do" % sl_], ["ps6"])
                        for ui, (h, g0, gn_) in enumerate(units):
                            p, hb = h // 2, 64 * (h % 2)
                            hs = slice(hb, hb + 64)
                            bnk = 4 + (ui % 2)
                            bk = "ps%d" % bnk
                            pt = pT[ui % 3]
                            for j in range(gn_):
                                dl = g0 + j
                                sl_ = (16 - dl) if samp else ((ti - dl) % RING)
                                S.mm(ps[bnk][:, j * 128:(j + 1) * 128], KTr[hs, p, sl_, :], qTb[hs, p, :], True, True,
                                     ["K%d" % sl_, "qTb"], [bk])
                            if pend is not None:
                                emit_pv(pend)
                            S.act(pt[:, 0:gn_ * 128], ps[bnk][:, 0:gn_ * 128], AF.Exp, [bk], [n(pt)], scale=HD ** -0.5)
                            pt3 = pt[:, 0:gn_ * 128].rearrange("p (a b) -> p a b", b=128)
                            S.tt("dve", pt3, pt3, amask[:, g0:g0 + gn_, :], ALU.mult, [n(pt), "amask"], [n(pt)])
                            pend = (h, g0, gn_, pt)
                            yield
                        emit_pv(pend)
                        O3 = ps[6][:, 0:390].rearrange("p (h e) -> p h e", e=65)
                        S.op("dve", lambda e: e.reciprocal(out=rl[:, :], in_=O3[:, :, 64]), ["ps6"], ["rl"])
                        S.tt("dve", ob[:, :].rearrange("p (h e) -> p h e", e=64), O3[:, :, 0:64], rl[:, :].unsqueeze(2).to_broadcast([128, 6, 64]),
                             ALU.mult, ["ps6", "rl"], ["ob"])
                        for p in range(3):
                            S.tr(ps[4][:, p * 128:(p + 1) * 128], ob[:, p * 128:(p + 1) * 128], ident, ["ob", "cf"], ["ps4"])
                        S.cp("act", mixT[k2][:, 3:6, :], ps[4][:, 0:384].rearrange("p (a b) -> p a b", b=128), ["ps4"], [mk2 + "b"])
                    if _STOP <= 5:
                        return
                    gens = [rwkv_core(), attn_core()]
                    while gens:
                        for g_ in list(gens):
                            try:
                                next(g_)
                            except StopIteration:
                                gens.remove(g_)
                    oS, sq = lw, kkn
                    S.cp("act", oS[:, :, :], v3(7), ["ps7o"], [n(oS)])
                    S.act(sqb[:, :, :], oS[:, :, :], AF.Square, [n(oS)], ["sqb"])
                    S.cp("dve", osb[:, :, :], oS[:, :, :], [n(oS)], ["osb"])
                    for p in range(3):
                        S.mm(ps[0][:, p * 128:(p + 1) * 128], cb16[:, 0, :], osb[:, p, :], True, True, ["cb16", "osb"], ["ps0"])
                        S.mm(ps[1][:, p * 128:(p + 1) * 128], cb16[:, 0, :], sqb[:, p, :], True, True, ["cb16", "sqb"], ["ps1"])
                    mean, var = kp, bb
                    S.ts("dve", mean[:, :, :], v3(0), 1.0 / 64, None, ALU.mult, None, ["ps0"], [n(mean)])
                    S.act(var[:, :, :], mean[:, :, :], AF.Square, [n(mean)], [n(var)])
                    S.stt(var[:, :, :], v3(1), 1.0 / 64, var[:, :, :], ALU.mult, ALU.subtract, ["ps1", n(var)], [n(var)])
                    S.ts("dve", var[:, :, :], var[:, :, :], GN_EPS, None, ALU.add, None, [n(var)], [n(var)])
                    S.act(var[:, :, :], var[:, :, :], AF.Ln, [n(var)], [n(var)])
                    S.act(var[:, :, :], var[:, :, :], AF.Exp, [n(var)], [n(var)], scale=-0.5)
                    S.tt("dve", oS[:, :, :], oS[:, :, :], mean[:, :, :], ALU.subtract, [n(oS), n(mean)], [n(oS)])
                    S.tt("dve", oS[:, :, :], oS[:, :, :], var[:, :, :], ALU.mult, [n(oS), n(var)], [n(oS)])
                    S.tt("dve", oS[:, :, :], oS[:, :, :], bc(5), ALU.mult, [n(oS), "pr"], [n(oS)])
                    S.tt("dve", oS[:, :, :], oS[:, :, :], bc(6), ALU.add, [n(oS), "pr"], [n(oS)])
                    S.tt("dve", oS[:, :, :], oS[:, :, :], bon[:, :, :], ALU.add, [n(oS), n(cum)], [n(oS)])
                    S.tt("dve", mixT[k2][:, 0:3, :], oS[:, :, :], gT[:, :, :], ALU.mult, [n(oS), n(gT)], [mk2 + "a"])
                    if _STOP <= 11:
                        return
                    qc, kc_, gc = H[:, CH_CQ:CH_CQ + 2, :], H[:, CH_CK:CH_CK + 2, :], H[:, CH_CG:CH_CG + 2, :]
                    ck_ = ["H4", "H5"]
                    S.cp("act", hb16[:, 0:4, :], H[:, CH_CQ:CH_CQ + 4, :], ck_, ["hb16"])
                    for c in range(4):
                        S.mm(ps[0][:, c * 128:(c + 1) * 128], cb16[:, 2, :], hb16[:, c, :], True, True, ["cb16", "hb16"], ["ps0"])
                    b4 = lambda a: a.unsqueeze(1).to_broadcast([128, 4, 128])
                    qkc = H[:, CH_CQ:CH_CQ + 4, :]
                    rt4 = t3[3]
                    r4 = S_r4
                    S.tt("dve", qkc, qkc, b4(cosC), ALU.mult, ck_ + [rk_], ck_)
                    S.tt("dve", r4[:, :, :], ps[0][:, :].rearrange("p (a b) -> p a b", b=128), b4(sinC), ALU.mult, ["ps0", rk_], ["r4"])
                    S.tt("dve", qkc, qkc, r4[:, :, :], ALU.add, ck_ + ["r4"], ck_)
                    if samp:
                        S.memset("dve", H[:, CH_CK:CH_CK + 2, 0:PADN], 0.0, ck_)
                    qd_t = cf[:, 12:15:2, :]
                    kd_t = cf[:, 13:16:2, :]
                    S.cp("act", qcb[:, :, :], qc, ck_, ["qcb"])
                    S.act(kcb[:, :, :], kc_, AF.Copy, ck_, ["kcb"], scale=HD ** -0.5)
                    S.tt("dve", qdb[:, :, :], qc, qd_t, ALU.mult, ck_ + ["cf"], ["qdb"])
                    S.tt("dve", kdb[:, :, :], kc_, kd_t, ALU.mult, ck_ + ["cf"], ["kdb"])
                    for p in range(2):
                        S.tr(psb[1][:, p * 128:(p + 1) * 128], kdb[:, p, :], identb[:, :], ["kdb", "identb"], ["ps1"])
                    S.cp("dve", kdtok[:, :], psb[1][:, 0:256], ["ps1"], ["kdtok"])
                    for h in range(4):
                        p, hb = h // 2, 64 * (h % 2)
                        hs = slice(hb, hb + 64)
                        bnk = 2 + h % 2
                        bk = "ps%d" % bnk
                        S.mm(ps[bnk][:, 0:128], kcb[hs, p, :], qcb[hs, p, :], True, True, ["kcb", "qcb"], [bk + "a"])
                        ab = attb[h % 2]
                        S.tt("dve", ab[:, :], ps[bnk][:, 0:128], cf[:, CI["dmask"] + h, :], ALU.mult, [bk + "a", "cf"], [n(ab)])
                        oreg = ps[4][hs, p * 128:(p + 1) * 128]
                        S.mm(oreg, vcb[:, h * 64:(h + 1) * 64], ab[:, :], True, False, ["vcb", n(ab)], ["ps4"])
                        S.mm(oreg, RbP[h][:, :], qdb[:, p, :], False, True, ["RbP%d" % h, "qdb"], ["ps4"])
                        S.mm(ps[bnk][hs, 128:192], kdtok[:, h * 64:(h + 1) * 64], vcb[:, h * 64:(h + 1) * 64], True, True,
                             ["kdtok", "vcb"], [bk + "s"])
                        S.stt(R[hs, p, :], R[hs, p, :], GAMMA[h] ** 128.0, ps[bnk][hs, 128:192], ALU.mult, ALU.add, ["R", bk + "s"], ["R"])
                        S.cp("act", RbP[h][hs, :], R[hs, p, :], ["R"], ["RbP%d" % h])
                    oc, sq2, mn2, vr2 = c2
                    v2 = lambda b: ps[b][:, 0:256].rearrange("p (a b) -> p a b", b=128)
                    S.cp("act", oc[:, :, :], v2(4), ["ps4"], [n(oc)])
                    S.act(sqb[:, 0:2, :], oc[:, :, :], AF.Square, [n(oc)], ["sqb"])
                    S.cp("dve", osb[:, 0:2, :], oc[:, :, :], [n(oc)], ["osb"])
                    for p in range(2):
                        S.mm(ps[0][:, p * 128:(p + 1) * 128], cb16[:, 0, :], osb[:, p, :], True, True, ["cb16", "osb"], ["ps0"])
                        S.mm(ps[1][:, p * 128:(p + 1) * 128], cb16[:, 0, :], sqb[:, p, :], True, True, ["cb16", "sqb"], ["ps1"])
                    S.ts("dve", mn2[:, :, :], v2(0), 1.0 / 64, None, ALU.mult, None, ["ps0"], [n(mn2)])
                    S.act(vr2[:, :, :], mn2[:, :, :], AF.Square, [n(mn2)], [n(vr2)])
                    S.stt(vr2[:, :, :], v2(1), 1.0 / 64, vr2[:, :, :], ALU.mult, ALU.subtract, ["ps1", n(vr2)], [n(vr2)])
                    S.ts("dve", vr2[:, :, :], vr2[:, :, :], LN_EPS, None, ALU.add, None, [n(vr2)], [n(vr2)])
                    S.act(vr2[:, :, :], vr2[:, :, :], AF.Ln, [n(vr2)], [n(vr2)])
                    S.act(vr2[:, :, :], vr2[:, :, :], AF.Exp, [n(vr2)], [n(vr2)], scale=-0.5)
                    S.tt("dve", oc[:, :, :], oc[:, :, :], mn2[:, :, :], ALU.subtract, [n(oc), n(mn2)], [n(oc)])
                    S.tt("dve", oc[:, :, :], oc[:, :, :], vr2[:, :, :], ALU.mult, [n(oc), n(vr2)], [n(oc)])
                    bcc = lambda k: prc[:, k, :].unsqueeze(2).to_broadcast([128, 2, 128])
                    S.tt("dve", oc[:, :, :], oc[:, :, :], bcc(0), ALU.mult, [n(oc), "prc"], [n(oc)])
                    S.tt("dve", oc[:, :, :], oc[:, :, :], bcc(1), ALU.add, [n(oc), "prc"], [n(oc)])
                    S.act(sq2[:, :, :], gc, AF.Exp, ["H5"], [n(sq2)], scale=-1.0)
                    S.ts("dve", sq2[:, :, :], sq2[:, :, :], 1.0, None, ALU.add, None, [n(sq2)], [n(sq2)])
                    S.op("dve", lambda e: e.reciprocal(out=sq2[:, :, :], in_=sq2[:, :, :]), [n(sq2)], [n(sq2)])
                    S.tt("dve", sq2[:, :, :], sq2[:, :, :], gc, ALU.mult, [n(sq2), "H5"], [n(sq2)])
                    S.tt("dve", mixT[k2][:, 6:8, :], oc[:, :, :], sq2[:, :, :], ALU.mult, [n(oc), n(sq2)], [mk2 + "c"])
                    if _STOP <= 12:
                        return
                    S.dma("pool", mixs[:, :, gi * 128:(gi + 1) * 128].rearrange("k p t -> p k t"), mixT[k2][:, :, :], "stm%d" % k2,
                          reads=[mk2 + "a", mk2 + "b", mk2 + "c"])
                    last = samp or ti == NT - 1
                    if last:
                        for p in range(3):
                            S.tr(ps[6][0:64, p * 128:(p + 1) * 128], U[:, p, :], ident, ["U", "cf"], ["ps6"])
                        S.cp("dve", cstage[0:64, :], ps[6][0:64, 0:384], ["ps6"], ["cstage"])
                        dst = o_swkv[l, seq] if samp else o_pwkv[l]
                        S.dma("pool", dst.rearrange("h v k -> v h k"), cstage[0:64, :].rearrange("v (h k) -> v h k", k=64), "sts", reads=["cstage"])
                        for p in range(2):
                            for hh in range(2):
                                dst = o_sret[l, seq, 2 * p + hh] if samp else o_pret[l, 2 * p + hh]
                                S.dma("pool", dst, R[hh * 64:(hh + 1) * 64, p, :], "sts", reads=["R"])

                for k_ in range(3):
                    S.memset("pool", rhsb[k_][:, :], 0.0, [S.kn(rhsb[k_])])
                    S.memset("pool", umb[k_][:, :], 0.0, [S.kn(umb[k_])])
                onesT = S.sb("onesT", [128, 128])
                S.memset("dve", onesT[:, :], 1.0, ["onesT"])
                S_r4 = S.sb("r4", [128, 4, 128])

                for ti in range(NT):
                    process_tile("p", ti, NT, ti)
                for sb_ in range(NSB):
                    if not _NOSAMP:
                        process_tile(sb_, 0, 1, NT + sb_)
                while bg_jobs:
                    d_, s_src = bg_jobs.pop(0)
                    S.dma("pool", d_, s_src, "bgc")
                S.barrier()
            with ExitStack() as es2:
                S.es = es2
                wo = S.sb("wo", [128, 8, D], BF16)
                wg = S.sb("wg", [128, 8, DFF], BF16)
                wu = S.sb("wu", [128, 8, DFF], BF16)
                wd = S.sb("wd", [128, NFC, D], BF16)
                lnp = S.sb("lnp", [128, 4, D])
                identf = S.sb("identf", [128, 128])
                mh1 = S.sb("mh1", [128, 1])
                xt2 = [S.sb("x2t%d" % k, [128, D]) for k in range(2)]
                mT2 = [S.sb("mT2_%d" % k, [128, 8, 128], BF16) for k in range(2)]
                pre = S.sb("pre", [128, D])
                x1 = S.sb("x1", [128, D])
                x1T = S.sb("x1T", [128, 8, 128], BF16)
                aT = S.sb("aT", [128, NFC, 128], BF16)
                sg = [S.sb("sg%d" % k, [128, 512]) for k in range(2)]
                outt = [S.sb("outt%d" % k, [128, D]) for k in range(1)]
                atok = S.sb("atok", [128, DFF], BF16)
                identb2 = S.sb("identb2", [128, 128], BF16)
                st6 = S.sb("st6", [128, 12])
                mv = S.sb("mv", [128, 4])

                S.dma("sp", wo[:, :, :], wob[l].rearrange("(k p) d -> p k d", p=128), "w2", writes=["wo"])
                for kc in range(0, 8, 2):
                    S.dma("sp", wg[:, kc:kc + 2, :], wgb[l, kc * 128:(kc + 2) * 128, :].rearrange("(k p) d -> p k d", p=128), "w2", writes=["wg"])
                    S.dma("act", wu[:, kc:kc + 2, :], wub[l, kc * 128:(kc + 2) * 128, :].rearrange("(k p) d -> p k d", p=128), "w2", writes=["wu"])
                S.dma("sp", wd[:, 0:11, :], wdb[l, 0:11 * 128, :].rearrange("(k p) d -> p k d", p=128), "w2", writes=["wd"])
                S.dma("act", wd[:, 11:22, :], wdb[l, 11 * 128:22 * 128, :].rearrange("(k p) d -> p k d", p=128), "w2", writes=["wd"])
                if l + 1 < DEPTH:
                    for kc in range(8):
                        bg_jobs.append((winb[kc * 128:(kc + 1) * 128, :], w_in[l + 1, kc * 128:(kc + 1) * 128, :]))
                for k, prm in enumerate([ln1_g, ln1_b, ln2_g, ln2_b]):
                    S.dma("sp", lnp[:, k, :], prm[l:l + 1, :].partition_broadcast(128), "c2", writes=["lnp"])
                S.dma("sp", identf[:, :], c_f[:, 0, :], "c2", writes=["identf"])
                S.memset("dve", mh1[:, :], -0.5, ["mh1"])
                S.cp("dve", identb2[:, :], identf[:, :], ["identf"], ["identb2"])

                def layer_norm(src, dst, gk, bk_, eps):
                    for c in range(2):
                        S.op("dve", lambda e, c=c: e.bn_stats(out=st6[:, c * 6:(c + 1) * 6], in_=src[:, c * 512:(c + 1) * 512]),
                             [S.kn(src)], ["st6"])
                    S.op("dve", lambda e: e.bn_aggr(out=mv[:, 0:2], in_=st6[:, 0:12]), ["st6"], ["mv"])
                    S.ts("dve", mv[:, 2:3], mv[:, 1:2], eps, None, ALU.add, None, ["mv"], ["mv"])
                    S.act(mv[:, 3:4], mv[:, 2:3], AF.Ln, ["mv"], ["mv"])
                    S.act(mv[:, 3:4], mv[:, 3:4], AF.Exp, ["mv"], ["mv"], scale=-0.5)
                    S.ts("dve", dst[:, :], src[:, :], mv[:, 0:1], mv[:, 3:4], ALU.subtract, ALU.mult, [S.kn(src), "mv"], [S.kn(dst)])
                    S.tt("dve", dst[:, :], dst[:, :], lnp[:, gk, :], ALU.mult, [S.kn(dst), "lnp"], [S.kn(dst)])
                    S.tt("dve", dst[:, :], dst[:, :], lnp[:, bk_, :], ALU.add, [S.kn(dst), "lnp"], [S.kn(dst)])

                for gi in range(NTS if not _NOP2 else 0):
                    k2 = gi % 2
                    samp = gi >= NT
                    xk = "x2t%d" % k2
                    mk = "mT2_%d" % k2
                    if bg_jobs:
                        d_, s_src = bg_jobs.pop(0)
                        S.dma("pool", d_, s_src, "bgc")
                    if samp and l == 0:
                        S.memset("pool", xt2[k2][:, :], 0.0, [xk])
                        S.dma("sp", xt2[k2][PADN:128, :], xs[(gi - NT) * 4:(gi - NT + 1) * 4, :], "l2x%d" % k2, writes=[xk])
                    else:
                        S.dma("sp", xt2[k2][:, :], x_src(l, gi), "l2x%d" % k2, writes=[xk])
                    S.dma("sp", mT2[k2][:, :, :], mixs[:, :, gi * 128:(gi + 1) * 128].rearrange("k p t -> p k t"), "l2x%d" % k2, writes=[mk])
                    for half in range(2):
                        for kc in range(8):
                            S.mm(ps[half][:, :], mT2[k2][:, kc, :], wo[:, kc, half * 512:(half + 1) * 512], kc == 0, kc == 7,
                                 [mk, "wo"], ["ps%d" % half])
                        S.stt(pre[:, half * 512:(half + 1) * 512], xt2[k2][:, half * 512:(half + 1) * 512], ALPHA, ps[half][:, :],
                              ALU.mult, ALU.add, [xk, "ps%d" % half], ["pre"])
                    layer_norm(pre, x1, 0, 1, LN_EPS)
                    for half in range(2):
                        for j in range(4):
                            kc = half * 4 + j
                            S.tr(ps[2 + half][:, j * 128:(j + 1) * 128], x1[:, kc * 128:(kc + 1) * 128], identf[:, :], ["x1", "identf"],
                                 ["ps%d" % (2 + half)])
                        S.cp("act", x1T[:, half * 4:(half + 1) * 4, :], ps[2 + half][:, :].rearrange("p (a b) -> p a b", b=128),
                             ["ps%d" % (2 + half)], ["x1T"])
                    NG = (DFF + 511) // 512
                    for g_ in range(NG):
                        c0 = g_ * 512
                        cw = min(512, DFF - c0)
                        bg, bu = 4 + (g_ % 2) * 2, 5 + (g_ % 2) * 2
                        for kc in range(8):
                            S.mm(ps[bg][:, 0:cw], x1T[:, kc, :], wg[:, kc, c0:c0 + cw], kc == 0, kc == 7, ["wg", "x1T"], ["ps%d" % bg])
                        for kc in range(8):
                            S.mm(ps[bu][:, 0:cw], x1T[:, kc, :], wu[:, kc, c0:c0 + cw], kc == 0, kc == 7, ["wu", "x1T"], ["ps%d" % bu])
                        s_ = sg[g_ % 2]
                        S.act(s_[:, 0:cw], ps[bg][:, 0:cw], AF.Silu, ["ps%d" % bg], [S.kn(s_)])
                        S.tt("dve", atok[:, c0:c0 + cw], s_[:, 0:cw], ps[bu][:, 0:cw], ALU.mult, [S.kn(s_), "ps%d" % bu], ["atok%d" % g_])
                        nch = cw // 128
                        bt = 2 + g_ % 2
                        for j in range(nch):
                            S.tr(psb[bt][:, j * 128:(j + 1) * 128], atok[:, c0 + j * 128:c0 + (j + 1) * 128], identb2[:, :],
                                 ["atok%d" % g_, "identb2"], ["ps%d" % bt])
                        S.cp("act", aT[:, g_ * 4:g_ * 4 + nch, :], psb[bt][:, 0:nch * 128].rearrange("p (a b) -> p a b", b=128),
                             ["ps%d" % bt], ["aT"])
                    for half in range(2):
                        for c in range(NFC):
                            S.mm(ps[half][:, :], aT[:, c, :], wd[:, c, half * 512:(half + 1) * 512], c == 0, c == NFC - 1,
                                 ["aT", "wd"], ["ps%d" % half])
                        S.stt(pre[:, half * 512:(half + 1) * 512], x1[:, half * 512:(half + 1) * 512], ALPHA, ps[half][:, :],
                              ALU.mult, ALU.add, ["x1", "ps%d" % half], ["pre"])
                    ot = outt[0]
                    layer_norm(pre, ot, 2, 3, LN_EPS)
                    if l < DEPTH - 1:
                        S.dma("pool", x2s[gi * 128:(gi + 1) * 128, :], ot[:, :], "st2_%d" % k2, reads=[S.kn(ot)])
                    elif samp:
                        sb_ = gi - NT
                        S.dma("pool", y_s[sb_ * 4:(sb_ + 1) * 4, :], ot[PADN:128, :], "st2_%d" % k2, reads=[S.kn(ot)])
                    else:
                        S.dma("pool", y_p[gi * 128:(gi + 1) * 128, :], ot[:, :], "st2_%d" % k2, reads=[S.kn(ot)])
                while bg_jobs:
                    d_, s_src = bg_jobs.pop(0)
                    S.dma("pool", d_, s_src, "bgc")
                S.barrier()
        S.es = es_top
    return nc


_W_NAMES = ["w_in", "rwkv_mu", "rwkv_w0", "rwkv_w_lora", "rwkv_a0", "rwkv_a_lora", "rwkv_g_lora", "rwkv_k_k", "rwkv_k_a",
            "rwkv_r_k", "rwkv_gn_g", "rwkv_gn_b", "ret_gn_g", "ret_gn_b", "w_out", "ln1_g", "ln1_b", "w_ffn_gate", "w_ffn_up",
            "w_ffn_down", "ln2_g", "ln2_b"]


def run(inputs, T, n_cores=8, trace=False):
    f = lambda a: np.ascontiguousarray(np.asarray(a, dtype=np.float32))
    x_prompt = f(inputs["x_prompt"])
    x_sample = f(inputs["x_sample"])
    nb = x_prompt.shape[0]
    consts = make_consts(T)
    shared = {k: f(inputs[k]) for k in _W_NAMES}
    shared["rwkv_r_k"] = shared["rwkv_r_k"].reshape(DEPTH, W_A)
    shared["cf"] = consts["cf"]
    shared["amask"] = consts["amask"]
    shared["rot"] = consts["rot"]
    st_shift, st_wkv = f(inputs["state_rwkv_shift"]), f(inputs["state_rwkv_wkv"])
    ckk, cvv, st_ret = f(inputs["cache_win_k"]), f(inputs["cache_win_v"]), f(inputs["state_ret"])
    in_maps = []
    for c in range(n_cores):
        b = (c * nb) // n_cores
        s0 = c * NSB
        m = dict(shared)
        m["xp"] = x_prompt[b]
        m["xs"] = np.ascontiguousarray(x_sample[s0:s0 + NSB].reshape(NSB * 4, D))
        m["st_shift"] = np.ascontiguousarray(st_shift[:, s0:s0 + NSB])
        m["st_wkv"] = np.ascontiguousarray(st_wkv[:, s0:s0 + NSB])
        m["ck"] = np.ascontiguousarray(ckk[:, s0:s0 + NSB].reshape(DEPTH, NSB, WIN, W_B))
        m["cv"] = np.ascontiguousarray(cvv[:, s0:s0 + NSB].reshape(DEPTH, NSB, WIN, W_B))
        m["st_ret"] = np.ascontiguousarray(st_ret[:, s0:s0 + NSB])
        in_maps.append(m)
    nc = build(T)
    res = run_bass_kernel_spmd(nc, in_maps, core_ids=list(range(n_cores)), trace=trace)
    R = res.results
    per = n_cores // nb
    WK = min(WIN, T)
    own = [R[b * per] for b in range(nb)]
    y_prompt = np.stack([o["y_p"] for o in own])
    y_sample = np.concatenate([r["y_s"].reshape(NSB, 4, D) for r in R], axis=0)
    p_shift = np.stack([o["p_shift"] for o in own], axis=1)
    p_wkv = np.stack([o["p_wkv"] for o in own], axis=1)
    p_wk = np.stack([o["p_wk"].reshape(DEPTH, WK, H_B, HD) for o in own], axis=1)
    p_wv = np.stack([o["p_wv"].reshape(DEPTH, WK, H_B, HD) for o in own], axis=1)
    p_ret = np.stack([o["p_ret"] for o in own], axis=1)
    s_shift = np.concatenate([r["s_shift"] for r in R], axis=1)
    s_wkv = np.concatenate([r["s_wkv"] for r in R], axis=1)
    s_wk = np.concatenate([r["s_wk"].reshape(DEPTH, NSB, WIN, H_B, HD) for r in R], axis=1)
    s_wv = np.concatenate([r["s_wv"].reshape(DEPTH, NSB, WIN, H_B, HD) for r in R], axis=1)
    s_ret = np.concatenate([r["s_ret"] for r in R], axis=1)
    outs = (y_prompt, y_sample, p_shift, p_wkv, p_wk, p_wv, p_ret, s_shift, s_wkv, s_wk, s_wv, s_ret)
    return tuple(np.ascontiguousarray(o, dtype=np.float32) for o in outs), res


def kernel(**inputs):
    T = int(np.asarray(inputs["x_prompt"]).shape[1])
    outs, _ = run(inputs, T)
    return outs
```

```python
import math
import os
_STOP = float(os.environ.get('KSTOP', '99'))
_NOP2 = int(os.environ.get('KNOP2', '0'))
_NOSAMP = int(os.environ.get('KNOSAMP', '0'))
from contextlib import ExitStack
import numpy as np
import ml_dtypes
import concourse.bass as bass
import concourse.mybir as mybir
from concourse.bass_utils import run_bass_kernel_spmd

F32 = mybir.dt.float32
BF16 = mybir.dt.bfloat16
AF = mybir.ActivationFunctionType
ALU = mybir.AluOpType
AX = mybir.AxisListType

D = 1024
DFF = 2816
NFC = DFF // 128
HD = 64
H_A, H_B, H_C = 6, 6, 4
W_A, W_B, W_C = 384, 384, 256
A_COLS = 1408
IN_COLS = 3584
WIN = 2048
NBLK = 17
RING = 18
PAST = 8192
DEPTH = 2
ALPHA = (2 * DEPTH) ** 0.25
DECAY = math.exp(-0.5)
GN_EPS = 64e-5
LN_EPS = 1e-5
GAMMA = [1.0 - 2.0 ** (-5.0 - h) for h in range(H_C)]
PADN = 124
NSB = 4
FM_COLS = [(0, 1408), (1408, 1408 + 768), (2560, 2560 + 512), (3328, 3584)]
FM_CHUNK_COL = []
for a_, b_ in FM_COLS:
    for c_ in range(a_, b_, 128):
        FM_CHUNK_COL.append(c_)
NFM = len(FM_CHUNK_COL)
CH_BQ, CH_BK, CH_CQ, CH_CK, CH_CG = 11, 14, 17, 19, 21
BV_COL, CV_COL = 2176, 3072


class Sched:
    ENGS = ("pe", "dve", "act", "pool", "sp")

    def __init__(self, nc, es):
        self.nc = nc
        self.es_sem = es
        self.es = es
        self.eng = {"pe": nc.tensor, "dve": nc.vector, "act": nc.scalar, "pool": nc.gpsimd, "sp": nc.sync}
        self.sem, self.cnt = {}, {}
        for e in self.ENGS:
            self.sem[e] = es.enter_context(nc.semaphore("q_" + e))
            self.cnt[e] = 0
        self.waited = {e: {} for e in self.ENGS}
        self.last_w, self.readers, self.dsem = {}, {}, {}
        self.kids = {}
        self.bank_last = {}
        self.n_ins = 0
        self.uid = 0
        self.keyname = {}
        self.keep = []

    def _rel(self, k):
        if k.startswith("ps") and len(k) >= 3 and k[2].isdigit():
            bank = k[:3]
            if k == bank:
                return [bank] + list(self.kids.get(bank, ()))
            self.kids.setdefault(bank, set()).add(k)
            return [k, bank]
        return [k]

    def sb(self, name, shape, dt=F32):
        self.uid += 1
        t = self.es.enter_context(self.nc.sbuf_tensor("%s_u%d" % (name, self.uid), list(shape), dt))
        self.keyname[id(t)] = name
        self.keep.append(t)
        return t

    def kn(self, t):
        return self.keyname[id(t)]

    def ps(self, name, shape, dt=F32):
        return self.es.enter_context(self.nc.psum_tensor(name, list(shape), dt))

    def _dma_sem(self, key):
        if key not in self.dsem:
            p = "d_" + key
            self.sem[p] = self.es_sem.enter_context(self.nc.semaphore(p))
            self.cnt[p] = 0
            self.dsem[key] = p
        return self.dsem[key]

    def _wait(self, E, p, c):
        if self.waited[E].get(p, 0) >= c:
            return
        self.eng[E].wait_ge(self.sem[p], c)
        self.waited[E][p] = c

    @staticmethod
    def _bank(k):
        if k.startswith("ps") and len(k) >= 3 and k[2].isdigit():
            return k[:3]
        return None

    def _need(self, E, reads, writes):
        need = {}

        def add(p, c):
            if c > need.get(p, 0):
                need[p] = c
        for r in reads:
            b = self._bank(r)
            if b is not None:
                for p, c in self.bank_last.get(b, {}).items():
                    if p != E:
                        add(p, c)
                continue
            lw = self.last_w.get(r)
            if lw is not None:
                add(*lw)
        for w in writes:
            b = self._bank(w)
            if b is not None:
                for p, c in self.bank_last.get(b, {}).items():
                    if p != E:
                        add(p, c)
                continue
            lw = self.last_w.get(w)
            if lw is not None and lw[0] != E:
                add(*lw)
            for p, c in self.readers.get(w, {}).items():
                if p != E:
                    add(p, c)
        for p, c in need.items():
            if p.startswith("d_"):
                c = self.cnt[p]
            self._wait(E, p, c)

    def _commit(self, P, c, reads, writes):
        for r in reads:
            b = self._bank(r)
            if b is not None:
                self.bank_last.setdefault(b, {})[P] = c
                continue
            self.readers.setdefault(r, {})[P] = c
        for w in writes:
            b = self._bank(w)
            if b is not None:
                self.bank_last.setdefault(b, {})[P] = c
                continue
            self.last_w[w] = (P, c)
            self.readers[w] = {}

    def op(self, E, fn, reads=(), writes=()):
        self._need(E, reads, writes)
        ins = fn(self.eng[E])
        self.cnt[E] += 1
        ins.then_inc(self.sem[E], 1)
        self._commit(E, self.cnt[E], reads, writes)
        self.n_ins += 1
        return ins

    def mm(self, out, lhsT, rhs, start, stop, reads, writes):
        return self.op("pe", lambda e: e.matmul(out, lhsT=lhsT, rhs=rhs, start=start, stop=stop), reads, writes)

    def tr(self, out, in_, ident, reads, writes):
        return self.op("pe", lambda e: e.transpose(out=out, in_=in_, identity=ident), reads, writes)

    def dma(self, Q, out, in_, key, reads=(), writes=()):
        p = self._dma_sem(key)
        self._need(Q, reads, writes)
        ins = self.eng[Q].dma_start(out=out, in_=in_)
        self.cnt[p] += 16
        ins.then_inc(self.sem[p], 16)
        self._commit(p, self.cnt[p], reads, writes)
        self.n_ins += 1
        return ins

    def barrier(self):
        for E in self.ENGS:
            for p in list(self.sem.keys()):
                if p != E and self.cnt[p] > 0:
                    self._wait(E, p, self.cnt[p])
        self.last_w, self.readers = {}, {}
        self.bank_last = {}

    def tt(self, E, out, in0, in1, op, r, w):
        return self.op(E, lambda e: e.tensor_tensor(out=out, in0=in0, in1=in1, op=op), r, w)

    def ts(self, E, out, in0, s1, s2, op0, op1, r, w):
        if s2 is None:
            return self.op(E, lambda e: e.tensor_scalar(out=out, in0=in0, scalar1=s1, scalar2=None, op0=op0), r, w)
        return self.op(E, lambda e: e.tensor_scalar(out=out, in0=in0, scalar1=s1, scalar2=s2, op0=op0, op1=op1), r, w)

    def stt(self, out, in0, sc, in1, op0, op1, r, w):
        return self.op("dve", lambda e: e.scalar_tensor_tensor(out=out, in0=in0, scalar=sc, in1=in1, op0=op0, op1=op1), r, w)

    def act(self, out, in_, func, r, w, scale=None, bias=None):
        kw = {}
        if scale is not None:
            kw["scale"] = scale
        if bias is not None:
            kw["bias"] = bias
        return self.op("act", lambda e: e.activation(out=out, in_=in_, func=func, **kw), r, w)

    def cp(self, E, out, in_, r, w):
        if E == "act":
            return self.act(out, in_, AF.Copy, r, w)
        return self.op(E, lambda e: e.tensor_copy(out=out, in_=in_), r, w)

    def memset(self, E, ap, val, w):
        return self.op(E, lambda e: e.memset(ap, val), (), w)


def _mult(d):
    d = np.asarray(d)
    ok = d >= 0
    m = ((d <= 128) & ok).astype(np.float32)
    m += ((d % 4 == 0) & (d <= 512) & ok)
    m += ((d % 16 == 0) & (d <= 2048) & ok)
    return m.astype(np.float32)


def make_consts(T):
    NT = T // 128
    c = {}
    i = np.arange(128)
    same = (i[:, None] // 64) == (i[None, :] // 64)
    su = ((i[:, None] < i[None, :]) & same).astype(np.float32)
    ui = ((i[:, None] <= i[None, :]) & same).astype(np.float32)
    ident = np.eye(128, dtype=np.float32)
    bones = same.astype(np.float32)
    def pm(half, hd=64):
        m = np.zeros((128, 128), np.float32)
        for blk in range(2):
            o = blk * hd
            for e in range(half):
                m[o + e + half, o + e] = -1.0
                m[o + e, o + e + half] = 1.0
        return m
    packs = [ident, su, ui, su.T.copy(), -ui, bones, pm(8), pm(32)]
    names = ["ident", "su", "ui", "sl", "nui", "bones", "pmb", "pmc"]
    for h in range(H_C):
        g = GAMMA[h]
        rel = i[None, :] - i[:, None]
        dm = np.where(rel >= 0, np.exp(np.log(g) * np.maximum(rel, 0)), 0.0).astype(np.float32)
        packs.append(dm)
        names.append("dmask%d" % h)
    for p in range(2):
        qd = np.zeros((128, 128), np.float32)
        kd = np.zeros((128, 128), np.float32)
        for hh in range(2):
            g = GAMMA[2 * p + hh]
            qd[hh * 64:(hh + 1) * 64, :] = np.exp(np.log(g) * (i + 1.0))[None, :]
            kd[hh * 64:(hh + 1) * 64, :] = (np.exp(np.log(g) * (127.0 - i)) * (HD ** -0.5))[None, :]
        packs += [qd, kd]
        names += ["qdec%d" % p, "kdec%d" % p]
    c["cf"] = np.stack(packs, axis=1).astype(np.float32)
    c["cf_names"] = names
    mk = np.zeros((128, NBLK, 128), np.float32)
    for dl in range(NBLK):
        d = dl * 128 + i[None, :] - i[:, None]
        mk[:, dl, :] = _mult(d)
    c["amask"] = mk.astype(ml_dtypes.bfloat16)
    ntile = NT + 1
    rt = np.zeros((ntile, 128, 4, 128), np.float32)
    inv_b = (500000.0 ** (-np.arange(0, 16, 2, dtype=np.float32) / np.float32(16))).astype(np.float32)
    inv_c = (1.0 / (np.float32(10000.0) ** np.linspace(0.0, 1.0, 32, dtype=np.float32))).astype(np.float32)
    for ti in range(ntile):
        if ti < NT:
            pos = (ti * 128 + i).astype(np.float32)
        else:
            pos = (np.float32(PAST) + np.maximum(i - PADN, 0)).astype(np.float32)
        angb = (pos[:, None] * inv_b[None, :]).astype(np.float32)
        angc = (pos[:, None] * inv_c[None, :]).astype(np.float32)
        cb = np.ones((64, 128), np.float32)
        sbb = np.zeros((64, 128), np.float32)
        cb[0:8] = np.cos(angb).T
        cb[8:16] = np.cos(angb).T
        sbb[0:8] = np.sin(angb).T
        sbb[8:16] = np.sin(angb).T
        cc = np.concatenate([np.cos(angc).T, np.cos(angc).T], axis=0)
        sc = np.concatenate([np.sin(angc).T, np.sin(angc).T], axis=0)
        for hh in range(2):
            rt[ti, hh * 64:(hh + 1) * 64, 0] = cb
            rt[ti, hh * 64:(hh + 1) * 64, 1] = sbb
            rt[ti, hh * 64:(hh + 1) * 64, 2] = cc
            rt[ti, hh * 64:(hh + 1) * 64, 3] = sc
    c["rot"] = rt
    return c


def build(T):
    NT = T // 128
    NTS = NT + NSB
    WK = min(WIN, T)
    NKEEP = WK // 128
    nc = bass.Bass("TRN2", target_bir_lowering=False)

    def din(name, shape, dt=F32):
        return nc.dram_tensor(name, list(shape), dt, kind="ExternalInput").ap()

    def dout(name, shape, dt=F32):
        return nc.dram_tensor(name, list(shape), dt, kind="ExternalOutput").ap()

    def dscr(name, shape, dt=F32):
        return nc.dram_tensor(name, list(shape), dt, kind="Internal").ap()

    xp = din("xp", [T, D])
    xs = din("xs", [NSB * 4, D])
    st_shift = din("st_shift", [DEPTH, NSB, A_COLS])
    st_wkv = din("st_wkv", [DEPTH, NSB, H_A, 64, 64])
    ck = din("ck", [DEPTH, NSB, WIN, W_B])
    cv = din("cv", [DEPTH, NSB, WIN, W_B])
    st_ret = din("st_ret", [DEPTH, NSB, H_C, 64, 64])
    w_in = din("w_in", [DEPTH, D, IN_COLS])
    p_mu = din("rwkv_mu", [DEPTH, A_COLS])
    p_w0 = din("rwkv_w0", [DEPTH, W_A])
    p_wl = din("rwkv_w_lora", [DEPTH, 64, W_A])
    p_a0 = din("rwkv_a0", [DEPTH, W_A])
    p_al = din("rwkv_a_lora", [DEPTH, 64, W_A])
    p_gl = din("rwkv_g_lora", [DEPTH, 128, W_A])
    p_kk = din("rwkv_k_k", [DEPTH, W_A])
    p_ka = din("rwkv_k_a", [DEPTH, W_A])
    p_rk = din("rwkv_r_k", [DEPTH, W_A])
    p_gg = din("rwkv_gn_g", [DEPTH, W_A])
    p_gb = din("rwkv_gn_b", [DEPTH, W_A])
    p_rg = din("ret_gn_g", [DEPTH, W_C])
    p_rb = din("ret_gn_b", [DEPTH, W_C])
    w_out = din("w_out", [DEPTH, D, D])
    ln1_g = din("ln1_g", [DEPTH, D])
    ln1_b = din("ln1_b", [DEPTH, D])
    w_gate = din("w_ffn_gate", [DEPTH, D, DFF])
    w_up = din("w_ffn_up", [DEPTH, D, DFF])
    w_down = din("w_ffn_down", [DEPTH, DFF, D])
    ln2_g = din("ln2_g", [DEPTH, D])
    ln2_b = din("ln2_b", [DEPTH, D])
    c_f = din("cf", [128, 16, 128])
    c_am = din("amask", [128, NBLK, 128], BF16)
    c_rot = din("rot", [NT + 1, 128, 4, 128])

    y_p = dout("y_p", [T, D])
    y_s = dout("y_s", [NSB * 4, D])
    o_pshift = dout("p_shift", [DEPTH, A_COLS])
    o_pwkv = dout("p_wkv", [DEPTH, H_A, 64, 64])
    o_pwk = dout("p_wk", [DEPTH, WK, W_B])
    o_pwv = dout("p_wv", [DEPTH, WK, W_B])
    o_pret = dout("p_ret", [DEPTH, H_C, 64, 64])
    o_sshift = dout("s_shift", [DEPTH, NSB, A_COLS])
    o_swkv = dout("s_wkv", [DEPTH, NSB, H_A, 64, 64])
    o_swk = dout("s_wk", [DEPTH, NSB, WIN, W_B])
    o_swv = dout("s_wv", [DEPTH, NSB, WIN, W_B])
    o_sret = dout("s_ret", [DEPTH, NSB, H_C, 64, 64])

    mixs = dscr("mixs", [8, 128, NTS * 128], BF16)
    wob = dscr("wob", [DEPTH, D, D], BF16)
    wgb = dscr("wgb", [DEPTH, D, DFF], BF16)
    wub = dscr("wub", [DEPTH, D, DFF], BF16)
    wdb = dscr("wdb", [DEPTH, DFF, D], BF16)
    winb = dscr("winb", [D, IN_COLS], BF16)
    bg_jobs = []
    x2s = dscr("x2s", [NTS * 128, D])

    CI = {"ident": 0, "su": 1, "ui": 2, "sl": 3, "nui": 4, "bones": 5, "pmb": 6, "pmc": 7,
          "dmask": 8, "qdec0": 12, "kdec0": 13, "qdec1": 14, "kdec1": 15}

    with ExitStack() as es_top:
        es_top.enter_context(nc.allow_non_contiguous_dma(reason="small strided parameter / state transfers"))
        S = Sched(nc, es_top)
        ps = [S.ps("ps%d" % b, [128, 512]) for b in range(8)]
        psb = [p[:, :].bitcast(BF16) for p in ps]

        def x_src(l, ti):
            if l == 0:
                return xp[ti * 128:(ti + 1) * 128, :]
            return x2s[ti * 128:(ti + 1) * 128, :]

        for l in range(DEPTH):
            with ExitStack() as es1:
                S.es = es1
                win = S.sb("win", [128, 8, IN_COLS], BF16)
                cf = S.sb("cf", [128, 16, 128])
                identb = S.sb("identb", [128, 128], BF16)
                amask = S.sb("amask", [128, NBLK, 128], BF16)
                mhalf = S.sb("mhalf", [128, 384])
                muc = S.sb("muc", [128, 11])
                pr = S.sb("pr", [128, 8, 3])
                prc = S.sb("prc", [128, 2, 2])
                wl = S.sb("wl", [128, 384])
                al = S.sb("al", [128, 384])
                gl = S.sb("gl", [128, 384])
                xt = [S.sb("xt%d" % k, [128, D]) for k in range(2)]
                xT = S.sb("xT", [128, 8, 128], BF16)
                H = S.sb("H", [128, NFM, 128])
                hm = S.sb("hm", [128, 11, 128])
                hml = S.sb("hml", [128, 11])
                rot = [S.sb("rot%d" % k, [128, 4, 128]) for k in range(2)]
                LI = S.sb("LI", [128, 2, 128])
                t3 = [S.sb("t3_%d" % k, [128, 3, 128]) for k in range(10)]
                TTt = S.sb("TTt", [128, 3, 4, 128], BF16)
                ktok = S.sb("ktok", [128, 384], BF16)
                nbtok = S.sb("nbtok", [128, 384], BF16)
                vtok = S.sb("vtok", [128, 384], BF16)
                U = S.sb("U", [128, 3, 64])
                UbP = [S.sb("UbP%d" % h, [128, 64], BF16) for h in range(6)]
                RbP = [S.sb("RbP%d" % h, [128, 64], BF16) for h in range(4)]
                chP = [[S.sb("chP%d_%d" % (h, k), [128, 128], BF16) for k in range(2)] for h in range(3)]
                chQ = [[S.sb("chQ%d_%d" % (h, k), [128, 128], BF16) for k in range(2)] for h in range(3)]
                chT = [[S.sb("chT%d_%d" % (h, k), [128, 128], BF16) for k in range(2)] for h in range(3)]
                cb16 = S.sb("cb16", [128, 3, 128], BF16)
                sqb = S.sb("sqb", [128, 3, 128], BF16)
                osb = S.sb("osb", [128, 3, 128], BF16)
                hb16 = S.sb("hb16", [128, 6, 128], BF16)
                invT = [S.sb("invT%d" % h, [128, 128], BF16) for h in range(3)]
                m4t = [S.sb("m4t%d" % h, [128, 128], BF16) for h in range(3)]
                lm3 = [S.sb("lm3_%d" % h, [128, 256], BF16) for h in range(3)]
                rhsb = [S.sb("rhsb%d" % h, [128, 64], BF16) for h in range(3)]
                umb = [S.sb("umb%d" % h, [128, 64], BF16) for h in range(3)]
                utmp = [S.sb("utmp%d" % h, [128, 64]) for h in range(3)]
                KTr = S.sb("KTr", [128, 3, RING, 128], BF16)
                Vr = S.sb("Vr", [128, RING, 6, 66], BF16)
                qTb = S.sb("qTb", [128, 3, 128], BF16)
                krot = S.sb("krot", [128, 3, 128])
                stg = [S.sb("stg%d" % k, [128, 384]) for k in range(2)]
                vstg = [S.sb("vstg%d" % k, [128, 384]) for k in range(2)]
                pT = [S.sb("pT%d" % k, [128, 512], BF16) for k in range(3)]
                ob = S.sb("ob", [128, 384])
                rl = S.sb("rl", [128, 6])
                c2 = [S.sb("c2_%d" % k, [128, 2, 128]) for k in range(4)]
                qcb = S.sb("qcb", [128, 2, 128], BF16)
                kcb = S.sb("kcb", [128, 2, 128], BF16)
                qdb = S.sb("qdb", [128, 2, 128], BF16)
                kdb = S.sb("kdb", [128, 2, 128], BF16)
                kdtok = S.sb("kdtok", [128, 256], BF16)
                vcb = S.sb("vcb", [128, 256], BF16)
                attb = [S.sb("attb%d" % k, [128, 128], BF16) for k in range(2)]
                R = S.sb("R", [128, 2, 64])
                mixT = [S.sb("mixT%d" % k, [128, 8, 128], BF16) for k in range(2)]
                cstage = S.sb("cstage", [128, 384])
                sst = S.sb("sst", [128, 32])
                swk = S.sb("swk", [128, 64])

                ident = cf[:, CI["ident"], :]

                for kc in range(8):
                    if l == 0:
                        S.dma("pool", win[:, kc, :], w_in[l, kc * 128:(kc + 1) * 128, :], "w1", writes=["win"])
                    else:
                        S.dma("sp", win[:, kc, :], winb[kc * 128:(kc + 1) * 128, :], "w1", writes=["win"])
                for kc in range(8):
                    bg_jobs.append((wob[l, kc * 128:(kc + 1) * 128, :], w_out[l, kc * 128:(kc + 1) * 128, :]))
                for kc in range(8):
                    bg_jobs.append((wgb[l, kc * 128:(kc + 1) * 128, :], w_gate[l, kc * 128:(kc + 1) * 128, :]))
                    bg_jobs.append((wub[l, kc * 128:(kc + 1) * 128, :], w_up[l, kc * 128:(kc + 1) * 128, :]))
                for c in range(NFC):
                    bg_jobs.append((wdb[l, c * 128:(c + 1) * 128, :], w_down[l, c * 128:(c + 1) * 128, :]))
                S.dma("sp", cf[:, :, :], c_f[:, :, :], "c1", writes=["cf"])
                S.dma("sp", amask[:, :, :], c_am[:, :, :], "c1", writes=["amask"])
                S.dma("sp", muc[:, :], p_mu[l].rearrange("(c p) -> p c", p=128), "c1", writes=["muc"])
                for k, prm in enumerate([p_w0, p_a0, p_kk, p_ka, p_rk, p_gg, p_gb]):
                    S.dma("sp", pr[:, k, :], prm[l].rearrange("(c p) -> p c", p=128), "c1", writes=["pr"])
                S.dma("sp", prc[:, 0, :], p_rg[l].rearrange("(c p) -> p c", p=128), "c1", writes=["prc"])
                S.dma("sp", prc[:, 1, :], p_rb[l].rearrange("(c p) -> p c", p=128), "c1", writes=["prc"])
                S.dma("sp", wl[0:64, :], p_wl[l], "c1", writes=["wl"])
                S.dma("sp", al[64:128, :], p_al[l], "c1", writes=["al"])
                S.dma("sp", gl[:, :], p_gl[l], "c1", writes=["gl"])
                S.cp("dve", identb[:, :], ident, ["cf"], ["identb"])
                S.cp("dve", cb16[:, :, :], cf[:, CI["bones"]:CI["bones"] + 3, :], ["cf"], ["cb16"])
                S.memset("dve", mhalf[:, :], -0.5, ["mhalf"])
                S.ts("dve", pr[:, 0:2, :], pr[:, 0:2, :], -1.0, None, ALU.mult, None, ["pr"], ["pr"])

                for sb_ in range(NSB):
                    S.dma("act", o_swk[l, sb_, 0:WIN - 4, :], ck[l, sb_, 4:WIN, :], "cpk")
                    S.dma("act", o_swv[l, sb_, 0:WIN - 4, :], cv[l, sb_, 4:WIN, :], "cpv")
                tile_ctr = [0]

                def process_tile(seq, ti, nseq, gi):
                    samp = seq != "p"
                    if _STOP <= 0:
                        return
                    for _ in range(2):
                        if bg_jobs:
                            d_, s_src = bg_jobs.pop(0)
                            S.dma("pool", d_, s_src, "bgc")
                    k2 = tile_ctr[0] % 2
                    tile_ctr[0] += 1
                    first = ti == 0
                    xk = "xt%d" % k2
                    rk_ = "rot%d" % k2
                    if samp and l == 0:
                        S.memset("pool", xt[k2][:, :], 0.0, [xk])
                        S.dma("sp", xt[k2][PADN:128, :], xs[seq * 4:(seq + 1) * 4, :], "ldx%d" % k2, writes=[xk])
                    else:
                        S.dma("sp", xt[k2][:, :], x_src(l, gi), "ldx%d" % k2, writes=[xk])
                    rti = NT if samp else ti
                    S.dma("sp", rot[k2][:, :, :], c_rot[rti], "ldx%d" % k2, writes=[rk_])
                    for half in range(2):
                        for j in range(4):
                            kc = half * 4 + j
                            S.tr(ps[half][:, j * 128:(j + 1) * 128], xt[k2][:, kc * 128:(kc + 1) * 128], ident,
                                 [xk, "cf"], ["ps%d" % half])
                        S.cp("act" if half == 0 else "dve", xT[:, half * 4:(half + 1) * 4, :],
                             ps[half][:, :].rearrange("p (a b) -> p a b", b=128), ["ps%d" % half], ["xT%d" % half])
                    for c in range(NFM):
                        b, j = c // 4, c % 4
                        col = FM_CHUNK_COL[c]
                        for kc in range(8):
                            S.mm(ps[b][:, j * 128:(j + 1) * 128], win[:, kc, col:col + 128], xT[:, kc, :],
                                 kc == 0, kc == 7, ["win", "xT0", "xT1"], ["ps%d" % b])
                    for kc in range(8):
                        S.mm(ps[6][:, 0:384], xT[:, kc, :], win[:, kc, BV_COL:BV_COL + 384], kc == 0, kc == 7,
                             ["win", "xT0", "xT1"], ["ps6"])
                    for kc in range(8):
                        S.mm(ps[7][:, 0:256], xT[:, kc, :], win[:, kc, CV_COL:CV_COL + 256], kc == 0, kc == 7,
                             ["win", "xT0", "xT1"], ["ps7"])
                    for b in range(6):
                        n = min(4, NFM - 4 * b)
                        S.cp("act" if b % 2 == 0 else "dve", H[:, 4 * b:4 * b + n, :],
                             ps[b][:, 0:n * 128].rearrange("p (a b) -> p a b", b=128), ["ps%d" % b], ["H%d" % b])
                    Hk = ["H%d" % b for b in range(6)]
                    if _STOP <= 1:
                        return
                    slot = ti % RING if not samp else 16
                    vk = "V%d" % slot
                    S.cp("dve", Vr[:, slot, :, 0:64], ps[6][:, 0:384].rearrange("p (h e) -> p h e", e=64), ["ps6"], [vk])
                    if _STOP <= 1.1:
                        return
                    S.memset("pool", Vr[:, slot, :, 64:65], 1.0, [vk + "o"])
                    if _STOP <= 1.2:
                        return
                    keep = samp or (ti >= NT - NKEEP)
                    if keep:
                        S.cp("dve", stg[0][:, :], ps[6][:, 0:384], ["ps6"], ["stg0"])
                        if samp:
                            S.dma("pool", o_swv[l, seq, WIN - 4:WIN, :], stg[0][PADN:128, :], "stv", reads=["stg0"])
                        else:
                            r0 = (ti - (NT - NKEEP)) * 128
                            if os.environ.get("KV1") == "nodma":
                                pass
                            else:
                                S.dma(os.environ.get("KV1", "pool"), o_pwv[l, r0:r0 + 128, :], stg[0][:, :], "stv", reads=["stg0"])
                    if _STOP <= 1.3:
                        return
                    S.cp("act", vcb[:, :], ps[7][:, 0:256], ["ps7"], ["vcb"])
                    if _STOP <= 1.5:
                        return
                    if samp:
                        S.cp("dve", sst[:, 0:11], H[:, 0:11, 127], Hk[0:3], ["sst"])
                        S.dma("pool", o_sshift[l, seq].rearrange("(c p) -> p c", p=128), sst[:, 0:11], "sts", reads=["sst"])
                        S.dma("sp", sst[:, 16:27], st_shift[l, seq].rearrange("(c p) -> p c", p=128), "ldx%d" % k2, writes=["sst2"])
                        S.cp("dve", H[:, 0:11, PADN - 1], sst[:, 16:27], ["sst2"] + Hk[0:3], Hk[0:3])
                    elif ti == NT - 1:
                        S.cp("dve", sst[:, 0:11], H[:, 0:11, 127], Hk[0:3], ["sst"])
                        S.dma("pool", o_pshift[l].rearrange("(c p) -> p c", p=128), sst[:, 0:11], "sts", reads=["sst"])
                    if _STOP <= 1.7:
                        return
                    if first:
                        S.memset("dve", hml[:, :], 0.0, ["hml"])
                    HA = H[:, 0:11, :]
                    S.tt("dve", hm[:, :, :], HA, muc[:, 0:11].unsqueeze(2).to_broadcast([128, 11, 128]), ALU.mult,
                         Hk[0:3] + ["muc"], ["hm"])
                    S.tt("dve", HA, HA, hm[:, :, :], ALU.subtract, Hk[0:3] + ["hm"], Hk[0:3])
                    S.tt("dve", H[:, 0:11, 1:128], H[:, 0:11, 1:128], hm[:, :, 0:127], ALU.add, Hk[0:3] + ["hm"], Hk[0:3])
                    S.tt("dve", H[:, 0:11, 0], H[:, 0:11, 0], hml[:, :], ALU.add, Hk[0:3] + ["hml"], Hk[0:3])
                    S.cp("dve", hml[:, :], hm[:, :, 127], ["hm"], ["hml"])
                    rT, kT, vT = H[:, 0:3, :], H[:, 3:6, :], H[:, 6:9, :]
                    if _STOP <= 2:
                        return
                    HAk = Hk[0:3]
                    S.act(LI[0:64, 0, :], H[0:64, 9, :], AF.Exp, HAk, ["LIa"], scale=-2.0)
                    S.ts("dve", LI[0:64, 0, :], LI[0:64, 0, :], 1.0, None, ALU.add, None, ["LIa"], ["LIa"])
                    S.op("dve", lambda e: e.reciprocal(out=LI[0:64, 0, :], in_=LI[0:64, 0, :]), ["LIa"], ["LIa"])
                    S.ts("dve", LI[0:64, 0, :], LI[0:64, 0, :], 2.0, -1.0, ALU.mult, ALU.add, ["LIa"], ["LIa"])
                    S.cp("act", LI[64:128, 0, :], H[64:128, 9, :], HAk, ["LIb"])
                    S.act(LI[:, 1, :], H[:, 10, :], AF.Exp, HAk, ["LIc"], scale=-1.0)
                    S.ts("dve", LI[:, 1, :], LI[:, 1, :], 1.0, None, ALU.add, None, ["LIc"], ["LIc"])
                    S.op("dve", lambda e: e.reciprocal(out=LI[:, 1, :], in_=LI[:, 1, :]), ["LIc"], ["LIc"])
                    for p in range(3):
                        S.mm(ps[0][:, p * 128:(p + 1) * 128], wl[0:64, p * 128:(p + 1) * 128], LI[0:64, 0, :], True, True,
                             ["wl", "LIa"], ["ps0"])
                        S.mm(ps[1][:, p * 128:(p + 1) * 128], al[64:128, p * 128:(p + 1) * 128], LI[64:128, 0, :], True, True,
                             ["al", "LIb"], ["ps1"])
                        S.mm(ps[2][:, p * 128:(p + 1) * 128], gl[:, p * 128:(p + 1) * 128], LI[:, 1, :], True, True,
                             ["gl", "LIc"], ["ps2"])
                    lw, cum, Wc, iW, Wp, aa, gT, kkn, kp, bb = t3
                    if _STOP <= 3:
                        return
                    n = S.kn
                    v3 = lambda b: ps[b][:, 0:384].rearrange("p (a b) -> p a b", b=128)
                    for p in range(3):
                        S.act(lw[:, p, :], ps[0][:, p * 128:(p + 1) * 128], AF.Exp, ["ps0", "pr"], [n(lw)], scale=-1.0, bias=pr[:, 0, p:p + 1])
                        S.act(aa[:, p, :], ps[1][:, p * 128:(p + 1) * 128], AF.Exp, ["ps1", "pr"], [n(aa)], scale=-1.0, bias=pr[:, 1, p:p + 1])
                    S.cp("act", gT[:, :, :], v3(2), ["ps2"], [n(gT)])
                    S.ts("dve", lw[:, :, :], lw[:, :, :], 1.0, None, ALU.add, None, [n(lw)], [n(lw)])
                    S.op("dve", lambda e: e.reciprocal(out=lw[:, :, :], in_=lw[:, :, :]), [n(lw)], [n(lw)])
                    S.ts("dve", lw[:, :, :], lw[:, :, :], -DECAY, None, ALU.mult, None, [n(lw)], [n(lw)])
                    S.ts("dve", aa[:, :, :], aa[:, :, :], 1.0, None, ALU.add, None, [n(aa)], [n(aa)])
                    S.op("dve", lambda e: e.reciprocal(out=aa[:, :, :], in_=aa[:, :, :]), [n(aa)], [n(aa)])
                    if samp:
                        S.memset("dve", lw[:, :, 0:PADN], 0.0, [n(lw)])
                    ones3 = mhalf
                    for p in range(3):
                        for cc in range(2):
                            sl = slice(cc * 64, (cc + 1) * 64)
                            S.op("dve", lambda e, p=p, sl=sl: e.tensor_tensor_scan(
                                out=cum[:, p, sl], data0=onesT[:, sl], data1=lw[:, p, sl], initial=0.0,
                                op0=ALU.mult, op1=ALU.add), [n(lw), "onesT"], [n(cum)])
                    S.act(Wc[:, :, :], cum[:, :, :], AF.Exp, [n(cum)], [n(Wc)])
                    S.act(iW[:, :, :], cum[:, :, :], AF.Exp, [n(cum)], [n(iW)], scale=-1.0)
                    S.tt("dve", Wp[:, :, :], cum[:, :, :], lw[:, :, :], ALU.subtract, [n(cum), n(lw)], [n(Wp)])
                    S.act(Wp[:, :, :], Wp[:, :, :], AF.Exp, [n(Wp)], [n(Wp)])
                    bc = lambda k: pr[:, k, :].unsqueeze(2).to_broadcast([128, 3, 128])
                    S.tt("dve", kkn[:, :, :], kT, bc(2), ALU.mult, HAk + ["pr"], [n(kkn)])
                    S.act(sqb[:, :, :], kkn[:, :, :], AF.Square, [n(kkn)], ["sqb"])
                    for p in range(3):
                        S.mm(ps[3][:, p * 128:(p + 1) * 128], cb16[:, 0, :], sqb[:, p, :], True, True, ["cb16", "sqb"], ["ps3"])
                    S.ts("dve", bb[:, :, :], v3(3), 1e-12, None, ALU.max, None, ["ps3"], [n(bb)])
                    S.act(bb[:, :, :], bb[:, :, :], AF.Ln, [n(bb)], [n(bb)])
                    S.act(bb[:, :, :], bb[:, :, :], AF.Exp, [n(bb)], [n(bb)], scale=-0.5)
                    S.tt("dve", kkn[:, :, :], kkn[:, :, :], bb[:, :, :], ALU.mult, [n(kkn), n(bb)], [n(kkn)])
                    S.stt(kp[:, :, :], aa[:, :, :], -1.0, bc(3), ALU.add, ALU.mult, [n(aa), "pr"], [n(kp)])
                    S.stt(kp[:, :, :], kp[:, :, :], 1.0, kT, ALU.add, ALU.mult, [n(kp)] + HAk, [n(kp)])
                    S.tt("dve", bb[:, :, :], kkn[:, :, :], aa[:, :, :], ALU.mult, [n(kkn), n(aa)], [n(bb)])
                    if samp:
                        S.memset("dve", kp[:, :, 0:PADN], 0.0, [n(kp)])
                        S.memset("pool", bb[:, :, 0:PADN], 0.0, [n(bb)])
                    S.tt("dve", TTt[:, :, 0, :], kkn[:, :, :], Wp[:, :, :], ALU.mult, [n(kkn), n(Wp)], ["TT0"])
                    S.tt("dve", TTt[:, :, 1, :], rT, Wc[:, :, :], ALU.mult, HAk + [n(Wc)], ["TT1"])
                    S.tt("dve", TTt[:, :, 2, :], kp[:, :, :], iW[:, :, :], ALU.mult, [n(kp), n(iW)], ["TT2"])
                    S.tt("dve", TTt[:, :, 3, :], bb[:, :, :], iW[:, :, :], ALU.mult, [n(bb), n(iW)], ["TT3"])
                    bon = cum
                    S.tt("dve", aa[:, :, :], rT, kp[:, :, :], ALU.mult, HAk + [n(kp), n(bb)], [n(aa)])
                    S.tt("dve", osb[:, :, :], aa[:, :, :], bc(4), ALU.mult, [n(aa), "pr"], ["osb"])
                    for p in range(3):
                        S.mm(ps[3][:, p * 128:(p + 1) * 128], cb16[:, 0, :], osb[:, p, :], True, True, ["cb16", "osb"], ["ps3"])
                    S.tt("dve", bon[:, :, :], v3(3), vT, ALU.mult, ["ps3", n(Wp), n(Wc), n(iW)] + HAk, [n(cum)])
                    for p in range(3):
                        S.tr(psb[4][:, p * 128:(p + 1) * 128], TTt[:, p, 2, :], identb[:, :], ["TT2", "identb"], ["ps4"])
                        S.tr(psb[4][:, 384 + p * 128:384 + (p + 1) * 128], TTt[:, p, 3, :], identb[:, :], ["TT3", "identb"], ["ps4"])
                        S.tr(ps[5][:, p * 128:(p + 1) * 128], H[:, 6 + p, :], ident, HAk + ["cf"], ["ps5"])
                    S.cp("dve", ktok[:, :], psb[4][:, 0:384], ["ps4"], ["ktok"])
                    S.ts("dve", nbtok[:, :], psb[4][:, 384:768], -1.0, None, ALU.mult, None, ["ps4"], ["nbtok"])
                    S.cp("act", vtok[:, :], ps[5][:, 0:384], ["ps5"], ["vtok"])
                    if _STOP <= 4:
                        return
                    if first:
                        if samp:
                            for p in range(3):
                                S.dma("sp", cstage[0:64, p * 128:(p + 1) * 128].rearrange("v (h k) -> v h k", k=64),
                                      st_wkv[l, seq, 2 * p:2 * p + 2].rearrange("h v k -> v h k"), "ldx%d" % k2, writes=["cstage"])
                            for p in range(3):
                                S.tr(ps[6][:, p * 64:(p + 1) * 64], cstage[0:64, p * 128:(p + 1) * 128], cf[0:64, CI["ident"], 0:64],
                                     ["cstage", "cf"], ["ps6"])
                            S.cp("dve", U[:, :, :], ps[6][:, 0:192].rearrange("p (a b) -> p a b", b=64), ["ps6"], ["U"])
                            for h in range(6):
                                p_, hb_ = h // 2, 64 * (h % 2)
                                S.memset("pool", UbP[h][:, :], 0.0, ["UbP%d" % h])
                                S.cp("dve", UbP[h][hb_:hb_ + 64, :], ps[6][hb_:hb_ + 64, p_ * 64:(p_ + 1) * 64], ["ps6"], ["UbP%d" % h])
                            for p in range(2):
                                for hh in range(2):
                                    S.dma("sp", R[hh * 64:(hh + 1) * 64, p, :], st_ret[l, seq, 2 * p + hh], "ldx%d" % k2, writes=["R"])
                            for p in range(2):
                                for hh in range(2):
                                    g = GAMMA[2 * p + hh] ** (-float(PADN))
                                    S.ts("dve", R[hh * 64:(hh + 1) * 64, p, :], R[hh * 64:(hh + 1) * 64, p, :], g, None, ALU.mult, None, ["R"], ["R"])
                            for h in range(4):
                                p_, hb_ = h // 2, 64 * (h % 2)
                                S.memset("pool", RbP[h][:, :], 0.0, ["RbP%d" % h])
                                S.cp("act", RbP[h][hb_:hb_ + 64, :], R[hb_:hb_ + 64, p_, :], ["R"], ["RbP%d" % h])
                        else:
                            S.memset("dve", U[:, :, :], 0.0, ["U"])
                            for h in range(6):
                                S.memset("pool", UbP[h][:, :], 0.0, ["UbP%d" % h])
                            S.memset("dve", R[:, :, :], 0.0, ["R"])
                            for h in range(4):
                                S.memset("pool", RbP[h][:, :], 0.0, ["RbP%d" % h])
                    mk2 = "mixT%d" % k2
                    cosB, sinB = rot[k2][:, 0, :], rot[k2][:, 1, :]
                    cosC, sinC = rot[k2][:, 2, :], rot[k2][:, 3, :]
                    S.cp("act", hb16[:, :, :], H[:, CH_BQ:CH_BQ + 6, :], ["H2", "H3", "H4"], ["hb16"])
                    for c in range(6):
                        S.mm(ps[0 + c // 4][:, (c % 4) * 128:(c % 4 + 1) * 128], cb16[:, 1, :], hb16[:, c, :], True, True,
                             ["cb16", "hb16"], ["ps%d" % (c // 4)])
                    qk = H[:, CH_BQ:CH_BQ + 6, :]
                    qkk = ["H2", "H3", "H4"]
                    b6 = lambda a: a.unsqueeze(1).to_broadcast([128, 6, 128])
                    S.tt("dve", qk, qk, b6(cosB), ALU.mult, qkk + [rk_], qkk)
                    rtmp = t3[3]
                    S.tt("dve", rtmp[:, :, :], v3(0) if False else ps[0][:, 0:384].rearrange("p (a b) -> p a b", b=128),
                         sinB.unsqueeze(1).to_broadcast([128, 3, 128]), ALU.mult, ["ps0", rk_], [n(rtmp)])
                    S.tt("dve", H[:, CH_BQ:CH_BQ + 3, :], H[:, CH_BQ:CH_BQ + 3, :], rtmp[:, :, :], ALU.add, qkk + [n(rtmp)], qkk)
                    S.tt("dve", rtmp[:, 0, :], ps[0][:, 384:512], sinB, ALU.mult, ["ps0", rk_], [n(rtmp)])
                    S.tt("dve", rtmp[:, 1:3, :], ps[1][:, 0:256].rearrange("p (a b) -> p a b", b=128),
                         sinB.unsqueeze(1).to_broadcast([128, 2, 128]), ALU.mult, ["ps1", rk_], [n(rtmp)])
                    S.tt("dve", krot[:, :, :], H[:, CH_BK:CH_BK + 3, :], rtmp[:, :, :], ALU.add, qkk + [n(rtmp)], ["krot"])
                    S.cp("act", qTb[:, :, :], H[:, CH_BQ:CH_BQ + 3, :], qkk, ["qTb"])
                    S.cp("act", KTr[:, :, slot, :], krot[:, :, :], ["krot"], ["K%d" % slot])
                    if keep:
                        for p in range(3):
                            S.tr(ps[2][:, p * 128:(p + 1) * 128], krot[:, p, :], ident, ["krot", "cf"], ["ps2"])
                        S.cp("dve", stg[1][:, :], ps[2][:, 0:384], ["ps2"], ["stg1"])
                        if samp:
                            S.dma("pool", o_swk[l, seq, WIN - 4:WIN, :], stg[1][PADN:128, :], "stk", reads=["stg1"])
                        else:
                            r0 = (ti - (NT - NKEEP)) * 128
                            S.dma("pool", o_pwk[l, r0:r0 + 128, :], stg[1][:, :], "stk", reads=["stg1"])
                    if samp:
                        for dl in range(NBLK):
                            sl_ = 16 - dl
                            r_lo = WIN - PADN - 128 * dl
                            lo = max(0, -r_lo)
                            hi = 128 if dl > 0 else PADN
                            sk = stg[dl % 2]
                            skn = "stg%d" % (dl % 2)
                            if lo > 0:
                                S.memset("pool", sk[0:lo, :], 0.0, [skn])
                            S.dma("sp", sk[lo:hi, :], ck[l, seq, r_lo + lo:r_lo + hi, :], "ldc%d" % (dl % 2), writes=[skn])
                            bnk = 2 + dl % 2
                            for p in range(3):
                                S.tr(ps[bnk][:, p * 128:(p + 1) * 128], sk[:, p * 128:(p + 1) * 128], ident, [skn, "cf"], ["ps%d" % bnk])
                            src = ps[bnk][:, 0:384].rearrange("p (a b) -> p a b", b=128)
                            if dl == 0:
                                S.cp("act", KTr[:, :, sl_, 0:PADN], src[:, :, 0:PADN], ["ps%d" % bnk], ["K%d" % sl_])
                            else:
                                S.cp("act", KTr[:, :, sl_, :], src, ["ps%d" % bnk], ["K%d" % sl_])
                            vkk = "V%d" % sl_
                            vs_ = vstg[dl % 2]
                            vsn = "vstg%d" % (dl % 2)
                            if lo > 0:
                                S.memset("pool", vs_[0:lo, :], 0.0, [vsn])
                            S.dma("sp", vs_[lo:hi, :], cv[l, seq, r_lo + lo:r_lo + hi, :], "ldcv%d" % (dl % 2), writes=[vsn])
                            S.cp("act", Vr[0:hi, sl_, :, 0:64], vs_[0:hi, :].rearrange("r (h e) -> r h e", e=64), [vsn], [vkk])
                            if dl > 0:
                                S.memset("pool", Vr[:, sl_, :, 64:65], 1.0, [vkk + "o"])
                    def rwkv_core():
                        su_ui = cf[:, CI["su"]:CI["su"] + 2, :]
                        for grp in range(3):
                            heads = [grp * 2 + k for k in range(2)]
                            for k, h in enumerate(heads):
                                p, hb = h // 2, 64 * (h % 2)
                                bA, bB = ps[2 * k], ps[2 * k + 1]
                                kA, kB = "ps%d" % (2 * k), "ps%d" % (2 * k + 1)
                                KKR = TTt[hb:hb + 64, p, 0:2, :]
                                S.mm(bA[:, 0:256], TTt[hb:hb + 64, p, 3, :], KKR, True, True, ["TT0", "TT1", "TT3"], [kA + "a"])
                                S.mm(bA[:, 256:512], TTt[hb:hb + 64, p, 2, :], KKR, True, True, ["TT0", "TT1", "TT2"], [kA + "b"])
                                S.mm(bB[:, 0:128], TTt[hb:hb + 64, p, 0, :], TTt[hb:hb + 64, p, 3, :], True, True, ["TT0", "TT3"], [kB + "a"])
                            yield
                            for k, h in enumerate(heads):
                                bA, bB = ps[2 * k], ps[2 * k + 1]
                                kA, kB = "ps%d" % (2 * k), "ps%d" % (2 * k + 1)
                                P0, Q0, T0 = chP[k][0], chQ[k][0], chT[k][0]
                                S.tt("dve", P0[:, :], bA[:, 0:128], cf[:, CI["su"], :], ALU.mult, [kA + "a", "cf"], [n(P0)])
                                S.tt("dve", m4t[k][:, :], bA[:, 128:256], cf[:, CI["nui"], :], ALU.mult, [kA + "a", "cf"], [n(m4t[k])])
                                S.tt("dve", lm3[k][:, :].rearrange("p (a b) -> p a b", b=128), bA[:, 256:512].rearrange("p (a b) -> p a b", b=128),
                                     su_ui, ALU.mult, [kA + "b", "cf"], [n(lm3[k])])
                                S.tt("dve", Q0[:, :], bB[:, 0:128], cf[:, CI["sl"], :], ALU.mult, [kB + "a", "cf"], [n(Q0)])
                                S.tt("dve", T0[:, :], identb[:, :], P0[:, :], ALU.subtract, ["identb", n(P0)], [n(T0)])
                            yield
                            cur = [0, 0, 0]
                            for lev in range(1, 6):
                                st = []
                                for k, h in enumerate(heads):
                                    c0 = cur[k]
                                    st.append((k, ps[2 * k], ps[2 * k + 1], "ps%d" % (2 * k), "ps%d" % (2 * k + 1),
                                               chP[k][c0], chQ[k][c0], chT[k][c0], chP[k][1 - c0], chQ[k][1 - c0], chT[k][1 - c0]))
                                    cur[k] = 1 - c0
                                for (k, bA, bB, kA, kB, Pc, Qc, Tc, Pn, Qn, Tn) in st:
                                    S.mm(bB[:, 256:384], Pc[:, :], Qc[:, :], True, True, [n(Pc), n(Qc)], [kB + "c"])
                                    if lev < 5:
                                        S.mm(bB[:, 128:256], Qc[:, :], Pc[:, :], True, True, [n(Pc), n(Qc)], [kB + "b"])
                                yield
                                for (k, bA, bB, kA, kB, Pc, Qc, Tc, Pn, Qn, Tn) in st:
                                    S.cp("dve", Qn[:, :], bB[:, 256:384], [kB + "c"], [n(Qn)])
                                    if lev < 5:
                                        S.cp("dve", Pn[:, :], bB[:, 128:256], [kB + "b"], [n(Pn)])
                                for (k, bA, bB, kA, kB, Pc, Qc, Tc, Pn, Qn, Tn) in st:
                                    S.mm(bA[:, 0:128], Qn[:, :], Tc[:, :], True, True, [n(Qn), n(Tc)], [kA + "d"])
                                yield
                                for (k, bA, bB, kA, kB, Pc, Qc, Tc, Pn, Qn, Tn) in st:
                                    if lev < 5:
                                        S.tt("dve", Tn[:, :], Tc[:, :], bA[:, 0:128], ALU.add, [n(Tc), kA + "d"], [n(Tn)])
                                    else:
                                        S.tt("dve", invT[k][:, :], Tc[:, :], bA[:, 0:128], ALU.add, [n(Tc), kA + "d"], [n(invT[k])])
                            for cidx in range(2):
                                pb = 64 * cidx
                                tk = slice(pb, pb + 64)
                                hd = []
                                for k, h in enumerate(heads):
                                    p, hb = h // 2, 64 * (h % 2)
                                    hd.append((k, h, p, slice(hb, hb + 64), ps[2 * k], ps[2 * k + 1], "ps%d" % (2 * k), "ps%d" % (2 * k + 1),
                                               "UbP%d" % h, vtok[:, h * 64:(h + 1) * 64], vtok[tk, h * 64:(h + 1) * 64]))
                                for (k, h, p, hs, bA, bB, kA, kB, ukey, vall, vh) in hd:
                                    S.mm(bA[tk, 0:64], TTt[:, p, 0, tk], UbP[h][:, :], True, False, ["TT0", ukey], [kA + "r"])
                                    S.mm(bA[tk, 0:64], lm3[k][:, pb:pb + 64], vall, False, True, [n(lm3[k]), "vtok"], [kA + "r"])
                                yield
                                for (k, h, p, hs, bA, bB, kA, kB, ukey, vall, vh) in hd:
                                    S.cp("dve", rhsb[k][tk, :], bA[tk, 0:64], [kA + "r"], [n(rhsb[k])])
                                for (k, h, p, hs, bA, bB, kA, kB, ukey, vall, vh) in hd:
                                    S.mm(bB[tk, 0:64], invT[k][tk, pb:pb + 64], rhsb[k][tk, :], True, True, [n(invT[k]), n(rhsb[k])], [kB + "u"])
                                yield
                                for (k, h, p, hs, bA, bB, kA, kB, ukey, vall, vh) in hd:
                                    S.cp("dve", umb[k][tk, :], bB[tk, 0:64], [kB + "u"], [n(umb[k])])
                                for (k, h, p, hs, bA, bB, kA, kB, ukey, vall, vh) in hd:
                                    oreg = ps[7][hs, p * 128 + pb:p * 128 + pb + 64]
                                    S.mm(oreg, UbP[h][:, :], TTt[:, p, 1, tk], True, False, [ukey, "TT1"], ["ps7o"])
                                    S.mm(oreg, vall, lm3[k][:, 128 + pb:128 + pb + 64], False, False, ["vtok", n(lm3[k])], ["ps7o"])
                                    S.mm(oreg, umb[k][:, :], m4t[k][:, pb:pb + 64], False, True, [n(umb[k]), n(m4t[k])], ["ps7o"])
                                    S.mm(bA[hs, 64:128], ktok[tk, h * 64:(h + 1) * 64], vh, True, False, ["ktok", "vtok"], [kA + "s"])
                                    S.mm(bA[hs, 64:128], nbtok[tk, h * 64:(h + 1) * 64], umb[k][tk, :], False, True, ["nbtok", n(umb[k])], [kA + "s"])
                                yield
                                for (k, h, p, hs, bA, bB, kA, kB, ukey, vall, vh) in hd:
                                    S.tt("dve", utmp[k][hs, :], U[hs, p, :], bA[hs, 64:128], ALU.add, ["U", kA + "s"], [n(utmp[k])])
                                    wcol = Wc[hs, p, pb + 63:pb + 64]
                                    S.ts("dve", U[hs, p, :], utmp[k][hs, :], wcol, None, ALU.mult, None, [n(utmp[k]), n(Wc)], ["U"])
                                    S.act(UbP[h][hs, :], utmp[k][hs, :], AF.Copy, [n(utmp[k]), n(Wc)], [ukey], scale=wcol)
                                yield
                    def attn_core():
                        unit = [0]
                        nb = NBLK if samp else min(NBLK, ti + 1)
                        units = []
                        for h in range(6):
                            for g0 in range(0, nb, 4):
                                units.append((h, g0, min(4, nb - g0)))
                        pend = None

                        def emit_pv(u):
                            (h, g0, gn_, pt) = u
                            for j in range(gn_):
                                dl = g0 + j
                                sl_ = (16 - dl) if samp else ((ti - dl) % RING)
                                S.mm(ps[6][:, h * 65:(h + 1) * 65], pt[:, j * 128:(j + 1) * 128], Vr[:, sl_, h, 0:65], dl == 0, dl == nb - 1,
                                     [n(pt), "V%d" % sl_, "V%do" % sl_], ["ps6"])
                        for ui, (h, g0, gn_) in enumerate(units):
                            p, hb = h // 2, 64 * (h % 2)
                            hs = slice(hb, hb + 64)
                            bnk = 4 + (ui % 2)
                            bk = "ps%d" % bnk
                            pt = pT[ui % 3]
                            for j in range(gn_):
                                dl = g0 + j
                                sl_ = (16 - dl) if samp else ((ti - dl) % RING)
                                S.mm(ps[bnk][:, j * 128:(j + 1) * 128], KTr[hs, p, sl_, :], qTb[hs, p, :], True, True,
                                     ["K%d" % sl_, "qTb"], [bk])
                            if pend is not None:
                                emit_pv(pend)
                            S.act(pt[:, 0:gn_ * 128], ps[bnk][:, 0:gn_ * 128], AF.Exp, [bk], [n(pt)], scale=HD ** -0.5)
                            pt3 = pt[:, 0:gn_ * 128].rearrange("p (a b) -> p a b", b=128)
                            S.tt("dve", pt3, pt3, amask[:, g0:g0 + gn_, :], ALU.mult, [n(pt), "amask"], [n(pt)])
                            pend = (h, g0, gn_, pt)
                            yield
                        emit_pv(pend)
                        O3 = ps[6][:, 0:390].rearrange("p (h e) -> p h e", e=65)
                        S.op("dve", lambda e: e.reciprocal(out=rl[:, :], in_=O3[:, :, 64]), ["ps6"], ["rl"])
                        S.tt("dve", ob[:, :].rearrange("p (h e) -> p h e", e=64), O3[:, :, 0:64], rl[:, :].unsqueeze(2).to_broadcast([128, 6, 64]),
                             ALU.mult, ["ps6", "rl"], ["ob"])
                        for p in range(3):
                            S.tr(ps[4][:, p * 128:(p + 1) * 128], ob[:, p * 128:(p + 1) * 128], ident, ["ob", "cf"], ["ps4"])
                        S.cp("act", mixT[k2][:, 3:6, :], ps[4][:, 0:384].rearrange("p (a b) -> p a b", b=128), ["ps4"], [mk2 + "b"])
                    if _STOP <= 5:
                        return
                    ga_, gb_ = rwkv_core(), attn_core()
                    a_live, b_live = True, True
                    while a_live or b_live:
                        if a_live:
                            try:
                                next(ga_)
                            except StopIteration:
                                a_live = False
                        for _ in range(2):
                            if b_live:
                                try:
                                    next(gb_)
                                except StopIteration:
                                    b_live = False
                    oS, sq = lw, kkn
                    S.cp("act", oS[:, :, :], v3(7), ["ps7o"], [n(oS)])
                    S.act(sqb[:, :, :], oS[:, :, :], AF.Square, [n(oS)], ["sqb"])
                    S.cp("dve", osb[:, :, :], oS[:, :, :], [n(oS)], ["osb"])
                    for p in range(3):
                        S.mm(ps[0][:, p * 128:(p + 1) * 128], cb16[:, 0, :], osb[:, p, :], True, True, ["cb16", "osb"], ["ps0"])
                        S.mm(ps[1][:, p * 128:(p + 1) * 128], cb16[:, 0, :], sqb[:, p, :], True, True, ["cb16", "sqb"], ["ps1"])
                    mean, var = kp, bb
                    S.ts("dve", mean[:, :, :], v3(0), 1.0 / 64, None, ALU.mult, None, ["ps0"], [n(mean)])
                    S.act(var[:, :, :], mean[:, :, :], AF.Square, [n(mean)], [n(var)])
                    S.stt(var[:, :, :], v3(1), 1.0 / 64, var[:, :, :], ALU.mult, ALU.subtract, ["ps1", n(var)], [n(var)])
                    S.ts("dve", var[:, :, :], var[:, :, :], GN_EPS, None, ALU.add, None, [n(var)], [n(var)])
                    S.act(var[:, :, :], var[:, :, :], AF.Ln, [n(var)], [n(var)])
                    S.act(var[:, :, :], var[:, :, :], AF.Exp, [n(var)], [n(var)], scale=-0.5)
                    S.tt("dve", oS[:, :, :], oS[:, :, :], mean[:, :, :], ALU.subtract, [n(oS), n(mean)], [n(oS)])
                    S.tt("dve", oS[:, :, :], oS[:, :, :], var[:, :, :], ALU.mult, [n(oS), n(var)], [n(oS)])
                    S.tt("dve", oS[:, :, :], oS[:, :, :], bc(5), ALU.mult, [n(oS), "pr"], [n(oS)])
                    S.tt("dve", oS[:, :, :], oS[:, :, :], bc(6), ALU.add, [n(oS), "pr"], [n(oS)])
                    S.tt("dve", oS[:, :, :], oS[:, :, :], bon[:, :, :], ALU.add, [n(oS), n(cum)], [n(oS)])
                    S.tt("dve", mixT[k2][:, 0:3, :], oS[:, :, :], gT[:, :, :], ALU.mult, [n(oS), n(gT)], [mk2 + "a"])
                    if _STOP <= 11:
                        return
                    qc, kc_, gc = H[:, CH_CQ:CH_CQ + 2, :], H[:, CH_CK:CH_CK + 2, :], H[:, CH_CG:CH_CG + 2, :]
                    ck_ = ["H4", "H5"]
                    S.cp("act", hb16[:, 0:4, :], H[:, CH_CQ:CH_CQ + 4, :], ck_, ["hb16"])
                    for c in range(4):
                        S.mm(ps[0][:, c * 128:(c + 1) * 128], cb16[:, 2, :], hb16[:, c, :], True, True, ["cb16", "hb16"], ["ps0"])
                    b4 = lambda a: a.unsqueeze(1).to_broadcast([128, 4, 128])
                    qkc = H[:, CH_CQ:CH_CQ + 4, :]
                    rt4 = t3[3]
                    r4 = S_r4
                    S.tt("dve", qkc, qkc, b4(cosC), ALU.mult, ck_ + [rk_], ck_)
                    S.tt("dve", r4[:, :, :], ps[0][:, :].rearrange("p (a b) -> p a b", b=128), b4(sinC), ALU.mult, ["ps0", rk_], ["r4"])
                    S.tt("dve", qkc, qkc, r4[:, :, :], ALU.add, ck_ + ["r4"], ck_)
                    if samp:
                        S.memset("dve", H[:, CH_CK:CH_CK + 2, 0:PADN], 0.0, ck_)
                    qd_t = cf[:, 12:15:2, :]
                    kd_t = cf[:, 13:16:2, :]
                    S.cp("act", qcb[:, :, :], qc, ck_, ["qcb"])
                    S.act(kcb[:, :, :], kc_, AF.Copy, ck_, ["kcb"], scale=HD ** -0.5)
                    S.tt("dve", qdb[:, :, :], qc, qd_t, ALU.mult, ck_ + ["cf"], ["qdb"])
                    S.tt("dve", kdb[:, :, :], kc_, kd_t, ALU.mult, ck_ + ["cf"], ["kdb"])
                    for p in range(2):
                        S.tr(psb[1][:, p * 128:(p + 1) * 128], kdb[:, p, :], identb[:, :], ["kdb", "identb"], ["ps1"])
                    S.cp("dve", kdtok[:, :], psb[1][:, 0:256], ["ps1"], ["kdtok"])
                    for h in range(4):
                        p, hb = h // 2, 64 * (h % 2)
                        hs = slice(hb, hb + 64)
                        bnk = 2 + h % 2
                        bk = "ps%d" % bnk
                        S.mm(ps[bnk][:, 0:128], kcb[hs, p, :], qcb[hs, p, :], True, True, ["kcb", "qcb"], [bk + "a"])
                        ab = attb[h % 2]
                        S.tt("dve", ab[:, :], ps[bnk][:, 0:128], cf[:, CI["dmask"] + h, :], ALU.mult, [bk + "a", "cf"], [n(ab)])
                        oreg = ps[4][hs, p * 128:(p + 1) * 128]
                        S.mm(oreg, vcb[:, h * 64:(h + 1) * 64], ab[:, :], True, False, ["vcb", n(ab)], ["ps4"])
                        S.mm(oreg, RbP[h][:, :], qdb[:, p, :], False, True, ["RbP%d" % h, "qdb"], ["ps4"])
                        S.mm(ps[bnk][hs, 128:192], kdtok[:, h * 64:(h + 1) * 64], vcb[:, h * 64:(h + 1) * 64], True, True,
                             ["kdtok", "vcb"], [bk + "s"])
                        S.stt(R[hs, p, :], R[hs, p, :], GAMMA[h] ** 128.0, ps[bnk][hs, 128:192], ALU.mult, ALU.add, ["R", bk + "s"], ["R"])
                        S.cp("act", RbP[h][hs, :], R[hs, p, :], ["R"], ["RbP%d" % h])
                    oc, sq2, mn2, vr2 = c2
                    v2 = lambda b: ps[b][:, 0:256].rearrange("p (a b) -> p a b", b=128)
                    S.cp("act", oc[:, :, :], v2(4), ["ps4"], [n(oc)])
                    S.act(sqb[:, 0:2, :], oc[:, :, :], AF.Square, [n(oc)], ["sqb"])
                    S.cp("dve", osb[:, 0:2, :], oc[:, :, :], [n(oc)], ["osb"])
                    for p in range(2):
                        S.mm(ps[0][:, p * 128:(p + 1) * 128], cb16[:, 0, :], osb[:, p, :], True, True, ["cb16", "osb"], ["ps0"])
                        S.mm(ps[1][:, p * 128:(p + 1) * 128], cb16[:, 0, :], sqb[:, p, :], True, True, ["cb16", "sqb"], ["ps1"])
                    S.ts("dve", mn2[:, :, :], v2(0), 1.0 / 64, None, ALU.mult, None, ["ps0"], [n(mn2)])
                    S.act(vr2[:, :, :], mn2[:, :, :], AF.Square, [n(mn2)], [n(vr2)])
                    S.stt(vr2[:, :, :], v2(1), 1.0 / 64, vr2[:, :, :], ALU.mult, ALU.subtract, ["ps1", n(vr2)], [n(vr2)])
                    S.ts("dve", vr2[:, :, :], vr2[:, :, :], LN_EPS, None, ALU.add, None, [n(vr2)], [n(vr2)])
                    S.act(vr2[:, :, :], vr2[:, :, :], AF.Ln, [n(vr2)], [n(vr2)])
                    S.act(vr2[:, :, :], vr2[:, :, :], AF.Exp, [n(vr2)], [n(vr2)], scale=-0.5)
                    S.tt("dve", oc[:, :, :], oc[:, :, :], mn2[:, :, :], ALU.subtract, [n(oc), n(mn2)], [n(oc)])
                    S.tt("dve", oc[:, :, :], oc[:, :, :], vr2[:, :, :], ALU.mult, [n(oc), n(vr2)], [n(oc)])
                    bcc = lambda k: prc[:, k, :].unsqueeze(2).to_broadcast([128, 2, 128])
                    S.tt("dve", oc[:, :, :], oc[:, :, :], bcc(0), ALU.mult, [n(oc), "prc"], [n(oc)])
                    S.tt("dve", oc[:, :, :], oc[:, :, :], bcc(1), ALU.add, [n(oc), "prc"], [n(oc)])
                    S.act(sq2[:, :, :], gc, AF.Exp, ["H5"], [n(sq2)], scale=-1.0)
                    S.ts("dve", sq2[:, :, :], sq2[:, :, :], 1.0, None, ALU.add, None, [n(sq2)], [n(sq2)])
                    S.op("dve", lambda e: e.reciprocal(out=sq2[:, :, :], in_=sq2[:, :, :]), [n(sq2)], [n(sq2)])
                    S.tt("dve", sq2[:, :, :], sq2[:, :, :], gc, ALU.mult, [n(sq2), "H5"], [n(sq2)])
                    S.tt("dve", mixT[k2][:, 6:8, :], oc[:, :, :], sq2[:, :, :], ALU.mult, [n(oc), n(sq2)], [mk2 + "c"])
                    if _STOP <= 12:
                        return
                    S.dma("pool", mixs[:, :, gi * 128:(gi + 1) * 128].rearrange("k p t -> p k t"), mixT[k2][:, :, :], "stm%d" % k2,
                          reads=[mk2 + "a", mk2 + "b", mk2 + "c"])
                    last = samp or ti == NT - 1
                    if last:
                        for p in range(3):
                            S.tr(ps[6][0:64, p * 128:(p + 1) * 128], U[:, p, :], ident, ["U", "cf"], ["ps6"])
                        S.cp("dve", cstage[0:64, :], ps[6][0:64, 0:384], ["ps6"], ["cstage"])
                        dst = o_swkv[l, seq] if samp else o_pwkv[l]
                        S.dma("pool", dst.rearrange("h v k -> v h k"), cstage[0:64, :].rearrange("v (h k) -> v h k", k=64), "sts", reads=["cstage"])
                        for p in range(2):
                            for hh in range(2):
                                dst = o_sret[l, seq, 2 * p + hh] if samp else o_pret[l, 2 * p + hh]
                                S.dma("pool", dst, R[hh * 64:(hh + 1) * 64, p, :], "sts", reads=["R"])

                for k_ in range(3):
                    S.memset("pool", rhsb[k_][:, :], 0.0, [S.kn(rhsb[k_])])
                    S.memset("pool", umb[k_][:, :], 0.0, [S.kn(umb[k_])])
                onesT = S.sb("onesT", [128, 128])
                S.memset("dve", onesT[:, :], 1.0, ["onesT"])
                S_r4 = S.sb("r4", [128, 4, 128])

                for ti in range(NT):
                    process_tile("p", ti, NT, ti)
                for sb_ in range(NSB):
                    if not _NOSAMP:
                        process_tile(sb_, 0, 1, NT + sb_)
                while bg_jobs:
                    d_, s_src = bg_jobs.pop(0)
                    S.dma("pool", d_, s_src, "bgc")
                S.barrier()
            with ExitStack() as es2:
                S.es = es2
                wo = S.sb("wo", [128, 8, D], BF16)
                wg = S.sb("wg", [128, 8, DFF], BF16)
                wu = S.sb("wu", [128, 8, DFF], BF16)
                wd = S.sb("wd", [128, NFC, D], BF16)
                lnp = S.sb("lnp", [128, 4, D])
                identf = S.sb("identf", [128, 128])
                mh1 = S.sb("mh1", [128, 1])
                xt2 = [S.sb("x2t%d" % k, [128, D]) for k in range(2)]
                mT2 = [S.sb("mT2_%d" % k, [128, 8, 128], BF16) for k in range(2)]
                pre = S.sb("pre", [128, D])
                x1 = S.sb("x1", [128, D])
                x1T = S.sb("x1T", [128, 8, 128], BF16)
                aT = S.sb("aT", [128, NFC, 128], BF16)
                sg = [S.sb("sg%d" % k, [128, 512]) for k in range(2)]
                outt = [S.sb("outt%d" % k, [128, D]) for k in range(1)]
                atok = S.sb("atok", [128, DFF], BF16)
                identb2 = S.sb("identb2", [128, 128], BF16)
                st6 = S.sb("st6", [128, 12])
                mv = S.sb("mv", [128, 4])

                S.dma("sp", wo[:, :, :], wob[l].rearrange("(k p) d -> p k d", p=128), "w2", writes=["wo"])
                for kc in range(0, 8, 2):
                    S.dma("sp", wg[:, kc:kc + 2, :], wgb[l, kc * 128:(kc + 2) * 128, :].rearrange("(k p) d -> p k d", p=128), "w2", writes=["wg"])
                    S.dma("act", wu[:, kc:kc + 2, :], wub[l, kc * 128:(kc + 2) * 128, :].rearrange("(k p) d -> p k d", p=128), "w2", writes=["wu"])
                S.dma("sp", wd[:, 0:11, :], wdb[l, 0:11 * 128, :].rearrange("(k p) d -> p k d", p=128), "w2", writes=["wd"])
                S.dma("act", wd[:, 11:22, :], wdb[l, 11 * 128:22 * 128, :].rearrange("(k p) d -> p k d", p=128), "w2", writes=["wd"])
                if l + 1 < DEPTH:
                    for kc in range(8):
                        bg_jobs.append((winb[kc * 128:(kc + 1) * 128, :], w_in[l + 1, kc * 128:(kc + 1) * 128, :]))
                for k, prm in enumerate([ln1_g, ln1_b, ln2_g, ln2_b]):
                    S.dma("sp", lnp[:, k, :], prm[l:l + 1, :].partition_broadcast(128), "c2", writes=["lnp"])
                S.dma("sp", identf[:, :], c_f[:, 0, :], "c2", writes=["identf"])
                S.memset("dve", mh1[:, :], -0.5, ["mh1"])
                S.cp("dve", identb2[:, :], identf[:, :], ["identf"], ["identb2"])

                def layer_norm(src, dst, gk, bk_, eps):
                    for c in range(2):
                        S.op("dve", lambda e, c=c: e.bn_stats(out=st6[:, c * 6:(c + 1) * 6], in_=src[:, c * 512:(c + 1) * 512]),
                             [S.kn(src)], ["st6"])
                    S.op("dve", lambda e: e.bn_aggr(out=mv[:, 0:2], in_=st6[:, 0:12]), ["st6"], ["mv"])
                    S.ts("dve", mv[:, 2:3], mv[:, 1:2], eps, None, ALU.add, None, ["mv"], ["mv"])
                    S.act(mv[:, 3:4], mv[:, 2:3], AF.Ln, ["mv"], ["mv"])
                    S.act(mv[:, 3:4], mv[:, 3:4], AF.Exp, ["mv"], ["mv"], scale=-0.5)
                    S.ts("dve", dst[:, :], src[:, :], mv[:, 0:1], mv[:, 3:4], ALU.subtract, ALU.mult, [S.kn(src), "mv"], [S.kn(dst)])
                    S.tt("dve", dst[:, :], dst[:, :], lnp[:, gk, :], ALU.mult, [S.kn(dst), "lnp"], [S.kn(dst)])
                    S.tt("dve", dst[:, :], dst[:, :], lnp[:, bk_, :], ALU.add, [S.kn(dst), "lnp"], [S.kn(dst)])

                for gi in range(NTS if not _NOP2 else 0):
                    k2 = gi % 2
                    samp = gi >= NT
                    xk = "x2t%d" % k2
                    mk = "mT2_%d" % k2
                    if bg_jobs:
                        d_, s_src = bg_jobs.pop(0)
                        S.dma("pool", d_, s_src, "bgc")
                    if samp and l == 0:
                        S.memset("pool", xt2[k2][:, :], 0.0, [xk])
                        S.dma("sp", xt2[k2][PADN:128, :], xs[(gi - NT) * 4:(gi - NT + 1) * 4, :], "l2x%d" % k2, writes=[xk])
                    else:
                        S.dma("sp", xt2[k2][:, :], x_src(l, gi), "l2x%d" % k2, writes=[xk])
                    S.dma("sp", mT2[k2][:, :, :], mixs[:, :, gi * 128:(gi + 1) * 128].rearrange("k p t -> p k t"), "l2x%d" % k2, writes=[mk])
                    for half in range(2):
                        for kc in range(8):
                            S.mm(ps[half][:, :], mT2[k2][:, kc, :], wo[:, kc, half * 512:(half + 1) * 512], kc == 0, kc == 7,
                                 [mk, "wo"], ["ps%d" % half])
                        S.stt(pre[:, half * 512:(half + 1) * 512], xt2[k2][:, half * 512:(half + 1) * 512], ALPHA, ps[half][:, :],
                              ALU.mult, ALU.add, [xk, "ps%d" % half], ["pre"])
                    layer_norm(pre, x1, 0, 1, LN_EPS)
                    for half in range(2):
                        for j in range(4):
                            kc = half * 4 + j
                            S.tr(ps[2 + half][:, j * 128:(j + 1) * 128], x1[:, kc * 128:(kc + 1) * 128], identf[:, :], ["x1", "identf"],
                                 ["ps%d" % (2 + half)])
                        S.cp("act", x1T[:, half * 4:(half + 1) * 4, :], ps[2 + half][:, :].rearrange("p (a b) -> p a b", b=128),
                             ["ps%d" % (2 + half)], ["x1T"])
                    NG = (DFF + 511) // 512
                    for g_ in range(NG):
                        c0 = g_ * 512
                        cw = min(512, DFF - c0)
                        bg, bu = 4 + (g_ % 2) * 2, 5 + (g_ % 2) * 2
                        for kc in range(8):
                            S.mm(ps[bg][:, 0:cw], x1T[:, kc, :], wg[:, kc, c0:c0 + cw], kc == 0, kc == 7, ["wg", "x1T"], ["ps%d" % bg])
                        for kc in range(8):
                            S.mm(ps[bu][:, 0:cw], x1T[:, kc, :], wu[:, kc, c0:c0 + cw], kc == 0, kc == 7, ["wu", "x1T"], ["ps%d" % bu])
                        s_ = sg[g_ % 2]
                        S.act(s_[:, 0:cw], ps[bg][:, 0:cw], AF.Silu, ["ps%d" % bg], [S.kn(s_)])
                        S.tt("dve", atok[:, c0:c0 + cw], s_[:, 0:cw], ps[bu][:, 0:cw], ALU.mult, [S.kn(s_), "ps%d" % bu], ["atok%d" % g_])
                        nch = cw // 128
                        bt = 2 + g_ % 2
                        for j in range(nch):
                            S.tr(psb[bt][:, j * 128:(j + 1) * 128], atok[:, c0 + j * 128:c0 + (j + 1) * 128], identb2[:, :],
                                 ["atok%d" % g_, "identb2"], ["ps%d" % bt])
                        S.cp("act", aT[:, g_ * 4:g_ * 4 + nch, :], psb[bt][:, 0:nch * 128].rearrange("p (a b) -> p a b", b=128),
                             ["ps%d" % bt], ["aT"])
                    for half in range(2):
                        for c in range(NFC):
                            S.mm(ps[half][:, :], aT[:, c, :], wd[:, c, half * 512:(half + 1) * 512], c == 0, c == NFC - 1,
                                 ["aT", "wd"], ["ps%d" % half])
                        S.stt(pre[:, half * 512:(half + 1) * 512], x1[:, half * 512:(half + 1) * 512], ALPHA, ps[half][:, :],
                              ALU.mult, ALU.add, ["x1", "ps%d" % half], ["pre"])
                    ot = outt[0]
                    layer_norm(pre, ot, 2, 3, LN_EPS)
                    if l < DEPTH - 1:
                        S.dma("pool", x2s[gi * 128:(gi + 1) * 128, :], ot[:, :], "st2_%d" % k2, reads=[S.kn(ot)])
                    elif samp:
                        sb_ = gi - NT
                        S.dma("pool", y_s[sb_ * 4:(sb_ + 1) * 4, :], ot[PADN:128, :], "st2_%d" % k2, reads=[S.kn(ot)])
                    else:
                        S.dma("pool", y_p[gi * 128:(gi + 1) * 128, :], ot[:, :], "st2_%d" % k2, reads=[S.kn(ot)])
                while bg_jobs:
                    d_, s_src = bg_jobs.pop(0)
                    S.dma("pool", d_, s_src, "bgc")
                S.barrier()
        S.es = es_top
    return nc


_W_NAMES = ["w_in", "rwkv_mu", "rwkv_w0", "rwkv_w_lora", "rwkv_a0", "rwkv_a_lora", "rwkv_g_lora", "rwkv_k_k", "rwkv_k_a",
            "rwkv_r_k", "rwkv_gn_g", "rwkv_gn_b", "ret_gn_g", "ret_gn_b", "w_out", "ln1_g", "ln1_b", "w_ffn_gate", "w_ffn_up",
            "w_ffn_down", "ln2_g", "ln2_b"]


def run(inputs, T, n_cores=8, trace=False):
    f = lambda a: np.ascontiguousarray(np.asarray(a, dtype=np.float32))
    x_prompt = f(inputs["x_prompt"])
    x_sample = f(inputs["x_sample"])
    nb = x_prompt.shape[0]
    consts = make_consts(T)
    shared = {k: f(inputs[k]) for k in _W_NAMES}
    shared["rwkv_r_k"] = shared["rwkv_r_k"].reshape(DEPTH, W_A)
    shared["cf"] = consts["cf"]
    shared["amask"] = consts["amask"]
    shared["rot"] = consts["rot"]
    st_shift, st_wkv = f(inputs["state_rwkv_shift"]), f(inputs["state_rwkv_wkv"])
    ckk, cvv, st_ret = f(inputs["cache_win_k"]), f(inputs["cache_win_v"]), f(inputs["state_ret"])
    in_maps = []
    for c in range(n_cores):
        b = (c * nb) // n_cores
        s0 = c * NSB
        m = dict(shared)
        m["xp"] = x_prompt[b]
        m["xs"] = np.ascontiguousarray(x_sample[s0:s0 + NSB].reshape(NSB * 4, D))
        m["st_shift"] = np.ascontiguousarray(st_shift[:, s0:s0 + NSB])
        m["st_wkv"] = np.ascontiguousarray(st_wkv[:, s0:s0 + NSB])
        m["ck"] = np.ascontiguousarray(ckk[:, s0:s0 + NSB].reshape(DEPTH, NSB, WIN, W_B))
        m["cv"] = np.ascontiguousarray(cvv[:, s0:s0 + NSB].reshape(DEPTH, NSB, WIN, W_B))
        m["st_ret"] = np.ascontiguousarray(st_ret[:, s0:s0 + NSB])
        in_maps.append(m)
    nc = build(T)
    res = run_bass_kernel_spmd(nc, in_maps, core_ids=list(range(n_cores)), trace=trace)
    R = res.results
    per = n_cores // nb
    WK = min(WIN, T)
    own = [R[b * per] for b in range(nb)]
    y_prompt = np.stack([o["y_p"] for o in own])
    y_sample = np.concatenate([r["y_s"].reshape(NSB, 4, D) for r in R], axis=0)
    p_shift = np.stack([o["p_shift"] for o in own], axis=1)
    p_wkv = np.stack([o["p_wkv"] for o in own], axis=1)
    p_wk = np.stack([o["p_wk"].reshape(DEPTH, WK, H_B, HD) for o in own], axis=1)
    p_wv = np.stack([o["p_wv"].reshape(DEPTH, WK, H_B, HD) for o in own], axis=1)
    p_ret = np.stack([o["p_ret"] for o in own], axis=1)
    s_shift = np.concatenate([r["s_shift"] for r in R], axis=1)
    s_wkv = np.concatenate([r["s_wkv"] for r in R], axis=1)
    s_wk = np.concatenate([r["s_wk"].reshape(DEPTH, NSB, WIN, H_B, HD) for r in R], axis=1)
    s_wv = np.concatenate([r["s_wv"].reshape(DEPTH, NSB, WIN, H_B, HD) for r in R], axis=1)
    s_ret = np.concatenate([r["s_ret"] for r in R], axis=1)
    outs = (y_prompt, y_sample, p_shift, p_wkv, p_wk, p_wv, p_ret, s_shift, s_wkv, s_wk, s_wv, s_ret)
    return tuple(np.ascontiguousarray(o, dtype=np.float32) for o in outs), res


def kernel(**inputs):
    T = int(np.asarray(inputs["x_prompt"]).shape[1])
    outs, _ = run(inputs, T)
    return outs
```

```python
import math
import os
_STOP = float(os.environ.get('KSTOP', '99'))
_NOP2 = int(os.environ.get('KNOP2', '0'))
_NOSAMP = int(os.environ.get('KNOSAMP', '0'))
from contextlib import ExitStack
import numpy as np
import ml_dtypes
import concourse.bass as bass
import concourse.mybir as mybir
from concourse.bass_utils import run_bass_kernel_spmd

F32 = mybir.dt.float32
BF16 = mybir.dt.bfloat16
AF = mybir.ActivationFunctionType
ALU = mybir.AluOpType
AX = mybir.AxisListType

D = 1024
DFF = 2816
NFC = DFF // 128
HD = 64
H_A, H_B, H_C = 6, 6, 4
W_A, W_B, W_C = 384, 384, 256
A_COLS = 1408
IN_COLS = 3584
WIN = 2048
NBLK = 17
RING = 18
PAST = 8192
DEPTH = 2
ALPHA = (2 * DEPTH) ** 0.25
DECAY = math.exp(-0.5)
GN_EPS = 64e-5
LN_EPS = 1e-5
GAMMA = [1.0 - 2.0 ** (-5.0 - h) for h in range(H_C)]
PADN = 124
NSB = 4
FM_COLS = [(0, 1408), (1408, 1408 + 768), (2560, 2560 + 512), (3328, 3584)]
FM_CHUNK_COL = []
for a_, b_ in FM_COLS:
    for c_ in range(a_, b_, 128):
        FM_CHUNK_COL.append(c_)
NFM = len(FM_CHUNK_COL)
CH_BQ, CH_BK, CH_CQ, CH_CK, CH_CG = 11, 14, 17, 19, 21
BV_COL, CV_COL = 2176, 3072


class Sched:
    ENGS = ("pe", "dve", "act", "pool", "sp")

    def __init__(self, nc, es):
        self.nc = nc
        self.es_sem = es
        self.es = es
        self.eng = {"pe": nc.tensor, "dve": nc.vector, "act": nc.scalar, "pool": nc.gpsimd, "sp": nc.sync}
        self.sem, self.cnt = {}, {}
        for e in self.ENGS:
            self.sem[e] = es.enter_context(nc.semaphore("q_" + e))
            self.cnt[e] = 0
        self.waited = {e: {} for e in self.ENGS}
        self.last_w, self.readers, self.dsem = {}, {}, {}
        self.kids = {}
        self.bank_last = {}
        self.n_ins = 0
        self.uid = 0
        self.keyname = {}
        self.keep = []

    def _rel(self, k):
        if k.startswith("ps") and len(k) >= 3 and k[2].isdigit():
            bank = k[:3]
            if k == bank:
                return [bank] + list(self.kids.get(bank, ()))
            self.kids.setdefault(bank, set()).add(k)
            return [k, bank]
        return [k]

    def sb(self, name, shape, dt=F32):
        self.uid += 1
        t = self.es.enter_context(self.nc.sbuf_tensor("%s_u%d" % (name, self.uid), list(shape), dt))
        self.keyname[id(t)] = name
        self.keep.append(t)
        return t

    def kn(self, t):
        return self.keyname[id(t)]

    def ps(self, name, shape, dt=F32):
        return self.es.enter_context(self.nc.psum_tensor(name, list(shape), dt))

    def _dma_sem(self, key):
        if key not in self.dsem:
            p = "d_" + key
            self.sem[p] = self.es_sem.enter_context(self.nc.semaphore(p))
            self.cnt[p] = 0
            self.dsem[key] = p
        return self.dsem[key]

    def _wait(self, E, p, c):
        if self.waited[E].get(p, 0) >= c:
            return
        self.eng[E].wait_ge(self.sem[p], c)
        self.waited[E][p] = c

    @staticmethod
    def _bank(k):
        if k.startswith("ps") and len(k) >= 3 and k[2].isdigit():
            return k[:3]
        return None

    def _need(self, E, reads, writes):
        need = {}

        def add(p, c):
            if c > need.get(p, 0):
                need[p] = c
        for r in reads:
            b = self._bank(r)
            if b is not None:
                for p, c in self.bank_last.get(b, {}).items():
                    if p != E:
                        add(p, c)
                continue
            lw = self.last_w.get(r)
            if lw is not None:
                add(*lw)
        for w in writes:
            b = self._bank(w)
            if b is not None:
                for p, c in self.bank_last.get(b, {}).items():
                    if p != E:
                        add(p, c)
                continue
            lw = self.last_w.get(w)
            if lw is not None and lw[0] != E:
                add(*lw)
            for p, c in self.readers.get(w, {}).items():
                if p != E:
                    add(p, c)
        for p, c in need.items():
            if p.startswith("d_"):
                c = self.cnt[p]
            self._wait(E, p, c)

    def _commit(self, P, c, reads, writes):
        for r in reads:
            b = self._bank(r)
            if b is not None:
                self.bank_last.setdefault(b, {})[P] = c
                continue
            self.readers.setdefault(r, {})[P] = c
        for w in writes:
            b = self._bank(w)
            if b is not None:
                self.bank_last.setdefault(b, {})[P] = c
                continue
            self.last_w[w] = (P, c)
            self.readers[w] = {}

    def op(self, E, fn, reads=(), writes=()):
        self._need(E, reads, writes)
        ins = fn(self.eng[E])
        self.cnt[E] += 1
        ins.then_inc(self.sem[E], 1)
        self._commit(E, self.cnt[E], reads, writes)
        self.n_ins += 1
        return ins

    def mm(self, out, lhsT, rhs, start, stop, reads, writes):
        return self.op("pe", lambda e: e.matmul(out, lhsT=lhsT, rhs=rhs, start=start, stop=stop), reads, writes)

    def tr(self, out, in_, ident, reads, writes):
        return self.op("pe", lambda e: e.transpose(out=out, in_=in_, identity=ident), reads, writes)

    def dma(self, Q, out, in_, key, reads=(), writes=()):
        p = self._dma_sem(key)
        self._need(Q, reads, writes)
        ins = self.eng[Q].dma_start(out=out, in_=in_)
        self.cnt[p] += 16
        ins.then_inc(self.sem[p], 16)
        self._commit(p, self.cnt[p], reads, writes)
        self.n_ins += 1
        return ins

    def barrier(self):
        for E in self.ENGS:
            for p in list(self.sem.keys()):
                if p != E and self.cnt[p] > 0:
                    self._wait(E, p, self.cnt[p])
        self.last_w, self.readers = {}, {}
        self.bank_last = {}

    def tt(self, E, out, in0, in1, op, r, w):
        return self.op(E, lambda e: e.tensor_tensor(out=out, in0=in0, in1=in1, op=op), r, w)

    def ts(self, E, out, in0, s1, s2, op0, op1, r, w):
        if s2 is None:
            return self.op(E, lambda e: e.tensor_scalar(out=out, in0=in0, scalar1=s1, scalar2=None, op0=op0), r, w)
        return self.op(E, lambda e: e.tensor_scalar(out=out, in0=in0, scalar1=s1, scalar2=s2, op0=op0, op1=op1), r, w)

    def stt(self, out, in0, sc, in1, op0, op1, r, w):
        return self.op("dve", lambda e: e.scalar_tensor_tensor(out=out, in0=in0, scalar=sc, in1=in1, op0=op0, op1=op1), r, w)

    def act(self, out, in_, func, r, w, scale=None, bias=None):
        kw = {}
        if scale is not None:
            kw["scale"] = scale
        if bias is not None:
            kw["bias"] = bias
        return self.op("act", lambda e: e.activation(out=out, in_=in_, func=func, **kw), r, w)

    def cp(self, E, out, in_, r, w):
        if E == "act":
            return self.act(out, in_, AF.Copy, r, w)
        return self.op(E, lambda e: e.tensor_copy(out=out, in_=in_), r, w)

    def memset(self, E, ap, val, w):
        return self.op(E, lambda e: e.memset(ap, val), (), w)


def _mult(d):
    d = np.asarray(d)
    ok = d >= 0
    m = ((d <= 128) & ok).astype(np.float32)
    m += ((d % 4 == 0) & (d <= 512) & ok)
    m += ((d % 16 == 0) & (d <= 2048) & ok)
    return m.astype(np.float32)


def make_consts(T):
    NT = T // 128
    c = {}
    i = np.arange(128)
    same = (i[:, None] // 64) == (i[None, :] // 64)
    su = ((i[:, None] < i[None, :]) & same).astype(np.float32)
    ui = ((i[:, None] <= i[None, :]) & same).astype(np.float32)
    ident = np.eye(128, dtype=np.float32)
    bones = same.astype(np.float32)
    def pm(half, hd=64):
        m = np.zeros((128, 128), np.float32)
        for blk in range(2):
            o = blk * hd
            for e in range(half):
                m[o + e + half, o + e] = -1.0
                m[o + e, o + e + half] = 1.0
        return m
    packs = [ident, su, ui, su.T.copy(), -ui, bones, pm(8), pm(32)]
    names = ["ident", "su", "ui", "sl", "nui", "bones", "pmb", "pmc"]
    for h in range(H_C):
        g = GAMMA[h]
        rel = i[None, :] - i[:, None]
        dm = np.where(rel >= 0, np.exp(np.log(g) * np.maximum(rel, 0)), 0.0).astype(np.float32)
        packs.append(dm)
        names.append("dmask%d" % h)
    for p in range(2):
        qd = np.zeros((128, 128), np.float32)
        kd = np.zeros((128, 128), np.float32)
        for hh in range(2):
            g = GAMMA[2 * p + hh]
            qd[hh * 64:(hh + 1) * 64, :] = np.exp(np.log(g) * (i + 1.0))[None, :]
            kd[hh * 64:(hh + 1) * 64, :] = (np.exp(np.log(g) * (127.0 - i)) * (HD ** -0.5))[None, :]
        packs += [qd, kd]
        names += ["qdec%d" % p, "kdec%d" % p]
    c["cf"] = np.stack(packs, axis=1).astype(np.float32)
    c["cf_names"] = names
    mk = np.zeros((128, NBLK, 128), np.float32)
    for dl in range(NBLK):
        d = dl * 128 + i[None, :] - i[:, None]
        mk[:, dl, :] = _mult(d)
    c["amask"] = mk.astype(ml_dtypes.bfloat16)
    ntile = NT + 1
    rt = np.zeros((ntile, 128, 4, 128), np.float32)
    inv_b = (500000.0 ** (-np.arange(0, 16, 2, dtype=np.float32) / np.float32(16))).astype(np.float32)
    inv_c = (1.0 / (np.float32(10000.0) ** np.linspace(0.0, 1.0, 32, dtype=np.float32))).astype(np.float32)
    for ti in range(ntile):
        if ti < NT:
            pos = (ti * 128 + i).astype(np.float32)
        else:
            pos = (np.float32(PAST) + np.maximum(i - PADN, 0)).astype(np.float32)
        angb = (pos[:, None] * inv_b[None, :]).astype(np.float32)
        angc = (pos[:, None] * inv_c[None, :]).astype(np.float32)
        cb = np.ones((64, 128), np.float32)
        sbb = np.zeros((64, 128), np.float32)
        cb[0:8] = np.cos(angb).T
        cb[8:16] = np.cos(angb).T
        sbb[0:8] = np.sin(angb).T
        sbb[8:16] = np.sin(angb).T
        cc = np.concatenate([np.cos(angc).T, np.cos(angc).T], axis=0)
        sc = np.concatenate([np.sin(angc).T, np.sin(angc).T], axis=0)
        for hh in range(2):
            rt[ti, hh * 64:(hh + 1) * 64, 0] = cb
            rt[ti, hh * 64:(hh + 1) * 64, 1] = sbb
            rt[ti, hh * 64:(hh + 1) * 64, 2] = cc
            rt[ti, hh * 64:(hh + 1) * 64, 3] = sc
    c["rot"] = rt
    return c


def build(T):
    NT = T // 128
    NTS = NT + NSB
    WK = min(WIN, T)
    NKEEP = WK // 128
    nc = bass.Bass("TRN2", target_bir_lowering=False)

    def din(name, shape, dt=F32):
        return nc.dram_tensor(name, list(shape), dt, kind="ExternalInput").ap()

    def dout(name, shape, dt=F32):
        return nc.dram_tensor(name, list(shape), dt, kind="ExternalOutput").ap()

    def dscr(name, shape, dt=F32):
        return nc.dram_tensor(name, list(shape), dt, kind="Internal").ap()

    xp = din("xp", [T, D])
    xs = din("xs", [NSB * 4, D])
    st_shift = din("st_shift", [DEPTH, NSB, A_COLS])
    st_wkv = din("st_wkv", [DEPTH, NSB, H_A, 64, 64])
    ck = din("ck", [DEPTH, NSB, WIN, W_B])
    cv = din("cv", [DEPTH, NSB, WIN, W_B])
    st_ret = din("st_ret", [DEPTH, NSB, H_C, 64, 64])
    w_in = din("w_in", [DEPTH, D, IN_COLS])
    p_mu = din("rwkv_mu", [DEPTH, A_COLS])
    p_w0 = din("rwkv_w0", [DEPTH, W_A])
    p_wl = din("rwkv_w_lora", [DEPTH, 64, W_A])
    p_a0 = din("rwkv_a0", [DEPTH, W_A])
    p_al = din("rwkv_a_lora", [DEPTH, 64, W_A])
    p_gl = din("rwkv_g_lora", [DEPTH, 128, W_A])
    p_kk = din("rwkv_k_k", [DEPTH, W_A])
    p_ka = din("rwkv_k_a", [DEPTH, W_A])
    p_rk = din("rwkv_r_k", [DEPTH, W_A])
    p_gg = din("rwkv_gn_g", [DEPTH, W_A])
    p_gb = din("rwkv_gn_b", [DEPTH, W_A])
    p_rg = din("ret_gn_g", [DEPTH, W_C])
    p_rb = din("ret_gn_b", [DEPTH, W_C])
    w_out = din("w_out", [DEPTH, D, D])
    ln1_g = din("ln1_g", [DEPTH, D])
    ln1_b = din("ln1_b", [DEPTH, D])
    w_gate = din("w_ffn_gate", [DEPTH, D, DFF])
    w_up = din("w_ffn_up", [DEPTH, D, DFF])
    w_down = din("w_ffn_down", [DEPTH, DFF, D])
    ln2_g = din("ln2_g", [DEPTH, D])
    ln2_b = din("ln2_b", [DEPTH, D])
    c_f = din("cf", [128, 16, 128])
    c_am = din("amask", [128, NBLK, 128], BF16)
    c_rot = din("rot", [NT + 1, 128, 4, 128])

    y_p = dout("y_p", [T, D])
    y_s = dout("y_s", [NSB * 4, D])
    o_pshift = dout("p_shift", [DEPTH, A_COLS])
    o_pwkv = dout("p_wkv", [DEPTH, H_A, 64, 64])
    o_pwk = dout("p_wk", [DEPTH, WK, W_B])
    o_pwv = dout("p_wv", [DEPTH, WK, W_B])
    o_pret = dout("p_ret", [DEPTH, H_C, 64, 64])
    o_sshift = dout("s_shift", [DEPTH, NSB, A_COLS])
    o_swkv = dout("s_wkv", [DEPTH, NSB, H_A, 64, 64])
    o_swk = dout("s_wk", [DEPTH, NSB, WIN, W_B])
    o_swv = dout("s_wv", [DEPTH, NSB, WIN, W_B])
    o_sret = dout("s_ret", [DEPTH, NSB, H_C, 64, 64])

    mixs = dscr("mixs", [8, 128, NTS * 128], BF16)
    wob = dscr("wob", [DEPTH, D, D], BF16)
    wgb = dscr("wgb", [DEPTH, D, DFF], BF16)
    wub = dscr("wub", [DEPTH, D, DFF], BF16)
    wdb = dscr("wdb", [DEPTH, DFF, D], BF16)
    winb = dscr("winb", [D, IN_COLS], BF16)
    bg_jobs = []
    x2s = dscr("x2s", [NTS * 128, D])

    CI = {"ident": 0, "su": 1, "ui": 2, "sl": 3, "nui": 4, "bones": 5, "pmb": 6, "pmc": 7,
          "dmask": 8, "qdec0": 12, "kdec0": 13, "qdec1": 14, "kdec1": 15}

    with ExitStack() as es_top:
        es_top.enter_context(nc.allow_non_contiguous_dma(reason="small strided parameter / state transfers"))
        S = Sched(nc, es_top)
        ps = [S.ps("ps%d" % b, [128, 512]) for b in range(8)]
        psb = [p[:, :].bitcast(BF16) for p in ps]

        def x_src(l, ti):
            if l == 0:
                return xp[ti * 128:(ti + 1) * 128, :]
            return x2s[ti * 128:(ti + 1) * 128, :]

        for l in range(DEPTH):
            with ExitStack() as es1:
                S.es = es1
                win = S.sb("win", [128, 8, IN_COLS], BF16)
                cf = S.sb("cf", [128, 16, 128])
                identb = S.sb("identb", [128, 128], BF16)
                amask = S.sb("amask", [128, NBLK, 128], BF16)
                mhalf = S.sb("mhalf", [128, 384])
                muc = S.sb("muc", [128, 11])
                pr = S.sb("pr", [128, 8, 3])
                prc = S.sb("prc", [128, 2, 2])
                wl = S.sb("wl", [128, 384])
                al = S.sb("al", [128, 384])
                gl = S.sb("gl", [128, 384])
                xt = [S.sb("xt%d" % k, [128, D]) for k in range(2)]
                xT = S.sb("xT", [128, 8, 128], BF16)
                H = S.sb("H", [128, NFM, 128])
                hm = S.sb("hm", [128, 11, 128])
                hml = S.sb("hml", [128, 11])
                rot = [S.sb("rot%d" % k, [128, 4, 128]) for k in range(2)]
                LI = S.sb("LI", [128, 2, 128])
                t3 = [S.sb("t3_%d" % k, [128, 3, 128]) for k in range(10)]
                TTt = S.sb("TTt", [128, 3, 4, 128], BF16)
                ktok = S.sb("ktok", [128, 384], BF16)
                nbtok = S.sb("nbtok", [128, 384], BF16)
                vtok = S.sb("vtok", [128, 384], BF16)
                U = S.sb("U", [128, 3, 64])
                UbP = [S.sb("UbP%d" % h, [128, 64], BF16) for h in range(6)]
                RbP = [S.sb("RbP%d" % h, [128, 64], BF16) for h in range(4)]
                chP = [[S.sb("chP%d_%d" % (h, k), [128, 128], BF16) for k in range(2)] for h in range(3)]
                chQ = [[S.sb("chQ%d_%d" % (h, k), [128, 128], BF16) for k in range(2)] for h in range(3)]
                chT = [[S.sb("chT%d_%d" % (h, k), [128, 128], BF16) for k in range(2)] for h in range(3)]
                cb16 = S.sb("cb16", [128, 3, 128], BF16)
                sqb = S.sb("sqb", [128, 3, 128], BF16)
                osb = S.sb("osb", [128, 3, 128], BF16)
                hb16 = S.sb("hb16", [128, 6, 128], BF16)
                invT = [S.sb("invT%d" % h, [128, 128], BF16) for h in range(3)]
                m4t = [S.sb("m4t%d" % h, [128, 128], BF16) for h in range(3)]
                lm3 = [S.sb("lm3_%d" % h, [128, 256], BF16) for h in range(3)]
                rhsb = [S.sb("rhsb%d" % h, [128, 64], BF16) for h in range(3)]
                umb = [S.sb("umb%d" % h, [128, 64], BF16) for h in range(3)]
                utmp = [S.sb("utmp%d" % h, [128, 64]) for h in range(3)]
                KTr = S.sb("KTr", [128, 3, RING, 128], BF16)
                Vr = S.sb("Vr", [128, RING, 6, 66], BF16)
                qTb = S.sb("qTb", [128, 3, 128], BF16)
                krot = S.sb("krot", [128, 3, 128])
                stg = [S.sb("stg%d" % k, [128, 384]) for k in range(2)]
                vstg = [S.sb("vstg%d" % k, [128, 384]) for k in range(2)]
                pT = [S.sb("pT%d" % k, [128, 512], BF16) for k in range(3)]
                ob = S.sb("ob", [128, 384])
                rl = S.sb("rl", [128, 6])
                c2 = [S.sb("c2_%d" % k, [128, 2, 128]) for k in range(4)]
                qcb = S.sb("qcb", [128, 2, 128], BF16)
                kcb = S.sb("kcb", [128, 2, 128], BF16)
                qdb = S.sb("qdb", [128, 2, 128], BF16)
                kdb = S.sb("kdb", [128, 2, 128], BF16)
                kdtok = S.sb("kdtok", [128, 256], BF16)
                vcb = S.sb("vcb", [128, 256], BF16)
                attb = [S.sb("attb%d" % k, [128, 128], BF16) for k in range(2)]
                R = S.sb("R", [128, 2, 64])
                mixT = [S.sb("mixT%d" % k, [128, 8, 128], BF16) for k in range(2)]
                cstage = S.sb("cstage", [128, 384])
                sst = S.sb("sst", [128, 32])
                swk = S.sb("swk", [128, 64])

                ident = cf[:, CI["ident"], :]

                for kc in range(8):
                    if l == 0:
                        S.dma("pool", win[:, kc, :], w_in[l, kc * 128:(kc + 1) * 128, :], "w1", writes=["win"])
                    else:
                        S.dma("sp", win[:, kc, :], winb[kc * 128:(kc + 1) * 128, :], "w1", writes=["win"])
                for kc in range(8):
                    bg_jobs.append((wob[l, kc * 128:(kc + 1) * 128, :], w_out[l, kc * 128:(kc + 1) * 128, :]))
                for kc in range(8):
                    bg_jobs.append((wgb[l, kc * 128:(kc + 1) * 128, :], w_gate[l, kc * 128:(kc + 1) * 128, :]))
                    bg_jobs.append((wub[l, kc * 128:(kc + 1) * 128, :], w_up[l, kc * 128:(kc + 1) * 128, :]))
                for c in range(NFC):
                    bg_jobs.append((wdb[l, c * 128:(c + 1) * 128, :], w_down[l, c * 128:(c + 1) * 128, :]))
                S.dma("sp", cf[:, :, :], c_f[:, :, :], "c1", writes=["cf"])
                S.dma("sp", amask[:, :, :], c_am[:, :, :], "c1", writes=["amask"])
                S.dma("sp", muc[:, :], p_mu[l].rearrange("(c p) -> p c", p=128), "c1", writes=["muc"])
                for k, prm in enumerate([p_w0, p_a0, p_kk, p_ka, p_rk, p_gg, p_gb]):
                    S.dma("sp", pr[:, k, :], prm[l].rearrange("(c p) -> p c", p=128), "c1", writes=["pr"])
                S.dma("sp", prc[:, 0, :], p_rg[l].rearrange("(c p) -> p c", p=128), "c1", writes=["prc"])
                S.dma("sp", prc[:, 1, :], p_rb[l].rearrange("(c p) -> p c", p=128), "c1", writes=["prc"])
                S.dma("sp", wl[0:64, :], p_wl[l], "c1", writes=["wl"])
                S.dma("sp", al[64:128, :], p_al[l], "c1", writes=["al"])
                S.dma("sp", gl[:, :], p_gl[l], "c1", writes=["gl"])
                S.cp("dve", identb[:, :], ident, ["cf"], ["identb"])
                S.cp("dve", cb16[:, :, :], cf[:, CI["bones"]:CI["bones"] + 3, :], ["cf"], ["cb16"])
                S.memset("dve", mhalf[:, :], -0.5, ["mhalf"])
                S.ts("dve", pr[:, 0:2, :], pr[:, 0:2, :], -1.0, None, ALU.mult, None, ["pr"], ["pr"])

                for sb_ in range(NSB):
                    S.dma("act", o_swk[l, sb_, 0:WIN - 4, :], ck[l, sb_, 4:WIN, :], "cpk")
                    S.dma("act", o_swv[l, sb_, 0:WIN - 4, :], cv[l, sb_, 4:WIN, :], "cpv")
                tile_ctr = [0]

                def process_tile(seq, ti, nseq, gi):
                    samp = seq != "p"
                    if _STOP <= 0:
                        return
                    for _ in range(2):
                        if bg_jobs:
                            d_, s_src = bg_jobs.pop(0)
                            S.dma("pool", d_, s_src, "bgc")
                    k2 = tile_ctr[0] % 2
                    tile_ctr[0] += 1
                    first = ti == 0
                    xk = "xt%d" % k2
                    rk_ = "rot%d" % k2
                    if samp and l == 0:
                        S.memset("pool", xt[k2][:, :], 0.0, [xk])
                        S.dma("sp", xt[k2][PADN:128, :], xs[seq * 4:(seq + 1) * 4, :], "ldx%d" % k2, writes=[xk])
                    else:
                        S.dma("sp", xt[k2][:, :], x_src(l, gi), "ldx%d" % k2, writes=[xk])
                    rti = NT if samp else ti
                    S.dma("sp", rot[k2][:, :, :], c_rot[rti], "ldx%d" % k2, writes=[rk_])
                    for half in range(2):
                        for j in range(4):
                            kc = half * 4 + j
                            S.tr(ps[half][:, j * 128:(j + 1) * 128], xt[k2][:, kc * 128:(kc + 1) * 128], ident,
                                 [xk, "cf"], ["ps%d" % half])
                        S.cp("act" if half == 0 else "dve", xT[:, half * 4:(half + 1) * 4, :],
                             ps[half][:, :].rearrange("p (a b) -> p a b", b=128), ["ps%d" % half], ["xT%d" % half])
                    for c in range(NFM):
                        b, j = c // 4, c % 4
                        col = FM_CHUNK_COL[c]
                        for kc in range(8):
                            S.mm(ps[b][:, j * 128:(j + 1) * 128], win[:, kc, col:col + 128], xT[:, kc, :],
                                 kc == 0, kc == 7, ["win", "xT0", "xT1"], ["ps%d" % b])
                    for kc in range(8):
                        S.mm(ps[6][:, 0:384], xT[:, kc, :], win[:, kc, BV_COL:BV_COL + 384], kc == 0, kc == 7,
                             ["win", "xT0", "xT1"], ["ps6"])
                    for kc in range(8):
                        S.mm(ps[7][:, 0:256], xT[:, kc, :], win[:, kc, CV_COL:CV_COL + 256], kc == 0, kc == 7,
                             ["win", "xT0", "xT1"], ["ps7"])
                    for b in range(6):
                        n = min(4, NFM - 4 * b)
                        S.cp("act" if b % 2 == 0 else "dve", H[:, 4 * b:4 * b + n, :],
                             ps[b][:, 0:n * 128].rearrange("p (a b) -> p a b", b=128), ["ps%d" % b], ["H%d" % b])
                    Hk = ["H%d" % b for b in range(6)]
                    if _STOP <= 1:
                        return
                    slot = ti % RING if not samp else 16
                    vk = "V%d" % slot
                    S.cp("dve", Vr[:, slot, :, 0:64], ps[6][:, 0:384].rearrange("p (h e) -> p h e", e=64), ["ps6"], [vk])
                    if _STOP <= 1.1:
                        return
                    S.memset("pool", Vr[:, slot, :, 64:65], 1.0, [vk + "o"])
                    if _STOP <= 1.2:
                        return
                    keep = samp or (ti >= NT - NKEEP)
                    if keep:
                        S.cp("dve", stg[0][:, :], ps[6][:, 0:384], ["ps6"], ["stg0"])
                        if samp:
                            S.dma("pool", o_swv[l, seq, WIN - 4:WIN, :], stg[0][PADN:128, :], "stv", reads=["stg0"])
                        else:
                            r0 = (ti - (NT - NKEEP)) * 128
                            if os.environ.get("KV1") == "nodma":
                                pass
                            else:
                                S.dma(os.environ.get("KV1", "pool"), o_pwv[l, r0:r0 + 128, :], stg[0][:, :], "stv", reads=["stg0"])
                    if _STOP <= 1.3:
                        return
                    S.cp("act", vcb[:, :], ps[7][:, 0:256], ["ps7"], ["vcb"])
                    if _STOP <= 1.5:
                        return
                    if samp:
                        S.cp("dve", sst[:, 0:11], H[:, 0:11, 127], Hk[0:3], ["sst"])
                        S.dma("pool", o_sshift[l, seq].rearrange("(c p) -> p c", p=128), sst[:, 0:11], "sts", reads=["sst"])
                        S.dma("sp", sst[:, 16:27], st_shift[l, seq].rearrange("(c p) -> p c", p=128), "ldx%d" % k2, writes=["sst2"])
                        S.cp("dve", H[:, 0:11, PADN - 1], sst[:, 16:27], ["sst2"] + Hk[0:3], Hk[0:3])
                    elif ti == NT - 1:
                        S.cp("dve", sst[:, 0:11], H[:, 0:11, 127], Hk[0:3], ["sst"])
                        S.dma("pool", o_pshift[l].rearrange("(c p) -> p c", p=128), sst[:, 0:11], "sts", reads=["sst"])
                    if _STOP <= 1.7:
                        return
                    if first:
                        S.memset("dve", hml[:, :], 0.0, ["hml"])
                    HA = H[:, 0:11, :]
                    S.tt("dve", hm[:, :, :], HA, muc[:, 0:11].unsqueeze(2).to_broadcast([128, 11, 128]), ALU.mult,
                         Hk[0:3] + ["muc"], ["hm"])
                    S.tt("dve", HA, HA, hm[:, :, :], ALU.subtract, Hk[0:3] + ["hm"], Hk[0:3])
                    S.tt("dve", H[:, 0:11, 1:128], H[:, 0:11, 1:128], hm[:, :, 0:127], ALU.add, Hk[0:3] + ["hm"], Hk[0:3])
                    S.tt("dve", H[:, 0:11, 0], H[:, 0:11, 0], hml[:, :], ALU.add, Hk[0:3] + ["hml"], Hk[0:3])
                    S.cp("dve", hml[:, :], hm[:, :, 127], ["hm"], ["hml"])
                    rT, kT, vT = H[:, 0:3, :], H[:, 3:6, :], H[:, 6:9, :]
                    if _STOP <= 2:
                        return
                    HAk = Hk[0:3]
                    S.act(LI[0:64, 0, :], H[0:64, 9, :], AF.Exp, HAk, ["LIa"], scale=-2.0)
                    S.ts("dve", LI[0:64, 0, :], LI[0:64, 0, :], 1.0, None, ALU.add, None, ["LIa"], ["LIa"])
                    S.op("dve", lambda e: e.reciprocal(out=LI[0:64, 0, :], in_=LI[0:64, 0, :]), ["LIa"], ["LIa"])
                    S.ts("dve", LI[0:64, 0, :], LI[0:64, 0, :], 2.0, -1.0, ALU.mult, ALU.add, ["LIa"], ["LIa"])
                    S.cp("act", LI[64:128, 0, :], H[64:128, 9, :], HAk, ["LIb"])
                    S.act(LI[:, 1, :], H[:, 10, :], AF.Exp, HAk, ["LIc"], scale=-1.0)
                    S.ts("dve", LI[:, 1, :], LI[:, 1, :], 1.0, None, ALU.add, None, ["LIc"], ["LIc"])
                    S.op("dve", lambda e: e.reciprocal(out=LI[:, 1, :], in_=LI[:, 1, :]), ["LIc"], ["LIc"])
                    for p in range(3):
                        S.mm(ps[0][:, p * 128:(p + 1) * 128], wl[0:64, p * 128:(p + 1) * 128], LI[0:64, 0, :], True, True,
                             ["wl", "LIa"], ["ps0"])
                        S.mm(ps[1][:, p * 128:(p + 1) * 128], al[64:128, p * 128:(p + 1) * 128], LI[64:128, 0, :], True, True,
                             ["al", "LIb"], ["ps1"])
                        S.mm(ps[2][:, p * 128:(p + 1) * 128], gl[:, p * 128:(p + 1) * 128], LI[:, 1, :], True, True,
                             ["gl", "LIc"], ["ps2"])
                    lw, cum, Wc, iW, Wp, aa, gT, kkn, kp, bb = t3
                    if _STOP <= 3:
                        return
                    n = S.kn
                    v3 = lambda b: ps[b][:, 0:384].rearrange("p (a b) -> p a b", b=128)
                    for p in range(3):
                        S.act(lw[:, p, :], ps[0][:, p * 128:(p + 1) * 128], AF.Exp, ["ps0", "pr"], [n(lw)], scale=-1.0, bias=pr[:, 0, p:p + 1])
                        S.act(aa[:, p, :], ps[1][:, p * 128:(p + 1) * 128], AF.Exp, ["ps1", "pr"], [n(aa)], scale=-1.0, bias=pr[:, 1, p:p + 1])
                    S.cp("act", gT[:, :, :], v3(2), ["ps2"], [n(gT)])
                    S.ts("dve", lw[:, :, :], lw[:, :, :], 1.0, None, ALU.add, None, [n(lw)], [n(lw)])
                    S.op("dve", lambda e: e.reciprocal(out=lw[:, :, :], in_=lw[:, :, :]), [n(lw)], [n(lw)])
                    S.ts("dve", lw[:, :, :], lw[:, :, :], -DECAY, None, ALU.mult, None, [n(lw)], [n(lw)])
                    S.ts("dve", aa[:, :, :], aa[:, :, :], 1.0, None, ALU.add, None, [n(aa)], [n(aa)])
                    S.op("dve", lambda e: e.reciprocal(out=aa[:, :, :], in_=aa[:, :, :]), [n(aa)], [n(aa)])
                    if samp:
                        S.memset("dve", lw[:, :, 0:PADN], 0.0, [n(lw)])
                    ones3 = mhalf
                    for p in range(3):
                        for cc in range(2):
                            sl = slice(cc * 64, (cc + 1) * 64)
                            S.op("dve", lambda e, p=p, sl=sl: e.tensor_tensor_scan(
                                out=cum[:, p, sl], data0=onesT[:, sl], data1=lw[:, p, sl], initial=0.0,
                                op0=ALU.mult, op1=ALU.add), [n(lw), "onesT"], [n(cum)])
                    S.act(Wc[:, :, :], cum[:, :, :], AF.Exp, [n(cum)], [n(Wc)])
                    S.act(iW[:, :, :], cum[:, :, :], AF.Exp, [n(cum)], [n(iW)], scale=-1.0)
                    S.tt("dve", Wp[:, :, :], cum[:, :, :], lw[:, :, :], ALU.subtract, [n(cum), n(lw)], [n(Wp)])
                    S.act(Wp[:, :, :], Wp[:, :, :], AF.Exp, [n(Wp)], [n(Wp)])
                    bc = lambda k: pr[:, k, :].unsqueeze(2).to_broadcast([128, 3, 128])
                    S.tt("dve", kkn[:, :, :], kT, bc(2), ALU.mult, HAk + ["pr"], [n(kkn)])
                    S.act(sqb[:, :, :], kkn[:, :, :], AF.Square, [n(kkn)], ["sqb"])
                    for p in range(3):
                        S.mm(ps[3][:, p * 128:(p + 1) * 128], cb16[:, 0, :], sqb[:, p, :], True, True, ["cb16", "sqb"], ["ps3"])
                    S.ts("dve", bb[:, :, :], v3(3), 1e-12, None, ALU.max, None, ["ps3"], [n(bb)])
                    S.act(bb[:, :, :], bb[:, :, :], AF.Ln, [n(bb)], [n(bb)])
                    S.act(bb[:, :, :], bb[:, :, :], AF.Exp, [n(bb)], [n(bb)], scale=-0.5)
                    S.tt("dve", kkn[:, :, :], kkn[:, :, :], bb[:, :, :], ALU.mult, [n(kkn), n(bb)], [n(kkn)])
                    S.stt(kp[:, :, :], aa[:, :, :], -1.0, bc(3), ALU.add, ALU.mult, [n(aa), "pr"], [n(kp)])
                    S.stt(kp[:, :, :], kp[:, :, :], 1.0, kT, ALU.add, ALU.mult, [n(kp)] + HAk, [n(kp)])
                    S.tt("dve", bb[:, :, :], kkn[:, :, :], aa[:, :, :], ALU.mult, [n(kkn), n(aa)], [n(bb)])
                    if samp:
                        S.memset("dve", kp[:, :, 0:PADN], 0.0, [n(kp)])
                        S.memset("pool", bb[:, :, 0:PADN], 0.0, [n(bb)])
                    S.tt("dve", TTt[:, :, 0, :], kkn[:, :, :], Wp[:, :, :], ALU.mult, [n(kkn), n(Wp)], ["TT0"])
                    S.tt("dve", TTt[:, :, 1, :], rT, Wc[:, :, :], ALU.mult, HAk + [n(Wc)], ["TT1"])
                    S.tt("dve", TTt[:, :, 2, :], kp[:, :, :], iW[:, :, :], ALU.mult, [n(kp), n(iW)], ["TT2"])
                    S.tt("dve", TTt[:, :, 3, :], bb[:, :, :], iW[:, :, :], ALU.mult, [n(bb), n(iW)], ["TT3"])
                    bon = cum
                    S.tt("dve", aa[:, :, :], rT, kp[:, :, :], ALU.mult, HAk + [n(kp), n(bb)], [n(aa)])
                    S.tt("dve", osb[:, :, :], aa[:, :, :], bc(4), ALU.mult, [n(aa), "pr"], ["osb"])
                    for p in range(3):
                        S.mm(ps[3][:, p * 128:(p + 1) * 128], cb16[:, 0, :], osb[:, p, :], True, True, ["cb16", "osb"], ["ps3"])
                    S.tt("dve", bon[:, :, :], v3(3), vT, ALU.mult, ["ps3", n(Wp), n(Wc), n(iW)] + HAk, [n(cum)])
                    for p in range(3):
                        S.tr(psb[4][:, p * 128:(p + 1) * 128], TTt[:, p, 2, :], identb[:, :], ["TT2", "identb"], ["ps4"])
                        S.tr(psb[4][:, 384 + p * 128:384 + (p + 1) * 128], TTt[:, p, 3, :], identb[:, :], ["TT3", "identb"], ["ps4"])
                        S.tr(ps[5][:, p * 128:(p + 1) * 128], H[:, 6 + p, :], ident, HAk + ["cf"], ["ps5"])
                    S.cp("dve", ktok[:, :], psb[4][:, 0:384], ["ps4"], ["ktok"])
                    S.ts("dve", nbtok[:, :], psb[4][:, 384:768], -1.0, None, ALU.mult, None, ["ps4"], ["nbtok"])
                    S.cp("act", vtok[:, :], ps[5][:, 0:384], ["ps5"], ["vtok"])
                    if _STOP <= 4:
                        return
                    if first:
                        if samp:
                            for p in range(3):
                                S.dma("sp", cstage[0:64, p * 128:(p + 1) * 128].rearrange("v (h k) -> v h k", k=64),
                                      st_wkv[l, seq, 2 * p:2 * p + 2].rearrange("h v k -> v h k"), "ldx%d" % k2, writes=["cstage"])
                            for p in range(3):
                                S.tr(ps[6][:, p * 64:(p + 1) * 64], cstage[0:64, p * 128:(p + 1) * 128], cf[0:64, CI["ident"], 0:64],
                                     ["cstage", "cf"], ["ps6"])
                            S.cp("dve", U[:, :, :], ps[6][:, 0:192].rearrange("p (a b) -> p a b", b=64), ["ps6"], ["U"])
                            for h in range(6):
                                p_, hb_ = h // 2, 64 * (h % 2)
                                S.memset("pool", UbP[h][:, :], 0.0, ["UbP%d" % h])
                                S.cp("dve", UbP[h][hb_:hb_ + 64, :], ps[6][hb_:hb_ + 64, p_ * 64:(p_ + 1) * 64], ["ps6"], ["UbP%d" % h])
                            for p in range(2):
                                for hh in range(2):
                                    S.dma("sp", R[hh * 64:(hh + 1) * 64, p, :], st_ret[l, seq, 2 * p + hh], "ldx%d" % k2, writes=["R"])
                            for p in range(2):
                                for hh in range(2):
                                    g = GAMMA[2 * p + hh] ** (-float(PADN))
                                    S.ts("dve", R[hh * 64:(hh + 1) * 64, p, :], R[hh * 64:(hh + 1) * 64, p, :], g, None, ALU.mult, None, ["R"], ["R"])
                            for h in range(4):
                                p_, hb_ = h // 2, 64 * (h % 2)
                                S.memset("pool", RbP[h][:, :], 0.0, ["RbP%d" % h])
                                S.cp("act", RbP[h][hb_:hb_ + 64, :], R[hb_:hb_ + 64, p_, :], ["R"], ["RbP%d" % h])
                        else:
                            S.memset("dve", U[:, :, :], 0.0, ["U"])
                            for h in range(6):
                                S.memset("pool", UbP[h][:, :], 0.0, ["UbP%d" % h])
                            S.memset("dve", R[:, :, :], 0.0, ["R"])
                            for h in range(4):
                                S.memset("pool", RbP[h][:, :], 0.0, ["RbP%d" % h])
                    mk2 = "mixT%d" % k2
                    cosB, sinB = rot[k2][:, 0, :], rot[k2][:, 1, :]
                    cosC, sinC = rot[k2][:, 2, :], rot[k2][:, 3, :]
                    S.cp("act", hb16[:, :, :], H[:, CH_BQ:CH_BQ + 6, :], ["H2", "H3", "H4"], ["hb16"])
                    for c in range(6):
                        S.mm(ps[0 + c // 4][:, (c % 4) * 128:(c % 4 + 1) * 128], cb16[:, 1, :], hb16[:, c, :], True, True,
                             ["cb16", "hb16"], ["ps%d" % (c // 4)])
                    qk = H[:, CH_BQ:CH_BQ + 6, :]
                    qkk = ["H2", "H3", "H4"]
                    b6 = lambda a: a.unsqueeze(1).to_broadcast([128, 6, 128])
                    S.tt("dve", qk, qk, b6(cosB), ALU.mult, qkk + [rk_], qkk)
                    rtmp = t3[3]
                    S.tt("dve", rtmp[:, :, :], v3(0) if False else ps[0][:, 0:384].rearrange("p (a b) -> p a b", b=128),
                         sinB.unsqueeze(1).to_broadcast([128, 3, 128]), ALU.mult, ["ps0", rk_], [n(rtmp)])
                    S.tt("dve", H[:, CH_BQ:CH_BQ + 3, :], H[:, CH_BQ:CH_BQ + 3, :], rtmp[:, :, :], ALU.add, qkk + [n(rtmp)], qkk)
                    S.tt("dve", rtmp[:, 0, :], ps[0][:, 384:512], sinB, ALU.mult, ["ps0", rk_], [n(rtmp)])
                    S.tt("dve", rtmp[:, 1:3, :], ps[1][:, 0:256].rearrange("p (a b) -> p a b", b=128),
                         sinB.unsqueeze(1).to_broadcast([128, 2, 128]), ALU.mult, ["ps1", rk_], [n(rtmp)])
                    S.tt("dve", krot[:, :, :], H[:, CH_BK:CH_BK + 3, :], rtmp[:, :, :], ALU.add, qkk + [n(rtmp)], ["krot"])
                    S.cp("act", qTb[:, :, :], H[:, CH_BQ:CH_BQ + 3, :], qkk, ["qTb"])
                    S.cp("act", KTr[:, :, slot, :], krot[:, :, :], ["krot"], ["K%d" % slot])
                    if keep:
                        for p in range(3):
                            S.tr(ps[2][:, p * 128:(p + 1) * 128], krot[:, p, :], ident, ["krot", "cf"], ["ps2"])
                        S.cp("dve", stg[1][:, :], ps[2][:, 0:384], ["ps2"], ["stg1"])
                        if samp:
                            S.dma("pool", o_swk[l, seq, WIN - 4:WIN, :], stg[1][PADN:128, :], "stk", reads=["stg1"])
                        else:
                            r0 = (ti - (NT - NKEEP)) * 128
                            S.dma("pool", o_pwk[l, r0:r0 + 128, :], stg[1][:, :], "stk", reads=["stg1"])
                    if samp:
                        for dl in range(NBLK):
                            sl_ = 16 - dl
                            r_lo = WIN - PADN - 128 * dl
                            lo = max(0, -r_lo)
                            hi = 128 if dl > 0 else PADN
                            sk = stg[dl % 2]
                            skn = "stg%d" % (dl % 2)
                            if lo > 0:
                                S.memset("pool", sk[0:lo, :], 0.0, [skn])
                            S.dma("sp", sk[lo:hi, :], ck[l, seq, r_lo + lo:r_lo + hi, :], "ldc%d" % (dl % 2), writes=[skn])
                            bnk = 2 + dl % 2
                            for p in range(3):
                                S.tr(ps[bnk][:, p * 128:(p + 1) * 128], sk[:, p * 128:(p + 1) * 128], ident, [skn, "cf"], ["ps%d" % bnk])
                            src = ps[bnk][:, 0:384].rearrange("p (a b) -> p a b", b=128)
                            if dl == 0:
                                S.cp("act", KTr[:, :, sl_, 0:PADN], src[:, :, 0:PADN], ["ps%d" % bnk], ["K%d" % sl_])
                            else:
                                S.cp("act", KTr[:, :, sl_, :], src, ["ps%d" % bnk], ["K%d" % sl_])
                            vkk = "V%d" % sl_
                            vs_ = vstg[dl % 2]
                            vsn = "vstg%d" % (dl % 2)
                            if lo > 0:
                                S.memset("pool", vs_[0:lo, :], 0.0, [vsn])
                            S.dma("sp", vs_[lo:hi, :], cv[l, seq, r_lo + lo:r_lo + hi, :], "ldcv%d" % (dl % 2), writes=[vsn])
                            S.cp("act", Vr[0:hi, sl_, :, 0:64], vs_[0:hi, :].rearrange("r (h e) -> r h e", e=64), [vsn], [vkk])
                            if dl > 0:
                                S.memset("pool", Vr[:, sl_, :, 64:65], 1.0, [vkk + "o"])
                    def rwkv_core():
                        su_ui = cf[:, CI["su"]:CI["su"] + 2, :]
                        for grp in range(3):
                            heads = [grp * 2 + k for k in range(2)]
                            for k, h in enumerate(heads):
                                p, hb = h // 2, 64 * (h % 2)
                                bA, bB = ps[2 * k], ps[2 * k + 1]
                                kA, kB = "ps%d" % (2 * k), "ps%d" % (2 * k + 1)
                                KKR = TTt[hb:hb + 64, p, 0:2, :]
                                S.mm(bA[:, 0:256], TTt[hb:hb + 64, p, 3, :], KKR, True, True, ["TT0", "TT1", "TT3"], [kA + "a"])
                                S.mm(bA[:, 256:512], TTt[hb:hb + 64, p, 2, :], KKR, True, True, ["TT0", "TT1", "TT2"], [kA + "b"])
                                S.mm(bB[:, 0:128], TTt[hb:hb + 64, p, 0, :], TTt[hb:hb + 64, p, 3, :], True, True, ["TT0", "TT3"], [kB + "a"])
                            yield
                            for k, h in enumerate(heads):
                                bA, bB = ps[2 * k], ps[2 * k + 1]
                                kA, kB = "ps%d" % (2 * k), "ps%d" % (2 * k + 1)
                                P0, Q0, T0 = chP[k][0], chQ[k][0], chT[k][0]
                                S.tt("dve", P0[:, :], bA[:, 0:128], cf[:, CI["su"], :], ALU.mult, [kA + "a", "cf"], [n(P0)])
                                S.tt("dve", m4t[k][:, :], bA[:, 128:256], cf[:, CI["nui"], :], ALU.mult, [kA + "a", "cf"], [n(m4t[k])])
                                S.tt("dve", lm3[k][:, :].rearrange("p (a b) -> p a b", b=128), bA[:, 256:512].rearrange("p (a b) -> p a b", b=128),
                                     su_ui, ALU.mult, [kA + "b", "cf"], [n(lm3[k])])
                                S.tt("dve", Q0[:, :], bB[:, 0:128], cf[:, CI["sl"], :], ALU.mult, [kB + "a", "cf"], [n(Q0)])
                                S.tt("dve", T0[:, :], identb[:, :], P0[:, :], ALU.subtract, ["identb", n(P0)], [n(T0)])
                            yield
                            cur = [0, 0, 0]
                            for lev in range(1, 6):
                                st = []
                                for k, h in enumerate(heads):
                                    c0 = cur[k]
                                    st.append((k, ps[2 * k], ps[2 * k + 1], "ps%d" % (2 * k), "ps%d" % (2 * k + 1),
                                               chP[k][c0], chQ[k][c0], chT[k][c0], chP[k][1 - c0], chQ[k][1 - c0], chT[k][1 - c0]))
                                    cur[k] = 1 - c0
                                for (k, bA, bB, kA, kB, Pc, Qc, Tc, Pn, Qn, Tn) in st:
                                    S.mm(bB[:, 256:384], Pc[:, :], Qc[:, :], True, True, [n(Pc), n(Qc)], [kB + "c"])
                                    if lev < 5:
                                        S.mm(bB[:, 128:256], Qc[:, :], Pc[:, :], True, True, [n(Pc), n(Qc)], [kB + "b"])
                                yield
                                for (k, bA, bB, kA, kB, Pc, Qc, Tc, Pn, Qn, Tn) in st:
                                    S.cp("dve", Qn[:, :], bB[:, 256:384], [kB + "c"], [n(Qn)])
                                    if lev < 5:
                                        S.cp("dve", Pn[:, :], bB[:, 128:256], [kB + "b"], [n(Pn)])
                                for (k, bA, bB, kA, kB, Pc, Qc, Tc, Pn, Qn, Tn) in st:
                                    S.mm(bA[:, 0:128], Qn[:, :], Tc[:, :], True, True, [n(Qn), n(Tc)], [kA + "d"])
                                yield
                                for (k, bA, bB, kA, kB, Pc, Qc, Tc, Pn, Qn, Tn) in st:
                                    if lev < 5:
                                        S.tt("dve", Tn[:, :], Tc[:, :], bA[:, 0:128], ALU.add, [n(Tc), kA + "d"], [n(Tn)])
                                    else:
                                        S.tt("dve", invT[k][:, :], Tc[:, :], bA[:, 0:128], ALU.add, [n(Tc), kA + "d"], [n(invT[k])])
                            for cidx in range(2):
                                pb = 64 * cidx
                                tk = slice(pb, pb + 64)
                                hd = []
                                for k, h in enumerate(heads):
                                    p, hb = h // 2, 64 * (h % 2)
                                    hd.append((k, h, p, slice(hb, hb + 64), ps[2 * k], ps[2 * k + 1], "ps%d" % (2 * k), "ps%d" % (2 * k + 1),
                                               "UbP%d" % h, vtok[:, h * 64:(h + 1) * 64], vtok[tk, h * 64:(h + 1) * 64]))
                                for (k, h, p, hs, bA, bB, kA, kB, ukey, vall, vh) in hd:
                                    S.mm(bA[tk, 0:64], TTt[:, p, 0, tk], UbP[h][:, :], True, False, ["TT0", ukey], [kA + "r"])
                                    S.mm(bA[tk, 0:64], lm3[k][:, pb:pb + 64], vall, False, True, [n(lm3[k]), "vtok"], [kA + "r"])
                                yield
                                for (k, h, p, hs, bA, bB, kA, kB, ukey, vall, vh) in hd:
                                    S.cp("dve", rhsb[k][tk, :], bA[tk, 0:64], [kA + "r"], [n(rhsb[k])])
                                for (k, h, p, hs, bA, bB, kA, kB, ukey, vall, vh) in hd:
                                    S.mm(bB[tk, 0:64], invT[k][tk, pb:pb + 64], rhsb[k][tk, :], True, True, [n(invT[k]), n(rhsb[k])], [kB + "u"])
                                yield
                                for (k, h, p, hs, bA, bB, kA, kB, ukey, vall, vh) in hd:
                                    S.cp("dve", umb[k][tk, :], bB[tk, 0:64], [kB + "u"], [n(umb[k])])
                                for (k, h, p, hs, bA, bB, kA, kB, ukey, vall, vh) in hd:
                                    oreg = ps[7][hs, p * 128 + pb:p * 128 + pb + 64]
                                    S.mm(oreg, UbP[h][:, :], TTt[:, p, 1, tk], True, False, [ukey, "TT1"], ["ps7o"])
                                    S.mm(oreg, vall, lm3[k][:, 128 + pb:128 + pb + 64], False, False, ["vtok", n(lm3[k])], ["ps7o"])
                                    S.mm(oreg, umb[k][:, :], m4t[k][:, pb:pb + 64], False, True, [n(umb[k]), n(m4t[k])], ["ps7o"])
                                    S.mm(bA[hs, 64:128], ktok[tk, h * 64:(h + 1) * 64], vh, True, False, ["ktok", "vtok"], [kA + "s"])
                                    S.mm(bA[hs, 64:128], nbtok[tk, h * 64:(h + 1) * 64], umb[k][tk, :], False, True, ["nbtok", n(umb[k])], [kA + "s"])
                                yield
                                for (k, h, p, hs, bA, bB, kA, kB, ukey, vall, vh) in hd:
                                    S.tt("dve", utmp[k][hs, :], U[hs, p, :], bA[hs, 64:128], ALU.add, ["U", kA + "s"], [n(utmp[k])])
                                    wcol = Wc[hs, p, pb + 63:pb + 64]
                                    S.ts("dve", U[hs, p, :], utmp[k][hs, :], wcol, None, ALU.mult, None, [n(utmp[k]), n(Wc)], ["U"])
                                    S.act(UbP[h][hs, :], utmp[k][hs, :], AF.Copy, [n(utmp[k]), n(Wc)], [ukey], scale=wcol)
                                yield
                    def attn_core():
                        unit = [0]
                        nb = NBLK if samp else min(NBLK, ti + 1)
                        units = []
                        for h in range(6):
                            for g0 in range(0, nb, 4):
                                units.append((h, g0, min(4, nb - g0)))
                        pend = None

                        def emit_pv(u):
                            (h, g0, gn_, pt) = u
                            for j in range(gn_):
                                dl = g0 + j
                                sl_ = (16 - dl) if samp else ((ti - dl) % RING)
                                S.mm(ps[6][:, h * 65:(h + 1) * 65], pt[:, j * 128:(j + 1) * 128], Vr[:, sl_, h, 0:65], dl == 0, dl == nb - 1,
                                     [n(pt), "V%d" % sl_, "V%do" % sl_], ["ps6"])
                        for ui, (h, g0, gn_) in enumerate(units):
                            p, hb = h // 2, 64 * (h % 2)
                            hs = slice(hb, hb + 64)
                            bnk = 4 + (ui % 2)
                            bk = "ps%d" % bnk
                            pt = pT[ui % 3]
                            for j in range(gn_):
                                dl = g0 + j
                                sl_ = (16 - dl) if samp else ((ti - dl) % RING)
                                S.mm(ps[bnk][:, j * 128:(j + 1) * 128], KTr[hs, p, sl_, :], qTb[hs, p, :], True, True,
                                     ["K%d" % sl_, "qTb"], [bk])
                            if pend is not None:
                                emit_pv(pend)
                            S.act(pt[:, 0:gn_ * 128], ps[bnk][:, 0:gn_ * 128], AF.Exp, [bk], [n(pt)], scale=HD ** -0.5)
                            pt3 = pt[:, 0:gn_ * 128].rearrange("p (a b) -> p a b", b=128)
                            S.tt("dve", pt3, pt3, amask[:, g0:g0 + gn_, :], ALU.mult, [n(pt), "amask"], [n(pt)])
                            pend = (h, g0, gn_, pt)
                            yield
                        emit_pv(pend)
                        O3 = ps[6][:, 0:390].rearrange("p (h e) -> p h e", e=65)
                        S.op("dve", lambda e: e.reciprocal(out=rl[:, :], in_=O3[:, :, 64]), ["ps6"], ["rl"])
                        S.tt("dve", ob[:, :].rearrange("p (h e) -> p h e", e=64), O3[:, :, 0:64], rl[:, :].unsqueeze(2).to_broadcast([128, 6, 64]),
                             ALU.mult, ["ps6", "rl"], ["ob"])
                        for p in range(3):
                            S.tr(ps[4][:, p * 128:(p + 1) * 128], ob[:, p * 128:(p + 1) * 128], ident, ["ob", "cf"], ["ps4"])
                        S.cp("act", mixT[k2][:, 3:6, :], ps[4][:, 0:384].rearrange("p (a b) -> p a b", b=128), ["ps4"], [mk2 + "b"])
                    if _STOP <= 5:
                        return
                    ga_, gb_ = rwkv_core(), attn_core()
                    a_live, b_live = True, True
                    while a_live or b_live:
                        if a_live:
                            try:
                                next(ga_)
                            except StopIteration:
                                a_live = False
                        for _ in range(4):
                            if b_live:
                                try:
                                    next(gb_)
                                except StopIteration:
                                    b_live = False
                    oS, sq = lw, kkn
                    S.cp("act", oS[:, :, :], v3(7), ["ps7o"], [n(oS)])
                    S.act(sqb[:, :, :], oS[:, :, :], AF.Square, [n(oS)], ["sqb"])
                    S.cp("dve", osb[:, :, :], oS[:, :, :], [n(oS)], ["osb"])
                    for p in range(3):
                        S.mm(ps[0][:, p * 128:(p + 1) * 128], cb16[:, 0, :], osb[:, p, :], True, True, ["cb16", "osb"], ["ps0"])
                        S.mm(ps[1][:, p * 128:(p + 1) * 128], cb16[:, 0, :], sqb[:, p, :], True, True, ["cb16", "sqb"], ["ps1"])
                    mean, var = kp, bb
                    S.ts("dve", mean[:, :, :], v3(0), 1.0 / 64, None, ALU.mult, None, ["ps0"], [n(mean)])
                    S.act(var[:, :, :], mean[:, :, :], AF.Square, [n(mean)], [n(var)])
                    S.stt(var[:, :, :], v3(1), 1.0 / 64, var[:, :, :], ALU.mult, ALU.subtract, ["ps1", n(var)], [n(var)])
                    S.ts("dve", var[:, :, :], var[:, :, :], GN_EPS, None, ALU.add, None, [n(var)], [n(var)])
                    S.act(var[:, :, :], var[:, :, :], AF.Ln, [n(var)], [n(var)])
                    S.act(var[:, :, :], var[:, :, :], AF.Exp, [n(var)], [n(var)], scale=-0.5)
                    S.tt("dve", oS[:, :, :], oS[:, :, :], mean[:, :, :], ALU.subtract, [n(oS), n(mean)], [n(oS)])
                    S.tt("dve", oS[:, :, :], oS[:, :, :], var[:, :, :], ALU.mult, [n(oS), n(var)], [n(oS)])
                    S.tt("dve", oS[:, :, :], oS[:, :, :], bc(5), ALU.mult, [n(oS), "pr"], [n(oS)])
                    S.tt("dve", oS[:, :, :], oS[:, :, :], bc(6), ALU.add, [n(oS), "pr"], [n(oS)])
                    S.tt("dve", oS[:, :, :], oS[:, :, :], bon[:, :, :], ALU.add, [n(oS), n(cum)], [n(oS)])
                    S.tt("dve", mixT[k2][:, 0:3, :], oS[:, :, :], gT[:, :, :], ALU.mult, [n(oS), n(gT)], [mk2 + "a"])
                    if _STOP <= 11:
                        return
                    qc, kc_, gc = H[:, CH_CQ:CH_CQ + 2, :], H[:, CH_CK:CH_CK + 2, :], H[:, CH_CG:CH_CG + 2, :]
                    ck_ = ["H4", "H5"]
                    S.cp("act", hb16[:, 0:4, :], H[:, CH_CQ:CH_CQ + 4, :], ck_, ["hb16"])
                    for c in range(4):
                        S.mm(ps[0][:, c * 128:(c + 1) * 128], cb16[:, 2, :], hb16[:, c, :], True, True, ["cb16", "hb16"], ["ps0"])
                    b4 = lambda a: a.unsqueeze(1).to_broadcast([128, 4, 128])
                    qkc = H[:, CH_CQ:CH_CQ + 4, :]
                    rt4 = t3[3]
                    r4 = S_r4
                    S.tt("dve", qkc, qkc, b4(cosC), ALU.mult, ck_ + [rk_], ck_)
                    S.tt("dve", r4[:, :, :], ps[0][:, :].rearrange("p (a b) -> p a b", b=128), b4(sinC), ALU.mult, ["ps0", rk_], ["r4"])
                    S.tt("dve", qkc, qkc, r4[:, :, :], ALU.add, ck_ + ["r4"], ck_)
                    if samp:
                        S.memset("dve", H[:, CH_CK:CH_CK + 2, 0:PADN], 0.0, ck_)
                    qd_t = cf[:, 12:15:2, :]
                    kd_t = cf[:, 13:16:2, :]
                    S.cp("act", qcb[:, :, :], qc, ck_, ["qcb"])
                    S.act(kcb[:, :, :], kc_, AF.Copy, ck_, ["kcb"], scale=HD ** -0.5)
                    S.tt("dve", qdb[:, :, :], qc, qd_t, ALU.mult, ck_ + ["cf"], ["qdb"])
                    S.tt("dve", kdb[:, :, :], kc_, kd_t, ALU.mult, ck_ + ["cf"], ["kdb"])
                    for p in range(2):
                        S.tr(psb[1][:, p * 128:(p + 1) * 128], kdb[:, p, :], identb[:, :], ["kdb", "identb"], ["ps1"])
                    S.cp("dve", kdtok[:, :], psb[1][:, 0:256], ["ps1"], ["kdtok"])
                    for h in range(4):
                        p, hb = h // 2, 64 * (h % 2)
                        hs = slice(hb, hb + 64)
                        bnk = 2 + h % 2
                        bk = "ps%d" % bnk
                        S.mm(ps[bnk][:, 0:128], kcb[hs, p, :], qcb[hs, p, :], True, True, ["kcb", "qcb"], [bk + "a"])
                        ab = attb[h % 2]
                        S.tt("dve", ab[:, :], ps[bnk][:, 0:128], cf[:, CI["dmask"] + h, :], ALU.mult, [bk + "a", "cf"], [n(ab)])
                        oreg = ps[4][hs, p * 128:(p + 1) * 128]
                        S.mm(oreg, vcb[:, h * 64:(h + 1) * 64], ab[:, :], True, False, ["vcb", n(ab)], ["ps4"])
                        S.mm(oreg, RbP[h][:, :], qdb[:, p, :], False, True, ["RbP%d" % h, "qdb"], ["ps4"])
                        S.mm(ps[bnk][hs, 128:192], kdtok[:, h * 64:(h + 1) * 64], vcb[:, h * 64:(h + 1) * 64], True, True,
                             ["kdtok", "vcb"], [bk + "s"])
                        S.stt(R[hs, p, :], R[hs, p, :], GAMMA[h] ** 128.0, ps[bnk][hs, 128:192], ALU.mult, ALU.add, ["R", bk + "s"], ["R"])
                        S.cp("act", RbP[h][hs, :], R[hs, p, :], ["R"], ["RbP%d" % h])
                    oc, sq2, mn2, vr2 = c2
                    v2 = lambda b: ps[b][:, 0:256].rearrange("p (a b) -> p a b", b=128)
                    S.cp("act", oc[:, :, :], v2(4), ["ps4"], [n(oc)])
                    S.act(sqb[:, 0:2, :], oc[:, :, :], AF.Square, [n(oc)], ["sqb"])
                    S.cp("dve", osb[:, 0:2, :], oc[:, :, :], [n(oc)], ["osb"])
                    for p in range(2):
                        S.mm(ps[0][:, p * 128:(p + 1) * 128], cb16[:, 0, :], osb[:, p, :], True, True, ["cb16", "osb"], ["ps0"])
                        S.mm(ps[1][:, p * 128:(p + 1) * 128], cb16[:, 0, :], sqb[:, p, :], True, True, ["cb16", "sqb"], ["ps1"])
                    S.ts("dve", mn2[:, :, :], v2(0), 1.0 / 64, None, ALU.mult, None, ["ps0"], [n(mn2)])
                    S.act(vr2[:, :, :], mn2[:, :, :], AF.Square, [n(mn2)], [n(vr2)])
                    S.stt(vr2[:, :, :], v2(1), 1.0 / 64, vr2[:, :, :], ALU.mult, ALU.subtract, ["ps1", n(vr2)], [n(vr2)])
                    S.ts("dve", vr2[:, :, :], vr2[:, :, :], LN_EPS, None, ALU.add, None, [n(vr2)], [n(vr2)])
                    S.act(vr2[:, :, :], vr2[:, :, :], AF.Ln, [n(vr2)], [n(vr2)])
                    S.act(vr2[:, :, :], vr2[:, :, :], AF.Exp, [n(vr2)], [n(vr2)], scale=-0.5)
                    S.tt("dve", oc[:, :, :], oc[:, :, :], mn2[:, :, :], ALU.subtract, [n(oc), n(mn2)], [n(oc)])
                    S.tt("dve", oc[:, :, :], oc[:, :, :], vr2[:, :, :], ALU.mult, [n(oc), n(vr2)], [n(oc)])
                    bcc = lambda k: prc[:, k, :].unsqueeze(2).to_broadcast([128, 2, 128])
                    S.tt("dve", oc[:, :, :], oc[:, :, :], bcc(0), ALU.mult, [n(oc), "prc"], [n(oc)])
                    S.tt("dve", oc[:, :, :], oc[:, :, :], bcc(1), ALU.add, [n(oc), "prc"], [n(oc)])
                    S.act(sq2[:, :, :], gc, AF.Exp, ["H5"], [n(sq2)], scale=-1.0)
                    S.ts("dve", sq2[:, :, :], sq2[:, :, :], 1.0, None, ALU.add, None, [n(sq2)], [n(sq2)])
                    S.op("dve", lambda e: e.reciprocal(out=sq2[:, :, :], in_=sq2[:, :, :]), [n(sq2)], [n(sq2)])
                    S.tt("dve", sq2[:, :, :], sq2[:, :, :], gc, ALU.mult, [n(sq2), "H5"], [n(sq2)])
                    S.tt("dve", mixT[k2][:, 6:8, :], oc[:, :, :], sq2[:, :, :], ALU.mult, [n(oc), n(sq2)], [mk2 + "c"])
                    if _STOP <= 12:
                        return
                    S.dma("pool", mixs[:, :, gi * 128:(gi + 1) * 128].rearrange("k p t -> p k t"), mixT[k2][:, :, :], "stm%d" % k2,
                          reads=[mk2 + "a", mk2 + "b", mk2 + "c"])
                    last = samp or ti == NT - 1
                    if last:
                        for p in range(3):
                            S.tr(ps[6][0:64, p * 128:(p + 1) * 128], U[:, p, :], ident, ["U", "cf"], ["ps6"])
                        S.cp("dve", cstage[0:64, :], ps[6][0:64, 0:384], ["ps6"], ["cstage"])
                        dst = o_swkv[l, seq] if samp else o_pwkv[l]
                        S.dma("pool", dst.rearrange("h v k -> v h k"), cstage[0:64, :].rearrange("v (h k) -> v h k", k=64), "sts", reads=["cstage"])
                        for p in range(2):
                            for hh in range(2):
                                dst = o_sret[l, seq, 2 * p + hh] if samp else o_pret[l, 2 * p + hh]
                                S.dma("pool", dst, R[hh * 64:(hh + 1) * 64, p, :], "sts", reads=["R"])

                for k_ in range(3):
                    S.memset("pool", rhsb[k_][:, :], 0.0, [S.kn(rhsb[k_])])
                    S.memset("pool", umb[k_][:, :], 0.0, [S.kn(umb[k_])])
                onesT = S.sb("onesT", [128, 128])
                S.memset("dve", onesT[:, :], 1.0, ["onesT"])
                S_r4 = S.sb("r4", [128, 4, 128])

                for ti in range(NT):
                    process_tile("p", ti, NT, ti)
                for sb_ in range(NSB):
                    if not _NOSAMP:
                        process_tile(sb_, 0, 1, NT + sb_)
                while bg_jobs:
                    d_, s_src = bg_jobs.pop(0)
                    S.dma("pool", d_, s_src, "bgc")
                S.barrier()
            with ExitStack() as es2:
                S.es = es2
                wo = S.sb("wo", [128, 8, D], BF16)
                wg = S.sb("wg", [128, 8, DFF], BF16)
                wu = S.sb("wu", [128, 8, DFF], BF16)
                wd = S.sb("wd", [128, NFC, D], BF16)
                lnp = S.sb("lnp", [128, 4, D])
                identf = S.sb("identf", [128, 128])
                mh1 = S.sb("mh1", [128, 1])
                xt2 = [S.sb("x2t%d" % k, [128, D]) for k in range(2)]
                mT2 = [S.sb("mT2_%d" % k, [128, 8, 128], BF16) for k in range(2)]
                pre = S.sb("pre", [128, D])
                x1 = S.sb("x1", [128, D])
                x1T = S.sb("x1T", [128, 8, 128], BF16)
                aT = S.sb("aT", [128, NFC, 128], BF16)
                sg = [S.sb("sg%d" % k, [128, 512]) for k in range(2)]
                outt = [S.sb("outt%d" % k, [128, D]) for k in range(1)]
                atok = S.sb("atok", [128, DFF], BF16)
                identb2 = S.sb("identb2", [128, 128], BF16)
                st6 = S.sb("st6", [128, 12])
                mv = S.sb("mv", [128, 4])

                S.dma("sp", wo[:, :, :], wob[l].rearrange("(k p) d -> p k d", p=128), "w2", writes=["wo"])
                for kc in range(0, 8, 2):
                    S.dma("sp", wg[:, kc:kc + 2, :], wgb[l, kc * 128:(kc + 2) * 128, :].rearrange("(k p) d -> p k d", p=128), "w2", writes=["wg"])
                    S.dma("act", wu[:, kc:kc + 2, :], wub[l, kc * 128:(kc + 2) * 128, :].rearrange("(k p) d -> p k d", p=128), "w2", writes=["wu"])
                S.dma("sp", wd[:, 0:11, :], wdb[l, 0:11 * 128, :].rearrange("(k p) d -> p k d", p=128), "w2", writes=["wd"])
                S.dma("act", wd[:, 11:22, :], wdb[l, 11 * 128:22 * 128, :].rearrange("(k p) d -> p k d", p=128), "w2", writes=["wd"])
                if l + 1 < DEPTH:
                    for kc in range(8):
                        bg_jobs.append((winb[kc * 128:(kc + 1) * 128, :], w_in[l + 1, kc * 128:(kc + 1) * 128, :]))
                for k, prm in enumerate([ln1_g, ln1_b, ln2_g, ln2_b]):
                    S.dma("sp", lnp[:, k, :], prm[l:l + 1, :].partition_broadcast(128), "c2", writes=["lnp"])
                S.dma("sp", identf[:, :], c_f[:, 0, :], "c2", writes=["identf"])
                S.memset("dve", mh1[:, :], -0.5, ["mh1"])
                S.cp("dve", identb2[:, :], identf[:, :], ["identf"], ["identb2"])

                def layer_norm(src, dst, gk, bk_, eps):
                    for c in range(2):
                        S.op("dve", lambda e, c=c: e.bn_stats(out=st6[:, c * 6:(c + 1) * 6], in_=src[:, c * 512:(c + 1) * 512]),
                             [S.kn(src)], ["st6"])
                    S.op("dve", lambda e: e.bn_aggr(out=mv[:, 0:2], in_=st6[:, 0:12]), ["st6"], ["mv"])
                    S.ts("dve", mv[:, 2:3], mv[:, 1:2], eps, None, ALU.add, None, ["mv"], ["mv"])
                    S.act(mv[:, 3:4], mv[:, 2:3], AF.Ln, ["mv"], ["mv"])
                    S.act(mv[:, 3:4], mv[:, 3:4], AF.Exp, ["mv"], ["mv"], scale=-0.5)
                    S.ts("dve", dst[:, :], src[:, :], mv[:, 0:1], mv[:, 3:4], ALU.subtract, ALU.mult, [S.kn(src), "mv"], [S.kn(dst)])
                    S.tt("dve", dst[:, :], dst[:, :], lnp[:, gk, :], ALU.mult, [S.kn(dst), "lnp"], [S.kn(dst)])
                    S.tt("dve", dst[:, :], dst[:, :], lnp[:, bk_, :], ALU.add, [S.kn(dst), "lnp"], [S.kn(dst)])

                for gi in range(NTS if not _NOP2 else 0):
                    k2 = gi % 2
                    samp = gi >= NT
                    xk = "x2t%d" % k2
                    mk = "mT2_%d" % k2
                    if bg_jobs:
                        d_, s_src = bg_jobs.pop(0)
                        S.dma("pool", d_, s_src, "bgc")
                    if samp and l == 0:
                        S.memset("pool", xt2[k2][:, :], 0.0, [xk])
                        S.dma("sp", xt2[k2][PADN:128, :], xs[(gi - NT) * 4:(gi - NT + 1) * 4, :], "l2x%d" % k2, writes=[xk])
                    else:
                        S.dma("sp", xt2[k2][:, :], x_src(l, gi), "l2x%d" % k2, writes=[xk])
                    S.dma("sp", mT2[k2][:, :, :], mixs[:, :, gi * 128:(gi + 1) * 128].rearrange("k p t -> p k t"), "l2x%d" % k2, writes=[mk])
                    for half in range(2):
                        for kc in range(8):
                            S.mm(ps[half][:, :], mT2[k2][:, kc, :], wo[:, kc, half * 512:(half + 1) * 512], kc == 0, kc == 7,
                                 [mk, "wo"], ["ps%d" % half])
                        S.stt(pre[:, half * 512:(half + 1) * 512], xt2[k2][:, half * 512:(half + 1) * 512], ALPHA, ps[half][:, :],
                              ALU.mult, ALU.add, [xk, "ps%d" % half], ["pre"])
                    layer_norm(pre, x1, 0, 1, LN_EPS)
                    for half in range(2):
                        for j in range(4):
                            kc = half * 4 + j
                            S.tr(ps[2 + half][:, j * 128:(j + 1) * 128], x1[:, kc * 128:(kc + 1) * 128], identf[:, :], ["x1", "identf"],
                                 ["ps%d" % (2 + half)])
                        S.cp("act", x1T[:, half * 4:(half + 1) * 4, :], ps[2 + half][:, :].rearrange("p (a b) -> p a b", b=128),
                             ["ps%d" % (2 + half)], ["x1T"])
                    NG = (DFF + 511) // 512
                    for g_ in range(NG):
                        c0 = g_ * 512
                        cw = min(512, DFF - c0)
                        bg, bu = 4 + (g_ % 2) * 2, 5 + (g_ % 2) * 2
                        for kc in range(8):
                            S.mm(ps[bg][:, 0:cw], x1T[:, kc, :], wg[:, kc, c0:c0 + cw], kc == 0, kc == 7, ["wg", "x1T"], ["ps%d" % bg])
                        for kc in range(8):
                            S.mm(ps[bu][:, 0:cw], x1T[:, kc, :], wu[:, kc, c0:c0 + cw], kc == 0, kc == 7, ["wu", "x1T"], ["ps%d" % bu])
                        s_ = sg[g_ % 2]
                        S.act(s_[:, 0:cw], ps[bg][:, 0:cw], AF.Silu, ["ps%d" % bg], [S.kn(s_)])
                        S.tt("dve", atok[:, c0:c0 + cw], s_[:, 0:cw], ps[bu][:, 0:cw], ALU.mult, [S.kn(s_), "ps%d" % bu], ["atok%d" % g_])
                        nch = cw // 128
                        bt = 2 + g_ % 2
                        for j in range(nch):
                            S.tr(psb[bt][:, j * 128:(j + 1) * 128], atok[:, c0 + j * 128:c0 + (j + 1) * 128], identb2[:, :],
                                 ["atok%d" % g_, "identb2"], ["ps%d" % bt])
                        S.cp("act", aT[:, g_ * 4:g_ * 4 + nch, :], psb[bt][:, 0:nch * 128].rearrange("p (a b) -> p a b", b=128),
                             ["ps%d" % bt], ["aT"])
                    for half in range(2):
                        for c in range(NFC):
                            S.mm(ps[half][:, :], aT[:, c, :], wd[:, c, half * 512:(half + 1) * 512], c == 0, c == NFC - 1,
                                 ["aT", "wd"], ["ps%d" % half])
                        S.stt(pre[:, half * 512:(half + 1) * 512], x1[:, half * 512:(half + 1) * 512], ALPHA, ps[half][:, :],
                              ALU.mult, ALU.add, ["x1", "ps%d" % half], ["pre"])
                    ot = outt[0]
                    layer_norm(pre, ot, 2, 3, LN_EPS)
                    if l < DEPTH - 1:
                        S.dma("pool", x2s[gi * 128:(gi + 1) * 128, :], ot[:, :], "st2_%d" % k2, reads=[S.kn(ot)])
                    elif samp:
                        sb_ = gi - NT
                        S.dma("pool", y_s[sb_ * 4:(sb_ + 1) * 4, :], ot[PADN:128, :], "st2_%d" % k2, reads=[S.kn(ot)])
                    else:
                        S.dma("pool", y_p[gi * 128:(gi + 1) * 128, :], ot[:, :], "st2_%d" % k2, reads=[S.kn(ot)])
                while bg_jobs:
                    d_, s_src = bg_jobs.pop(0)
                    S.dma("pool", d_, s_src, "bgc")
                S.barrier()
        S.es = es_top
    return nc


_W_NAMES = ["w_in", "rwkv_mu", "rwkv_w0", "rwkv_w_lora", "rwkv_a0", "rwkv_a_lora", "rwkv_g_lora", "rwkv_k_k", "rwkv_k_a",
            "rwkv_r_k", "rwkv_gn_g", "rwkv_gn_b", "ret_gn_g", "ret_gn_b", "w_out", "ln1_g", "ln1_b", "w_ffn_gate", "w_ffn_up",
            "w_ffn_down", "ln2_g", "ln2_b"]


def run(inputs, T, n_cores=8, trace=False):
    f = lambda a: np.ascontiguousarray(np.asarray(a, dtype=np.float32))
    x_prompt = f(inputs["x_prompt"])
    x_sample = f(inputs["x_sample"])
    nb = x_prompt.shape[0]
    consts = make_consts(T)
    shared = {k: f(inputs[k]) for k in _W_NAMES}
    shared["rwkv_r_k"] = shared["rwkv_r_k"].reshape(DEPTH, W_A)
    shared["cf"] = consts["cf"]
    shared["amask"] = consts["amask"]
    shared["rot"] = consts["rot"]
    st_shift, st_wkv = f(inputs["state_rwkv_shift"]), f(inputs["state_rwkv_wkv"])
    ckk, cvv, st_ret = f(inputs["cache_win_k"]), f(inputs["cache_win_v"]), f(inputs["state_ret"])
    in_maps = []
    for c in range(n_cores):
        b = (c * nb) // n_cores
        s0 = c * NSB
        m = dict(shared)
        m["xp"] = x_prompt[b]
        m["xs"] = np.ascontiguousarray(x_sample[s0:s0 + NSB].reshape(NSB * 4, D))
        m["st_shift"] = np.ascontiguousarray(st_shift[:, s0:s0 + NSB])
        m["st_wkv"] = np.ascontiguousarray(st_wkv[:, s0:s0 + NSB])
        m["ck"] = np.ascontiguousarray(ckk[:, s0:s0 + NSB].reshape(DEPTH, NSB, WIN, W_B))
        m["cv"] = np.ascontiguousarray(cvv[:, s0:s0 + NSB].reshape(DEPTH, NSB, WIN, W_B))
        m["st_ret"] = np.ascontiguousarray(st_ret[:, s0:s0 + NSB])
        in_maps.append(m)
    nc = build(T)
    res = run_bass_kernel_spmd(nc, in_maps, core_ids=list(range(n_cores)), trace=trace)
    R = res.results
    per = n_cores // nb
    WK = min(WIN, T)
    own = [R[b * per] for b in range(nb)]
    y_prompt = np.stack([o["y_p"] for o in own])
    y_sample = np.concatenate([r["y_s"].reshape(NSB, 4, D) for r in R], axis=0)
    p_shift = np.stack([o["p_shift"] for o in own], axis=1)
    p_wkv = np.stack([o["p_wkv"] for o in own], axis=1)
    p_wk = np.stack([o["p_wk"].reshape(DEPTH, WK, H_B, HD) for o in own], axis=1)
    p_wv = np.stack([o["p_wv"].reshape(DEPTH, WK, H_B, HD) for o in own], axis=1)
    p_ret = np.stack([o["p_ret"] for o in own], axis=1)
    s_shift = np.concatenate([r["s_shift"] for r in R], axis=1)
    s_wkv = np.concatenate([r["s_wkv"] for r in R], axis=1)
    s_wk = np.concatenate([r["s_wk"].reshape(DEPTH, NSB, WIN, H_B, HD) for r in R], axis=1)
    s_wv = np.concatenate([r["s_wv"].reshape(DEPTH, NSB, WIN, H_B, HD) for r in R], axis=1)
    s_ret = np.concatenate([r["s_ret"] for r in R], axis=1)
    outs = (y_prompt, y_sample, p_shift, p_wkv, p_wk, p_wv, p_ret, s_shift, s_wkv, s_wk, s_wv, s_ret)
    return tuple(np.ascontiguousarray(o, dtype=np.float32) for o in outs), res


def kernel(**inputs):
    T = int(np.asarray(inputs["x_prompt"]).shape[1])
    outs, _ = run(inputs, T)
    return outs
```

```python
import math
import os
_STOP = float(os.environ.get('KSTOP', '99'))
_NOP2 = int(os.environ.get('KNOP2', '0'))
_NOSAMP = int(os.environ.get('KNOSAMP', '0'))
from contextlib import ExitStack
import numpy as np
import ml_dtypes
import concourse.bass as bass
import concourse.mybir as mybir
from concourse.bass_utils import run_bass_kernel_spmd

F32 = mybir.dt.float32
BF16 = mybir.dt.bfloat16
AF = mybir.ActivationFunctionType
ALU = mybir.AluOpType
AX = mybir.AxisListType

D = 1024
DFF = 2816
NFC = DFF // 128
HD = 64
H_A, H_B, H_C = 6, 6, 4
W_A, W_B, W_C = 384, 384, 256
A_COLS = 1408
IN_COLS = 3584
WIN = 2048
NBLK = 17
RING = 18
PAST = 8192
DEPTH = 2
ALPHA = (2 * DEPTH) ** 0.25
DECAY = math.exp(-0.5)
GN_EPS = 64e-5
LN_EPS = 1e-5
GAMMA = [1.0 - 2.0 ** (-5.0 - h) for h in range(H_C)]
PADN = 124
NSB = 4
FM_COLS = [(0, 1408), (1408, 1408 + 768), (2560, 2560 + 512), (3328, 3584)]
FM_CHUNK_COL = []
for a_, b_ in FM_COLS:
    for c_ in range(a_, b_, 128):
        FM_CHUNK_COL.append(c_)
NFM = len(FM_CHUNK_COL)
CH_BQ, CH_BK, CH_CQ, CH_CK, CH_CG = 11, 14, 17, 19, 21
BV_COL, CV_COL = 2176, 3072


class Sched:
    ENGS = ("pe", "dve", "act", "pool", "sp")

    def __init__(self, nc, es):
        self.nc = nc
        self.es_sem = es
        self.es = es
        self.eng = {"pe": nc.tensor, "dve": nc.vector, "act": nc.scalar, "pool": nc.gpsimd, "sp": nc.sync}
        self.sem, self.cnt = {}, {}
        for e in self.ENGS:
            self.sem[e] = es.enter_context(nc.semaphore("q_" + e))
            self.cnt[e] = 0
        self.waited = {e: {} for e in self.ENGS}
        self.last_w, self.readers, self.dsem = {}, {}, {}
        self.kids = {}
        self.bank_last = {}
        self.n_ins = 0
        self.uid = 0
        self.keyname = {}
        self.keep = []

    def _rel(self, k):
        if k.startswith("ps") and len(k) >= 3 and k[2].isdigit():
            bank = k[:3]
            if k == bank:
                return [bank] + list(self.kids.get(bank, ()))
            self.kids.setdefault(bank, set()).add(k)
            return [k, bank]
        return [k]

    def sb(self, name, shape, dt=F32):
        self.uid += 1
        t = self.es.enter_context(self.nc.sbuf_tensor("%s_u%d" % (name, self.uid), list(shape), dt))
        self.keyname[id(t)] = name
        self.keep.append(t)
        return t

    def kn(self, t):
        return self.keyname[id(t)]

    def ps(self, name, shape, dt=F32):
        return self.es.enter_context(self.nc.psum_tensor(name, list(shape), dt))

    def _dma_sem(self, key):
        if key not in self.dsem:
            p = "d_" + key
            self.sem[p] = self.es_sem.enter_context(self.nc.semaphore(p))
            self.cnt[p] = 0
            self.dsem[key] = p
        return self.dsem[key]

    def _wait(self, E, p, c):
        if self.waited[E].get(p, 0) >= c:
            return
        self.eng[E].wait_ge(self.sem[p], c)
        self.waited[E][p] = c

    @staticmethod
    def _bank(k):
        if k.startswith("ps") and len(k) >= 3 and k[2].isdigit():
            return k[:3]
        return None

    def _need(self, E, reads, writes):
        need = {}

        def add(p, c):
            if c > need.get(p, 0):
                need[p] = c
        for r in reads:
            b = self._bank(r)
            if b is not None:
                for p, c in self.bank_last.get(b, {}).items():
                    if p != E:
                        add(p, c)
                continue
            lw = self.last_w.get(r)
            if lw is not None:
                add(*lw)
        for w in writes:
            b = self._bank(w)
            if b is not None:
                for p, c in self.bank_last.get(b, {}).items():
                    if p != E:
                        add(p, c)
                continue
            lw = self.last_w.get(w)
            if lw is not None and lw[0] != E:
                add(*lw)
            for p, c in self.readers.get(w, {}).items():
                if p != E:
                    add(p, c)
        for p, c in need.items():
            if p.startswith("d_"):
                c = self.cnt[p]
            self._wait(E, p, c)

    def _commit(self, P, c, reads, writes):
        for r in reads:
            b = self._bank(r)
            if b is not None:
                self.bank_last.setdefault(b, {})[P] = c
                continue
            self.readers.setdefault(r, {})[P] = c
        for w in writes:
            b = self._bank(w)
            if b is not None:
                self.bank_last.setdefault(b, {})[P] = c
                continue
            self.last_w[w] = (P, c)
            self.readers[w] = {}

    def op(self, E, fn, reads=(), writes=()):
        self._need(E, reads, writes)
        ins = fn(self.eng[E])
        self.cnt[E] += 1
        ins.then_inc(self.sem[E], 1)
        self._commit(E, self.cnt[E], reads, writes)
        self.n_ins += 1
        return ins

    def mm(self, out, lhsT, rhs, start, stop, reads, writes):
        return self.op("pe", lambda e: e.matmul(out, lhsT=lhsT, rhs=rhs, start=start, stop=stop), reads, writes)

    def tr(self, out, in_, ident, reads, writes):
        return self.op("pe", lambda e: e.transpose(out=out, in_=in_, identity=ident), reads, writes)

    def dma(self, Q, out, in_, key, reads=(), writes=()):
        p = self._dma_sem(key)
        self._need(Q, reads, writes)
        ins = self.eng[Q].dma_start(out=out, in_=in_)
        self.cnt[p] += 16
        ins.then_inc(self.sem[p], 16)
        self._commit(p, self.cnt[p], reads, writes)
        self.n_ins += 1
        return ins

    def barrier(self):
        for E in self.ENGS:
            for p in list(self.sem.keys()):
                if p != E and self.cnt[p] > 0:
                    self._wait(E, p, self.cnt[p])
        self.last_w, self.readers = {}, {}
        self.bank_last = {}

    def tt(self, E, out, in0, in1, op, r, w):
        return self.op(E, lambda e: e.tensor_tensor(out=out, in0=in0, in1=in1, op=op), r, w)

    def ts(self, E, out, in0, s1, s2, op0, op1, r, w):
        if s2 is None:
            return self.op(E, lambda e: e.tensor_scalar(out=out, in0=in0, scalar1=s1, scalar2=None, op0=op0), r, w)
        return self.op(E, lambda e: e.tensor_scalar(out=out, in0=in0, scalar1=s1, scalar2=s2, op0=op0, op1=op1), r, w)

    def stt(self, out, in0, sc, in1, op0, op1, r, w):
        return self.op("dve", lambda e: e.scalar_tensor_tensor(out=out, in0=in0, scalar=sc, in1=in1, op0=op0, op1=op1), r, w)

    def act(self, out, in_, func, r, w, scale=None, bias=None):
        kw = {}
        if scale is not None:
            kw["scale"] = scale
        if bias is not None:
            kw["bias"] = bias
        return self.op("act", lambda e: e.activation(out=out, in_=in_, func=func, **kw), r, w)

    def cp(self, E, out, in_, r, w):
        if E == "act":
            return self.act(out, in_, AF.Copy, r, w)
        return self.op(E, lambda e: e.tensor_copy(out=out, in_=in_), r, w)

    def memset(self, E, ap, val, w):
        return self.op(E, lambda e: e.memset(ap, val), (), w)


def _mult(d):
    d = np.asarray(d)
    ok = d >= 0
    m = ((d <= 128) & ok).astype(np.float32)
    m += ((d % 4 == 0) & (d <= 512) & ok)
    m += ((d % 16 == 0) & (d <= 2048) & ok)
    return m.astype(np.float32)


def make_consts(T):
    NT = T // 128
    c = {}
    i = np.arange(128)
    same = (i[:, None] // 64) == (i[None, :] // 64)
    su = ((i[:, None] < i[None, :]) & same).astype(np.float32)
    ui = ((i[:, None] <= i[None, :]) & same).astype(np.float32)
    ident = np.eye(128, dtype=np.float32)
    bones = same.astype(np.float32)
    def pm(half, hd=64):
        m = np.zeros((128, 128), np.float32)
        for blk in range(2):
            o = blk * hd
            for e in range(half):
                m[o + e + half, o + e] = -1.0
                m[o + e, o + e + half] = 1.0
        return m
    packs = [ident, su, ui, su.T.copy(), -ui, bones, pm(8), pm(32)]
    names = ["ident", "su", "ui", "sl", "nui", "bones", "pmb", "pmc"]
    for h in range(H_C):
        g = GAMMA[h]
        rel = i[None, :] - i[:, None]
        dm = np.where(rel >= 0, np.exp(np.log(g) * np.maximum(rel, 0)), 0.0).astype(np.float32)
        packs.append(dm)
        names.append("dmask%d" % h)
    for p in range(2):
        qd = np.zeros((128, 128), np.float32)
        kd = np.zeros((128, 128), np.float32)
        for hh in range(2):
            g = GAMMA[2 * p + hh]
            qd[hh * 64:(hh + 1) * 64, :] = np.exp(np.log(g) * (i + 1.0))[None, :]
            kd[hh * 64:(hh + 1) * 64, :] = (np.exp(np.log(g) * (127.0 - i)) * (HD ** -0.5))[None, :]
        packs += [qd, kd]
        names += ["qdec%d" % p, "kdec%d" % p]
    c["cf"] = np.stack(packs, axis=1).astype(np.float32)
    c["cf_names"] = names
    mk = np.zeros((128, NBLK, 128), np.float32)
    for dl in range(NBLK):
        d = dl * 128 + i[None, :] - i[:, None]
        mk[:, dl, :] = _mult(d)
    c["amask"] = mk.astype(ml_dtypes.bfloat16)
    ntile = NT + 1
    rt = np.zeros((ntile, 128, 4, 128), np.float32)
    inv_b = (500000.0 ** (-np.arange(0, 16, 2, dtype=np.float32) / np.float32(16))).astype(np.float32)
    inv_c = (1.0 / (np.float32(10000.0) ** np.linspace(0.0, 1.0, 32, dtype=np.float32))).astype(np.float32)
    for ti in range(ntile):
        if ti < NT:
            pos = (ti * 128 + i).astype(np.float32)
        else:
            pos = (np.float32(PAST) + np.maximum(i - PADN, 0)).astype(np.float32)
        angb = (pos[:, None] * inv_b[None, :]).astype(np.float32)
        angc = (pos[:, None] * inv_c[None, :]).astype(np.float32)
        cb = np.ones((64, 128), np.float32)
        sbb = np.zeros((64, 128), np.float32)
        cb[0:8] = np.cos(angb).T
        cb[8:16] = np.cos(angb).T
        sbb[0:8] = np.sin(angb).T
        sbb[8:16] = np.sin(angb).T
        cc = np.concatenate([np.cos(angc).T, np.cos(angc).T], axis=0)
        sc = np.concatenate([np.sin(angc).T, np.sin(angc).T], axis=0)
        for hh in range(2):
            rt[ti, hh * 64:(hh + 1) * 64, 0] = cb
            rt[ti, hh * 64:(hh + 1) * 64, 1] = sbb
            rt[ti, hh * 64:(hh + 1) * 64, 2] = cc
            rt[ti, hh * 64:(hh + 1) * 64, 3] = sc
    c["rot"] = rt
    return c


def build(T):
    NT = T // 128
    NTS = NT + NSB
    WK = min(WIN, T)
    NKEEP = WK // 128
    nc = bass.Bass("TRN2", target_bir_lowering=False)

    def din(name, shape, dt=F32):
        return nc.dram_tensor(name, list(shape), dt, kind="ExternalInput").ap()

    def dout(name, shape, dt=F32):
        return nc.dram_tensor(name, list(shape), dt, kind="ExternalOutput").ap()

    def dscr(name, shape, dt=F32):
        return nc.dram_tensor(name, list(shape), dt, kind="Internal").ap()

    xp = din("xp", [T, D])
    xs = din("xs", [NSB * 4, D])
    st_shift = din("st_shift", [DEPTH, NSB, A_COLS])
    st_wkv = din("st_wkv", [DEPTH, NSB, H_A, 64, 64])
    ck = din("ck", [DEPTH, NSB, WIN, W_B])
    cv = din("cv", [DEPTH, NSB, WIN, W_B])
    st_ret = din("st_ret", [DEPTH, NSB, H_C, 64, 64])
    w_in = din("w_in", [DEPTH, D, IN_COLS])
    p_mu = din("rwkv_mu", [DEPTH, A_COLS])
    p_w0 = din("rwkv_w0", [DEPTH, W_A])
    p_wl = din("rwkv_w_lora", [DEPTH, 64, W_A])
    p_a0 = din("rwkv_a0", [DEPTH, W_A])
    p_al = din("rwkv_a_lora", [DEPTH, 64, W_A])
    p_gl = din("rwkv_g_lora", [DEPTH, 128, W_A])
    p_kk = din("rwkv_k_k", [DEPTH, W_A])
    p_ka = din("rwkv_k_a", [DEPTH, W_A])
    p_rk = din("rwkv_r_k", [DEPTH, W_A])
    p_gg = din("rwkv_gn_g", [DEPTH, W_A])
    p_gb = din("rwkv_gn_b", [DEPTH, W_A])
    p_rg = din("ret_gn_g", [DEPTH, W_C])
    p_rb = din("ret_gn_b", [DEPTH, W_C])
    w_out = din("w_out", [DEPTH, D, D])
    ln1_g = din("ln1_g", [DEPTH, D])
    ln1_b = din("ln1_b", [DEPTH, D])
    w_gate = din("w_ffn_gate", [DEPTH, D, DFF])
    w_up = din("w_ffn_up", [DEPTH, D, DFF])
    w_down = din("w_ffn_down", [DEPTH, DFF, D])
    ln2_g = din("ln2_g", [DEPTH, D])
    ln2_b = din("ln2_b", [DEPTH, D])
    c_f = din("cf", [128, 16, 128])
    c_am = din("amask", [128, NBLK, 128], BF16)
    c_rot = din("rot", [NT + 1, 128, 4, 128])

    y_p = dout("y_p", [T, D])
    y_s = dout("y_s", [NSB * 4, D])
    o_pshift = dout("p_shift", [DEPTH, A_COLS])
    o_pwkv = dout("p_wkv", [DEPTH, H_A, 64, 64])
    o_pwk = dout("p_wk", [DEPTH, WK, W_B])
    o_pwv = dout("p_wv", [DEPTH, WK, W_B])
    o_pret = dout("p_ret", [DEPTH, H_C, 64, 64])
    o_sshift = dout("s_shift", [DEPTH, NSB, A_COLS])
    o_swkv = dout("s_wkv", [DEPTH, NSB, H_A, 64, 64])
    o_swk = dout("s_wk", [DEPTH, NSB, WIN, W_B])
    o_swv = dout("s_wv", [DEPTH, NSB, WIN, W_B])
    o_sret = dout("s_ret", [DEPTH, NSB, H_C, 64, 64])

    mixs = dscr("mixs", [8, 128, NTS * 128], BF16)
    wob = dscr("wob", [DEPTH, D, D], BF16)
    wgb = dscr("wgb", [DEPTH, D, DFF], BF16)
    wub = dscr("wub", [DEPTH, D, DFF], BF16)
    wdb = dscr("wdb", [DEPTH, DFF, D], BF16)
    winb = dscr("winb", [D, IN_COLS], BF16)
    bg_jobs = []
    x2s = dscr("x2s", [NTS * 128, D])

    CI = {"ident": 0, "su": 1, "ui": 2, "sl": 3, "nui": 4, "bones": 5, "pmb": 6, "pmc": 7,
          "dmask": 8, "qdec0": 12, "kdec0": 13, "qdec1": 14, "kdec1": 15}

    with ExitStack() as es_top:
        es_top.enter_context(nc.allow_non_contiguous_dma(reason="small strided parameter / state transfers"))
        S = Sched(nc, es_top)
        ps = [S.ps("ps%d" % b, [128, 512]) for b in range(8)]
        psb = [p[:, :].bitcast(BF16) for p in ps]

        def x_src(l, ti):
            if l == 0:
                return xp[ti * 128:(ti + 1) * 128, :]
            return x2s[ti * 128:(ti + 1) * 128, :]

        for l in range(DEPTH):
            with ExitStack() as es1:
                S.es = es1
                win = S.sb("win", [128, 8, IN_COLS], BF16)
                cf = S.sb("cf", [128, 16, 128])
                identb = S.sb("identb", [128, 128], BF16)
                amask = S.sb("amask", [128, NBLK, 128], BF16)
                mhalf = S.sb("mhalf", [128, 384])
                muc = S.sb("muc", [128, 11])
                pr = S.sb("pr", [128, 8, 3])
                prc = S.sb("prc", [128, 2, 2])
                wl = S.sb("wl", [128, 384])
                al = S.sb("al", [128, 384])
                gl = S.sb("gl", [128, 384])
                xt = [S.sb("xt%d" % k, [128, D]) for k in range(2)]
                xT = S.sb("xT", [128, 8, 128], BF16)
                H = S.sb("H", [128, NFM, 128])
                hm = S.sb("hm", [128, 11, 128])
                hml = S.sb("hml", [128, 11])
                rot = [S.sb("rot%d" % k, [128, 4, 128]) for k in range(2)]
                LI = S.sb("LI", [128, 2, 128])
                t3 = [S.sb("t3_%d" % k, [128, 3, 128]) for k in range(10)]
                TTt = S.sb("TTt", [128, 3, 4, 128], BF16)
                ktok = S.sb("ktok", [128, 384], BF16)
                nbtok = S.sb("nbtok", [128, 384], BF16)
                vtok = S.sb("vtok", [128, 384], BF16)
                U = S.sb("U", [128, 3, 64])
                UbP = [S.sb("UbP%d" % h, [128, 64], BF16) for h in range(6)]
                RbP = [S.sb("RbP%d" % h, [128, 64], BF16) for h in range(4)]
                chP = [[S.sb("chP%d_%d" % (h, k), [128, 128], BF16) for k in range(2)] for h in range(3)]
                chQ = [[S.sb("chQ%d_%d" % (h, k), [128, 128], BF16) for k in range(2)] for h in range(3)]
                chT = [[S.sb("chT%d_%d" % (h, k), [128, 128], BF16) for k in range(2)] for h in range(3)]
                cb16 = S.sb("cb16", [128, 3, 128], BF16)
                sqb = S.sb("sqb", [128, 3, 128], BF16)
                osb = S.sb("osb", [128, 3, 128], BF16)
                hb16 = S.sb("hb16", [128, 6, 128], BF16)
                invT = [S.sb("invT%d" % h, [128, 128], BF16) for h in range(3)]
                m4t = [S.sb("m4t%d" % h, [128, 128], BF16) for h in range(3)]
                lm3 = [S.sb("lm3_%d" % h, [128, 256], BF16) for h in range(3)]
                rhsb = [S.sb("rhsb%d" % h, [128, 64], BF16) for h in range(3)]
                umb = [S.sb("umb%d" % h, [128, 64], BF16) for h in range(3)]
                utmp = [S.sb("utmp%d" % h, [128, 64]) for h in range(3)]
                KTr = S.sb("KTr", [128, 3, RING, 128], BF16)
                Vr = S.sb("Vr", [128, RING, 6, 66], BF16)
                qTb = S.sb("qTb", [128, 3, 128], BF16)
                krot = S.sb("krot", [128, 3, 128])
                stg = [S.sb("stg%d" % k, [128, 384]) for k in range(2)]
                vstg = [S.sb("vstg%d" % k, [128, 384]) for k in range(2)]
                pT = [S.sb("pT%d" % k, [128, 512], BF16) for k in range(3)]
                ob = S.sb("ob", [128, 384])
                rl = S.sb("rl", [128, 6])
                c2 = [S.sb("c2_%d" % k, [128, 2, 128]) for k in range(4)]
                qcb = S.sb("qcb", [128, 2, 128], BF16)
                kcb = S.sb("kcb", [128, 2, 128], BF16)
                qdb = S.sb("qdb", [128, 2, 128], BF16)
                kdb = S.sb("kdb", [128, 2, 128], BF16)
                kdtok = S.sb("kdtok", [128, 256], BF16)
                vcb = S.sb("vcb", [128, 256], BF16)
                attb = [S.sb("attb%d" % k, [128, 128], BF16) for k in range(2)]
                R = S.sb("R", [128, 2, 64])
                mixT = [S.sb("mixT%d" % k, [128, 8, 128], BF16) for k in range(2)]
                cstage = S.sb("cstage", [128, 384])
                sst = S.sb("sst", [128, 32])
                swk = S.sb("swk", [128, 64])

                ident = cf[:, CI["ident"], :]

                for kc in range(8):
                    if l == 0:
                        S.dma("pool", win[:, kc, :], w_in[l, kc * 128:(kc + 1) * 128, :], "w1", writes=["win"])
                    else:
                        S.dma("sp", win[:, kc, :], winb[kc * 128:(kc + 1) * 128, :], "w1", writes=["win"])
                for kc in range(8):
                    bg_jobs.append((wob[l, kc * 128:(kc + 1) * 128, :], w_out[l, kc * 128:(kc + 1) * 128, :]))
                for kc in range(8):
                    bg_jobs.append((wgb[l, kc * 128:(kc + 1) * 128, :], w_gate[l, kc * 128:(kc + 1) * 128, :]))
                    bg_jobs.append((wub[l, kc * 128:(kc + 1) * 128, :], w_up[l, kc * 128:(kc + 1) * 128, :]))
                for c in range(NFC):
                    bg_jobs.append((wdb[l, c * 128:(c + 1) * 128, :], w_down[l, c * 128:(c + 1) * 128, :]))
                S.dma("sp", cf[:, :, :], c_f[:, :, :], "c1", writes=["cf"])
                S.dma("sp", amask[:, :, :], c_am[:, :, :], "c1", writes=["amask"])
                S.dma("sp", muc[:, :], p_mu[l].rearrange("(c p) -> p c", p=128), "c1", writes=["muc"])
                for k, prm in enumerate([p_w0, p_a0, p_kk, p_ka, p_rk, p_gg, p_gb]):
                    S.dma("sp", pr[:, k, :], prm[l].rearrange("(c p) -> p c", p=128), "c1", writes=["pr"])
                S.dma("sp", prc[:, 0, :], p_rg[l].rearrange("(c p) -> p c", p=128), "c1", writes=["prc"])
                S.dma("sp", prc[:, 1, :], p_rb[l].rearrange("(c p) -> p c", p=128), "c1", writes=["prc"])
                S.dma("sp", wl[0:64, :], p_wl[l], "c1", writes=["wl"])
                S.dma("sp", al[64:128, :], p_al[l], "c1", writes=["al"])
                S.dma("sp", gl[:, :], p_gl[l], "c1", writes=["gl"])
                S.cp("dve", identb[:, :], ident, ["cf"], ["identb"])
                S.cp("dve", cb16[:, :, :], cf[:, CI["bones"]:CI["bones"] + 3, :], ["cf"], ["cb16"])
                S.memset("dve", mhalf[:, :], -0.5, ["mhalf"])
                S.ts("dve", pr[:, 0:2, :], pr[:, 0:2, :], -1.0, None, ALU.mult, None, ["pr"], ["pr"])

                for sb_ in range(NSB):
                    S.dma("act", o_swk[l, sb_, 0:WIN - 4, :], ck[l, sb_, 4:WIN, :], "cpk")
                    S.dma("act", o_swv[l, sb_, 0:WIN - 4, :], cv[l, sb_, 4:WIN, :], "cpv")
                tile_ctr = [0]

                def process_tile(seq, ti, nseq, gi):
                    samp = seq != "p"
                    if _STOP <= 0:
                        return
                    for _ in range(2):
                        if bg_jobs:
                            d_, s_src = bg_jobs.pop(0)
                            S.dma("pool", d_, s_src, "bgc")
                    k2 = tile_ctr[0] % 2
                    tile_ctr[0] += 1
                    first = ti == 0
                    xk = "xt%d" % k2
                    rk_ = "rot%d" % k2
                    if samp and l == 0:
                        S.memset("pool", xt[k2][:, :], 0.0, [xk])
                        S.dma("sp", xt[k2][PADN:128, :], xs[seq * 4:(seq + 1) * 4, :], "ldx%d" % k2, writes=[xk])
                    else:
                        S.dma("sp", xt[k2][:, :], x_src(l, gi), "ldx%d" % k2, writes=[xk])
                    rti = NT if samp else ti
                    S.dma("sp", rot[k2][:, :, :], c_rot[rti], "ldx%d" % k2, writes=[rk_])
                    for half in range(2):
                        for j in range(4):
                            kc = half * 4 + j
                            S.tr(ps[half][:, j * 128:(j + 1) * 128], xt[k2][:, kc * 128:(kc + 1) * 128], ident,
                                 [xk, "cf"], ["ps%d" % half])
                        S.cp("act" if half == 0 else "dve", xT[:, half * 4:(half + 1) * 4, :],
                             ps[half][:, :].rearrange("p (a b) -> p a b", b=128), ["ps%d" % half], ["xT%d" % half])
                    for c in range(NFM):
                        b, j = c // 4, c % 4
                        col = FM_CHUNK_COL[c]
                        for kc in range(8):
                            S.mm(ps[b][:, j * 128:(j + 1) * 128], win[:, kc, col:col + 128], xT[:, kc, :],
                                 kc == 0, kc == 7, ["win", "xT0", "xT1"], ["ps%d" % b])
                    for kc in range(8):
                        S.mm(ps[6][:, 0:384], xT[:, kc, :], win[:, kc, BV_COL:BV_COL + 384], kc == 0, kc == 7,
                             ["win", "xT0", "xT1"], ["ps6"])
                    for kc in range(8):
                        S.mm(ps[7][:, 0:256], xT[:, kc, :], win[:, kc, CV_COL:CV_COL + 256], kc == 0, kc == 7,
                             ["win", "xT0", "xT1"], ["ps7"])
                    for b in range(6):
                        n = min(4, NFM - 4 * b)
                        S.cp("act" if b % 2 == 0 else "dve", H[:, 4 * b:4 * b + n, :],
                             ps[b][:, 0:n * 128].rearrange("p (a b) -> p a b", b=128), ["ps%d" % b], ["H%d" % b])
                    Hk = ["H%d" % b for b in range(6)]
                    if _STOP <= 1:
                        return
                    slot = ti % RING if not samp else 16
                    vk = "V%d" % slot
                    S.cp("dve", Vr[:, slot, :, 0:64], ps[6][:, 0:384].rearrange("p (h e) -> p h e", e=64), ["ps6"], [vk])
                    if _STOP <= 1.1:
                        return
                    S.memset("pool", Vr[:, slot, :, 64:65], 1.0, [vk + "o"])
                    if _STOP <= 1.2:
                        return
                    keep = samp or (ti >= NT - NKEEP)
                    if keep:
                        S.cp("dve", stg[0][:, :], ps[6][:, 0:384], ["ps6"], ["stg0"])
                        if samp:
                            S.dma("pool", o_swv[l, seq, WIN - 4:WIN, :], stg[0][PADN:128, :], "stv", reads=["stg0"])
                        else:
                            r0 = (ti - (NT - NKEEP)) * 128
                            if os.environ.get("KV1") == "nodma":
                                pass
                            else:
                                S.dma(os.environ.get("KV1", "pool"), o_pwv[l, r0:r0 + 128, :], stg[0][:, :], "stv", reads=["stg0"])
                    if _STOP <= 1.3:
                        return
                    S.cp("act", vcb[:, :], ps[7][:, 0:256], ["ps7"], ["vcb"])
                    if _STOP <= 1.5:
                        return
                    if samp:
                        S.cp("dve", sst[:, 0:11], H[:, 0:11, 127], Hk[0:3], ["sst"])
                        S.dma("pool", o_sshift[l, seq].rearrange("(c p) -> p c", p=128), sst[:, 0:11], "sts", reads=["sst"])
                        S.dma("sp", sst[:, 16:27], st_shift[l, seq].rearrange("(c p) -> p c", p=128), "ldx%d" % k2, writes=["sst2"])
                        S.cp("dve", H[:, 0:11, PADN - 1], sst[:, 16:27], ["sst2"] + Hk[0:3], Hk[0:3])
                    elif ti == NT - 1:
                        S.cp("dve", sst[:, 0:11], H[:, 0:11, 127], Hk[0:3], ["sst"])
                        S.dma("pool", o_pshift[l].rearrange("(c p) -> p c", p=128), sst[:, 0:11], "sts", reads=["sst"])
                    if _STOP <= 1.7:
                        return
                    if first:
                        S.memset("dve", hml[:, :], 0.0, ["hml"])
                    HA = H[:, 0:11, :]
                    S.tt("dve", hm[:, :, :], HA, muc[:, 0:11].unsqueeze(2).to_broadcast([128, 11, 128]), ALU.mult,
                         Hk[0:3] + ["muc"], ["hm"])
                    S.tt("dve", HA, HA, hm[:, :, :], ALU.subtract, Hk[0:3] + ["hm"], Hk[0:3])
                    S.tt("dve", H[:, 0:11, 1:128], H[:, 0:11, 1:128], hm[:, :, 0:127], ALU.add, Hk[0:3] + ["hm"], Hk[0:3])
                    S.tt("dve", H[:, 0:11, 0], H[:, 0:11, 0], hml[:, :], ALU.add, Hk[0:3] + ["hml"], Hk[0:3])
                    S.cp("dve", hml[:, :], hm[:, :, 127], ["hm"], ["hml"])
                    rT, kT, vT = H[:, 0:3, :], H[:, 3:6, :], H[:, 6:9, :]
                    if _STOP <= 2:
                        return
                    HAk = Hk[0:3]
                    S.act(LI[0:64, 0, :], H[0:64, 9, :], AF.Exp, HAk, ["LIa"], scale=-2.0)
                    S.ts("dve", LI[0:64, 0, :], LI[0:64, 0, :], 1.0, None, ALU.add, None, ["LIa"], ["LIa"])
                    S.op("dve", lambda e: e.reciprocal(out=LI[0:64, 0, :], in_=LI[0:64, 0, :]), ["LIa"], ["LIa"])
                    S.ts("dve", LI[0:64, 0, :], LI[0:64, 0, :], 2.0, -1.0, ALU.mult, ALU.add, ["LIa"], ["LIa"])
                    S.cp("act", LI[64:128, 0, :], H[64:128, 9, :], HAk, ["LIb"])
                    S.act(LI[:, 1, :], H[:, 10, :], AF.Exp, HAk, ["LIc"], scale=-1.0)
                    S.ts("dve", LI[:, 1, :], LI[:, 1, :], 1.0, None, ALU.add, None, ["LIc"], ["LIc"])
                    S.op("dve", lambda e: e.reciprocal(out=LI[:, 1, :], in_=LI[:, 1, :]), ["LIc"], ["LIc"])
                    for p in range(3):
                        S.mm(ps[0][:, p * 128:(p + 1) * 128], wl[0:64, p * 128:(p + 1) * 128], LI[0:64, 0, :], True, True,
                             ["wl", "LIa"], ["ps0"])
                        S.mm(ps[1][:, p * 128:(p + 1) * 128], al[64:128, p * 128:(p + 1) * 128], LI[64:128, 0, :], True, True,
                             ["al", "LIb"], ["ps1"])
                        S.mm(ps[2][:, p * 128:(p + 1) * 128], gl[:, p * 128:(p + 1) * 128], LI[:, 1, :], True, True,
                             ["gl", "LIc"], ["ps2"])
                    lw, cum, Wc, iW, Wp, aa, gT, kkn, kp, bb = t3
                    if _STOP <= 3:
                        return
                    n = S.kn
                    v3 = lambda b: ps[b][:, 0:384].rearrange("p (a b) -> p a b", b=128)
                    for p in range(3):
                        S.act(lw[:, p, :], ps[0][:, p * 128:(p + 1) * 128], AF.Exp, ["ps0", "pr"], [n(lw)], scale=-1.0, bias=pr[:, 0, p:p + 1])
                        S.act(aa[:, p, :], ps[1][:, p * 128:(p + 1) * 128], AF.Exp, ["ps1", "pr"], [n(aa)], scale=-1.0, bias=pr[:, 1, p:p + 1])
                    S.cp("act", gT[:, :, :], v3(2), ["ps2"], [n(gT)])
                    S.ts("dve", lw[:, :, :], lw[:, :, :], 1.0, None, ALU.add, None, [n(lw)], [n(lw)])
                    S.op("dve", lambda e: e.reciprocal(out=lw[:, :, :], in_=lw[:, :, :]), [n(lw)], [n(lw)])
                    S.ts("dve", lw[:, :, :], lw[:, :, :], -DECAY, None, ALU.mult, None, [n(lw)], [n(lw)])
                    S.ts("dve", aa[:, :, :], aa[:, :, :], 1.0, None, ALU.add, None, [n(aa)], [n(aa)])
                    S.op("dve", lambda e: e.reciprocal(out=aa[:, :, :], in_=aa[:, :, :]), [n(aa)], [n(aa)])
                    if samp:
                        S.memset("dve", lw[:, :, 0:PADN], 0.0, [n(lw)])
                    ones3 = mhalf
                    for p in range(3):
                        for cc in range(2):
                            sl = slice(cc * 64, (cc + 1) * 64)
                            S.op("dve", lambda e, p=p, sl=sl: e.tensor_tensor_scan(
                                out=cum[:, p, sl], data0=onesT[:, sl], data1=lw[:, p, sl], initial=0.0,
                                op0=ALU.mult, op1=ALU.add), [n(lw), "onesT"], [n(cum)])
                    S.act(Wc[:, :, :], cum[:, :, :], AF.Exp, [n(cum)], [n(Wc)])
                    S.act(iW[:, :, :], cum[:, :, :], AF.Exp, [n(cum)], [n(iW)], scale=-1.0)
                    S.tt("dve", Wp[:, :, :], cum[:, :, :], lw[:, :, :], ALU.subtract, [n(cum), n(lw)], [n(Wp)])
                    S.act(Wp[:, :, :], Wp[:, :, :], AF.Exp, [n(Wp)], [n(Wp)])
                    bc = lambda k: pr[:, k, :].unsqueeze(2).to_broadcast([128, 3, 128])
                    S.tt("dve", kkn[:, :, :], kT, bc(2), ALU.mult, HAk + ["pr"], [n(kkn)])
                    S.act(sqb[:, :, :], kkn[:, :, :], AF.Square, [n(kkn)], ["sqb"])
                    for p in range(3):
                        S.mm(ps[3][:, p * 128:(p + 1) * 128], cb16[:, 0, :], sqb[:, p, :], True, True, ["cb16", "sqb"], ["ps3"])
                    S.ts("dve", bb[:, :, :], v3(3), 1e-12, None, ALU.max, None, ["ps3"], [n(bb)])
                    S.act(bb[:, :, :], bb[:, :, :], AF.Ln, [n(bb)], [n(bb)])
                    S.act(bb[:, :, :], bb[:, :, :], AF.Exp, [n(bb)], [n(bb)], scale=-0.5)
                    S.tt("dve", kkn[:, :, :], kkn[:, :, :], bb[:, :, :], ALU.mult, [n(kkn), n(bb)], [n(kkn)])
                    S.stt(kp[:, :, :], aa[:, :, :], -1.0, bc(3), ALU.add, ALU.mult, [n(aa), "pr"], [n(kp)])
                    S.stt(kp[:, :, :], kp[:, :, :], 1.0, kT, ALU.add, ALU.mult, [n(kp)] + HAk, [n(kp)])
                    S.tt("dve", bb[:, :, :], kkn[:, :, :], aa[:, :, :], ALU.mult, [n(kkn), n(aa)], [n(bb)])
                    if samp:
                        S.memset("dve", kp[:, :, 0:PADN], 0.0, [n(kp)])
                        S.memset("pool", bb[:, :, 0:PADN], 0.0, [n(bb)])
                    S.tt("dve", TTt[:, :, 0, :], kkn[:, :, :], Wp[:, :, :], ALU.mult, [n(kkn), n(Wp)], ["TT0"])
                    S.tt("dve", TTt[:, :, 1, :], rT, Wc[:, :, :], ALU.mult, HAk + [n(Wc)], ["TT1"])
                    S.tt("dve", TTt[:, :, 2, :], kp[:, :, :], iW[:, :, :], ALU.mult, [n(kp), n(iW)], ["TT2"])
                    S.tt("dve", TTt[:, :, 3, :], bb[:, :, :], iW[:, :, :], ALU.mult, [n(bb), n(iW)], ["TT3"])
                    bon = cum
                    S.tt("dve", aa[:, :, :], rT, kp[:, :, :], ALU.mult, HAk + [n(kp), n(bb)], [n(aa)])
                    S.tt("dve", osb[:, :, :], aa[:, :, :], bc(4), ALU.mult, [n(aa), "pr"], ["osb"])
                    for p in range(3):
                        S.mm(ps[3][:, p * 128:(p + 1) * 128], cb16[:, 0, :], osb[:, p, :], True, True, ["cb16", "osb"], ["ps3"])
                    S.tt("dve", bon[:, :, :], v3(3), vT, ALU.mult, ["ps3", n(Wp), n(Wc), n(iW)] + HAk, [n(cum)])
                    for p in range(3):
                        S.tr(psb[4][:, p * 128:(p + 1) * 128], TTt[:, p, 2, :], identb[:, :], ["TT2", "identb"], ["ps4"])
                        S.tr(psb[4][:, 384 + p * 128:384 + (p + 1) * 128], TTt[:, p, 3, :], identb[:, :], ["TT3", "identb"], ["ps4"])
                        S.tr(ps[5][:, p * 128:(p + 1) * 128], H[:, 6 + p, :], ident, HAk + ["cf"], ["ps5"])
                    S.cp("dve", ktok[:, :], psb[4][:, 0:384], ["ps4"], ["ktok"])
                    S.ts("dve", nbtok[:, :], psb[4][:, 384:768], -1.0, None, ALU.mult, None, ["ps4"], ["nbtok"])
                    S.cp("act", vtok[:, :], ps[5][:, 0:384], ["ps5"], ["vtok"])
                    if _STOP <= 4:
                        return
                    if first:
                        if samp:
                            for p in range(3):
                                S.dma("sp", cstage[0:64, p * 128:(p + 1) * 128].rearrange("v (h k) -> v h k", k=64),
                                      st_wkv[l, seq, 2 * p:2 * p + 2].rearrange("h v k -> v h k"), "ldx%d" % k2, writes=["cstage"])
                            for p in range(3):
                                S.tr(ps[6][:, p * 64:(p + 1) * 64], cstage[0:64, p * 128:(p + 1) * 128], cf[0:64, CI["ident"], 0:64],
                                     ["cstage", "cf"], ["ps6"])
                            S.cp("dve", U[:, :, :], ps[6][:, 0:192].rearrange("p (a b) -> p a b", b=64), ["ps6"], ["U"])
                            for h in range(6):
                                p_, hb_ = h // 2, 64 * (h % 2)
                                S.memset("pool", UbP[h][:, :], 0.0, ["UbP%d" % h])
                                S.cp("dve", UbP[h][hb_:hb_ + 64, :], ps[6][hb_:hb_ + 64, p_ * 64:(p_ + 1) * 64], ["ps6"], ["UbP%d" % h])
                            for p in range(2):
                                for hh in range(2):
                                    S.dma("sp", R[hh * 64:(hh + 1) * 64, p, :], st_ret[l, seq, 2 * p + hh], "ldx%d" % k2, writes=["R"])
                            for p in range(2):
                                for hh in range(2):
                                    g = GAMMA[2 * p + hh] ** (-float(PADN))
                                    S.ts("dve", R[hh * 64:(hh + 1) * 64, p, :], R[hh * 64:(hh + 1) * 64, p, :], g, None, ALU.mult, None, ["R"], ["R"])
                            for h in range(4):
                                p_, hb_ = h // 2, 64 * (h % 2)
                                S.memset("pool", RbP[h][:, :], 0.0, ["RbP%d" % h])
                                S.cp("act", RbP[h][hb_:hb_ + 64, :], R[hb_:hb_ + 64, p_, :], ["R"], ["RbP%d" % h])
                        else:
                            S.memset("dve", U[:, :, :], 0.0, ["U"])
                            for h in range(6):
                                S.memset("pool", UbP[h][:, :], 0.0, ["UbP%d" % h])
                            S.memset("dve", R[:, :, :], 0.0, ["R"])
                            for h in range(4):
                                S.memset("pool", RbP[h][:, :], 0.0, ["RbP%d" % h])
                    mk2 = "mixT%d" % k2
                    cosB, sinB = rot[k2][:, 0, :], rot[k2][:, 1, :]
                    cosC, sinC = rot[k2][:, 2, :], rot[k2][:, 3, :]
                    S.cp("act", hb16[:, :, :], H[:, CH_BQ:CH_BQ + 6, :], ["H2", "H3", "H4"], ["hb16"])
                    for c in range(6):
                        S.mm(ps[0 + c // 4][:, (c % 4) * 128:(c % 4 + 1) * 128], cb16[:, 1, :], hb16[:, c, :], True, True,
                             ["cb16", "hb16"], ["ps%d" % (c // 4)])
                    qk = H[:, CH_BQ:CH_BQ + 6, :]
                    qkk = ["H2", "H3", "H4"]
                    b6 = lambda a: a.unsqueeze(1).to_broadcast([128, 6, 128])
                    S.tt("dve", qk, qk, b6(cosB), ALU.mult, qkk + [rk_], qkk)
                    rtmp = t3[3]
                    S.tt("dve", rtmp[:, :, :], v3(0) if False else ps[0][:, 0:384].rearrange("p (a b) -> p a b", b=128),
                         sinB.unsqueeze(1).to_broadcast([128, 3, 128]), ALU.mult, ["ps0", rk_], [n(rtmp)])
                    S.tt("dve", H[:, CH_BQ:CH_BQ + 3, :], H[:, CH_BQ:CH_BQ + 3, :], rtmp[:, :, :], ALU.add, qkk + [n(rtmp)], qkk)
                    S.tt("dve", rtmp[:, 0, :], ps[0][:, 384:512], sinB, ALU.mult, ["ps0", rk_], [n(rtmp)])
                    S.tt("dve", rtmp[:, 1:3, :], ps[1][:, 0:256].rearrange("p (a b) -> p a b", b=128),
                         sinB.unsqueeze(1).to_broadcast([128, 2, 128]), ALU.mult, ["ps1", rk_], [n(rtmp)])
                    S.tt("dve", krot[:, :, :], H[:, CH_BK:CH_BK + 3, :], rtmp[:, :, :], ALU.add, qkk + [n(rtmp)], ["krot"])
                    S.cp("act", qTb[:, :, :], H[:, CH_BQ:CH_BQ + 3, :], qkk, ["qTb"])
                    S.cp("act", KTr[:, :, slot, :], krot[:, :, :], ["krot"], ["K%d" % slot])
                    if keep:
                        for p in range(3):
                            S.tr(ps[2][:, p * 128:(p + 1) * 128], krot[:, p, :], ident, ["krot", "cf"], ["ps2"])
                        S.cp("dve", stg[1][:, :], ps[2][:, 0:384], ["ps2"], ["stg1"])
                        if samp:
                            S.dma("pool", o_swk[l, seq, WIN - 4:WIN, :], stg[1][PADN:128, :], "stk", reads=["stg1"])
                        else:
                            r0 = (ti - (NT - NKEEP)) * 128
                            S.dma("pool", o_pwk[l, r0:r0 + 128, :], stg[1][:, :], "stk", reads=["stg1"])
                    if samp:
                        for dl in range(NBLK):
                            sl_ = 16 - dl
                            r_lo = WIN - PADN - 128 * dl
                            lo = max(0, -r_lo)
                            hi = 128 if dl > 0 else PADN
                            sk = stg[dl % 2]
                            skn = "stg%d" % (dl % 2)
                            if lo > 0:
                                S.memset("pool", sk[0:lo, :], 0.0, [skn])
                            S.dma("sp", sk[lo:hi, :], ck[l, seq, r_lo + lo:r_lo + hi, :], "ldc%d" % (dl % 2), writes=[skn])
                            bnk = 2 + dl % 2
                            for p in range(3):
                                S.tr(ps[bnk][:, p * 128:(p + 1) * 128], sk[:, p * 128:(p + 1) * 128], ident, [skn, "cf"], ["ps%d" % bnk])
                            src = ps[bnk][:, 0:384].rearrange("p (a b) -> p a b", b=128)
                            if dl == 0:
                                S.cp("act", KTr[:, :, sl_, 0:PADN], src[:, :, 0:PADN], ["ps%d" % bnk], ["K%d" % sl_])
                            else:
                                S.cp("act", KTr[:, :, sl_, :], src, ["ps%d" % bnk], ["K%d" % sl_])
                            vkk = "V%d" % sl_
                            vs_ = vstg[dl % 2]
                            vsn = "vstg%d" % (dl % 2)
                            if lo > 0:
                                S.memset("pool", vs_[0:lo, :], 0.0, [vsn])
                            S.dma("sp", vs_[lo:hi, :], cv[l, seq, r_lo + lo:r_lo + hi, :], "ldcv%d" % (dl % 2), writes=[vsn])
                            S.cp("act", Vr[0:hi, sl_, :, 0:64], vs_[0:hi, :].rearrange("r (h e) -> r h e", e=64), [vsn], [vkk])
                            if dl > 0:
                                S.memset("pool", Vr[:, sl_, :, 64:65], 1.0, [vkk + "o"])
                    def rwkv_core():
                        su_ui = cf[:, CI["su"]:CI["su"] + 2, :]
                        for grp in range(3):
                            heads = [grp * 2 + k for k in range(2)]
                            for k, h in enumerate(heads):
                                p, hb = h // 2, 64 * (h % 2)
                                bA, bB = ps[2 * k], ps[2 * k + 1]
                                kA, kB = "ps%d" % (2 * k), "ps%d" % (2 * k + 1)
                                KKR = TTt[hb:hb + 64, p, 0:2, :]
                                S.mm(bA[:, 0:256], TTt[hb:hb + 64, p, 3, :], KKR, True, True, ["TT0", "TT1", "TT3"], [kA + "a"])
                                S.mm(bA[:, 256:512], TTt[hb:hb + 64, p, 2, :], KKR, True, True, ["TT0", "TT1", "TT2"], [kA + "b"])
                                S.mm(bB[:, 0:128], TTt[hb:hb + 64, p, 0, :], TTt[hb:hb + 64, p, 3, :], True, True, ["TT0", "TT3"], [kB + "a"])
                            yield
                            for k, h in enumerate(heads):
                                bA, bB = ps[2 * k], ps[2 * k + 1]
                                kA, kB = "ps%d" % (2 * k), "ps%d" % (2 * k + 1)
                                P0, Q0, T0 = chP[k][0], chQ[k][0], chT[k][0]
                                S.tt("dve", P0[:, :], bA[:, 0:128], cf[:, CI["su"], :], ALU.mult, [kA + "a", "cf"], [n(P0)])
                                S.tt("dve", m4t[k][:, :], bA[:, 128:256], cf[:, CI["nui"], :], ALU.mult, [kA + "a", "cf"], [n(m4t[k])])
                                S.tt("dve", lm3[k][:, :].rearrange("p (a b) -> p a b", b=128), bA[:, 256:512].rearrange("p (a b) -> p a b", b=128),
                                     su_ui, ALU.mult, [kA + "b", "cf"], [n(lm3[k])])
                                S.tt("dve", Q0[:, :], bB[:, 0:128], cf[:, CI["sl"], :], ALU.mult, [kB + "a", "cf"], [n(Q0)])
                                S.tt("dve", T0[:, :], identb[:, :], P0[:, :], ALU.subtract, ["identb", n(P0)], [n(T0)])
                            yield
                            cur = [0, 0, 0]
                            for lev in range(1, 6):
                                st = []
                                for k, h in enumerate(heads):
                                    c0 = cur[k]
                                    st.append((k, ps[2 * k], ps[2 * k + 1], "ps%d" % (2 * k), "ps%d" % (2 * k + 1),
                                               chP[k][c0], chQ[k][c0], chT[k][c0], chP[k][1 - c0], chQ[k][1 - c0], chT[k][1 - c0]))
                                    cur[k] = 1 - c0
                                for (k, bA, bB, kA, kB, Pc, Qc, Tc, Pn, Qn, Tn) in st:
                                    S.mm(bB[:, 256:384], Pc[:, :], Qc[:, :], True, True, [n(Pc), n(Qc)], [kB + "c"])
                                    if lev < 5:
                                        S.mm(bB[:, 128:256], Qc[:, :], Pc[:, :], True, True, [n(Pc), n(Qc)], [kB + "b"])
                                yield
                                for (k, bA, bB, kA, kB, Pc, Qc, Tc, Pn, Qn, Tn) in st:
                                    S.cp("dve", Qn[:, :], bB[:, 256:384], [kB + "c"], [n(Qn)])
                                    if lev < 5:
                                        S.cp("dve", Pn[:, :], bB[:, 128:256], [kB + "b"], [n(Pn)])
                                for (k, bA, bB, kA, kB, Pc, Qc, Tc, Pn, Qn, Tn) in st:
                                    S.mm(bA[:, 0:128], Qn[:, :], Tc[:, :], True, True, [n(Qn), n(Tc)], [kA + "d"])
                                yield
                                for (k, bA, bB, kA, kB, Pc, Qc, Tc, Pn, Qn, Tn) in st:
                                    if lev < 5:
                                        S.tt("dve", Tn[:, :], Tc[:, :], bA[:, 0:128], ALU.add, [n(Tc), kA + "d"], [n(Tn)])
                                    else:
                                        S.tt("dve", invT[k][:, :], Tc[:, :], bA[:, 0:128], ALU.add, [n(Tc), kA + "d"], [n(invT[k])])
                            for cidx in range(2):
                                pb = 64 * cidx
                                tk = slice(pb, pb + 64)
                                hd = []
                                for k, h in enumerate(heads):
                                    p, hb = h // 2, 64 * (h % 2)
                                    hd.append((k, h, p, slice(hb, hb + 64), ps[2 * k], ps[2 * k + 1], "ps%d" % (2 * k), "ps%d" % (2 * k + 1),
                                               "UbP%d" % h, vtok[:, h * 64:(h + 1) * 64], vtok[tk, h * 64:(h + 1) * 64]))
                                for (k, h, p, hs, bA, bB, kA, kB, ukey, vall, vh) in hd:
                                    S.mm(bA[tk, 0:64], TTt[:, p, 0, tk], UbP[h][:, :], True, False, ["TT0", ukey], [kA + "r"])
                                    S.mm(bA[tk, 0:64], lm3[k][:, pb:pb + 64], vall, False, True, [n(lm3[k]), "vtok"], [kA + "r"])
                                yield
                                for (k, h, p, hs, bA, bB, kA, kB, ukey, vall, vh) in hd:
                                    S.cp("dve", rhsb[k][tk, :], bA[tk, 0:64], [kA + "r"], [n(rhsb[k])])
                                for (k, h, p, hs, bA, bB, kA, kB, ukey, vall, vh) in hd:
                                    S.mm(bB[tk, 0:64], invT[k][tk, pb:pb + 64], rhsb[k][tk, :], True, True, [n(invT[k]), n(rhsb[k])], [kB + "u"])
                                yield
                                for (k, h, p, hs, bA, bB, kA, kB, ukey, vall, vh) in hd:
                                    S.cp("dve", umb[k][tk, :], bB[tk, 0:64], [kB + "u"], [n(umb[k])])
                                for (k, h, p, hs, bA, bB, kA, kB, ukey, vall, vh) in hd:
                                    oreg = ps[7][hs, p * 128 + pb:p * 128 + pb + 64]
                                    S.mm(oreg, UbP[h][:, :], TTt[:, p, 1, tk], True, False, [ukey, "TT1"], ["ps7o"])
                                    S.mm(oreg, vall, lm3[k][:, 128 + pb:128 + pb + 64], False, False, ["vtok", n(lm3[k])], ["ps7o"])
                                    S.mm(oreg, umb[k][:, :], m4t[k][:, pb:pb + 64], False, True, [n(umb[k]), n(m4t[k])], ["ps7o"])
                                    S.mm(bA[hs, 64:128], ktok[tk, h * 64:(h + 1) * 64], vh, True, False, ["ktok", "vtok"], [kA + "s"])
                                    S.mm(bA[hs, 64:128], nbtok[tk, h * 64:(h + 1) * 64], umb[k][tk, :], False, True, ["nbtok", n(umb[k])], [kA + "s"])
                                yield
                                for (k, h, p, hs, bA, bB, kA, kB, ukey, vall, vh) in hd:
                                    S.tt("dve", utmp[k][hs, :], U[hs, p, :], bA[hs, 64:128], ALU.add, ["U", kA + "s"], [n(utmp[k])])
                                    wcol = Wc[hs, p, pb + 63:pb + 64]
                                    S.ts("dve", U[hs, p, :], utmp[k][hs, :], wcol, None, ALU.mult, None, [n(utmp[k]), n(Wc)], ["U"])
                                    S.act(UbP[h][hs, :], utmp[k][hs, :], AF.Copy, [n(utmp[k]), n(Wc)], [ukey], scale=wcol)
                                yield
                    def attn_core():
                        unit = [0]
                        nb = NBLK if samp else min(NBLK, ti + 1)
                        units = []
                        for h in range(6):
                            for g0 in range(0, nb, 4):
                                units.append((h, g0, min(4, nb - g0)))
                        pend = None

                        def emit_pv(u):
                            (h, g0, gn_, pt) = u
                            for j in range(gn_):
                                dl = g0 + j
                                sl_ = (16 - dl) if samp else ((ti - dl) % RING)
                                S.mm(ps[6][:, h * 65:(h + 1) * 65], pt[:, j * 128:(j + 1) * 128], Vr[:, sl_, h, 0:65], dl == 0, dl == nb - 1,
                                     [n(pt), "V%d" % sl_, "V%do" % sl_], ["ps6"])
                        for ui, (h, g0, gn_) in enumerate(units):
                            p, hb = h // 2, 64 * (h % 2)
                            hs = slice(hb, hb + 64)
                            bnk = 4 + (ui % 2)
                            bk = "ps%d" % bnk
                            pt = pT[ui % 3]
                            for j in range(gn_):
                                dl = g0 + j
                                sl_ = (16 - dl) if samp else ((ti - dl) % RING)
                                S.mm(ps[bnk][:, j * 128:(j + 1) * 128], KTr[hs, p, sl_, :], qTb[hs, p, :], True, True,
                                     ["K%d" % sl_, "qTb"], [bk])
                            if pend is not None:
                                emit_pv(pend)
                            S.act(pt[:, 0:gn_ * 128], ps[bnk][:, 0:gn_ * 128], AF.Exp, [bk], [n(pt)], scale=HD ** -0.5)
                            pt3 = pt[:, 0:gn_ * 128].rearrange("p (a b) -> p a b", b=128)
                            S.tt("dve", pt3, pt3, amask[:, g0:g0 + gn_, :], ALU.mult, [n(pt), "amask"], [n(pt)])
                            pend = (h, g0, gn_, pt)
                            yield
                        emit_pv(pend)
                        O3 = ps[6][:, 0:390].rearrange("p (h e) -> p h e", e=65)
                        S.op("dve", lambda e: e.reciprocal(out=rl[:, :], in_=O3[:, :, 64]), ["ps6"], ["rl"])
                        S.tt("dve", ob[:, :].rearrange("p (h e) -> p h e", e=64), O3[:, :, 0:64], rl[:, :].unsqueeze(2).to_broadcast([128, 6, 64]),
                             ALU.mult, ["ps6", "rl"], ["ob"])
                        for p in range(3):
                            S.tr(ps[4][:, p * 128:(p + 1) * 128], ob[:, p * 128:(p + 1) * 128], ident, ["ob", "cf"], ["ps4"])
                        S.cp("act", mixT[k2][:, 3:6, :], ps[4][:, 0:384].rearrange("p (a b) -> p a b", b=128), ["ps4"], [mk2 + "b"])
                    if _STOP <= 5:
                        return
                    ga_, gb_ = rwkv_core(), attn_core()
                    a_live, b_live = True, True
                    while a_live or b_live:
                        if a_live:
                            try:
                                next(ga_)
                            except StopIteration:
                                a_live = False
                        for _ in range(2):
                            if b_live:
                                try:
                                    next(gb_)
                                except StopIteration:
                                    b_live = False
                    oS, sq = lw, kkn
                    S.cp("act", oS[:, :, :], v3(7), ["ps7o"], [n(oS)])
                    S.act(sqb[:, :, :], oS[:, :, :], AF.Square, [n(oS)], ["sqb"])
                    S.cp("dve", osb[:, :, :], oS[:, :, :], [n(oS)], ["osb"])
                    for p in range(3):
                        S.mm(ps[0][:, p * 128:(p + 1) * 128], cb16[:, 0, :], osb[:, p, :], True, True, ["cb16", "osb"], ["ps0"])
                        S.mm(ps[1][:, p * 128:(p + 1) * 128], cb16[:, 0, :], sqb[:, p, :], True, True, ["cb16", "sqb"], ["ps1"])
                    mean, var = kp, bb
                    S.ts("dve", mean[:, :, :], v3(0), 1.0 / 64, None, ALU.mult, None, ["ps0"], [n(mean)])
                    S.act(var[:, :, :], mean[:, :, :], AF.Square, [n(mean)], [n(var)])
                    S.stt(var[:, :, :], v3(1), 1.0 / 64, var[:, :, :], ALU.mult, ALU.subtract, ["ps1", n(var)], [n(var)])
                    S.ts("dve", var[:, :, :], var[:, :, :], GN_EPS, None, ALU.add, None, [n(var)], [n(var)])
                    S.act(var[:, :, :], var[:, :, :], AF.Ln, [n(var)], [n(var)])
                    S.act(var[:, :, :], var[:, :, :], AF.Exp, [n(var)], [n(var)], scale=-0.5)
                    S.tt("dve", oS[:, :, :], oS[:, :, :], mean[:, :, :], ALU.subtract, [n(oS), n(mean)], [n(oS)])
                    S.tt("dve", oS[:, :, :], oS[:, :, :], var[:, :, :], ALU.mult, [n(oS), n(var)], [n(oS)])
                    S.tt("dve", oS[:, :, :], oS[:, :, :], bc(5), ALU.mult, [n(oS), "pr"], [n(oS)])
                    S.tt("dve", oS[:, :, :], oS[:, :, :], bc(6), ALU.add, [n(oS), "pr"], [n(oS)])
                    S.tt("dve", oS[:, :, :], oS[:, :, :], bon[:, :, :], ALU.add, [n(oS), n(cum)], [n(oS)])
                    S.tt("dve", mixT[k2][:, 0:3, :], oS[:, :, :], gT[:, :, :], ALU.mult, [n(oS), n(gT)], [mk2 + "a"])
                    if _STOP <= 11:
                        return
                    qc, kc_, gc = H[:, CH_CQ:CH_CQ + 2, :], H[:, CH_CK:CH_CK + 2, :], H[:, CH_CG:CH_CG + 2, :]
                    ck_ = ["H4", "H5"]
                    S.cp("act", hb16[:, 0:4, :], H[:, CH_CQ:CH_CQ + 4, :], ck_, ["hb16"])
                    for c in range(4):
                        S.mm(ps[0][:, c * 128:(c + 1) * 128], cb16[:, 2, :], hb16[:, c, :], True, True, ["cb16", "hb16"], ["ps0"])
                    b4 = lambda a: a.unsqueeze(1).to_broadcast([128, 4, 128])
                    qkc = H[:, CH_CQ:CH_CQ + 4, :]
                    rt4 = t3[3]
                    r4 = S_r4
                    S.tt("dve", qkc, qkc, b4(cosC), ALU.mult, ck_ + [rk_], ck_)
                    S.tt("dve", r4[:, :, :], ps[0][:, :].rearrange("p (a b) -> p a b", b=128), b4(sinC), ALU.mult, ["ps0", rk_], ["r4"])
                    S.tt("dve", qkc, qkc, r4[:, :, :], ALU.add, ck_ + ["r4"], ck_)
                    if samp:
                        S.memset("dve", H[:, CH_CK:CH_CK + 2, 0:PADN], 0.0, ck_)
                    qd_t = cf[:, 12:15:2, :]
                    kd_t = cf[:, 13:16:2, :]
                    S.cp("act", qcb[:, :, :], qc, ck_, ["qcb"])
                    S.act(kcb[:, :, :], kc_, AF.Copy, ck_, ["kcb"], scale=HD ** -0.5)
                    S.tt("dve", qdb[:, :, :], qc, qd_t, ALU.mult, ck_ + ["cf"], ["qdb"])
                    S.tt("dve", kdb[:, :, :], kc_, kd_t, ALU.mult, ck_ + ["cf"], ["kdb"])
                    for p in range(2):
                        S.tr(psb[1][:, p * 128:(p + 1) * 128], kdb[:, p, :], identb[:, :], ["kdb", "identb"], ["ps1"])
                    S.cp("dve", kdtok[:, :], psb[1][:, 0:256], ["ps1"], ["kdtok"])
                    for h in range(4):
                        p, hb = h // 2, 64 * (h % 2)
                        hs = slice(hb, hb + 64)
                        bnk = 2 + h % 2
                        bk = "ps%d" % bnk
                        S.mm(ps[bnk][:, 0:128], kcb[hs, p, :], qcb[hs, p, :], True, True, ["kcb", "qcb"], [bk + "a"])
                        ab = attb[h % 2]
                        S.tt("dve", ab[:, :], ps[bnk][:, 0:128], cf[:, CI["dmask"] + h, :], ALU.mult, [bk + "a", "cf"], [n(ab)])
                        oreg = ps[4][hs, p * 128:(p + 1) * 128]
                        S.mm(oreg, vcb[:, h * 64:(h + 1) * 64], ab[:, :], True, False, ["vcb", n(ab)], ["ps4"])
                        S.mm(oreg, RbP[h][:, :], qdb[:, p, :], False, True, ["RbP%d" % h, "qdb"], ["ps4"])
                        S.mm(ps[bnk][hs, 128:192], kdtok[:, h * 64:(h + 1) * 64], vcb[:, h * 64:(h + 1) * 64], True, True,
                             ["kdtok", "vcb"], [bk + "s"])
                        S.stt(R[hs, p, :], R[hs, p, :], GAMMA[h] ** 128.0, ps[bnk][hs, 128:192], ALU.mult, ALU.add, ["R", bk + "s"], ["R"])
                        S.cp("act", RbP[h][hs, :], R[hs, p, :], ["R"], ["RbP%d" % h])
                    oc, sq2, mn2, vr2 = c2
                    v2 = lambda b: ps[b][:, 0:256].rearrange("p (a b) -> p a b", b=128)
                    S.cp("act", oc[:, :, :], v2(4), ["ps4"], [n(oc)])
                    S.act(sqb[:, 0:2, :], oc[:, :, :], AF.Square, [n(oc)], ["sqb"])
                    S.cp("dve", osb[:, 0:2, :], oc[:, :, :], [n(oc)], ["osb"])
                    for p in range(2):
                        S.mm(ps[0][:, p * 128:(p + 1) * 128], cb16[:, 0, :], osb[:, p, :], True, True, ["cb16", "osb"], ["ps0"])
                        S.mm(ps[1][:, p * 128:(p + 1) * 128], cb16[:, 0, :], sqb[:, p, :], True, True, ["cb16", "sqb"], ["ps1"])
                    S.ts("dve", mn2[:, :, :], v2(0), 1.0 / 64, None, ALU.mult, None, ["ps0"], [n(mn2)])
                    S.act(vr2[:, :, :], mn2[:, :, :], AF.Square, [n(mn2)], [n(vr2)])
                    S.stt(vr2[:, :, :], v2(1), 1.0 / 64, vr2[:, :, :], ALU.mult, ALU.subtract, ["ps1", n(vr2)], [n(vr2)])
                    S.ts("dve", vr2[:, :, :], vr2[:, :, :], LN_EPS, None, ALU.add, None, [n(vr2)], [n(vr2)])
                    S.act(vr2[:, :, :], vr2[:, :, :], AF.Ln, [n(vr2)], [n(vr2)])
                    S.act(vr2[:, :, :], vr2[:, :, :], AF.Exp, [n(vr2)], [n(vr2)], scale=-0.5)
                    S.tt("dve", oc[:, :, :], oc[:, :, :], mn2[:, :, :], ALU.subtract, [n(oc), n(mn2)], [n(oc)])
                    S.tt("dve", oc[:, :, :], oc[:, :, :], vr2[:, :, :], ALU.mult, [n(oc), n(vr2)], [n(oc)])
                    bcc = lambda k: prc[:, k, :].unsqueeze(2).to_broadcast([128, 2, 128])
                    S.tt("dve", oc[:, :, :], oc[:, :, :], bcc(0), ALU.mult, [n(oc), "prc"], [n(oc)])
                    S.tt("dve", oc[:, :, :], oc[:, :, :], bcc(1), ALU.add, [n(oc), "prc"], [n(oc)])
                    S.act(sq2[:, :, :], gc, AF.Exp, ["H5"], [n(sq2)], scale=-1.0)
                    S.ts("dve", sq2[:, :, :], sq2[:, :, :], 1.0, None, ALU.add, None, [n(sq2)], [n(sq2)])
                    S.op("dve", lambda e: e.reciprocal(out=sq2[:, :, :], in_=sq2[:, :, :]), [n(sq2)], [n(sq2)])
                    S.tt("dve", sq2[:, :, :], sq2[:, :, :], gc, ALU.mult, [n(sq2), "H5"], [n(sq2)])
                    S.tt("dve", mixT[k2][:, 6:8, :], oc[:, :, :], sq2[:, :, :], ALU.mult, [n(oc), n(sq2)], [mk2 + "c"])
                    if _STOP <= 12:
                        return
                    S.dma("pool", mixs[:, :, gi * 128:(gi + 1) * 128].rearrange("k p t -> p k t"), mixT[k2][:, :, :], "stm%d" % k2,
                          reads=[mk2 + "a", mk2 + "b", mk2 + "c"])
                    last = samp or ti == NT - 1
                    if last:
                        for p in range(3):
                            S.tr(ps[6][0:64, p * 128:(p + 1) * 128], U[:, p, :], ident, ["U", "cf"], ["ps6"])
                        S.cp("dve", cstage[0:64, :], ps[6][0:64, 0:384], ["ps6"], ["cstage"])
                        dst = o_swkv[l, seq] if samp else o_pwkv[l]
                        S.dma("pool", dst.rearrange("h v k -> v h k"), cstage[0:64, :].rearrange("v (h k) -> v h k", k=64), "sts", reads=["cstage"])
                        for p in range(2):
                            for hh in range(2):
                                dst = o_sret[l, seq, 2 * p + hh] if samp else o_pret[l, 2 * p + hh]
                                S.dma("pool", dst, R[hh * 64:(hh + 1) * 64, p, :], "sts", reads=["R"])

                for k_ in range(3):
                    S.memset("pool", rhsb[k_][:, :], 0.0, [S.kn(rhsb[k_])])
                    S.memset("pool", umb[k_][:, :], 0.0, [S.kn(umb[k_])])
                onesT = S.sb("onesT", [128, 128])
                S.memset("dve", onesT[:, :], 1.0, ["onesT"])
                S_r4 = S.sb("r4", [128, 4, 128])

                for ti in range(NT):
                    process_tile("p", ti, NT, ti)
                for sb_ in range(NSB):
                    if not _NOSAMP:
                        process_tile(sb_, 0, 1, NT + sb_)
                while bg_jobs:
                    d_, s_src = bg_jobs.pop(0)
                    S.dma("pool", d_, s_src, "bgc")
                S.barrier()
            with ExitStack() as es2:
                S.es = es2
                wo = S.sb("wo", [128, 8, D], BF16)
                wg = S.sb("wg", [128, 8, DFF], BF16)
                wu = S.sb("wu", [128, 8, DFF], BF16)
                wd = S.sb("wd", [128, NFC, D], BF16)
                lnp = S.sb("lnp", [128, 4, D])
                identf = S.sb("identf", [128, 128])
                mh1 = S.sb("mh1", [128, 1])
                xt2 = [S.sb("x2t%d" % k, [128, D]) for k in range(2)]
                mT2 = [S.sb("mT2_%d" % k, [128, 8, 128], BF16) for k in range(2)]
                pre = S.sb("pre", [128, D])
                x1 = S.sb("x1", [128, D])
                x1T = S.sb("x1T", [128, 8, 128], BF16)
                aT = S.sb("aT", [128, NFC, 128], BF16)
                sg = [S.sb("sg%d" % k, [128, 512]) for k in range(2)]
                outt = [S.sb("outt%d" % k, [128, D]) for k in range(1)]
                atok = S.sb("atok", [128, DFF], BF16)
                identb2 = S.sb("identb2", [128, 128], BF16)
                st6 = S.sb("st6", [128, 12])
                mv = S.sb("mv", [128, 4])

                S.dma("sp", wo[:, :, :], wob[l].rearrange("(k p) d -> p k d", p=128), "w2", writes=["wo"])
                for kc in range(0, 8, 2):
                    S.dma("sp", wg[:, kc:kc + 2, :], wgb[l, kc * 128:(kc + 2) * 128, :].rearrange("(k p) d -> p k d", p=128), "w2", writes=["wg"])
                    S.dma("act", wu[:, kc:kc + 2, :], wub[l, kc * 128:(kc + 2) * 128, :].rearrange("(k p) d -> p k d", p=128), "w2", writes=["wu"])
                S.dma("sp", wd[:, 0:11, :], wdb[l, 0:11 * 128, :].rearrange("(k p) d -> p k d", p=128), "w2", writes=["wd"])
                S.dma("act", wd[:, 11:22, :], wdb[l, 11 * 128:22 * 128, :].rearrange("(k p) d -> p k d", p=128), "w2", writes=["wd"])
                if l + 1 < DEPTH:
                    for kc in range(8):
                        bg_jobs.append((winb[kc * 128:(kc + 1) * 128, :], w_in[l + 1, kc * 128:(kc + 1) * 128, :]))
                for k, prm in enumerate([ln1_g, ln1_b, ln2_g, ln2_b]):
                    S.dma("sp", lnp[:, k, :], prm[l:l + 1, :].partition_broadcast(128), "c2", writes=["lnp"])
                S.dma("sp", identf[:, :], c_f[:, 0, :], "c2", writes=["identf"])
                S.memset("dve", mh1[:, :], -0.5, ["mh1"])
                S.cp("dve", identb2[:, :], identf[:, :], ["identf"], ["identb2"])

                def layer_norm(src, dst, gk, bk_, eps):
                    for c in range(2):
                        S.op("dve", lambda e, c=c: e.bn_stats(out=st6[:, c * 6:(c + 1) * 6], in_=src[:, c * 512:(c + 1) * 512]),
                             [S.kn(src)], ["st6"])
                    S.op("dve", lambda e: e.bn_aggr(out=mv[:, 0:2], in_=st6[:, 0:12]), ["st6"], ["mv"])
                    S.ts("dve", mv[:, 2:3], mv[:, 1:2], eps, None, ALU.add, None, ["mv"], ["mv"])
                    S.act(mv[:, 3:4], mv[:, 2:3], AF.Ln, ["mv"], ["mv"])
                    S.act(mv[:, 3:4], mv[:, 3:4], AF.Exp, ["mv"], ["mv"], scale=-0.5)
                    S.ts("dve", dst[:, :], src[:, :], mv[:, 0:1], mv[:, 3:4], ALU.subtract, ALU.mult, [S.kn(src), "mv"], [S.kn(dst)])
                    S.tt("dve", dst[:, :], dst[:, :], lnp[:, gk, :], ALU.mult, [S.kn(dst), "lnp"], [S.kn(dst)])
                    S.tt("dve", dst[:, :], dst[:, :], lnp[:, bk_, :], ALU.add, [S.kn(dst), "lnp"], [S.kn(dst)])

                for gi in range(NTS if not _NOP2 else 0):
                    k2 = gi % 2
                    samp = gi >= NT
                    xk = "x2t%d" % k2
                    mk = "mT2_%d" % k2
                    if bg_jobs:
                        d_, s_src = bg_jobs.pop(0)
                        S.dma("pool", d_, s_src, "bgc")
                    if samp and l == 0:
                        S.memset("pool", xt2[k2][:, :], 0.0, [xk])
                        S.dma("sp", xt2[k2][PADN:128, :], xs[(gi - NT) * 4:(gi - NT + 1) * 4, :], "l2x%d" % k2, writes=[xk])
                    else:
                        S.dma("sp", xt2[k2][:, :], x_src(l, gi), "l2x%d" % k2, writes=[xk])
                    S.dma("sp", mT2[k2][:, :, :], mixs[:, :, gi * 128:(gi + 1) * 128].rearrange("k p t -> p k t"), "l2x%d" % k2, writes=[mk])
                    for half in range(2):
                        for kc in range(8):
                            S.mm(ps[half][:, :], mT2[k2][:, kc, :], wo[:, kc, half * 512:(half + 1) * 512], kc == 0, kc == 7,
                                 [mk, "wo"], ["ps%d" % half])
                        S.stt(pre[:, half * 512:(half + 1) * 512], xt2[k2][:, half * 512:(half + 1) * 512], ALPHA, ps[half][:, :],
                              ALU.mult, ALU.add, [xk, "ps%d" % half], ["pre"])
                    layer_norm(pre, x1, 0, 1, LN_EPS)
                    for half in range(2):
                        for j in range(4):
                            kc = half * 4 + j
                            S.tr(ps[2 + half][:, j * 128:(j + 1) * 128], x1[:, kc * 128:(kc + 1) * 128], identf[:, :], ["x1", "identf"],
                                 ["ps%d" % (2 + half)])
                        S.cp("act", x1T[:, half * 4:(half + 1) * 4, :], ps[2 + half][:, :].rearrange("p (a b) -> p a b", b=128),
                             ["ps%d" % (2 + half)], ["x1T"])
                    NG = (DFF + 511) // 512
                    pend_t = None

                    def emit_tr(g_, c0, cw):
                        nch = cw // 128
                        bt = 2 + g_ % 2
                        for j in range(nch):
                            S.tr(psb[bt][:, j * 128:(j + 1) * 128], atok[:, c0 + j * 128:c0 + (j + 1) * 128], identb2[:, :],
                                 ["atok%d" % g_, "identb2"], ["ps%d" % bt])
                        S.cp("act", aT[:, g_ * 4:g_ * 4 + nch, :], psb[bt][:, 0:nch * 128].rearrange("p (a b) -> p a b", b=128),
                             ["ps%d" % bt], ["aT"])
                    for g_ in range(NG):
                        c0 = g_ * 512
                        cw = min(512, DFF - c0)
                        bg, bu = 4 + (g_ % 2) * 2, 5 + (g_ % 2) * 2
                        for kc in range(8):
                            S.mm(ps[bg][:, 0:cw], x1T[:, kc, :], wg[:, kc, c0:c0 + cw], kc == 0, kc == 7, ["wg", "x1T"], ["ps%d" % bg])
                        for kc in range(8):
                            S.mm(ps[bu][:, 0:cw], x1T[:, kc, :], wu[:, kc, c0:c0 + cw], kc == 0, kc == 7, ["wu", "x1T"], ["ps%d" % bu])
                        s_ = sg[g_ % 2]
                        S.act(s_[:, 0:cw], ps[bg][:, 0:cw], AF.Silu, ["ps%d" % bg], [S.kn(s_)])
                        S.tt("dve", atok[:, c0:c0 + cw], s_[:, 0:cw], ps[bu][:, 0:cw], ALU.mult, [S.kn(s_), "ps%d" % bu], ["atok%d" % g_])
                        if pend_t is not None:
                            emit_tr(*pend_t)
                        pend_t = (g_, c0, cw)
                    emit_tr(*pend_t)
                    for half in range(2):
                        for c in range(NFC):
                            S.mm(ps[half][:, :], aT[:, c, :], wd[:, c, half * 512:(half + 1) * 512], c == 0, c == NFC - 1,
                                 ["aT", "wd"], ["ps%d" % half])
                        S.stt(pre[:, half * 512:(half + 1) * 512], x1[:, half * 512:(half + 1) * 512], ALPHA, ps[half][:, :],
                              ALU.mult, ALU.add, ["x1", "ps%d" % half], ["pre"])
                    ot = outt[0]
                    layer_norm(pre, ot, 2, 3, LN_EPS)
                    if l < DEPTH - 1:
                        S.dma("pool", x2s[gi * 128:(gi + 1) * 128, :], ot[:, :], "st2_%d" % k2, reads=[S.kn(ot)])
                    elif samp:
                        sb_ = gi - NT
                        S.dma("pool", y_s[sb_ * 4:(sb_ + 1) * 4, :], ot[PADN:128, :], "st2_%d" % k2, reads=[S.kn(ot)])
                    else:
                        S.dma("pool", y_p[gi * 128:(gi + 1) * 128, :], ot[:, :], "st2_%d" % k2, reads=[S.kn(ot)])
                while bg_jobs:
                    d_, s_src = bg_jobs.pop(0)
                    S.dma("pool", d_, s_src, "bgc")
                S.barrier()
        S.es = es_top
    return nc


_W_NAMES = ["w_in", "rwkv_mu", "rwkv_w0", "rwkv_w_lora", "rwkv_a0", "rwkv_a_lora", "rwkv_g_lora", "rwkv_k_k", "rwkv_k_a",
            "rwkv_r_k", "rwkv_gn_g", "rwkv_gn_b", "ret_gn_g", "ret_gn_b", "w_out", "ln1_g", "ln1_b", "w_ffn_gate", "w_ffn_up",
            "w_ffn_down", "ln2_g", "ln2_b"]


def run(inputs, T, n_cores=8, trace=False):
    f = lambda a: np.ascontiguousarray(np.asarray(a, dtype=np.float32))
    x_prompt = f(inputs["x_prompt"])
    x_sample = f(inputs["x_sample"])
    nb = x_prompt.shape[0]
    consts = make_consts(T)
    shared = {k: f(inputs[k]) for k in _W_NAMES}
    shared["rwkv_r_k"] = shared["rwkv_r_k"].reshape(DEPTH, W_A)
    shared["cf"] = consts["cf"]
    shared["amask"] = consts["amask"]
    shared["rot"] = consts["rot"]
    st_shift, st_wkv = f(inputs["state_rwkv_shift"]), f(inputs["state_rwkv_wkv"])
    ckk, cvv, st_ret = f(inputs["cache_win_k"]), f(inputs["cache_win_v"]), f(inputs["state_ret"])
    in_maps = []
    for c in range(n_cores):
        b = (c * nb) // n_cores
        s0 = c * NSB
        m = dict(shared)
        m["xp"] = x_prompt[b]
        m["xs"] = np.ascontiguousarray(x_sample[s0:s0 + NSB].reshape(NSB * 4, D))
        m["st_shift"] = np.ascontiguousarray(st_shift[:, s0:s0 + NSB])
        m["st_wkv"] = np.ascontiguousarray(st_wkv[:, s0:s0 + NSB])
        m["ck"] = np.ascontiguousarray(ckk[:, s0:s0 + NSB].reshape(DEPTH, NSB, WIN, W_B))
        m["cv"] = np.ascontiguousarray(cvv[:, s0:s0 + NSB].reshape(DEPTH, NSB, WIN, W_B))
        m["st_ret"] = np.ascontiguousarray(st_ret[:, s0:s0 + NSB])
        in_maps.append(m)
    nc = build(T)
    res = run_bass_kernel_spmd(nc, in_maps, core_ids=list(range(n_cores)), trace=trace)
    R = res.results
    per = n_cores // nb
    WK = min(WIN, T)
    own = [R[b * per] for b in range(nb)]
    y_prompt = np.stack([o["y_p"] for o in own])
    y_sample = np.concatenate([r["y_s"].reshape(NSB, 4, D) for r in R], axis=0)
    p_shift = np.stack([o["p_shift"] for o in own], axis=1)
    p_wkv = np.stack([o["p_wkv"] for o in own], axis=1)
    p_wk = np.stack([o["p_wk"].reshape(DEPTH, WK, H_B, HD) for o in own], axis=1)
    p_wv = np.stack([o["p_wv"].reshape(DEPTH, WK, H_B, HD) for o in own], axis=1)
    p_ret = np.stack([o["p_ret"] for o in own], axis=1)
    s_shift = np.concatenate([r["s_shift"] for r in R], axis=1)
    s_wkv = np.concatenate([r["s_wkv"] for r in R], axis=1)
    s_wk = np.concatenate([r["s_wk"].reshape(DEPTH, NSB, WIN, H_B, HD) for r in R], axis=1)
    s_wv = np.concatenate([r["s_wv"].reshape(DEPTH, NSB, WIN, H_B, HD) for r in R], axis=1)
    s_ret = np.concatenate([r["s_ret"] for r in R], axis=1)
    outs = (y_prompt, y_sample, p_shift, p_wkv, p_wk, p_wv, p_ret, s_shift, s_wkv, s_wk, s_wv, s_ret)
    return tuple(np.ascontiguousarray(o, dtype=np.float32) for o in outs), res


def kernel(**inputs):
    T = int(np.asarray(inputs["x_prompt"]).shape[1])
    outs, _ = run(inputs, T)
    return outs
```
